# Optimizing a Trainium2 kernel written in Bass

```python
import math
import jax
import jax.numpy as jnp
from jax import lax
import numpy as np

D_MODEL = 1024
BATCH = 8
SEQ = 4096
DEPTH = 4

GRID_W = 64
CTX_LEN = 256
EPS = 1e-6

F_GROUPS = 4
F_GROUP_W = D_MODEL // 8
F_W = F_GROUPS * F_GROUP_W

ATT_HEADS = 4
ATT_QK = D_MODEL // 16
ATT_V = 2 * ATT_QK
ATT_QW = ATT_HEADS * 2 * ATT_QK
ATT_VW = ATT_HEADS * ATT_V
ROPE_AXIS = ATT_QK // 2
ROPE_BASE = 10000.0
Q_BLOCK = 128

D_INNER = D_MODEL // 2
SSD_P = 64
SSD_HEADS = D_INNER // SSD_P
SSD_GROUPS = 2
D_STATE = 128
CONV_W = 5
CHUNK = 128
XBC_W = D_INNER + 2 * SSD_GROUPS * D_STATE
DT_W = 2 * SSD_HEADS

N_BRANCH = 3
MERGE_W = N_BRANCH * D_MODEL
IN_SPLITS = (F_W, F_W, ATT_QW, ATT_QW, ATT_VW, ATT_VW, D_INNER, XBC_W, DT_W, MERGE_W)
IN_W = 2 * F_W + 2 * ATT_QW + 2 * ATT_VW + D_INNER + XBC_W + DT_W + MERGE_W

kernel_name = 'hybrid_fourier_diffattn_ssd_dit'


def rmsnorm(x, w):
    xf = x.astype(jnp.float32)
    y = xf * lax.rsqrt(jnp.mean(xf * xf, axis=-1, keepdims=True) + EPS)
    return (y * w.astype(jnp.float32)).astype(x.dtype)


def axial_rope_tables(n, dtype):
    rows = n // GRID_W
    row = jnp.repeat(jnp.arange(rows, dtype=jnp.float32), GRID_W)
    col = jnp.tile(jnp.arange(GRID_W, dtype=jnp.float32), rows)
    freqs = ROPE_BASE ** (-jnp.arange(0, ROPE_AXIS, 2, dtype=jnp.float32) / ROPE_AXIS)
    ang_r = row[:, None] * freqs
    ang_c = col[:, None] * freqs
    ang = jnp.concatenate([ang_r, ang_r, ang_c, ang_c], axis=-1)
    return jnp.cos(ang).astype(dtype), jnp.sin(ang).astype(dtype)


def apply_rope(t, cos, sin):
    tt = t.reshape(t.shape[:-1] + (2, 2, ROPE_AXIS // 2))
    rot = jnp.concatenate([-tt[..., 1:, :], tt[..., :1, :]], axis=-2).reshape(t.shape)
    return t * cos[None, :, None, None, :] + rot * sin[None, :, None, None, :]


def diff_attention(q, k, v, lam, lam_init, subln_w):
    s = jnp.einsum('bqhmd,bkhmd->bhmqk', q, k).astype(jnp.float32) * (ATT_QK ** -0.5)
    p = jax.nn.softmax(s, axis=-1)
    a = p[:, :, 0] - lam * p[:, :, 1]
    o = jnp.einsum('bhqk,bkhe->bqhe', a.astype(v.dtype), v)
    return rmsnorm(o, subln_w) * (1.0 - lam_init)


def fourier_mix(u):
    b, n, _ = u.shape
    uf = u.astype(jnp.float32).reshape(b, n, F_GROUPS, F_GROUP_W)
    y = jnp.fft.fftn(uf, axes=(1, 3), norm='ortho').real
    return y.reshape(b, n, F_W).astype(u.dtype)


def depthwise_conv(x, w, bias):
    y = lax.conv_general_dilated(x, w[:, None, :], window_strides=(1,),
                                 padding=[(CONV_W // 2, CONV_W // 2)],
                                 dimension_numbers=('NWC', 'WIO', 'NWC'),
                                 feature_group_count=x.shape[-1])
    return y + bias


def ssd_scan(x, dt, a, bm, cm, init):
    f32 = jnp.float32
    b, L, h, p = x.shape
    g, n = bm.shape[2], bm.shape[3]
    r = h // g
    c = L // CHUNK
    xd = (x.astype(f32) * dt[..., None]).reshape(b, c, CHUNK, g, r, p)
    ad = (dt * a).reshape(b, c, CHUNK, g, r).transpose(0, 3, 4, 1, 2)
    a_cs = jnp.cumsum(ad, axis=-1)
    bc = bm.astype(f32).reshape(b, c, CHUNK, g, n)
    cc = cm.astype(f32).reshape(b, c, CHUNK, g, n)
    tri = jnp.tril(jnp.ones((CHUNK, CHUNK), dtype=bool))
    diff = a_cs[..., :, None] - a_cs[..., None, :]
    lmat = jnp.exp(jnp.where(tri, diff, -jnp.inf))
    cb = jnp.einsum('bclgn,bcsgn->bcgls', cc, bc)
    y_diag = jnp.einsum('bcgls,bgrcls,bcsgrp->bclgrp', cb, lmat, xd)
    decay_states = jnp.exp(a_cs[..., -1:] - a_cs)
    states = jnp.einsum('bclgn,bgrcl,bclgrp->bcgrpn', bc, decay_states, xd)
    states = jnp.concatenate([init.reshape(b, g, r, p, n)[:, None], states], axis=1)
    chunk_a = jnp.pad(a_cs[..., -1], [(0, 0), (0, 0), (0, 0), (1, 0)])
    cs2 = jnp.cumsum(chunk_a, axis=-1)
    tri2 = jnp.tril(jnp.ones((c + 1, c + 1), dtype=bool))
    decay_chunk = jnp.exp(jnp.where(tri2, cs2[..., :, None] - cs2[..., None, :], -jnp.inf))
    new_states = jnp.einsum('bgrzk,bkgrpn->bzgrpn', decay_chunk, states)
    prev_states = new_states[:, :-1]
    final = new_states[:, -1]
    y_off = jnp.einsum('bclgn,bcgrpn,bgrcl->bclgrp', cc, prev_states, jnp.exp(a_cs))
    y = (y_diag + y_off).reshape(b, L, h, p)
    return y, final.reshape(b, h, p, n)


def ssd_branch(xbc, dt_raw, conv_w, conv_b, a_log, dt_bias, d_skip, init_f, init_b):
    b, n, _ = xbc.shape
    f32 = jnp.float32
    xbc = jax.nn.silu(depthwise_conv(xbc, conv_w, conv_b))
    xs, bm, cm = jnp.split(xbc, [D_INNER, D_INNER + SSD_GROUPS * D_STATE], axis=-1)
    xs = xs.reshape(b, n, SSD_HEADS, SSD_P)
    bm = bm.reshape(b, n, SSD_GROUPS, D_STATE)
    cm = cm.reshape(b, n, SSD_GROUPS, D_STATE)
    dt = jax.nn.softplus(dt_raw.astype(f32).reshape(b, n, 2, SSD_HEADS) + dt_bias.astype(f32))
    a = -jnp.exp(a_log.astype(f32))
    y_f, s_f = ssd_scan(xs, dt[:, :, 0], a[0], bm, cm, init_f)
    rev = lambda t: jnp.flip(t, axis=1)
    y_b, s_b = ssd_scan(rev(xs), rev(dt[:, :, 1]), a[1], rev(bm), rev(cm), init_b)
    y = y_f + rev(y_b) + d_skip.astype(f32)[:, None] * xs.astype(f32)
    return y.reshape(b, n, D_INNER), s_f, s_b


def branch_merge(f_u, f_g, att_o, a_g, ssd_y, z, gates, w_of, w_oa, w_os, w_out, ssd_norm_w):
    b, n, _ = f_u.shape
    y_f = (fourier_mix(f_u) * jax.nn.silu(f_g)) @ w_of
    y_a = (att_o.reshape(b, n, ATT_VW) * jax.nn.silu(a_g)) @ w_oa
    y_s = rmsnorm(ssd_y * jax.nn.silu(z.astype(jnp.float32)), ssd_norm_w).astype(z.dtype) @ w_os
    g = jax.nn.sigmoid(gates.astype(jnp.float32)).astype(f_u.dtype).reshape(b, n, N_BRANCH, D_MODEL)
    y = g[:, :, 0] * y_f + g[:, :, 1] * y_a + g[:, :, 2] * y_s
    return y @ w_out


def setup_inputs(seed: int = 0) -> dict:
    key = jax.random.key(seed)
    ks = jax.random.split(key, 24)
    f32 = jnp.float32
    nrm = lambda k, shape, scale: jax.random.normal(k, shape, f32) * scale
    x = nrm(ks[0], (BATCH, SEQ, D_MODEL), 1.0)
    c = nrm(ks[1], (BATCH, D_MODEL), 1.0)
    ctx = nrm(ks[2], (BATCH, CTX_LEN, D_MODEL), 1.0)
    c_ctx = nrm(ks[3], (D_MODEL,), 1.0)
    w_mod = nrm(ks[4], (DEPTH, D_MODEL, 3 * D_MODEL), 0.5 * D_MODEL ** -0.5)
    b_mod = nrm(ks[5], (DEPTH, 3 * D_MODEL), 0.02)
    norm_w = 1.0 + nrm(ks[6], (DEPTH, D_MODEL), 0.02)
    w_in = nrm(ks[7], (DEPTH, D_MODEL, IN_W), D_MODEL ** -0.5)
    conv_w = nrm(ks[8], (DEPTH, CONV_W, XBC_W), CONV_W ** -0.5)
    conv_b = nrm(ks[9], (DEPTH, XBC_W), 0.02)
    a_log = jnp.log(jax.random.uniform(ks[10], (DEPTH, 2, SSD_HEADS), f32, 1.0, 16.0))
    dt0 = jnp.exp(jax.random.uniform(ks[11], (DEPTH, 2, SSD_HEADS), f32,
                                     math.log(1e-3), math.log(1e-1)))
    dt_bias = dt0 + jnp.log(-jnp.expm1(-dt0))
    d_skip = 1.0 + nrm(ks[12], (DEPTH, SSD_HEADS), 0.1)
    ssd_norm_w = 1.0 + nrm(ks[13], (DEPTH, D_INNER), 0.02)
    lam = nrm(ks[14], (DEPTH, 4, ATT_QK), 0.1)
    subln_w = 1.0 + nrm(ks[15], (DEPTH, ATT_V), 0.02)
    w_of = nrm(ks[16], (DEPTH, F_W, D_MODEL), F_W ** -0.5)
    w_oa = nrm(ks[17], (DEPTH, ATT_VW, D_MODEL), ATT_VW ** -0.5)
    w_os = nrm(ks[18], (DEPTH, D_INNER, D_MODEL), D_INNER ** -0.5)
    w_out = nrm(ks[19], (DEPTH, D_MODEL, D_MODEL), D_MODEL ** -0.5)
    norm_f = 1.0 + nrm(ks[20], (D_MODEL,), 0.02)
    return {'x': x, 'c': c, 'ctx': ctx, 'c_ctx': c_ctx, 'w_mod': w_mod, 'b_mod': b_mod,
            'norm_w': norm_w, 'w_in': w_in, 'conv_w': conv_w, 'conv_b': conv_b,
            'a_log': a_log, 'dt_bias': dt_bias, 'd_skip': d_skip, 'ssd_norm_w': ssd_norm_w,
            'lam': lam, 'subln_w': subln_w, 'w_of': w_of, 'w_oa': w_oa, 'w_os': w_os,
            'w_out': w_out, 'norm_f': norm_f}


def reference(x, c, ctx, c_ctx, w_mod, b_mod, norm_w, w_in, conv_w, conv_b, a_log, dt_bias,
              d_skip, ssd_norm_w, lam, subln_w, w_of, w_oa, w_os, w_out, norm_f):
    f32 = jnp.float32
    b, n, _ = x.shape
    n_ctx = ctx.shape[1]
    n_blocks = n // Q_BLOCK
    cos, sin = axial_rope_tables(n, x.dtype)
    split_at = [int(v) for v in np.cumsum(IN_SPLITS)[:-1]]
    zero_state = jnp.zeros((b, SSD_HEADS, SSD_P, D_STATE), f32)
    xc = ctx
    for l in range(DEPTH):
        lam_init = 0.8 - 0.6 * math.exp(-0.3 * l)
        lp = lam[l].astype(f32)
        lam_l = jnp.exp(jnp.sum(lp[0] * lp[1])) - jnp.exp(jnp.sum(lp[2] * lp[3])) + lam_init
        mod_l = jax.nn.silu(c) @ w_mod[l] + b_mod[l]
        mod_c = jax.nn.silu(c_ctx) @ w_mod[l] + b_mod[l]
        sh_l, sc_l, g_l = jnp.split(mod_l[:, None, :], 3, axis=-1)
        sh_c, sc_c, g_c = jnp.split(mod_c, 3, axis=-1)
        h_l = rmsnorm(x, norm_w[l]) * (1.0 + sc_l) + sh_l
        h_c = rmsnorm(xc, norm_w[l]) * (1.0 + sc_c) + sh_c
        (fu_c, fg_c, q_c, k_c, v_c, ag_c, z_c, xbc_c, dt_c, gt_c) = jnp.split(h_c @ w_in[l], split_at, axis=-1)
        (fu_l, fg_l, q_l, k_l, v_l, ag_l, z_l, xbc_l, dt_l, gt_l) = jnp.split(h_l @ w_in[l], split_at, axis=-1)
        q_c = q_c.reshape(b, n_ctx, ATT_HEADS, 2, ATT_QK)
        k_c = k_c.reshape(b, n_ctx, ATT_HEADS, 2, ATT_QK)
        v_c = v_c.reshape(b, n_ctx, ATT_HEADS, ATT_V)
        q_l = apply_rope(q_l.reshape(b, n, ATT_HEADS, 2, ATT_QK), cos, sin)
        k_l = apply_rope(k_l.reshape(b, n, ATT_HEADS, 2, ATT_QK), cos, sin)
        v_l = v_l.reshape(b, n, ATT_HEADS, ATT_V)
        k_all = jnp.concatenate([k_c, k_l], axis=1)
        v_all = jnp.concatenate([v_c, v_l], axis=1)
        q_blocks = jnp.moveaxis(q_l.reshape(b, n_blocks, Q_BLOCK, ATT_HEADS, 2, ATT_QK), 1, 0)
        att_l = lax.map(lambda qb: diff_attention(qb, k_all, v_all, lam_l, lam_init, subln_w[l]), q_blocks)
        att_l = jnp.moveaxis(att_l, 0, 1).reshape(b, n, ATT_HEADS, ATT_V)
        ys_c, sf_c, sb_c = ssd_branch(xbc_c, dt_c, conv_w[l], conv_b[l], a_log[l], dt_bias[l],
                                      d_skip[l], zero_state, zero_state)
        ys_l, _, _ = ssd_branch(xbc_l, dt_l, conv_w[l], conv_b[l], a_log[l], dt_bias[l],
                                d_skip[l], sf_c, sb_c)
        out_l = branch_merge(fu_l, fg_l, att_l, ag_l, ys_l, z_l, gt_l,
                             w_of[l], w_oa[l], w_os[l], w_out[l], ssd_norm_w[l])
        if l < DEPTH - 1:
            att_c = diff_attention(q_c, k_c, v_c, lam_l, lam_init, subln_w[l])
            out_c = branch_merge(fu_c, fg_c, att_c, ag_c, ys_c, z_c, gt_c,
                                 w_of[l], w_oa[l], w_os[l], w_out[l], ssd_norm_w[l])
            xc = xc + g_c * out_c
        x = x + g_l * out_l
    return rmsnorm(x, norm_f)
```

```python
import math
from contextlib import ExitStack

import numpy as np
import ml_dtypes

import concourse.bass as bass
import concourse.mybir as mybir
from concourse.bass_utils import run_bass_kernel_spmd

F32 = mybir.dt.float32
BF16 = mybir.dt.bfloat16
AF = mybir.ActivationFunctionType
ALU = mybir.AluOpType
AX = mybir.AxisListType

D = 1024
SEQ = 4096
NCTX = 256
T = SEQ + NCTX
NT = T // 128
DEPTH = 4
EPS = 1e-6
IN_W = 7696
C_FU, C_FG, C_Q, C_K, C_V, C_AG, C_Z, C_XBC, C_DT, C_GT = 0, 512, 1024, 1536, 2048, 2560, 3072, 3584, 4608, 4624

COMPUTE = ("tensor", "vector", "scalar", "gpsimd")
STREAMS = ("tensor", "vector", "scalar", "gpsimd", "sync")
NS = 16
EPOCH = 20000


class Buf:
    __slots__ = ("name", "w", "r")

    def __init__(self, name=""):
        self.name = name
        self.w = None
        self.r = []


class Op:
    __slots__ = ("stream", "fn", "is_dma", "signal", "waits", "n", "slot", "seq", "clock", "semi", "semv")


class Prog:
    def __init__(self):
        self.streams = {s: [] for s in STREAMS}
        self.known = {s: {c: 0 for c in COMPUTE} for s in STREAMS}
        self.known_dma = {s: {} for s in STREAMS}
        self.nseq = {c: 0 for c in COMPUTE}
        self.dma_ops = {s: [] for s in STREAMS}
        self.last = {c: None for c in COMPUTE}
        self.pending = {s: [] for s in STREAMS}

    def op(self, stream, fn, reads=(), writes=(), dma=False):
        o = Op()
        o.stream = stream
        o.fn = fn
        o.is_dma = dma
        o.signal = False
        o.waits = []
        deps = []
        for b in reads:
            if b.w is not None:
                deps.append(b.w)
        for b in writes:
            if b.w is not None:
                deps.append(b.w)
            deps.extend(b.r)
        if self.pending[stream]:
            deps.extend(self.pending[stream])
            self.pending[stream] = []
        if dma:
            n = len(self.dma_ops[stream])
            o.n = n
            o.slot = n % NS
            if n >= NS:
                deps.append(self.dma_ops[stream][n - NS])
            self.dma_ops[stream].append(o)
        else:
            self.nseq[stream] += 1
            o.seq = self.nseq[stream]
            self.last[stream] = o
        kn = self.known[stream]
        kd = self.known_dma[stream]
        for d in deps:
            if d.is_dma:
                key = (d.stream, d.slot)
                if kd.get(key, -1) < d.n:
                    kd[key] = d.n
                    o.waits.append(d)
                    for c, v in d.clock.items():
                        if kn[c] < v:
                            kn[c] = v
            else:
                if d.stream == "tensor" and stream == "tensor" and not dma:
                    continue
                if kn[d.stream] < d.seq:
                    o.waits.append(d)
                    d.signal = True
                    kn[d.stream] = d.seq
                    for c, v in d.clock.items():
                        if kn[c] < v:
                            kn[c] = v
        o.clock = dict(kn)
        for b in reads:
            b.r.append(o)
        for b in writes:
            b.w = o
            b.r = []
        self.streams[stream].append(o)
        return o

    def barrier(self):
        ops = []
        for c in COMPUTE:
            if self.last[c] is not None:
                ops.append(self.last[c])
        for s in STREAMS:
            ops.extend(self.dma_ops[s][-NS:])
        for s in STREAMS:
            self.pending[s] = list(ops)

    def emit(self, nc):
        self.barrier()
        nsem = {}
        for c in COMPUTE:
            cnt = 0
            for o in self.streams[c]:
                if o.is_dma:
                    continue
                if o.signal:
                    o.semi = cnt // EPOCH
                    o.semv = cnt % EPOCH + 1
                    cnt += 1
            nsem[c] = max(1, (cnt + EPOCH - 1) // EPOCH)
        with ExitStack() as es:
            csem = {c: [es.enter_context(nc.semaphore(f"c_{c}_{i}")) for i in range(nsem[c])] for c in COMPUTE}
            dsem = {s: [es.enter_context(nc.semaphore(f"d_{s}_{i}")) for i in range(NS)]
                    for s in STREAMS if self.dma_ops[s]}

            def wait(e, d):
                if d.is_dma:
                    e.wait_ge(dsem[d.stream][d.slot], 16 * (d.n // NS + 1))
                else:
                    e.wait_ge(csem[d.stream][d.semi], d.semv)

            def body_for(stream):
                def body(e):
                    kn = self.known[stream]
                    kd = self.known_dma[stream]
                    for o in self.streams[stream]:
                        for d in o.waits:
                            wait(e, d)
                        ins = o.fn(e)
                        if o.is_dma:
                            ins.then_inc(dsem[stream][o.slot], 16)
                        elif o.signal:
                            ins.then_inc(csem[stream][o.semi], 1)
                    for d in self.pending[stream]:
                        if d.is_dma:
                            if kd.get((d.stream, d.slot), -1) < d.n:
                                wait(e, d)
                        elif d.signal and not (d.stream == stream):
                            if kn[d.stream] < d.seq:
                                wait(e, d)
                return body

            for c in COMPUTE:
                pass
            with nc.Block() as block:
                block.sync(body_for("sync"))
                block.tensor(body_for("tensor"))
                block.vector(body_for("vector"))
                block.scalar(body_for("scalar"))
                block.gpsimd(body_for("gpsimd"))


class Tl:
    __slots__ = ("h", "b")

    def __init__(self, h, name=""):
        self.h = h
        self.b = Buf(name)


class Ctx:
    pass


def build_program(n_layers=DEPTH, stop_after=None, dbg=()):
    nc = bass.Bass("TRN2", target_bir_lowering=False)
    P = Prog()
    K = Ctx()
    K.nc, K.P = nc, P
    K.dbgset = set(dbg)

    def din(name, shape, dt=F32):
        return nc.dram_tensor(name, list(shape), dt, kind="ExternalInput").ap()

    def dscr(name, shape, dt=BF16):
        kind = "ExternalOutput" if name in dbg else "Internal"
        return nc.dram_tensor(name, list(shape), dt, kind=kind).ap()

    I = Ctx()
    I.xin = din("xin", [T, D])
    I.c_t = din("c_t", [128, 8])
    I.cctx_t = din("cctx_t", [128, 8])
    I.w_mod = din("w_mod", [DEPTH, D, 3 * D])
    I.b_mod = din("b_mod", [DEPTH, 3 * D])
    I.norm_w = din("norm_w", [DEPTH, D])
    I.w_in = din("w_in", [DEPTH, D, IN_W])
    I.convw_t = din("convw_t", [DEPTH, 128, 8, 5])
    I.convb_t = din("convb_t", [DEPTH, 128, 8])
    I.a_log = din("a_log", [DEPTH, 16])
    I.dt_bias = din("dt_bias", [DEPTH, 16])
    I.d_skip = din("d_skip", [DEPTH, 8])
    I.ssd_norm_w = din("ssd_norm_w", [DEPTH, 512])
    I.lam = din("lam", [DEPTH, 256])
    I.subln_w = din("subln_w", [DEPTH, 128])
    I.w_of = din("w_of", [DEPTH, 512, D])
    I.w_oa = din("w_oa", [DEPTH, 512, D])
    I.w_os = din("w_os", [DEPTH, 512, D])
    I.w_out = din("w_out", [DEPTH, D, D])
    I.norm_f = din("norm_f", [D])
    I.ident = din("ident", [128, 128], BF16)
    I.cosT = din("cosT", [128, SEQ])
    I.sinT = din("sinT", [128, SEQ])
    I.dftB = din("dftB", [128, 256], BF16)
    I.C4 = din("C4", [SEQ, SEQ], BF16)
    I.S4 = din("S4", [SEQ, SEQ], BF16)
    I.C2 = din("C2", [NCTX, NCTX], BF16)
    I.S2 = din("S2", [NCTX, NCTX], BF16)
    I.tri = din("tri", [128, 128])
    I.triT = din("triT", [128, 128])
    I.mask_f = din("mask_f", [128, 128])
    I.mask_b = din("mask_b", [128, 128])
    K.I = I
    out = nc.dram_tensor("out", [SEQ, D], F32, kind="ExternalOutput").ap()
    K.out = out

    S = Ctx()
    S.xs = dscr("xs", [T, D], F32)
    S.fuT = dscr("fuT", [512, T])
    S.fgT = dscr("fgT", [512, T])
    S.qT = dscr("qT", [512, T])
    S.kT = dscr("kT", [512, T])
    S.v_tm = dscr("v_tm", [T, 512])
    S.agT = dscr("agT", [512, T])
    S.z_tm = dscr("z_tm", [T, 512])
    S.xbcT = dscr("xbcT", [1024, T])
    S.gT = dscr("gT", [3072, T])
    S.ufT = dscr("ufT", [512, T])
    S.uaT = dscr("uaT", [512, T])
    S.usT = dscr("usT", [512, T])
    S.dbg = dscr("dbg", [128, 8192], F32)
    K.S = S

    with ExitStack() as top:
        ARENA = 52800
        arena = top.enter_context(nc.sbuf_tensor("arena", [128, ARENA], F32))
        K.base = 0
        K.off = 0

        def sb(name, shape, dt=F32):
            n = 1
            for v in shape[1:]:
                n *= v
            words = n if dt == F32 else (n + 1) // 2
            words = (words + 7) // 8 * 8
            assert K.off + words <= ARENA, (name, K.off, words)
            ap = arena[:, K.off:K.off + words]
            K.off += words
            if dt != F32:
                ap = ap.bitcast(dt)
            ap = ap[:, 0:n]
            if len(shape) == 3:
                ap = ap.rearrange("p (a b) -> p a b", a=shape[1], b=shape[2])
            elif len(shape) == 4:
                ap = ap.rearrange("p (a b c) -> p a b c", a=shape[1], b=shape[2], c=shape[3])
            return Tl(ap, name)
        K.sb = sb
        K.ps = [Tl(top.enter_context(nc.psum_tensor(f"ps{i}", [128, 512], F32)), f"ps{i}") for i in range(8)]
        K.psb = [K.ps[i].h.bitcast(BF16) for i in range(8)]
        K.ident = sb("ident", [128, 128], BF16)
        K.ones_bf = sb("ones_bf", [128, 128], BF16)
        K.ones_f = sb("ones_f", [128, 128], F32)
        K.dtraw = sb("dtraw", [128, NT, 16], F32)
        K.mod = {k: sb("mod_" + k, [128, D], F32) for k in ("sc_l", "sh_l", "g_l", "sc_c", "sh_c", "g_c")}
        K.base = K.off
        P.op("sync", lambda e: e.dma_start(out=K.ident.h, in_=I.ident), writes=[K.ident.b], dma=True)
        P.op("vector", lambda e: e.memset(K.ones_bf.h, 1.0), writes=[K.ones_bf.b])
        P.op("vector", lambda e: e.memset(K.ones_f.h, 1.0), writes=[K.ones_f.b])
        phases = [phase_mod, phase_a1, phase_a2, phase_b, phase_c, phase_d, phase_e]
        done = False
        for l in range(n_layers):
            for ph in phases:
                ph(K, l, l == DEPTH - 1)
                P.barrier()
                if stop_after == (l, ph.__name__):
                    done = True
                    break
            if done:
                break
        P.emit(nc)
    return nc


def dma_in(K, out_ap, in_ap, writes, reads=(), q="sync"):
    return K.P.op(q, lambda e: e.dma_start(out=out_ap, in_=in_ap), reads=reads, writes=writes, dma=True)


def dma_out(K, out_ap, in_ap, reads, q="gpsimd"):
    return K.P.op(q, lambda e: e.dma_start(out=out_ap, in_=in_ap), reads=reads, writes=(), dma=True)


def mm(K, out_ap, pairs, reads, writes, start=True, stop=True):
    def fn(e):
        n = len(pairs)
        ins = None
        for i, (l, r) in enumerate(pairs):
            ins = e.matmul(out_ap, lhsT=l, rhs=r, start=(start and i == 0), stop=(stop and i == n - 1))
        return ins
    return K.P.op("tensor", fn, reads=reads, writes=writes)


def tok_tiles():
    r = [(0, NCTX, True)]
    for s in range(SEQ // 512):
        r.append((NCTX + s * 512, 512, False))
    return r


def phase_mod(K, l, last):
    P, I, sb = K.P, K.I, K.sb
    K.off = K.base
    cs = sb("cs", [128, 16], F32)
    lh = sb("lh", [128, 16, 128], F32)
    bB = sb("bB", [128, 3 * D], F32)
    nwB = sb("nwB", [128, D], F32)
    tmp = sb("mtmp", [128, 512], F32)
    wt = [sb(f"wmod{i}", [128, 1536], F32) for i in range(2)]
    dma_in(K, cs.h[:, 0:8], I.c_t, [cs.b])
    dma_in(K, cs.h[:, 8:16], I.cctx_t, [cs.b])
    dma_in(K, bB.h, I.b_mod[l].partition_broadcast(128), [bB.b])
    dma_in(K, nwB.h, I.norm_w[l].partition_broadcast(128), [nwB.b])
    P.op("scalar", lambda e: e.activation(out=cs.h, in_=cs.h, func=AF.Silu), reads=[cs.b], writes=[cs.b])
    P.op("vector", lambda e: e.tensor_copy(out=lh.h, in_=cs.h.unsqueeze(2).to_broadcast([128, 16, 128])),
         reads=[cs.b], writes=[lh.b])
    names = (("sh_l", "sc_l", "g_l"), ("sh_c", "sc_c", "g_c"))
    it = 0
    for half in range(2):
        for kc in range(8):
            w = wt[it % 2]
            it += 1
            dma_in(K, w.h, I.w_mod[l, kc * 128:(kc + 1) * 128, half * 1536:(half + 1) * 1536], [w.b])
            for who in range(2):
                for j in range(3):
                    bank = K.ps[who * 3 + j]
                    mm(K, bank.h[:, :], [(lh.h[:, who * 8 + kc, :], w.h[:, j * 512:(j + 1) * 512])],
                       reads=[lh.b, w.b], writes=[bank.b], start=(kc == 0), stop=(kc == 7))
        for who in range(2):
            for j in range(3):
                jb = half * 3 + j
                kind, off = jb // 2, (jb % 2) * 512
                bank = K.ps[who * 3 + j]
                dst = K.mod[names[who][kind]]
                bsl = bB.h[:, jb * 512:(jb + 1) * 512]
                if kind == 1:
                    P.op("vector", lambda e, bank=bank, bsl=bsl: e.tensor_tensor(out=tmp.h, in0=bank.h[:, :], in1=bsl, op=ALU.add),
                         reads=[bank.b, bB.b], writes=[tmp.b])
                    P.op("vector", lambda e, dst=dst, off=off: e.scalar_tensor_tensor(
                        out=dst.h[:, off:off + 512], in0=tmp.h, scalar=1.0, in1=nwB.h[:, off:off + 512],
                        op0=ALU.add, op1=ALU.mult), reads=[tmp.b, nwB.b], writes=[dst.b])
                else:
                    P.op("vector", lambda e, bank=bank, bsl=bsl, dst=dst, off=off: e.tensor_tensor(
                        out=dst.h[:, off:off + 512], in0=bank.h[:, :], in1=bsl, op=ALU.add),
                        reads=[bank.b, bB.b], writes=[dst.b])


def phase_a1(K, l, last):
    P, I, S, sb = K.P, K.I, K.S, K.sb
    K.off = K.base
    K.hT = sb("hT_all", [128, 8, T], BF16)
    K.a2_base = K.off
    xt = [sb(f"xt{i}", [128, D], F32) for i in range(2)]
    junk = sb("junk", [128, D], BF16)
    t1 = [sb(f"t1{i}", [128, D], F32) for i in range(2)]
    hb = [sb(f"hb{i}", [128, D], BF16) for i in range(2)]
    st = [sb(f"st{i}", [128, 2], F32) for i in range(2)]
    src = I.xin if l == 0 else S.xs
    for t in range(NT):
        x, tt, h, s2 = xt[t % 2], t1[t % 2], hb[t % 2], st[t % 2]
        ctx = t < 2
        sc = K.mod["sc_c" if ctx else "sc_l"]
        sh = K.mod["sh_c" if ctx else "sh_l"]
        bank = K.ps[t % 2]
        pb = K.psb[t % 2]
        dma_in(K, x.h, src[t * 128:(t + 1) * 128, :], [x.b])
        P.op("scalar", lambda e, x=x, s2=s2: e.activation(out=junk.h, in_=x.h, func=AF.Square, accum_out=s2.h[:, 0:1]),
             reads=[x.b], writes=[junk.b, s2.b])
        P.op("vector", lambda e, s2=s2: e.tensor_scalar(out=s2.h[:, 1:2], in0=s2.h[:, 0:1], scalar1=1.0 / D, scalar2=EPS,
                                                       op0=ALU.mult, op1=ALU.add), reads=[s2.b], writes=[s2.b])
        P.op("scalar", lambda e, s2=s2: e.sqrt(out=s2.h[:, 1:2], in_=s2.h[:, 1:2]), reads=[s2.b], writes=[s2.b])
        P.op("vector", lambda e, s2=s2: e.reciprocal(out=s2.h[:, 1:2], in_=s2.h[:, 1:2]), reads=[s2.b], writes=[s2.b])
        P.op("vector", lambda e, x=x, tt=tt, s2=s2, sc=sc: e.scalar_tensor_tensor(
            out=tt.h, in0=x.h, scalar=s2.h[:, 1:2], in1=sc.h, op0=ALU.mult, op1=ALU.mult),
            reads=[x.b, s2.b, sc.b], writes=[tt.b])
        P.op("gpsimd", lambda e, tt=tt, h=h, sh=sh: e.tensor_tensor(out=h.h, in0=tt.h, in1=sh.h, op=ALU.add),
             reads=[tt.b, sh.b], writes=[h.b])

        def tr(e, h=h, pb=pb):
            ins = None
            for kc in range(8):
                ins = e.transpose(out=pb[:, kc * 128:(kc + 1) * 128], in_=h.h[:, kc * 128:(kc + 1) * 128],
                                  identity=K.ident.h)
            return ins
        P.op("tensor", tr, reads=[h.b, K.ident.b], writes=[bank.b])
        P.op("scalar", lambda e, pb=pb, t=t: e.copy(out=K.hT.h[:, :, t * 128:(t + 1) * 128],
                                                   in_=pb.rearrange("p (a b) -> p a b", a=8, b=128)),
             reads=[bank.b], writes=[K.hT.b])


def phase_a2(K, l, last):
    P, I, S, sb = K.P, K.I, K.S, K.sb
    K.off = K.a2_base
    Wv = I.w_in[l].rearrange("(kc p) n -> p kc n", p=128)
    wf = sb("wf", [128, 8, 512], F32)
    wbs = [sb(f"wb{i}", [128, 8, 512], BF16) for i in range(2)]
    wr = sb("wr", [128, 8, 512], BF16)
    stg = [sb(f"stg{i}", [128, T], BF16) for i in range(2)]
    sub_base = K.off
    cnt = {"w": 0, "s": 0, "b": 0}
    tiles = tok_tiles()

    def load_group(c0, ncols=512, rot=False):
        wb = wbs[cnt["w"] % 2]
        cnt["w"] += 1
        dma_in(K, wf.h[:, :, 0:ncols], Wv[:, :, c0:c0 + ncols], [wf.b])
        P.op("vector", lambda e: e.tensor_copy(out=wb.h[:, 0:4, 0:ncols], in_=wf.h[:, 0:4, 0:ncols]),
             reads=[wf.b], writes=[wb.b])
        P.op("gpsimd", lambda e: e.tensor_copy(out=wb.h[:, 4:8, 0:ncols], in_=wf.h[:, 4:8, 0:ncols]),
             reads=[wf.b], writes=[wb.b])
        if rot:
            def r1(e):
                ins = None
                for kc in range(8):
                    src = wf.h[:, kc, :].rearrange("p (n h s) -> p n h s", h=2, s=16)
                    dst = wr.h[:, kc, :].rearrange("p (n h s) -> p n h s", h=2, s=16)
                    ins = e.mul(out=dst[:, :, 0, :], in_=src[:, :, 1, :], mul=-1.0)
                return ins

            def r2(e):
                ins = None
                for kc in range(8):
                    src = wf.h[:, kc, :].rearrange("p (n h s) -> p n h s", h=2, s=16)
                    dst = wr.h[:, kc, :].rearrange("p (n h s) -> p n h s", h=2, s=16)
                    ins = e.tensor_copy(out=dst[:, :, 1, :], in_=src[:, :, 0, :])
                return ins
            P.op("scalar", r1, reads=[wf.b], writes=[wr.b])
            P.op("vector", r2, reads=[wf.b], writes=[wr.b])
        return wb

    def next_bank():
        b = K.ps[2 + cnt["b"] % 4]
        cnt["b"] += 1
        return b

    def proj(bank, wt, j, tok0, n):
        mm(K, bank.h[:, 0:n], [(wt.h[:, kc, j * 128:(j + 1) * 128], K.hT.h[:, kc, tok0:tok0 + n]) for kc in range(8)],
           reads=[wt.b, K.hT.b], writes=[bank.b])

    def fm_group(c0, func, dest):
        wb = load_group(c0)
        for j in range(4):
            s = stg[cnt["s"] % 2]
            cnt["s"] += 1
            for si, (tok0, n, ctx) in enumerate(tiles):
                bank = next_bank()
                proj(bank, wb, j, tok0, n)
                if func is None and si % 2 == 1:
                    P.op("vector", lambda e, s=s, bank=bank, tok0=tok0, n=n: e.tensor_copy(
                        out=s.h[:, tok0:tok0 + n], in_=bank.h[:, 0:n]), reads=[bank.b], writes=[s.b])
                else:
                    f = AF.Copy if func is None else func
                    P.op("scalar", lambda e, s=s, bank=bank, tok0=tok0, n=n, f=f: e.activation(
                        out=s.h[:, tok0:tok0 + n], in_=bank.h[:, 0:n], func=f), reads=[bank.b], writes=[s.b])
            dma_out(K, dest[j * 128:(j + 1) * 128, :], s.h, reads=[s.b])

    def tm_group(c0, func, dest):
        wb = load_group(c0)
        tms = [sb(f"tm{c0}_{i}", [128, 4, 512], BF16) for i in range(2)]
        dv = dest.rearrange("(a p) n -> p a n", p=128)
        for t in range(NT):
            tm = tms[(t // 4) % 2]
            bank = next_bank()
            mm(K, bank.h[:, :], [(K.hT.h[:, kc, t * 128:(t + 1) * 128], wb.h[:, kc, :]) for kc in range(8)],
               reads=[wb.b, K.hT.b], writes=[bank.b])
            f = AF.Copy if func is None else func
            P.op("scalar", lambda e, tm=tm, bank=bank, t=t, f=f: e.activation(out=tm.h[:, t % 4, :], in_=bank.h[:, :], func=f),
                 reads=[bank.b], writes=[tm.b])
            if t % 4 == 3 or t == NT - 1:
                t0 = (t // 4) * 4
                na = t - t0 + 1
                dma_out(K, dv[:, t0:t0 + na, :], tm.h[:, 0:na, :], reads=[tm.b])

    fm_group(C_FU, None, S.fuT)
    fm_group(C_FG, AF.Silu, S.fgT)
    tabs = [sb(f"rtab{i}", [128, 2, 512], F32) for i in range(2)]
    rt1 = [sb(f"rt1_{i}", [128, 512], F32) for i in range(2)]
    rt2 = [sb(f"rt2_{i}", [128, 512], F32) for i in range(2)]
    ro = [sb(f"ro{i}", [128, 512], BF16) for i in range(4)]
    rc = 0
    for c0, dest in ((C_Q, S.qT), (C_K, S.kT)):
        wb = load_group(c0, rot=True)
        for si, (tok0, n, ctx) in enumerate(tiles):
            if not ctx:
                tab = tabs[si % 2]
                p0 = tok0 - NCTX
                dma_in(K, tab.h[:, 0, :], I.cosT[:, p0:p0 + 512], [tab.b])
                dma_in(K, tab.h[:, 1, :], I.sinT[:, p0:p0 + 512], [tab.b])
            for j in range(4):
                o = ro[rc % 4]
                a, b2 = rt1[rc % 2], rt2[rc % 2]
                rc += 1
                bank = next_bank()
                proj(bank, wb, j, tok0, n)
                if ctx:
                    P.op("scalar", lambda e, o=o, bank=bank, n=n: e.copy(out=o.h[:, 0:n], in_=bank.h[:, 0:n]),
                         reads=[bank.b], writes=[o.b])
                else:
                    bank2 = next_bank()
                    proj(bank2, wr, j, tok0, n)
                    P.op("vector", lambda e, a=a, bank=bank, tab=tab: e.tensor_tensor(
                        out=a.h, in0=bank.h[:, :], in1=tab.h[:, 0, :], op=ALU.mult), reads=[bank.b, tab.b], writes=[a.b])
                    P.op("vector", lambda e, b2=b2, bank2=bank2, tab=tab: e.tensor_tensor(
                        out=b2.h, in0=bank2.h[:, :], in1=tab.h[:, 1, :], op=ALU.mult), reads=[bank2.b, tab.b], writes=[b2.b])
                    P.op("gpsimd", lambda e, o=o, a=a, b2=b2: e.tensor_tensor(out=o.h, in0=a.h, in1=b2.h, op=ALU.add),
                         reads=[a.b, b2.b], writes=[o.b])
                dma_out(K, dest[j * 128:(j + 1) * 128, tok0:tok0 + n], o.h[:, 0:n], reads=[o.b])
    P.barrier()
    K.off = sub_base
    tm_group(C_V, None, S.v_tm)
    fm_group(C_AG, AF.Silu, S.agT)
    tm_group(C_Z, AF.Silu, S.z_tm)
    CW = T + 8
    cst = sb("cst", [128, CW], F32)
    acc = [sb(f"cacc{i}", [128, 512], F32) for i in range(2)]
    cw = sb("cw", [128, 8, 5], F32)
    cb = sb("cb", [128, 8], F32)
    dma_in(K, cw.h, I.convw_t[l], [cw.b])
    dma_in(K, cb.h, I.convb_t[l], [cb.b])
    P.op("gpsimd", lambda e: e.memset(cst.h, 0.0), writes=[cst.b])
    ci = 0
    for g in range(2):
        wb = load_group(C_XBC + g * 512)
        for j in range(4):
            cc = g * 4 + j
            s = stg[cnt["s"] % 2]
            cnt["s"] += 1
            for si, (tok0, n, ctx) in enumerate(tiles):
                bank = next_bank()
                proj(bank, wb, j, tok0, n)
                col = tok0 + (2 if ctx else 6)
                P.op("scalar", lambda e, bank=bank, col=col, n=n: e.copy(out=cst.h[:, col:col + n], in_=bank.h[:, 0:n]),
                     reads=[bank.b], writes=[cst.b])
            for si, (tok0, n, ctx) in enumerate(tiles):
                col = tok0 + (2 if ctx else 6)
                a = acc[ci % 2]
                eng = "vector"
                ci += 1

                P.op(eng, lambda e, a=a, col=col, n=n, cc=cc: e.tensor_scalar_mul(
                    out=a.h[:, 0:n], in0=cst.h[:, col - 2:col - 2 + n], scalar1=cw.h[:, cc, 0:1]),
                    reads=[cst.b, cw.b], writes=[a.b])
                for k in range(1, 5):
                    P.op(eng, lambda e, a=a, col=col, n=n, cc=cc, k=k: e.scalar_tensor_tensor(
                        out=a.h[:, 0:n], in0=cst.h[:, col - 2 + k:col - 2 + k + n], scalar=cw.h[:, cc, k:k + 1],
                        in1=a.h[:, 0:n], op0=ALU.mult, op1=ALU.add), reads=[cst.b, cw.b, a.b], writes=[a.b])
                P.op("scalar", lambda e, s=s, a=a, tok0=tok0, n=n, cc=cc: e.activation(
                    out=s.h[:, tok0:tok0 + n], in_=a.h[:, 0:n], func=AF.Silu, bias=cb.h[:, cc:cc + 1]),
                    reads=[a.b, cb.b], writes=[s.b])
            dma_out(K, S.xbcT[cc * 128:(cc + 1) * 128, :], s.h, reads=[s.b])
    wb = load_group(C_DT, ncols=16)
    for t in range(NT):
        bank = K.ps[6] if t < 32 else K.ps[7]
        r0 = (t % 32) * 16
        mm(K, bank.h[:, r0:r0 + 16], [(K.hT.h[:, kc, t * 128:(t + 1) * 128], wb.h[:, kc, 0:16]) for kc in range(8)],
           reads=[wb.b, K.hT.b], writes=[bank.b])
    P.op("vector", lambda e: e.tensor_copy(out=K.dtraw.h[:, 0:32, :],
                                           in_=K.ps[6].h[:, :].rearrange("p (a b) -> p a b", a=32, b=16)),
         reads=[K.ps[6].b], writes=[K.dtraw.b])
    P.op("vector", lambda e: e.tensor_copy(out=K.dtraw.h[:, 32:34, :],
                                           in_=K.ps[7].h[:, 0:32].rearrange("p (a b) -> p a b", a=2, b=16)),
         reads=[K.ps[7].b], writes=[K.dtraw.b])
    if "dbg" in K.dbgset:
        dma_out(K, S.dbg[:, 0:NT * 16], K.dtraw.h.rearrange("p a b -> p (a b)"), reads=[K.dtraw.b])
    for g in range(6):
        fm_group(C_GT + g * 512, AF.Sigmoid, S.gT[g * 512:(g + 1) * 512, :])


_CONST = {}


def _constants():
    if _CONST:
        return _CONST
    bf = ml_dtypes.bfloat16
    c = _CONST
    c["ident"] = np.eye(128, dtype=np.float32).astype(bf)
    rows = np.repeat(np.arange(SEQ // 64, dtype=np.float32), 64)
    cols = np.tile(np.arange(64, dtype=np.float32), SEQ // 64)
    freqs = (np.float32(10000.0) ** (-np.arange(0, 32, 2, dtype=np.float32) / np.float32(32))).astype(np.float32)
    ang_r = rows[:, None] * freqs
    ang_c = cols[:, None] * freqs
    ang = np.concatenate([ang_r, ang_r, ang_c, ang_c], axis=-1).astype(np.float32)
    cosT = np.cos(ang).astype(np.float32).T
    sinT = np.sin(ang).astype(np.float32).T
    c["cosT"] = np.ascontiguousarray(np.concatenate([cosT, cosT], axis=0))
    c["sinT"] = np.ascontiguousarray(np.concatenate([sinT, sinT], axis=0))
    j = np.arange(128, dtype=np.float64)
    angB = 2 * np.pi * np.outer(j, j) / 128.0
    c["dftB"] = (np.concatenate([np.cos(angB), -np.sin(angB)], axis=1) / np.sqrt(128.0)).astype(np.float32).astype(bf)
    for key, n in (("4", SEQ), ("2", NCTX)):
        idx = np.arange(n, dtype=np.int64)
        ph = (np.outer(idx, idx) % n).astype(np.float64) * (2 * np.pi / n)
        sc = 1.0 / np.sqrt(float(n))
        c["C" + key] = (np.cos(ph) * sc).astype(np.float32).astype(bf)
        c["S" + key] = (np.sin(ph) * sc).astype(np.float32).astype(bf)
    s_ = np.arange(128)[:, None]
    l_ = np.arange(128)[None, :]
    c["tri"] = (s_ <= l_).astype(np.float32)
    c["triT"] = (s_ >= l_).astype(np.float32)
    c["mask_f"] = (l_ >= s_).astype(np.float32)
    c["mask_b"] = (l_ <= s_).astype(np.float32)
    return c


def make_in_maps(inputs):
    f = lambda a: np.ascontiguousarray(np.asarray(a, dtype=np.float32))
    x, c, ctx = f(inputs["x"]), f(inputs["c"]), f(inputs["ctx"])
    shared = {
        "cctx_t": np.ascontiguousarray(f(inputs["c_ctx"]).reshape(8, 128).T),
        "w_mod": f(inputs["w_mod"]), "b_mod": f(inputs["b_mod"]), "norm_w": f(inputs["norm_w"]),
        "w_in": f(inputs["w_in"]),
        "convw_t": np.ascontiguousarray(f(inputs["conv_w"]).reshape(DEPTH, 5, 8, 128).transpose(0, 3, 2, 1)),
        "convb_t": np.ascontiguousarray(f(inputs["conv_b"]).reshape(DEPTH, 8, 128).transpose(0, 2, 1)),
        "a_log": f(inputs["a_log"]).reshape(DEPTH, 16), "dt_bias": f(inputs["dt_bias"]).reshape(DEPTH, 16),
        "d_skip": f(inputs["d_skip"]), "ssd_norm_w": f(inputs["ssd_norm_w"]),
        "lam": f(inputs["lam"]).reshape(DEPTH, 256), "subln_w": f(inputs["subln_w"]),
        "w_of": f(inputs["w_of"]), "w_oa": f(inputs["w_oa"]), "w_os": f(inputs["w_os"]), "w_out": f(inputs["w_out"]),
        "norm_f": f(inputs["norm_f"]),
    }
    shared.update(_constants())
    maps = []
    for b in range(x.shape[0]):
        m = dict(shared)
        m["xin"] = np.ascontiguousarray(np.concatenate([ctx[b], x[b]], axis=0))
        m["c_t"] = np.ascontiguousarray(c[b].reshape(8, 128).T)
        maps.append(m)
    return maps


_NC_CACHE = {}


def kernel(**inputs):
    if "nc" not in _NC_CACHE:
        _NC_CACHE["nc"] = build_program()
    nc = _NC_CACHE["nc"]
    maps = make_in_maps(inputs)
    res = run_bass_kernel_spmd(nc, maps, core_ids=list(range(len(maps))))
    return np.stack([np.asarray(r["out"], dtype=np.float32) for r in res.results], axis=0)


def phase_b(K, l, last):
    pass


def phase_c(K, l, last):
    pass


def phase_d(K, l, last):
    pass


def phase_e(K, l, last):
    pass


def phase_b(K, l, last):
    P, I, S, sb = K.P, K.I, K.S, K.sb
    K.off = K.base
    U = sb("U", [128, NT, 2, 512], BF16)
    dB = sb("dB", [128, 256], BF16)
    fin = [sb(f"fin{i}", [128, 4, 512], BF16) for i in range(2)]
    pieces = [(sb(f"Cp{i}", [128, 8, 512], BF16), sb(f"Sp{i}", [128, 8, 512], BF16)) for i in range(3)]
    fgb = [sb(f"fgb{i}", [128, 4, 512], BF16) for i in range(2)]
    uo = [sb(f"uo{i}", [128, 4, 512], BF16) for i in range(2)]
    fuv = S.fuT.rearrange("(g p) t -> p g t", p=128)
    fgv = S.fgT.rearrange("(g p) t -> p g t", p=128)
    ufv = S.ufT.rearrange("(g p) t -> p g t", p=128)
    dma_in(K, dB.h, I.dftB, [dB.b])
    tiles = tok_tiles()
    bi = 0
    for si, (tok0, n, ctx) in enumerate(tiles):
        if ctx and last:
            continue
        f = fin[si % 2]
        dma_in(K, f.h[:, :, 0:n], fuv[:, :, tok0:tok0 + n], [f.b])
        for tt in range(n // 128):
            t = tok0 // 128 + tt
            for b2 in range(2):
                bank = K.ps[(bi % 2) * 2 + b2]
                for gg in range(2):
                    g = b2 * 2 + gg
                    mm(K, bank.h[:, gg * 256:(gg + 1) * 256], [(f.h[:, g, tt * 128:(tt + 1) * 128], dB.h)],
                       reads=[f.b, dB.b], writes=[bank.b])
                bv = bank.h[:, :].rearrange("p (g c m) -> p g c m", g=2, c=2, m=128)
                P.op("scalar", lambda e, bv=bv, t=t, b2=b2: e.copy(
                    out=U.h[:, t, 0, b2 * 256:(b2 + 1) * 256].rearrange("p (g m) -> p g m", g=2), in_=bv[:, :, 0, :]),
                    reads=[bank.b], writes=[U.b])
                P.op("vector", lambda e, bv=bv, t=t, b2=b2: e.tensor_copy(
                    out=U.h[:, t, 1, b2 * 256:(b2 + 1) * 256].rearrange("p (g m) -> p g m", g=2), in_=bv[:, :, 1, :]),
                    reads=[bank.b], writes=[U.b])
            bi += 1
    C4v = I.C4.rearrange("(nt p) k -> p nt k", p=128)
    S4v = I.S4.rearrange("(nt p) k -> p nt k", p=128)
    pi = 0
    for kb in range(SEQ // 512):
        tok0 = NCTX + kb * 512
        fg = fgb[kb % 2]
        o = uo[kb % 2]
        dma_in(K, fg.h, fgv[:, :, tok0:tok0 + 512], [fg.b])
        banks = [K.ps[(kb % 2) * 4 + ch] for ch in range(4)]
        for pc in range(4):
            Cp, Sp = pieces[pi % 3]
            pi += 1
            dma_in(K, Cp.h, C4v[:, pc * 8:(pc + 1) * 8, kb * 512:(kb + 1) * 512], [Cp.b])
            dma_in(K, Sp.h, S4v[:, pc * 8:(pc + 1) * 8, kb * 512:(kb + 1) * 512], [Sp.b])
            for ch in range(4):
                pairs = []
                for nt in range(8):
                    t = 2 + pc * 8 + nt
                    pairs.append((U.h[:, t, 0, ch * 128:(ch + 1) * 128], Cp.h[:, nt, :]))
                    pairs.append((U.h[:, t, 1, ch * 128:(ch + 1) * 128], Sp.h[:, nt, :]))
                mm(K, banks[ch].h[:, :], pairs, reads=[U.b, Cp.b, Sp.b], writes=[banks[ch].b],
                   start=(pc == 0), stop=(pc == 3))
        for ch in range(4):
            P.op("vector", lambda e, o=o, fg=fg, ch=ch, bank=banks[ch]: e.tensor_tensor(
                out=o.h[:, ch, :], in0=bank.h[:, :], in1=fg.h[:, ch, :], op=ALU.mult),
                reads=[banks[ch].b, fg.b], writes=[o.b])
        dma_out(K, ufv[:, :, tok0:tok0 + 512], o.h, reads=[o.b])
    if not last:
        c2 = sb("c2", [128, 2, 2, 256], BF16)
        dma_in(K, c2.h[:, 0, :, :], I.C2.rearrange("(nt p) k -> p nt k", p=128), [c2.b])
        dma_in(K, c2.h[:, 1, :, :], I.S2.rearrange("(nt p) k -> p nt k", p=128), [c2.b])
        fg = fgb[0]
        o = uo[0]
        dma_in(K, fg.h[:, :, 0:NCTX], fgv[:, :, 0:NCTX], [fg.b])
        for ch in range(4):
            bank = K.ps[ch]
            pairs = []
            for nt in range(2):
                pairs.append((U.h[:, nt, 0, ch * 128:(ch + 1) * 128], c2.h[:, 0, nt, :]))
                pairs.append((U.h[:, nt, 1, ch * 128:(ch + 1) * 128], c2.h[:, 1, nt, :]))
            mm(K, bank.h[:, 0:NCTX], pairs, reads=[U.b, c2.b], writes=[bank.b])
            P.op("vector", lambda e, o=o, fg=fg, ch=ch, bank=bank: e.tensor_tensor(
                out=o.h[:, ch, 0:NCTX], in0=bank.h[:, 0:NCTX], in1=fg.h[:, ch, 0:NCTX], op=ALU.mult),
                reads=[bank.b, fg.b], writes=[o.b])
        dma_out(K, ufv[:, :, 0:NCTX], o.h[:, :, 0:NCTX], reads=[o.b])


def phase_c(K, l, last):
    P, I, S, sb = K.P, K.I, K.S, K.sb
    K.off = K.base
    lam_init = 0.8 - 0.6 * math.exp(-0.3 * l)
    kTs = sb("kTs", [128, 4, T], BF16)
    vs = sb("vs", [128, NT, 512], BF16)
    lamt = sb("lamt", [128, 4, 64], F32)
    lam2 = sb("lam2", [128, 2, 64], F32)
    ls = sb("ls", [128, 4], F32)
    wsub = sb("wsub", [128, 1], F32)
    qts = [sb(f"qt{i}", [128, 4, 512], BF16) for i in range(2)]
    ags = [sb(f"agt{i}", [128, 4, 512], BF16) for i in range(2)]
    uos = [sb(f"uao{i}", [128, 4, 512], BF16) for i in range(2)]
    Et = [sb(f"E{i}", [128, 512], BF16) for i in range(6)]
    R0, R1, T0, T1, A, RS, O = [sb(f"ep{i}", [128, 512], F32) for i in range(7)]
    SQ = sb("sq", [128, 512], BF16)
    kv = S.kT.rearrange("(h p) t -> p h t", p=128)
    qv = S.qT.rearrange("(h p) t -> p h t", p=128)
    agv = S.agT.rearrange("(h p) t -> p h t", p=128)
    uav = S.uaT.rearrange("(h p) t -> p h t", p=128)
    for h in range(4):
        dma_in(K, kTs.h[:, h, :], kv[:, h, :], [kTs.b])
    vv = S.v_tm.rearrange("(a p) n -> p a n", p=128)
    for a0 in range(0, NT, 8):
        a1 = min(NT, a0 + 8)
        dma_in(K, vs.h[:, a0:a1, :], vv[:, a0:a1, :], [vs.b])
    dma_in(K, lamt.h.rearrange("p a b -> p (a b)"), I.lam[l].partition_broadcast(128), [lamt.b])
    dma_in(K, wsub.h, I.subln_w[l].rearrange("(p o) -> p o", o=1), [wsub.b])
    P.op("vector", lambda e: e.tensor_tensor(out=lam2.h[:, 0, :], in0=lamt.h[:, 0, :], in1=lamt.h[:, 1, :], op=ALU.mult),
         reads=[lamt.b], writes=[lam2.b])
    P.op("vector", lambda e: e.tensor_tensor(out=lam2.h[:, 1, :], in0=lamt.h[:, 2, :], in1=lamt.h[:, 3, :], op=ALU.mult),
         reads=[lamt.b], writes=[lam2.b])
    P.op("vector", lambda e: e.reduce_sum(out=ls.h[:, 0:2], in_=lam2.h, axis=AX.X), reads=[lam2.b], writes=[ls.b])
    P.op("scalar", lambda e: e.activation(out=ls.h[:, 0:2], in_=ls.h[:, 0:2], func=AF.Exp), reads=[ls.b], writes=[ls.b])
    P.op("vector", lambda e: e.tensor_tensor(out=ls.h[:, 2:3], in0=ls.h[:, 1:2], in1=ls.h[:, 0:1], op=ALU.subtract),
         reads=[ls.b], writes=[ls.b])
    P.op("vector", lambda e: e.tensor_scalar_add(out=ls.h[:, 3:4], in0=ls.h[:, 2:3], scalar1=-lam_init),
         reads=[ls.b], writes=[ls.b])
    P.op("scalar", lambda e: e.mul(out=wsub.h, in_=wsub.h, mul=(1.0 - lam_init)), reads=[wsub.b], writes=[wsub.b])
    psO = [K.ps[4], K.ps[5]]
    psZ = [K.ps[6], K.ps[7]]
    ei = 0
    for bi, (tok0, n, ctx) in enumerate(tok_tiles()):
        if ctx and last:
            continue
        ktiles = [0, 1] if ctx else list(range(NT))
        qt, agt, uo = qts[bi % 2], ags[bi % 2], uos[bi % 2]
        dma_in(K, qt.h[:, :, 0:n], qv[:, :, tok0:tok0 + n], [qt.b])
        dma_in(K, agt.h[:, :, 0:n], agv[:, :, tok0:tok0 + n], [agt.b])
        for h in range(4):
            nk = len(ktiles)
            for ki, kt in enumerate(ktiles):
                for m in range(2):
                    bS = K.ps[(ki % 2) * 2 + m]
                    E = Et[ei % 6]
                    ei += 1
                    mm(K, bS.h[:, 0:n], [(kTs.h[m * 64:(m + 1) * 64, h, kt * 128:(kt + 1) * 128],
                                          qt.h[m * 64:(m + 1) * 64, h, 0:n])], reads=[kTs.b, qt.b], writes=[bS.b])
                    P.op("scalar", lambda e, E=E, bS=bS, n=n: e.activation(out=E.h[:, 0:n], in_=bS.h[:, 0:n], func=AF.Exp, scale=0.125),
                         reads=[bS.b], writes=[E.b])
                    mm(K, psO[m].h[:, 0:n], [(vs.h[:, kt, h * 128:(h + 1) * 128], E.h[:, 0:n])],
                       reads=[vs.b, E.b], writes=[psO[m].b], start=(ki == 0), stop=(ki == nk - 1))
                    mm(K, psZ[m].h[:, 0:n], [(K.ones_bf.h, E.h[:, 0:n])],
                       reads=[K.ones_bf.b, E.b], writes=[psZ[m].b], start=(ki == 0), stop=(ki == nk - 1))
            P.op("vector", lambda e, n=n: e.reciprocal(out=R0.h[:, 0:n], in_=psZ[0].h[:, 0:n]), reads=[psZ[0].b], writes=[R0.b])
            P.op("vector", lambda e, n=n: e.reciprocal(out=R1.h[:, 0:n], in_=psZ[1].h[:, 0:n]), reads=[psZ[1].b], writes=[R1.b])
            P.op("vector", lambda e, n=n: e.tensor_tensor(out=T0.h[:, 0:n], in0=psO[0].h[:, 0:n], in1=R0.h[:, 0:n], op=ALU.mult),
                 reads=[psO[0].b, R0.b], writes=[T0.b])
            P.op("vector", lambda e, n=n: e.tensor_tensor(out=T1.h[:, 0:n], in0=psO[1].h[:, 0:n], in1=R1.h[:, 0:n], op=ALU.mult),
                 reads=[psO[1].b, R1.b], writes=[T1.b])
            P.op("vector", lambda e, n=n: e.scalar_tensor_tensor(out=A.h[:, 0:n], in0=T1.h[:, 0:n], scalar=ls.h[:, 3:4],
                                                                in1=T0.h[:, 0:n], op0=ALU.mult, op1=ALU.add),
                 reads=[T0.b, T1.b, ls.b], writes=[A.b])
            P.op("gpsimd", lambda e, n=n: e.tensor_tensor(out=SQ.h[:, 0:n], in0=A.h[:, 0:n], in1=A.h[:, 0:n], op=ALU.mult),
                 reads=[A.b], writes=[SQ.b])
            bq = K.ps[0]
            mm(K, bq.h[:, 0:n], [(K.ones_bf.h, SQ.h[:, 0:n])], reads=[K.ones_bf.b, SQ.b], writes=[bq.b])
            P.op("vector", lambda e, n=n, bq=bq: e.tensor_scalar(out=RS.h[:, 0:n], in0=bq.h[:, 0:n], scalar1=1.0 / 128, scalar2=EPS,
                                                                op0=ALU.mult, op1=ALU.add), reads=[bq.b], writes=[RS.b])
            P.op("scalar", lambda e, n=n: e.sqrt(out=RS.h[:, 0:n], in_=RS.h[:, 0:n]), reads=[RS.b], writes=[RS.b])
            P.op("vector", lambda e, n=n: e.reciprocal(out=RS.h[:, 0:n], in_=RS.h[:, 0:n]), reads=[RS.b], writes=[RS.b])
            P.op("gpsimd", lambda e, n=n: e.tensor_tensor(out=O.h[:, 0:n], in0=A.h[:, 0:n], in1=RS.h[:, 0:n], op=ALU.mult),
                 reads=[A.b, RS.b], writes=[O.b])
            P.op("vector", lambda e, n=n, h=h, uo=uo, agt=agt: e.scalar_tensor_tensor(
                out=uo.h[:, h, 0:n], in0=O.h[:, 0:n], scalar=wsub.h[:, 0:1], in1=agt.h[:, h, 0:n], op0=ALU.mult, op1=ALU.mult),
                reads=[O.b, wsub.b, agt.b], writes=[uo.b])
        dma_out(K, uav[:, :, tok0:tok0 + n], uo.h[:, :, 0:n], reads=[uo.b])


def phase_d(K, l, last):
    P, I, S, sb = K.P, K.I, K.S, K.sb
    K.off = K.base
    par = sb("par", [128, 40], F32)
    snw = sb("snw", [128, 512], F32)
    tri = sb("tri", [128, 128], F32)
    triT = sb("triT", [128, 128], F32)
    mf = sb("mf", [128, 128], F32)
    mb = sb("mb", [128, 128], F32)
    dt, lndt, ad, acs, tot, eacs, wst, biasL = [sb(f"d_{nm}", [128, NT, 16], F32) for nm in
                                                ("dt", "lndt", "ad", "acs", "tot", "eacs", "wst", "biasL")]
    tmpd, dec = dt, tot
    CT = sb("CT", [128, 2, T], BF16)
    bt2 = [sb(f"bt2_{i}", [128, 2, 128], BF16) for i in range(2)]
    XB = sb("XB", [128, NT, 768], BF16)
    SbE = sb("SbE", [128, NT, 512], BF16)
    xin6 = [sb(f"xin6_{i}", [128, 6, 128], BF16) for i in range(2)]
    zts = [sb(f"zt{i}", [128, 512], BF16) for i in range(2)]
    Sst = [sb("Sf", [128, 512], F32), sb("Sb", [128, 512], F32)]
    SfE = [sb(f"SfE{i}", [128, 512], BF16) for i in range(2)]
    xw = [sb(f"xw{i}", [128, 512], BF16) for i in range(2)]
    CBs = sb("CBs", [128, 2, 128], F32)
    Lr = sb("Lr", [128, 8, 128], F32)
    Lm = Lr
    MT = [sb(f"MT{i}", [128, 8, 128], BF16) for i in range(2)]
    Y1, Y2, YZ = [sb(f"Y{i}", [128, 512], F32) for i in range(3)]
    junk = sb("djunk", [128, 512], BF16)
    s2 = sb("ds2", [128, 2], F32)
    usb = sb("usb", [128, 512], BF16)
    usT = [sb(f"usTs{i}", [128, 4, 512], BF16) for i in range(2)]

    dma_in(K, par.h[:, 0:16], I.a_log[l].partition_broadcast(128), [par.b])
    dma_in(K, par.h[:, 16:32], I.dt_bias[l].partition_broadcast(128), [par.b])
    dma_in(K, par.h[:, 32:40], I.d_skip[l].partition_broadcast(128), [par.b])
    dma_in(K, snw.h, I.ssd_norm_w[l].partition_broadcast(128), [snw.b])
    for t_, src in ((tri, I.tri), (triT, I.triT), (mf, I.mask_f), (mb, I.mask_b)):
        dma_in(K, t_.h, src, [t_.b])
    xv = S.xbcT.rearrange("(g p) t -> p g t", p=128)
    dma_in(K, CT.h, xv[:, 6:8, :], [CT.b])

    def bc16(ap):
        return ap.unsqueeze(1).to_broadcast([128, NT, 16])

    def bc64(ap):
        return ap.unsqueeze(2).to_broadcast([128, 8, 64])

    def v3(ap):
        return ap.rearrange("p (h q) -> p h q", h=8)

    P.op("vector", lambda e: e.tensor_tensor(out=dt.h, in0=K.dtraw.h, in1=bc16(par.h[:, 16:32]), op=ALU.add),
         reads=[K.dtraw.b, par.b], writes=[dt.b])
    P.op("scalar", lambda e: e.activation(out=dt.h, in_=dt.h, func=AF.Exp), reads=[dt.b], writes=[dt.b])
    P.op("scalar", lambda e: e.activation(out=dt.h, in_=dt.h, func=AF.Ln, bias=1.0), reads=[dt.b], writes=[dt.b])
    P.op("scalar", lambda e: e.activation(out=lndt.h, in_=dt.h, func=AF.Ln), reads=[dt.b], writes=[lndt.b])
    P.op("scalar", lambda e: e.activation(out=par.h[:, 0:16], in_=par.h[:, 0:16], func=AF.Exp), reads=[par.b], writes=[par.b])
    P.op("vector", lambda e: e.scalar_tensor_tensor(out=ad.h, in0=dt.h, scalar=-1.0, in1=bc16(par.h[:, 0:16]),
                                                    op0=ALU.mult, op1=ALU.mult), reads=[dt.b, par.b], writes=[ad.b])
    for c in range(NT):
        bank = K.ps[c // 16]
        r0 = (c % 16) * 32
        mm(K, bank.h[:, r0:r0 + 8], [(tri.h, ad.h[:, c, 0:8])], reads=[tri.b, ad.b], writes=[bank.b])
        mm(K, bank.h[:, r0 + 8:r0 + 16], [(triT.h, ad.h[:, c, 8:16])], reads=[triT.b, ad.b], writes=[bank.b])
        mm(K, bank.h[:, r0 + 16:r0 + 32], [(K.ones_f.h, ad.h[:, c, :])], reads=[K.ones_f.b, ad.b], writes=[bank.b])
    for bi_, (c0, ncz) in enumerate(((0, 16), (16, 16), (32, 2))):
        bank = K.ps[bi_]
        bv = bank.h[:, 0:ncz * 32].rearrange("p (a b) -> p a b", a=ncz, b=32)
        P.op("vector", lambda e, bv=bv, c0=c0, ncz=ncz: e.tensor_copy(out=acs.h[:, c0:c0 + ncz, :], in_=bv[:, :, 0:16]),
             reads=[bank.b], writes=[acs.b])
        P.op("vector", lambda e, bv=bv, c0=c0, ncz=ncz: e.tensor_copy(out=tot.h[:, c0:c0 + ncz, :], in_=bv[:, :, 16:32]),
             reads=[bank.b], writes=[tot.b])
    P.op("scalar", lambda e: e.activation(out=eacs.h, in_=acs.h, func=AF.Exp), reads=[acs.b], writes=[eacs.b])
    P.op("vector", lambda e: e.tensor_tensor(out=biasL.h, in0=lndt.h, in1=acs.h, op=ALU.subtract),
         reads=[lndt.b, acs.b], writes=[biasL.b])
    P.op("vector", lambda e: e.tensor_tensor(out=tmpd.h, in0=tot.h, in1=biasL.h, op=ALU.add),
         reads=[tot.b, biasL.b], writes=[tmpd.b])
    P.op("scalar", lambda e: e.activation(out=wst.h, in_=tmpd.h, func=AF.Exp), reads=[tmpd.b], writes=[wst.b])
    P.op("scalar", lambda e: e.activation(out=dec.h, in_=tot.h, func=AF.Exp), reads=[tot.b], writes=[dec.b])
    for c in range(NT):
        xi = xin6[c % 2]
        bank = K.ps[3 + c % 2]
        pb = K.psb[3 + c % 2]
        dma_in(K, xi.h, xv[:, 0:6, c * 128:(c + 1) * 128], [xi.b])

        def tr(e, xi=xi, pb=pb):
            ins = None
            for j in range(6):
                ins = e.transpose(out=pb[:, j * 128:(j + 1) * 128], in_=xi.h[:, j, :], identity=K.ident.h)
            return ins
        P.op("tensor", tr, reads=[xi.b, K.ident.b], writes=[bank.b])
        P.op("scalar", lambda e, pb=pb, c=c: e.copy(out=XB.h[:, c, :], in_=pb[:, 0:768]), reads=[bank.b], writes=[XB.b])

    psSt = K.ps[3]

    def state_update(c, d):
        Sd = Sst[d]
        w = xw[d]
        P.op("vector", lambda e: e.tensor_tensor(out=v3(w.h), in0=v3(XB.h[:, c, 0:512]),
                                                 in1=bc64(wst.h[:, c, d * 8:(d + 1) * 8]), op=ALU.mult),
             reads=[XB.b, wst.b], writes=[w.b])
        for g in range(2):
            mm(K, psSt.h[:, g * 256:(g + 1) * 256], [(XB.h[:, c, 512 + g * 128:512 + (g + 1) * 128], w.h[:, g * 256:(g + 1) * 256])],
               reads=[XB.b, w.b], writes=[psSt.b])
        P.op("vector", lambda e: e.tensor_tensor(out=v3(Sd.h), in0=v3(Sd.h), in1=bc64(dec.h[:, c, d * 8:(d + 1) * 8]), op=ALU.mult),
             reads=[Sd.b, dec.b], writes=[Sd.b])
        P.op("vector", lambda e: e.tensor_tensor(out=Sd.h, in0=Sd.h, in1=psSt.h[:, :], op=ALU.add),
             reads=[Sd.b, psSt.b], writes=[Sd.b])

    P.op("gpsimd", lambda e: e.memset(Sst[0].h, 0.0), writes=[Sst[0].b])
    P.op("gpsimd", lambda e: e.memset(Sst[1].h, 0.0), writes=[Sst[1].b])
    for c in [1, 0] + list(range(NT - 1, 1, -1)):
        P.op("gpsimd", lambda e, c=c: e.tensor_copy(out=SbE.h[:, c, :], in_=Sst[1].h), reads=[Sst[1].b], writes=[SbE.b])
        state_update(c, 1)
    zv = S.z_tm
    usv = S.usT.rearrange("(g p) t -> p g t", p=128)
    psY, psCB, psT = K.ps[7], K.ps[4], K.ps[2]
    psR = [K.ps[5], K.ps[6]]
    psYo = [K.ps[0], K.ps[1]]
    ui = 0
    for c in range(NT):
        ctx = c < 2
        sfe = SfE[c % 2]
        P.op("gpsimd", lambda e, sfe=sfe: e.tensor_copy(out=sfe.h, in_=Sst[0].h), reads=[Sst[0].b], writes=[sfe.b])
        if not (ctx and last):
            zt = zts[c % 2]
            dma_in(K, zt.h, zv[c * 128:(c + 1) * 128, :], [zt.b])
            bt = bt2[c % 2]
            dma_in(K, bt.h, xv[:, 4:6, c * 128:(c + 1) * 128], [bt.b])
            for g in range(2):
                mm(K, psCB.h[:, g * 128:(g + 1) * 128], [(bt.h[:, g, :], CT.h[:, g, c * 128:(c + 1) * 128])],
                   reads=[bt.b, CT.b], writes=[psCB.b])
            P.op("scalar", lambda e: e.copy(out=CBs.h.rearrange("p a b -> p (a b)"), in_=psCB.h[:, 0:256]),
                 reads=[psCB.b], writes=[CBs.b])
            for d in range(2):
                trd = tri if d == 0 else triT
                msk = mf if d == 0 else mb
                mt = MT[d]
                for h in range(8):
                    bk = psR[h // 4]
                    mm(K, bk.h[:, (h % 4) * 128:(h % 4 + 1) * 128],
                       [(ad.h[:, c, d * 8 + h:d * 8 + h + 1].to_broadcast([128, 128]), trd.h)],
                       reads=[ad.b, trd.b], writes=[bk.b])
                for h in range(8):
                    bk = psR[h // 4]
                    P.op("scalar", lambda e, bk=bk, h=h, c=c, d=d: e.activation(
                        out=Lr.h[:, h, :], in_=bk.h[:, (h % 4) * 128:(h % 4 + 1) * 128], func=AF.Exp,
                        bias=biasL.h[:, c, d * 8 + h:d * 8 + h + 1]), reads=[bk.b, biasL.b], writes=[Lr.b])
                P.op("vector", lambda e, msk=msk: e.scalar_tensor_tensor(
                    out=Lm.h, in0=Lr.h, scalar=1e30, in1=msk.h.unsqueeze(1).to_broadcast([128, 8, 128]),
                    op0=ALU.min, op1=ALU.mult), reads=[Lr.b, msk.b], writes=[Lm.b])
                for g in range(2):
                    P.op("vector", lambda e, g=g, mt=mt: e.tensor_tensor(
                        out=mt.h[:, g * 4:(g + 1) * 4, :], in0=Lm.h[:, g * 4:(g + 1) * 4, :],
                        in1=CBs.h[:, g, :].unsqueeze(1).to_broadcast([128, 4, 128]), op=ALU.mult),
                        reads=[Lm.b, CBs.b], writes=[mt.b])
                se = sfe.h if d == 0 else SbE.h[:, c, :]
                seb = sfe.b if d == 0 else SbE.b
                for g in range(2):
                    mm(K, psYo[d].h[:, g * 256:(g + 1) * 256], [(CT.h[:, g, c * 128:(c + 1) * 128], se[:, g * 256:(g + 1) * 256])],
                       reads=[CT.b, seb], writes=[psYo[d].b])
            for h in range(8):
                mm(K, psY.h[:, h * 64:(h + 1) * 64], [(MT[0].h[:, h, :], XB.h[:, c, h * 64:(h + 1) * 64]),
                                                      (MT[1].h[:, h, :], XB.h[:, c, h * 64:(h + 1) * 64])],
                   reads=[MT[0].b, MT[1].b, XB.b], writes=[psY.b])
            P.op("vector", lambda e, c=c: e.tensor_tensor(out=v3(Y1.h), in0=v3(psYo[0].h[:, :]), in1=bc64(eacs.h[:, c, 0:8]), op=ALU.mult),
                 reads=[psYo[0].b, eacs.b], writes=[Y1.b])
            P.op("vector", lambda e, c=c: e.tensor_tensor(out=v3(Y2.h), in0=v3(psYo[1].h[:, :]), in1=bc64(eacs.h[:, c, 8:16]), op=ALU.mult),
                 reads=[psYo[1].b, eacs.b], writes=[Y2.b])
            P.op("gpsimd", lambda e: e.tensor_tensor(out=Y1.h, in0=Y1.h, in1=Y2.h, op=ALU.add), reads=[Y1.b, Y2.b], writes=[Y1.b])
            P.op("gpsimd", lambda e, c=c: e.tensor_tensor(out=v3(Y2.h), in0=v3(XB.h[:, c, 0:512]), in1=bc64(par.h[:, 32:40]), op=ALU.mult),
                 reads=[XB.b, par.b, Y1.b], writes=[Y2.b])
            P.op("gpsimd", lambda e: e.tensor_tensor(out=Y1.h, in0=Y1.h, in1=Y2.h, op=ALU.add), reads=[Y1.b, Y2.b], writes=[Y1.b])
            P.op("vector", lambda e: e.tensor_tensor(out=Y1.h, in0=Y1.h, in1=psY.h[:, :], op=ALU.add), reads=[Y1.b, psY.b], writes=[Y1.b])
            P.op("vector", lambda e, zt=zt: e.tensor_tensor(out=YZ.h, in0=Y1.h, in1=zt.h, op=ALU.mult), reads=[Y1.b, zt.b], writes=[YZ.b])
            P.op("scalar", lambda e: e.activation(out=junk.h, in_=YZ.h, func=AF.Square, accum_out=s2.h[:, 0:1]),
                 reads=[YZ.b], writes=[junk.b, s2.b])
            P.op("vector", lambda e: e.tensor_scalar(out=s2.h[:, 1:2], in0=s2.h[:, 0:1], scalar1=1.0 / 512, scalar2=EPS,
                                                     op0=ALU.mult, op1=ALU.add), reads=[s2.b], writes=[s2.b])
            P.op("scalar", lambda e: e.sqrt(out=s2.h[:, 1:2], in_=s2.h[:, 1:2]), reads=[s2.b], writes=[s2.b])
            P.op("vector", lambda e: e.reciprocal(out=s2.h[:, 1:2], in_=s2.h[:, 1:2]), reads=[s2.b], writes=[s2.b])
            P.op("vector", lambda e: e.scalar_tensor_tensor(out=usb.h, in0=YZ.h, scalar=s2.h[:, 1:2], in1=snw.h,
                                                            op0=ALU.mult, op1=ALU.mult), reads=[YZ.b, s2.b, snw.b], writes=[usb.b])
            pbT = K.psb[2]

            def tr2(e, pbT=pbT):
                ins = None
                for j in range(4):
                    ins = e.transpose(out=pbT[:, j * 128:(j + 1) * 128], in_=usb.h[:, j * 128:(j + 1) * 128], identity=K.ident.h)
                return ins
            P.op("tensor", tr2, reads=[usb.b, K.ident.b], writes=[psT.b])
            if ctx:
                grp0, slot, glen = 0, c, 2
            else:
                grp0 = 2 + ((c - 2) // 4) * 4
                slot, glen = c - grp0, 4
            ut = usT[(0 if ctx else 1 + (c - 2) // 4) % 2]
            P.op("scalar", lambda e, ut=ut, slot=slot, pbT=pbT: e.copy(
                out=ut.h[:, :, slot * 128:(slot + 1) * 128], in_=pbT[:, 0:512].rearrange("p (g m) -> p g m", g=4)),
                reads=[psT.b], writes=[ut.b])
            if slot == glen - 1:
                dma_out(K, usv[:, :, grp0 * 128:(grp0 + glen) * 128], ut.h[:, :, 0:glen * 128], reads=[ut.b])
        state_update(c, 0)


def phase_e(K, l, last):
    P, I, S, sb = K.P, K.I, K.S, K.sb
    K.off = K.base
    wbr = [sb(f"wbr{i}", [128, 4, D], BF16) for i in range(3)]
    wo = sb("wo", [128, 8, D], BF16)
    wstage = [sb(f"wstage{i}", [128, 4, D], F32) for i in range(1)]
    uts = [sb(f"ut{i}", [128, 3, 4, 512], BF16) for i in range(2)]
    gt = sb("gt", [128, 24, 512], BF16)
    yTs = [sb(f"yT{i}", [128, 8, 512], BF16) for i in range(2)]
    M = [sb(f"M{i}", [128, 512], F32) for i in range(3)]
    xts = [sb(f"ext{i}", [128, D], F32) for i in range(2)]
    xns = [sb(f"exn{i}", [128, D], F32) for i in range(2)]
    tmp = [sb(f"etmp{i}", [128, 512], F32) for i in range(2)]
    srcs = [(I.w_of[l], wbr[0].h), (I.w_oa[l], wbr[1].h), (I.w_os[l], wbr[2].h),
            (I.w_out[l, 0:512, :], wo.h[:, 0:4, :]), (I.w_out[l, 512:1024, :], wo.h[:, 4:8, :])]
    wbufs = [wbr[0].b, wbr[1].b, wbr[2].b, wo.b, wo.b]
    for i, (src, dst) in enumerate(srcs):
        ws = wstage[0]
        dma_in(K, ws.h, src.rearrange("(kc p) n -> p kc n", p=128), [ws.b])
        P.op("vector", lambda e, ws=ws, dst=dst: e.tensor_copy(out=dst[:, 0:2, :], in_=ws.h[:, 0:2, :]), reads=[ws.b], writes=[wbufs[i]])
        P.op("gpsimd", lambda e, ws=ws, dst=dst: e.tensor_copy(out=dst[:, 2:4, :], in_=ws.h[:, 2:4, :]), reads=[ws.b], writes=[wbufs[i]])
    if last:
        nfB = sb("nfB", [128, D], F32)
        s2 = sb("es2", [128, 2], F32)
        junk = sb("ejunk", [128, D], BF16)
        dma_in(K, nfB.h, I.norm_f.partition_broadcast(128), [nfB.b])
    uviews = [t_.rearrange("(g p) t -> p g t", p=128) for t_ in (S.ufT, S.uaT, S.usT)]
    gv = S.gT.rearrange("(c p) t -> p c t", p=128)
    xsrc = I.xin if l == 0 else S.xs
    xi = 0
    for si, (tok0, n, ctx) in enumerate(tok_tiles()):
        if ctx and last:
            continue
        ut = uts[si % 2]
        yT = yTs[si % 2]
        gA = K.mod["g_c" if ctx else "g_l"]
        for br in range(3):
            dma_in(K, ut.h[:, br, :, 0:n], uviews[br][:, :, tok0:tok0 + n], [ut.b])
        for c0 in range(0, 24, 8):
            dma_in(K, gt.h[:, c0:c0 + 8, 0:n], gv[:, c0:c0 + 8, tok0:tok0 + n], [gt.b])
        for oc in range(8):
            banks = [K.ps[br + 3 * (oc % 2)] for br in range(3)]
            for br in range(3):
                mm(K, banks[br].h[:, 0:n], [(wbr[br].h[:, kc, oc * 128:(oc + 1) * 128], ut.h[:, br, kc, 0:n]) for kc in range(4)],
                   reads=[wbr[br].b, ut.b], writes=[banks[br].b])
            for br in range(3):
                P.op("vector", lambda e, br=br, oc=oc, n=n, bank=banks[br]: e.tensor_tensor(
                    out=M[br].h[:, 0:n], in0=bank.h[:, 0:n], in1=gt.h[:, br * 8 + oc, 0:n], op=ALU.mult),
                    reads=[banks[br].b, gt.b], writes=[M[br].b])
            P.op("gpsimd", lambda e, n=n: e.tensor_tensor(out=M[0].h[:, 0:n], in0=M[0].h[:, 0:n], in1=M[1].h[:, 0:n], op=ALU.add),
                 reads=[M[0].b, M[1].b], writes=[M[0].b])
            P.op("gpsimd", lambda e, n=n, oc=oc, yT=yT: e.tensor_tensor(out=yT.h[:, oc, 0:n], in0=M[0].h[:, 0:n], in1=M[2].h[:, 0:n], op=ALU.add),
                 reads=[M[0].b, M[2].b], writes=[yT.b])
        for tt in range(n // 128):
            r0 = tok0 + tt * 128
            xt, xn = xts[xi % 2], xns[xi % 2]
            xi += 1
            dma_in(K, xt.h, xsrc[r0:r0 + 128, :], [xt.b])
            for half in range(2):
                bank = K.ps[6 + half]
                tp = tmp[half]
                hs = slice(half * 512, (half + 1) * 512)
                mm(K, bank.h[:, :], [(yT.h[:, kc, tt * 128:(tt + 1) * 128], wo.h[:, kc, hs]) for kc in range(8)],
                   reads=[yT.b, wo.b], writes=[bank.b])
                P.op("vector", lambda e, bank=bank, tp=tp, hs=hs, gA=gA: e.tensor_tensor(out=tp.h, in0=bank.h[:, :], in1=gA.h[:, hs], op=ALU.mult),
                     reads=[bank.b, gA.b], writes=[tp.b])
                P.op("gpsimd", lambda e, tp=tp, xt=xt, xn=xn, hs=hs: e.tensor_tensor(out=xn.h[:, hs], in0=tp.h, in1=xt.h[:, hs], op=ALU.add),
                     reads=[tp.b, xt.b], writes=[xn.b])
            if not last:
                dma_out(K, S.xs[r0:r0 + 128, :], xn.h, reads=[xn.b])
            else:
                P.op("scalar", lambda e, xn=xn: e.activation(out=junk.h, in_=xn.h, func=AF.Square, accum_out=s2.h[:, 0:1]),
                     reads=[xn.b], writes=[junk.b, s2.b])
                P.op("vector", lambda e: e.tensor_scalar(out=s2.h[:, 1:2], in0=s2.h[:, 0:1], scalar1=1.0 / D, scalar2=EPS,
                                                         op0=ALU.mult, op1=ALU.add), reads=[s2.b], writes=[s2.b])
                P.op("scalar", lambda e: e.sqrt(out=s2.h[:, 1:2], in_=s2.h[:, 1:2]), reads=[s2.b], writes=[s2.b])
                P.op("vector", lambda e: e.reciprocal(out=s2.h[:, 1:2], in_=s2.h[:, 1:2]), reads=[s2.b], writes=[s2.b])
                P.op("vector", lambda e, xn=xn, xt=xt: e.scalar_tensor_tensor(out=xt.h, in0=xn.h, scalar=s2.h[:, 1:2], in1=nfB.h,
                                                                               op0=ALU.mult, op1=ALU.mult),
                     reads=[xn.b, s2.b, nfB.b], writes=[xt.b])
                dma_out(K, K.out[r0 - NCTX:r0 - NCTX + 128, :], xt.h, reads=[xt.b])
```

```python
import math
from contextlib import ExitStack

import numpy as np
import ml_dtypes

import concourse.bass as bass
import concourse.mybir as mybir
from concourse.bass_utils import run_bass_kernel_spmd

F32 = mybir.dt.float32
BF16 = mybir.dt.bfloat16
AF = mybir.ActivationFunctionType
ALU = mybir.AluOpType
AX = mybir.AxisListType

D = 1024
SEQ = 4096
NCTX = 256
T = SEQ + NCTX
NT = T // 128
DEPTH = 4
EPS = 1e-6
IN_W = 7696
C_FU, C_FG, C_Q, C_K, C_V, C_AG, C_Z, C_XBC, C_DT, C_GT = 0, 512, 1024, 1536, 2048, 2560, 3072, 3584, 4608, 4624

COMPUTE = ("tensor", "vector", "scalar", "gpsimd")
STREAMS = ("tensor", "vector", "scalar", "gpsimd", "sync")
NS = 16
EPOCH = 20000


class Buf:
    __slots__ = ("name", "w", "r")

    def __init__(self, name=""):
        self.name = name
        self.w = None
        self.r = []


class Op:
    __slots__ = ("stream", "fn", "is_dma", "signal", "waits", "n", "slot", "seq", "clock", "semi", "semv")


class Prog:
    def __init__(self):
        self.streams = {s: [] for s in STREAMS}
        self.known = {s: {c: 0 for c in COMPUTE} for s in STREAMS}
        self.known_dma = {s: {} for s in STREAMS}
        self.nseq = {c: 0 for c in COMPUTE}
        self.dma_ops = {s: [] for s in STREAMS}
        self.last = {c: None for c in COMPUTE}
        self.pending = {s: [] for s in STREAMS}

    def op(self, stream, fn, reads=(), writes=(), dma=False):
        o = Op()
        o.stream = stream
        o.fn = fn
        o.is_dma = dma
        o.signal = False
        o.waits = []
        deps = []
        for b in reads:
            if b.w is not None:
                deps.append(b.w)
        for b in writes:
            if b.w is not None:
                deps.append(b.w)
            deps.extend(b.r)
        if self.pending[stream]:
            deps.extend(self.pending[stream])
            self.pending[stream] = []
        if dma:
            n = len(self.dma_ops[stream])
            o.n = n
            o.slot = n % NS
            if n >= NS:
                deps.append(self.dma_ops[stream][n - NS])
            self.dma_ops[stream].append(o)
        else:
            self.nseq[stream] += 1
            o.seq = self.nseq[stream]
            self.last[stream] = o
        kn = self.known[stream]
        kd = self.known_dma[stream]
        for d in deps:
            if d.is_dma:
                key = (d.stream, d.slot)
                if kd.get(key, -1) < d.n:
                    kd[key] = d.n
                    o.waits.append(d)
                    for c, v in d.clock.items():
                        if kn[c] < v:
                            kn[c] = v
            else:
                if d.stream == "tensor" and stream == "tensor" and not dma:
                    continue
                if kn[d.stream] < d.seq:
                    o.waits.append(d)
                    d.signal = True
                    kn[d.stream] = d.seq
                    for c, v in d.clock.items():
                        if kn[c] < v:
                            kn[c] = v
        o.clock = dict(kn)
        for b in reads:
            b.r.append(o)
        for b in writes:
            b.w = o
            b.r = []
        self.streams[stream].append(o)
        return o

    def barrier(self):
        ops = []
        for c in COMPUTE:
            if self.last[c] is not None:
                ops.append(self.last[c])
        for s in STREAMS:
            ops.extend(self.dma_ops[s][-NS:])
        for s in STREAMS:
            self.pending[s] = list(ops)

    def emit(self, nc):
        self.barrier()
        nsem = {}
        for c in COMPUTE:
            cnt = 0
            for o in self.streams[c]:
                if o.is_dma:
                    continue
                if o.signal:
                    o.semi = cnt // EPOCH
                    o.semv = cnt % EPOCH + 1
                    cnt += 1
            nsem[c] = max(1, (cnt + EPOCH - 1) // EPOCH)
        with ExitStack() as es:
            csem = {c: [es.enter_context(nc.semaphore(f"c_{c}_{i}")) for i in range(nsem[c])] for c in COMPUTE}
            dsem = {s: [es.enter_context(nc.semaphore(f"d_{s}_{i}")) for i in range(NS)]
                    for s in STREAMS if self.dma_ops[s]}

            def wait(e, d):
                if d.is_dma:
                    e.wait_ge(dsem[d.stream][d.slot], 16 * (d.n // NS + 1))
                else:
                    e.wait_ge(csem[d.stream][d.semi], d.semv)

            def body_for(stream):
                def body(e):
                    kn = self.known[stream]
                    kd = self.known_dma[stream]
                    for o in self.streams[stream]:
                        for d in o.waits:
                            wait(e, d)
                        ins = o.fn(e)
                        if o.is_dma:
                            ins.then_inc(dsem[stream][o.slot], 16)
                        elif o.signal:
                            ins.then_inc(csem[stream][o.semi], 1)
                    for d in self.pending[stream]:
                        if d.is_dma:
                            if kd.get((d.stream, d.slot), -1) < d.n:
                                wait(e, d)
                        elif d.signal and not (d.stream == stream):
                            if kn[d.stream] < d.seq:
                                wait(e, d)
                return body

            for c in COMPUTE:
                pass
            with nc.Block() as block:
                block.sync(body_for("sync"))
                block.tensor(body_for("tensor"))
                block.vector(body_for("vector"))
                block.scalar(body_for("scalar"))
                block.gpsimd(body_for("gpsimd"))


class Tl:
    __slots__ = ("h", "b")

    def __init__(self, h, name=""):
        self.h = h
        self.b = Buf(name)


class Ctx:
    pass


def build_program(n_layers=DEPTH, stop_after=None, dbg=()):
    nc = bass.Bass("TRN2", target_bir_lowering=False)
    P = Prog()
    K = Ctx()
    K.nc, K.P = nc, P
    K.dbgset = set(dbg)

    def din(name, shape, dt=F32):
        return nc.dram_tensor(name, list(shape), dt, kind="ExternalInput").ap()

    def dscr(name, shape, dt=BF16):
        kind = "ExternalOutput" if name in dbg else "Internal"
        return nc.dram_tensor(name, list(shape), dt, kind=kind).ap()

    I = Ctx()
    I.xin = din("xin", [T, D])
    I.c_t = din("c_t", [128, 8])
    I.cctx_t = din("cctx_t", [128, 8])
    I.w_mod = din("w_mod", [DEPTH, D, 3 * D])
    I.b_mod = din("b_mod", [DEPTH, 3 * D])
    I.norm_w = din("norm_w", [DEPTH, D])
    I.w_in = din("w_in", [DEPTH, D, IN_W])
    I.convw_t = din("convw_t", [DEPTH, 128, 8, 5])
    I.convb_t = din("convb_t", [DEPTH, 128, 8])
    I.a_log = din("a_log", [DEPTH, 16])
    I.dt_bias = din("dt_bias", [DEPTH, 16])
    I.d_skip = din("d_skip", [DEPTH, 8])
    I.ssd_norm_w = din("ssd_norm_w", [DEPTH, 512])
    I.lam = din("lam", [DEPTH, 256])
    I.subln_w = din("subln_w", [DEPTH, 128])
    I.w_of = din("w_of", [DEPTH, 512, D])
    I.w_oa = din("w_oa", [DEPTH, 512, D])
    I.w_os = din("w_os", [DEPTH, 512, D])
    I.w_out = din("w_out", [DEPTH, D, D])
    I.norm_f = din("norm_f", [D])
    I.ident = din("ident", [128, 128], BF16)
    I.cosT = din("cosT", [128, SEQ])
    I.sinT = din("sinT", [128, SEQ])
    I.dftB = din("dftB", [128, 256], BF16)
    I.C4 = din("C4", [SEQ, SEQ], BF16)
    I.S4 = din("S4", [SEQ, SEQ], BF16)
    I.C2 = din("C2", [NCTX, NCTX], BF16)
    I.S2 = din("S2", [NCTX, NCTX], BF16)
    I.tri = din("tri", [128, 128])
    I.triT = din("triT", [128, 128])
    I.mask_f = din("mask_f", [128, 128])
    I.mask_b = din("mask_b", [128, 128])
    K.I = I
    out = nc.dram_tensor("out", [SEQ, D], F32, kind="ExternalOutput").ap()
    K.out = out

    S = Ctx()
    S.xs = dscr("xs", [T, D], F32)
    S.fuT = dscr("fuT", [512, T])
    S.fgT = dscr("fgT", [512, T])
    S.qT = dscr("qT", [512, T])
    S.kT = dscr("kT", [512, T])
    S.v_tm = dscr("v_tm", [T, 512])
    S.agT = dscr("agT", [512, T])
    S.z_tm = dscr("z_tm", [T, 512])
    S.xbcT = dscr("xbcT", [1024, T])
    S.gT = dscr("gT", [3072, T])
    S.ufT = dscr("ufT", [512, T])
    S.uaT = dscr("uaT", [512, T])
    S.usT = dscr("usT", [512, T])
    S.dbg = dscr("dbg", [128, 8192], F32)
    K.S = S

    with ExitStack() as top:
        ARENA = 52800
        arena = top.enter_context(nc.sbuf_tensor("arena", [128, ARENA], F32))
        K.base = 0
        K.off = 0

        def sb(name, shape, dt=F32):
            n = 1
            for v in shape[1:]:
                n *= v
            words = n if dt == F32 else (n + 1) // 2
            words = (words + 7) // 8 * 8
            assert K.off + words <= ARENA, (name, K.off, words)
            ap = arena[:, K.off:K.off + words]
            K.off += words
            if dt != F32:
                ap = ap.bitcast(dt)
            ap = ap[:, 0:n]
            if len(shape) == 3:
                ap = ap.rearrange("p (a b) -> p a b", a=shape[1], b=shape[2])
            elif len(shape) == 4:
                ap = ap.rearrange("p (a b c) -> p a b c", a=shape[1], b=shape[2], c=shape[3])
            return Tl(ap, name)
        K.sb = sb
        K.psall = top.enter_context(nc.psum_tensor("psall", [128, 4096], F32))
        K.ps = [Tl(K.psall[:, i * 512:(i + 1) * 512], f"ps{i}") for i in range(8)]
        K.psb = [K.psall[:, i * 512:(i + 1) * 512].bitcast(BF16) for i in range(8)]
        K.ident = sb("ident", [128, 128], BF16)
        K.ones_bf = sb("ones_bf", [128, 128], BF16)
        K.ones_f = sb("ones_f", [128, 128], F32)
        K.dtraw = sb("dtraw", [128, NT, 16], F32)
        K.mod = {k: sb("mod_" + k, [128, D], F32) for k in ("sc_l", "sh_l", "g_l", "sc_c", "sh_c", "g_c")}
        K.base = K.off
        P.op("sync", lambda e: e.dma_start(out=K.ident.h, in_=I.ident), writes=[K.ident.b], dma=True)
        P.op("vector", lambda e: e.memset(K.ones_bf.h, 1.0), writes=[K.ones_bf.b])
        P.op("vector", lambda e: e.memset(K.ones_f.h, 1.0), writes=[K.ones_f.b])
        phases = [phase_mod, phase_a1, phase_a2, phase_b, phase_c, phase_d, phase_e]
        done = False
        for l in range(n_layers):
            for ph in phases:
                ph(K, l, l == DEPTH - 1)
                P.barrier()
                if stop_after == (l, ph.__name__):
                    done = True
                    break
            if done:
                break
        P.emit(nc)
    return nc


def dma_in(K, out_ap, in_ap, writes, reads=(), q="sync"):
    return K.P.op(q, lambda e: e.dma_start(out=out_ap, in_=in_ap), reads=reads, writes=writes, dma=True)


def dma_out(K, out_ap, in_ap, reads, q="gpsimd"):
    return K.P.op(q, lambda e: e.dma_start(out=out_ap, in_=in_ap), reads=reads, writes=(), dma=True)


def mm(K, out_ap, pairs, reads, writes, start=True, stop=True):
    def fn(e):
        n = len(pairs)
        ins = None
        for i, (l, r) in enumerate(pairs):
            ins = e.matmul(out_ap, lhsT=l, rhs=r, start=(start and i == 0), stop=(stop and i == n - 1))
        return ins
    return K.P.op("tensor", fn, reads=reads, writes=writes)


def tok_tiles():
    r = [(0, NCTX, True)]
    for s in range(SEQ // 512):
        r.append((NCTX + s * 512, 512, False))
    return r


def phase_mod(K, l, last):
    P, I, sb = K.P, K.I, K.sb
    K.off = K.base
    cs = sb("cs", [128, 16], F32)
    lh = sb("lh", [128, 16, 128], F32)
    bB = sb("bB", [128, 3 * D], F32)
    nwB = sb("nwB", [128, D], F32)
    tmp = sb("mtmp", [128, 512], F32)
    wt = [sb(f"wmod{i}", [128, 1536], F32) for i in range(2)]
    dma_in(K, cs.h[:, 0:8], I.c_t, [cs.b])
    dma_in(K, cs.h[:, 8:16], I.cctx_t, [cs.b])
    dma_in(K, bB.h, I.b_mod[l].partition_broadcast(128), [bB.b])
    dma_in(K, nwB.h, I.norm_w[l].partition_broadcast(128), [nwB.b])
    P.op("scalar", lambda e: e.activation(out=cs.h, in_=cs.h, func=AF.Silu), reads=[cs.b], writes=[cs.b])
    P.op("vector", lambda e: e.tensor_copy(out=lh.h, in_=cs.h.unsqueeze(2).to_broadcast([128, 16, 128])),
         reads=[cs.b], writes=[lh.b])
    names = (("sh_l", "sc_l", "g_l"), ("sh_c", "sc_c", "g_c"))
    it = 0
    for half in range(2):
        for kc in range(8):
            w = wt[it % 2]
            it += 1
            dma_in(K, w.h, I.w_mod[l, kc * 128:(kc + 1) * 128, half * 1536:(half + 1) * 1536], [w.b])
            for who in range(2):
                for j in range(3):
                    bank = K.ps[who * 3 + j]
                    mm(K, bank.h[:, :], [(lh.h[:, who * 8 + kc, :], w.h[:, j * 512:(j + 1) * 512])],
                       reads=[lh.b, w.b], writes=[bank.b], start=(kc == 0), stop=(kc == 7))
        for who in range(2):
            for j in range(3):
                jb = half * 3 + j
                kind, off = jb // 2, (jb % 2) * 512
                bank = K.ps[who * 3 + j]
                dst = K.mod[names[who][kind]]
                bsl = bB.h[:, jb * 512:(jb + 1) * 512]
                if kind == 1:
                    P.op("vector", lambda e, bank=bank, bsl=bsl: e.tensor_tensor(out=tmp.h, in0=bank.h[:, :], in1=bsl, op=ALU.add),
                         reads=[bank.b, bB.b], writes=[tmp.b])
                    P.op("vector", lambda e, dst=dst, off=off: e.scalar_tensor_tensor(
                        out=dst.h[:, off:off + 512], in0=tmp.h, scalar=1.0, in1=nwB.h[:, off:off + 512],
                        op0=ALU.add, op1=ALU.mult), reads=[tmp.b, nwB.b], writes=[dst.b])
                else:
                    P.op("vector", lambda e, bank=bank, bsl=bsl, dst=dst, off=off: e.tensor_tensor(
                        out=dst.h[:, off:off + 512], in0=bank.h[:, :], in1=bsl, op=ALU.add),
                        reads=[bank.b, bB.b], writes=[dst.b])


def phase_a1(K, l, last):
    P, I, S, sb = K.P, K.I, K.S, K.sb
    K.off = K.base
    K.hT = sb("hT_all", [128, 8, T], BF16)
    K.a2_base = K.off
    xt = [sb(f"xt{i}", [128, D], F32) for i in range(2)]
    junk = sb("junk", [128, D], BF16)
    t1 = [sb(f"t1{i}", [128, D], F32) for i in range(2)]
    hb = [sb(f"hb{i}", [128, D], BF16) for i in range(2)]
    st = [sb(f"st{i}", [128, 2], F32) for i in range(2)]
    src = I.xin if l == 0 else S.xs
    for t in range(NT):
        x, tt, h, s2 = xt[t % 2], t1[t % 2], hb[t % 2], st[t % 2]
        ctx = t < 2
        sc = K.mod["sc_c" if ctx else "sc_l"]
        sh = K.mod["sh_c" if ctx else "sh_l"]
        bank = K.ps[t % 2]
        pb = K.psb[t % 2]
        dma_in(K, x.h, src[t * 128:(t + 1) * 128, :], [x.b])
        P.op("scalar", lambda e, x=x, s2=s2: e.activation(out=junk.h, in_=x.h, func=AF.Square, accum_out=s2.h[:, 0:1]),
             reads=[x.b], writes=[junk.b, s2.b])
        P.op("vector", lambda e, s2=s2: e.tensor_scalar(out=s2.h[:, 1:2], in0=s2.h[:, 0:1], scalar1=1.0 / D, scalar2=EPS,
                                                       op0=ALU.mult, op1=ALU.add), reads=[s2.b], writes=[s2.b])
        P.op("scalar", lambda e, s2=s2: e.sqrt(out=s2.h[:, 1:2], in_=s2.h[:, 1:2]), reads=[s2.b], writes=[s2.b])
        P.op("vector", lambda e, s2=s2: e.reciprocal(out=s2.h[:, 1:2], in_=s2.h[:, 1:2]), reads=[s2.b], writes=[s2.b])
        P.op("vector", lambda e, x=x, tt=tt, s2=s2, sc=sc: e.scalar_tensor_tensor(
            out=tt.h, in0=x.h, scalar=s2.h[:, 1:2], in1=sc.h, op0=ALU.mult, op1=ALU.mult),
            reads=[x.b, s2.b, sc.b], writes=[tt.b])
        P.op("gpsimd", lambda e, tt=tt, h=h, sh=sh: e.tensor_tensor(out=h.h, in0=tt.h, in1=sh.h, op=ALU.add),
             reads=[tt.b, sh.b], writes=[h.b])

        def tr(e, h=h, pb=pb):
            ins = None
            for kc in range(8):
                ins = e.transpose(out=pb[:, kc * 128:(kc + 1) * 128], in_=h.h[:, kc * 128:(kc + 1) * 128],
                                  identity=K.ident.h)
            return ins
        P.op("tensor", tr, reads=[h.b, K.ident.b], writes=[bank.b])
        P.op("scalar", lambda e, pb=pb, t=t: e.copy(out=K.hT.h[:, :, t * 128:(t + 1) * 128],
                                                   in_=pb.rearrange("p (a b) -> p a b", a=8, b=128)),
             reads=[bank.b], writes=[K.hT.b])


def phase_a2(K, l, last):
    P, I, S, sb = K.P, K.I, K.S, K.sb
    K.off = K.a2_base
    Wv = I.w_in[l].rearrange("(kc p) n -> p kc n", p=128)
    wf = sb("wf", [128, 8, 512], F32)
    wbs = [sb(f"wb{i}", [128, 8, 512], BF16) for i in range(2)]
    wr = sb("wr", [128, 8, 512], BF16)
    stg = [sb(f"stg{i}", [128, T], BF16) for i in range(2)]
    sub_base = K.off
    cnt = {"w": 0, "s": 0, "b": 0}
    tiles = tok_tiles()

    def load_group(c0, ncols=512, rot=False):
        wb = wbs[cnt["w"] % 2]
        cnt["w"] += 1
        dma_in(K, wf.h[:, :, 0:ncols], Wv[:, :, c0:c0 + ncols], [wf.b])
        P.op("vector", lambda e: e.tensor_copy(out=wb.h[:, 0:4, 0:ncols], in_=wf.h[:, 0:4, 0:ncols]),
             reads=[wf.b], writes=[wb.b])
        P.op("gpsimd", lambda e: e.tensor_copy(out=wb.h[:, 4:8, 0:ncols], in_=wf.h[:, 4:8, 0:ncols]),
             reads=[wf.b], writes=[wb.b])
        if rot:
            def r1(e):
                ins = None
                for kc in range(8):
                    src = wf.h[:, kc, :].rearrange("p (n h s) -> p n h s", h=2, s=16)
                    dst = wr.h[:, kc, :].rearrange("p (n h s) -> p n h s", h=2, s=16)
                    ins = e.mul(out=dst[:, :, 0, :], in_=src[:, :, 1, :], mul=-1.0)
                return ins

            def r2(e):
                ins = None
                for kc in range(8):
                    src = wf.h[:, kc, :].rearrange("p (n h s) -> p n h s", h=2, s=16)
                    dst = wr.h[:, kc, :].rearrange("p (n h s) -> p n h s", h=2, s=16)
                    ins = e.tensor_copy(out=dst[:, :, 1, :], in_=src[:, :, 0, :])
                return ins
            P.op("scalar", r1, reads=[wf.b], writes=[wr.b])
            P.op("vector", r2, reads=[wf.b], writes=[wr.b])
        return wb

    def next_bank():
        b = K.ps[2 + cnt["b"] % 4]
        cnt["b"] += 1
        return b

    def proj(bank, wt, j, tok0, n):
        mm(K, bank.h[:, 0:n], [(wt.h[:, kc, j * 128:(j + 1) * 128], K.hT.h[:, kc, tok0:tok0 + n]) for kc in range(8)],
           reads=[wt.b, K.hT.b], writes=[bank.b])

    def fm_group(c0, func, dest):
        wb = load_group(c0)
        for j in range(4):
            s = stg[cnt["s"] % 2]
            cnt["s"] += 1
            for si, (tok0, n, ctx) in enumerate(tiles):
                bank = next_bank()
                proj(bank, wb, j, tok0, n)
                if func is None and si % 2 == 1:
                    P.op("vector", lambda e, s=s, bank=bank, tok0=tok0, n=n: e.tensor_copy(
                        out=s.h[:, tok0:tok0 + n], in_=bank.h[:, 0:n]), reads=[bank.b], writes=[s.b])
                else:
                    f = AF.Copy if func is None else func
                    P.op("scalar", lambda e, s=s, bank=bank, tok0=tok0, n=n, f=f: e.activation(
                        out=s.h[:, tok0:tok0 + n], in_=bank.h[:, 0:n], func=f), reads=[bank.b], writes=[s.b])
            dma_out(K, dest[j * 128:(j + 1) * 128, :], s.h, reads=[s.b])

    def tm_group(c0, func, dest):
        wb = load_group(c0)
        tms = [sb(f"tm{c0}_{i}", [128, 4, 512], BF16) for i in range(2)]
        dv = dest.rearrange("(a p) n -> p a n", p=128)
        for t in range(NT):
            tm = tms[(t // 4) % 2]
            bank = next_bank()
            mm(K, bank.h[:, :], [(K.hT.h[:, kc, t * 128:(t + 1) * 128], wb.h[:, kc, :]) for kc in range(8)],
               reads=[wb.b, K.hT.b], writes=[bank.b])
            f = AF.Copy if func is None else func
            P.op("scalar", lambda e, tm=tm, bank=bank, t=t, f=f: e.activation(out=tm.h[:, t % 4, :], in_=bank.h[:, :], func=f),
                 reads=[bank.b], writes=[tm.b])
            if t % 4 == 3 or t == NT - 1:
                t0 = (t // 4) * 4
                na = t - t0 + 1
                dma_out(K, dv[:, t0:t0 + na, :], tm.h[:, 0:na, :], reads=[tm.b])

    fm_group(C_FU, None, S.fuT)
    fm_group(C_FG, AF.Silu, S.fgT)
    tabs = [sb(f"rtab{i}", [128, 2, 512], F32) for i in range(2)]
    rt1 = [sb(f"rt1_{i}", [128, 512], F32) for i in range(2)]
    rt2 = [sb(f"rt2_{i}", [128, 512], F32) for i in range(2)]
    ro = [sb(f"ro{i}", [128, 512], BF16) for i in range(4)]
    rc = 0
    for c0, dest in ((C_Q, S.qT), (C_K, S.kT)):
        wb = load_group(c0, rot=True)
        for si, (tok0, n, ctx) in enumerate(tiles):
            if not ctx:
                tab = tabs[si % 2]
                p0 = tok0 - NCTX
                dma_in(K, tab.h[:, 0, :], I.cosT[:, p0:p0 + 512], [tab.b])
                dma_in(K, tab.h[:, 1, :], I.sinT[:, p0:p0 + 512], [tab.b])
            for j in range(4):
                o = ro[rc % 4]
                a, b2 = rt1[rc % 2], rt2[rc % 2]
                rc += 1
                bank = next_bank()
                proj(bank, wb, j, tok0, n)
                if ctx:
                    P.op("scalar", lambda e, o=o, bank=bank, n=n: e.copy(out=o.h[:, 0:n], in_=bank.h[:, 0:n]),
                         reads=[bank.b], writes=[o.b])
                else:
                    bank2 = next_bank()
                    proj(bank2, wr, j, tok0, n)
                    P.op("vector", lambda e, a=a, bank=bank, tab=tab: e.tensor_tensor(
                        out=a.h, in0=bank.h[:, :], in1=tab.h[:, 0, :], op=ALU.mult), reads=[bank.b, tab.b], writes=[a.b])
                    P.op("vector", lambda e, b2=b2, bank2=bank2, tab=tab: e.tensor_tensor(
                        out=b2.h, in0=bank2.h[:, :], in1=tab.h[:, 1, :], op=ALU.mult), reads=[bank2.b, tab.b], writes=[b2.b])
                    P.op("gpsimd", lambda e, o=o, a=a, b2=b2: e.tensor_tensor(out=o.h, in0=a.h, in1=b2.h, op=ALU.add),
                         reads=[a.b, b2.b], writes=[o.b])
                dma_out(K, dest[j * 128:(j + 1) * 128, tok0:tok0 + n], o.h[:, 0:n], reads=[o.b])
    P.barrier()
    K.off = sub_base
    tm_group(C_V, None, S.v_tm)
    fm_group(C_AG, AF.Silu, S.agT)
    tm_group(C_Z, AF.Silu, S.z_tm)
    CW = T + 8
    cst = sb("cst", [128, CW], F32)
    acc = [sb(f"cacc{i}", [128, 512], F32) for i in range(2)]
    cw = sb("cw", [128, 8, 5], F32)
    cb = sb("cb", [128, 8], F32)
    dma_in(K, cw.h, I.convw_t[l], [cw.b])
    dma_in(K, cb.h, I.convb_t[l], [cb.b])
    P.op("gpsimd", lambda e: e.memset(cst.h, 0.0), writes=[cst.b])
    ci = 0
    for g in range(2):
        wb = load_group(C_XBC + g * 512)
        for j in range(4):
            cc = g * 4 + j
            s = stg[cnt["s"] % 2]
            cnt["s"] += 1
            for si, (tok0, n, ctx) in enumerate(tiles):
                bank = next_bank()
                proj(bank, wb, j, tok0, n)
                col = tok0 + (2 if ctx else 6)
                P.op("scalar", lambda e, bank=bank, col=col, n=n: e.copy(out=cst.h[:, col:col + n], in_=bank.h[:, 0:n]),
                     reads=[bank.b], writes=[cst.b])
            for si, (tok0, n, ctx) in enumerate(tiles):
                col = tok0 + (2 if ctx else 6)
                a = acc[ci % 2]
                eng = "vector"
                ci += 1

                P.op(eng, lambda e, a=a, col=col, n=n, cc=cc: e.tensor_scalar_mul(
                    out=a.h[:, 0:n], in0=cst.h[:, col - 2:col - 2 + n], scalar1=cw.h[:, cc, 0:1]),
                    reads=[cst.b, cw.b], writes=[a.b])
                for k in range(1, 5):
                    P.op(eng, lambda e, a=a, col=col, n=n, cc=cc, k=k: e.scalar_tensor_tensor(
                        out=a.h[:, 0:n], in0=cst.h[:, col - 2 + k:col - 2 + k + n], scalar=cw.h[:, cc, k:k + 1],
                        in1=a.h[:, 0:n], op0=ALU.mult, op1=ALU.add), reads=[cst.b, cw.b, a.b], writes=[a.b])
                P.op("scalar", lambda e, s=s, a=a, tok0=tok0, n=n, cc=cc: e.activation(
                    out=s.h[:, tok0:tok0 + n], in_=a.h[:, 0:n], func=AF.Silu, bias=cb.h[:, cc:cc + 1]),
                    reads=[a.b, cb.b], writes=[s.b])
            dma_out(K, S.xbcT[cc * 128:(cc + 1) * 128, :], s.h, reads=[s.b])
    wb = load_group(C_DT, ncols=16)
    for t in range(NT):
        bank = K.ps[6] if t < 32 else K.ps[7]
        r0 = (t % 32) * 16
        mm(K, bank.h[:, r0:r0 + 16], [(K.hT.h[:, kc, t * 128:(t + 1) * 128], wb.h[:, kc, 0:16]) for kc in range(8)],
           reads=[wb.b, K.hT.b], writes=[bank.b])
    P.op("vector", lambda e: e.tensor_copy(out=K.dtraw.h[:, 0:32, :],
                                           in_=K.ps[6].h[:, :].rearrange("p (a b) -> p a b", a=32, b=16)),
         reads=[K.ps[6].b], writes=[K.dtraw.b])
    P.op("vector", lambda e: e.tensor_copy(out=K.dtraw.h[:, 32:34, :],
                                           in_=K.ps[7].h[:, 0:32].rearrange("p (a b) -> p a b", a=2, b=16)),
         reads=[K.ps[7].b], writes=[K.dtraw.b])
    if "dbg" in K.dbgset:
        dma_out(K, S.dbg[:, 0:NT * 16], K.dtraw.h.rearrange("p a b -> p (a b)"), reads=[K.dtraw.b])
    for g in range(6):
        fm_group(C_GT + g * 512, AF.Sigmoid, S.gT[g * 512:(g + 1) * 512, :])


_CONST = {}


def _constants():
    if _CONST:
        return _CONST
    bf = ml_dtypes.bfloat16
    c = _CONST
    c["ident"] = np.eye(128, dtype=np.float32).astype(bf)
    rows = np.repeat(np.arange(SEQ // 64, dtype=np.float32), 64)
    cols = np.tile(np.arange(64, dtype=np.float32), SEQ // 64)
    freqs = (np.float32(10000.0) ** (-np.arange(0, 32, 2, dtype=np.float32) / np.float32(32))).astype(np.float32)
    ang_r = rows[:, None] * freqs
    ang_c = cols[:, None] * freqs
    ang = np.concatenate([ang_r, ang_r, ang_c, ang_c], axis=-1).astype(np.float32)
    cosT = np.cos(ang).astype(np.float32).T
    sinT = np.sin(ang).astype(np.float32).T
    c["cosT"] = np.ascontiguousarray(np.concatenate([cosT, cosT], axis=0))
    c["sinT"] = np.ascontiguousarray(np.concatenate([sinT, sinT], axis=0))
    j = np.arange(128, dtype=np.float64)
    angB = 2 * np.pi * np.outer(j, j) / 128.0
    c["dftB"] = (np.concatenate([np.cos(angB), -np.sin(angB)], axis=1) / np.sqrt(128.0)).astype(np.float32).astype(bf)
    for key, n in (("4", SEQ), ("2", NCTX)):
        idx = np.arange(n, dtype=np.int64)
        ph = (np.outer(idx, idx) % n).astype(np.float64) * (2 * np.pi / n)
        sc = 1.0 / np.sqrt(float(n))
        c["C" + key] = (np.cos(ph) * sc).astype(np.float32).astype(bf)
        c["S" + key] = (np.sin(ph) * sc).astype(np.float32).astype(bf)
    s_ = np.arange(128)[:, None]
    l_ = np.arange(128)[None, :]
    c["tri"] = (s_ <= l_).astype(np.float32)
    c["triT"] = (s_ >= l_).astype(np.float32)
    c["mask_f"] = (l_ >= s_).astype(np.float32)
    c["mask_b"] = (l_ <= s_).astype(np.float32)
    return c


def make_in_maps(inputs):
    f = lambda a: np.ascontiguousarray(np.asarray(a, dtype=np.float32))
    x, c, ctx = f(inputs["x"]), f(inputs["c"]), f(inputs["ctx"])
    shared = {
        "cctx_t": np.ascontiguousarray(f(inputs["c_ctx"]).reshape(8, 128).T),
        "w_mod": f(inputs["w_mod"]), "b_mod": f(inputs["b_mod"]), "norm_w": f(inputs["norm_w"]),
        "w_in": f(inputs["w_in"]),
        "convw_t": np.ascontiguousarray(f(inputs["conv_w"]).reshape(DEPTH, 5, 8, 128).transpose(0, 3, 2, 1)),
        "convb_t": np.ascontiguousarray(f(inputs["conv_b"]).reshape(DEPTH, 8, 128).transpose(0, 2, 1)),
        "a_log": f(inputs["a_log"]).reshape(DEPTH, 16), "dt_bias": f(inputs["dt_bias"]).reshape(DEPTH, 16),
        "d_skip": f(inputs["d_skip"]), "ssd_norm_w": f(inputs["ssd_norm_w"]),
        "lam": f(inputs["lam"]).reshape(DEPTH, 256), "subln_w": f(inputs["subln_w"]),
        "w_of": f(inputs["w_of"]), "w_oa": f(inputs["w_oa"]), "w_os": f(inputs["w_os"]), "w_out": f(inputs["w_out"]),
        "norm_f": f(inputs["norm_f"]),
    }
    shared.update(_constants())
    maps = []
    for b in range(x.shape[0]):
        m = dict(shared)
        m["xin"] = np.ascontiguousarray(np.concatenate([ctx[b], x[b]], axis=0))
        m["c_t"] = np.ascontiguousarray(c[b].reshape(8, 128).T)
        maps.append(m)
    return maps


_NC_CACHE = {}


def kernel(**inputs):
    if "nc" not in _NC_CACHE:
        _NC_CACHE["nc"] = build_program()
    nc = _NC_CACHE["nc"]
    maps = make_in_maps(inputs)
    res = run_bass_kernel_spmd(nc, maps, core_ids=list(range(len(maps))))
    return np.stack([np.asarray(r["out"], dtype=np.float32) for r in res.results], axis=0)


def phase_b(K, l, last):
    pass


def phase_c(K, l, last):
    pass


def phase_d(K, l, last):
    pass


def phase_e(K, l, last):
    pass


def phase_b(K, l, last):
    P, I, S, sb = K.P, K.I, K.S, K.sb
    K.off = K.base
    U = sb("U", [128, NT, 2, 512], BF16)
    dB = sb("dB", [128, 256], BF16)
    fin = [sb(f"fin{i}", [128, 4, 512], BF16) for i in range(2)]
    pieces = [(sb(f"Cp{i}", [128, 8, 512], BF16), sb(f"Sp{i}", [128, 8, 512], BF16)) for i in range(3)]
    fgb = [sb(f"fgb{i}", [128, 4, 512], BF16) for i in range(2)]
    uo = [sb(f"uo{i}", [128, 4, 512], BF16) for i in range(2)]
    fuv = S.fuT.rearrange("(g p) t -> p g t", p=128)
    fgv = S.fgT.rearrange("(g p) t -> p g t", p=128)
    ufv = S.ufT.rearrange("(g p) t -> p g t", p=128)
    dma_in(K, dB.h, I.dftB, [dB.b])
    tiles = tok_tiles()
    bi = 0
    for si, (tok0, n, ctx) in enumerate(tiles):
        if ctx and last:
            continue
        f = fin[si % 2]
        dma_in(K, f.h[:, :, 0:n], fuv[:, :, tok0:tok0 + n], [f.b])
        for tt in range(n // 128):
            t = tok0 // 128 + tt
            for b2 in range(2):
                bank = K.ps[(bi % 2) * 2 + b2]
                for gg in range(2):
                    g = b2 * 2 + gg
                    mm(K, bank.h[:, gg * 256:(gg + 1) * 256], [(f.h[:, g, tt * 128:(tt + 1) * 128], dB.h)],
                       reads=[f.b, dB.b], writes=[bank.b])
                bv = bank.h[:, :].rearrange("p (g c m) -> p g c m", g=2, c=2, m=128)
                P.op("scalar", lambda e, bv=bv, t=t, b2=b2: e.copy(
                    out=U.h[:, t, 0, b2 * 256:(b2 + 1) * 256].rearrange("p (g m) -> p g m", g=2), in_=bv[:, :, 0, :]),
                    reads=[bank.b], writes=[U.b])
                P.op("vector", lambda e, bv=bv, t=t, b2=b2: e.tensor_copy(
                    out=U.h[:, t, 1, b2 * 256:(b2 + 1) * 256].rearrange("p (g m) -> p g m", g=2), in_=bv[:, :, 1, :]),
                    reads=[bank.b], writes=[U.b])
            bi += 1
    C4v = I.C4.rearrange("(nt p) k -> p nt k", p=128)
    S4v = I.S4.rearrange("(nt p) k -> p nt k", p=128)
    pi = 0
    for kb in range(SEQ // 512):
        tok0 = NCTX + kb * 512
        fg = fgb[kb % 2]
        o = uo[kb % 2]
        dma_in(K, fg.h, fgv[:, :, tok0:tok0 + 512], [fg.b])
        banks = [K.ps[(kb % 2) * 4 + ch] for ch in range(4)]
        for pc in range(4):
            Cp, Sp = pieces[pi % 3]
            pi += 1
            dma_in(K, Cp.h, C4v[:, pc * 8:(pc + 1) * 8, kb * 512:(kb + 1) * 512], [Cp.b])
            dma_in(K, Sp.h, S4v[:, pc * 8:(pc + 1) * 8, kb * 512:(kb + 1) * 512], [Sp.b])
            for ch in range(4):
                pairs = []
                for nt in range(8):
                    t = 2 + pc * 8 + nt
                    pairs.append((U.h[:, t, 0, ch * 128:(ch + 1) * 128], Cp.h[:, nt, :]))
                    pairs.append((U.h[:, t, 1, ch * 128:(ch + 1) * 128], Sp.h[:, nt, :]))
                mm(K, banks[ch].h[:, :], pairs, reads=[U.b, Cp.b, Sp.b], writes=[banks[ch].b],
                   start=(pc == 0), stop=(pc == 3))
        for ch in range(4):
            P.op("vector", lambda e, o=o, fg=fg, ch=ch, bank=banks[ch]: e.tensor_tensor(
                out=o.h[:, ch, :], in0=bank.h[:, :], in1=fg.h[:, ch, :], op=ALU.mult),
                reads=[banks[ch].b, fg.b], writes=[o.b])
        dma_out(K, ufv[:, :, tok0:tok0 + 512], o.h, reads=[o.b])
    if not last:
        c2 = sb("c2", [128, 2, 2, 256], BF16)
        dma_in(K, c2.h[:, 0, :, :], I.C2.rearrange("(nt p) k -> p nt k", p=128), [c2.b])
        dma_in(K, c2.h[:, 1, :, :], I.S2.rearrange("(nt p) k -> p nt k", p=128), [c2.b])
        fg = fgb[0]
        o = uo[0]
        dma_in(K, fg.h[:, :, 0:NCTX], fgv[:, :, 0:NCTX], [fg.b])
        for ch in range(4):
            bank = K.ps[ch]
            pairs = []
            for nt in range(2):
                pairs.append((U.h[:, nt, 0, ch * 128:(ch + 1) * 128], c2.h[:, 0, nt, :]))
                pairs.append((U.h[:, nt, 1, ch * 128:(ch + 1) * 128], c2.h[:, 1, nt, :]))
            mm(K, bank.h[:, 0:NCTX], pairs, reads=[U.b, c2.b], writes=[bank.b])
            P.op("vector", lambda e, o=o, fg=fg, ch=ch, bank=bank: e.tensor_tensor(
                out=o.h[:, ch, 0:NCTX], in0=bank.h[:, 0:NCTX], in1=fg.h[:, ch, 0:NCTX], op=ALU.mult),
                reads=[bank.b, fg.b], writes=[o.b])
        dma_out(K, ufv[:, :, 0:NCTX], o.h[:, :, 0:NCTX], reads=[o.b])


def phase_c(K, l, last):
    P, I, S, sb = K.P, K.I, K.S, K.sb
    K.off = K.base
    lam_init = 0.8 - 0.6 * math.exp(-0.3 * l)
    kTs = sb("kTs", [128, 4, T], BF16)
    vs = sb("vs", [128, NT, 512], BF16)
    lamt = sb("lamt", [128, 4, 64], F32)
    lam2 = sb("lam2", [128, 2, 64], F32)
    ls = sb("ls", [128, 4], F32)
    wsub = sb("wsub", [128, 1], F32)
    qts = [sb(f"qt{i}", [128, 4, 512], BF16) for i in range(2)]
    ags = [sb(f"agt{i}", [128, 4, 512], BF16) for i in range(2)]
    uos = [sb(f"uao{i}", [128, 4, 512], BF16) for i in range(2)]
    Et = [sb(f"E{i}", [128, 512], BF16) for i in range(6)]
    R0, R1, T0, T1, A, RS, O = [sb(f"ep{i}", [128, 512], F32) for i in range(7)]
    SQ = sb("sq", [128, 512], BF16)
    ZA = [sb(f"za{i}", [128, 512], F32) for i in range(2)]
    OC = [sb(f"oc{i}", [128, 512], F32) for i in range(2)]
    kv = S.kT.rearrange("(h p) t -> p h t", p=128)
    qv = S.qT.rearrange("(h p) t -> p h t", p=128)
    agv = S.agT.rearrange("(h p) t -> p h t", p=128)
    uav = S.uaT.rearrange("(h p) t -> p h t", p=128)
    for h in range(4):
        dma_in(K, kTs.h[:, h, :], kv[:, h, :], [kTs.b])
    vv = S.v_tm.rearrange("(a p) n -> p a n", p=128)
    for a0 in range(0, NT, 8):
        a1 = min(NT, a0 + 8)
        dma_in(K, vs.h[:, a0:a1, :], vv[:, a0:a1, :], [vs.b])
    dma_in(K, lamt.h.rearrange("p a b -> p (a b)"), I.lam[l].partition_broadcast(128), [lamt.b])
    dma_in(K, wsub.h, I.subln_w[l].rearrange("(p o) -> p o", o=1), [wsub.b])
    P.op("vector", lambda e: e.tensor_tensor(out=lam2.h[:, 0, :], in0=lamt.h[:, 0, :], in1=lamt.h[:, 1, :], op=ALU.mult),
         reads=[lamt.b], writes=[lam2.b])
    P.op("vector", lambda e: e.tensor_tensor(out=lam2.h[:, 1, :], in0=lamt.h[:, 2, :], in1=lamt.h[:, 3, :], op=ALU.mult),
         reads=[lamt.b], writes=[lam2.b])
    P.op("vector", lambda e: e.reduce_sum(out=ls.h[:, 0:2], in_=lam2.h, axis=AX.X), reads=[lam2.b], writes=[ls.b])
    P.op("scalar", lambda e: e.activation(out=ls.h[:, 0:2], in_=ls.h[:, 0:2], func=AF.Exp), reads=[ls.b], writes=[ls.b])
    P.op("vector", lambda e: e.tensor_tensor(out=ls.h[:, 2:3], in0=ls.h[:, 1:2], in1=ls.h[:, 0:1], op=ALU.subtract),
         reads=[ls.b], writes=[ls.b])
    P.op("vector", lambda e: e.tensor_scalar_add(out=ls.h[:, 3:4], in0=ls.h[:, 2:3], scalar1=-lam_init),
         reads=[ls.b], writes=[ls.b])
    P.op("scalar", lambda e: e.mul(out=wsub.h, in_=wsub.h, mul=(1.0 - lam_init)), reads=[wsub.b], writes=[wsub.b])
    psO = [K.ps[4], K.ps[5]]
    psZ = [K.ps[6], K.ps[7]]
    ei = 0
    for bi, (tok0, n, ctx) in enumerate(tok_tiles()):
        if ctx and last:
            continue
        ktiles = [0, 1] if ctx else list(range(NT))
        qt, agt, uo = qts[bi % 2], ags[bi % 2], uos[bi % 2]
        dma_in(K, qt.h[:, :, 0:n], qv[:, :, tok0:tok0 + n], [qt.b])
        dma_in(K, agt.h[:, :, 0:n], agv[:, :, tok0:tok0 + n], [agt.b])
        for h in range(4):
            nk = len(ktiles)

            def issue_S(ki, h=h, n=n, qt=qt):
                kt = ktiles[ki]
                for m in range(2):
                    bS = K.ps[(ki % 2) * 2 + m]
                    mm(K, bS.h[:, 0:n], [(kTs.h[m * 64:(m + 1) * 64, h, kt * 128:(kt + 1) * 128],
                                          qt.h[m * 64:(m + 1) * 64, h, 0:n])], reads=[kTs.b, qt.b], writes=[bS.b])
            issue_S(0)
            for ki, kt in enumerate(ktiles):
                if ki + 1 < nk:
                    issue_S(ki + 1)
                b0 = (ki % 2) * 2
                for m in range(2):
                    E = Et[ei % 6]
                    ei += 1
                    Em = E.h[:, 0:n]
                    P.op("scalar", lambda e, Em=Em, b0=b0, m=m, n=n: e.activation(
                        out=Em, in_=K.ps[b0 + m].h[:, 0:n], func=AF.Exp, scale=0.125),
                        reads=[K.ps[b0 + m].b], writes=[E.b])
                    mm(K, psO[m].h[:, 0:n], [(vs.h[:, kt, h * 128:(h + 1) * 128], Em)],
                       reads=[vs.b, E.b], writes=[psO[m].b], start=(ki == 0), stop=(ki == nk - 1))
                    if m == 0:
                        mm(K, psZ[0].h[:, 0:n], [(K.ones_bf.h, Em)], reads=[K.ones_bf.b, E.b], writes=[psZ[0].b],
                           start=(ki == 0), stop=(ki == nk - 1))
                    elif ki == 0:
                        P.op("vector", lambda e, Em=Em, n=n: e.tensor_copy(out=ZA[1].h[:, 0:n], in_=Em),
                             reads=[E.b], writes=[ZA[1].b])
                    else:
                        P.op("vector", lambda e, Em=Em, n=n: e.tensor_tensor(out=ZA[1].h[:, 0:n], in0=ZA[1].h[:, 0:n], in1=Em, op=ALU.add),
                             reads=[E.b, ZA[1].b], writes=[ZA[1].b])
            mm(K, psZ[1].h[:, 0:n], [(K.ones_f.h, ZA[1].h[:, 0:n])], reads=[K.ones_f.b, ZA[1].b], writes=[psZ[1].b])
            P.op("vector", lambda e, n=n: e.tensor_copy(out=OC[0].h[:, 0:n], in_=psO[0].h[:, 0:n]), reads=[psO[0].b], writes=[OC[0].b])
            P.op("vector", lambda e, n=n: e.tensor_copy(out=OC[1].h[:, 0:n], in_=psO[1].h[:, 0:n]), reads=[psO[1].b], writes=[OC[1].b])
            P.op("vector", lambda e, n=n: e.tensor_copy(out=R0.h[:, 0:n], in_=psZ[0].h[:, 0:n]), reads=[psZ[0].b], writes=[R0.b])
            P.op("vector", lambda e, n=n: e.reciprocal(out=R0.h[:, 0:n], in_=R0.h[:, 0:n]), reads=[R0.b], writes=[R0.b])
            P.op("vector", lambda e, n=n: e.reciprocal(out=R1.h[:, 0:n], in_=psZ[1].h[:, 0:n]), reads=[psZ[1].b], writes=[R1.b])
            P.op("gpsimd", lambda e, n=n: e.tensor_tensor(out=T0.h[:, 0:n], in0=OC[0].h[:, 0:n], in1=R0.h[:, 0:n], op=ALU.mult),
                 reads=[OC[0].b, R0.b], writes=[T0.b])
            P.op("vector", lambda e, n=n: e.tensor_tensor(out=T1.h[:, 0:n], in0=OC[1].h[:, 0:n], in1=R1.h[:, 0:n], op=ALU.mult),
                 reads=[OC[1].b, R1.b], writes=[T1.b])
            P.op("vector", lambda e, n=n: e.scalar_tensor_tensor(out=A.h[:, 0:n], in0=T1.h[:, 0:n], scalar=ls.h[:, 3:4],
                                                                in1=T0.h[:, 0:n], op0=ALU.mult, op1=ALU.add),
                 reads=[T0.b, T1.b, ls.b], writes=[A.b])
            P.op("gpsimd", lambda e, n=n: e.tensor_tensor(out=SQ.h[:, 0:n], in0=A.h[:, 0:n], in1=A.h[:, 0:n], op=ALU.mult),
                 reads=[A.b], writes=[SQ.b])
            bq = psZ[1]
            mm(K, bq.h[:, 0:n], [(K.ones_bf.h, SQ.h[:, 0:n])], reads=[K.ones_bf.b, SQ.b], writes=[bq.b])
            P.op("scalar", lambda e, n=n, bq=bq: e.activation(out=RS.h[:, 0:n], in_=bq.h[:, 0:n], func=AF.Ln, scale=1.0 / 128, bias=EPS),
                 reads=[bq.b], writes=[RS.b])
            P.op("scalar", lambda e, n=n: e.activation(out=RS.h[:, 0:n], in_=RS.h[:, 0:n], func=AF.Exp, scale=-0.5),
                 reads=[RS.b], writes=[RS.b])
            P.op("gpsimd", lambda e, n=n: e.tensor_tensor(out=O.h[:, 0:n], in0=A.h[:, 0:n], in1=RS.h[:, 0:n], op=ALU.mult),
                 reads=[A.b, RS.b], writes=[O.b])
            P.op("vector", lambda e, n=n, h=h, uo=uo, agt=agt: e.scalar_tensor_tensor(
                out=uo.h[:, h, 0:n], in0=O.h[:, 0:n], scalar=wsub.h[:, 0:1], in1=agt.h[:, h, 0:n], op0=ALU.mult, op1=ALU.mult),
                reads=[O.b, wsub.b, agt.b], writes=[uo.b])
        dma_out(K, uav[:, :, tok0:tok0 + n], uo.h[:, :, 0:n], reads=[uo.b])


def phase_d(K, l, last):
    P, I, S, sb = K.P, K.I, K.S, K.sb
    K.off = K.base
    par = sb("par", [128, 40], F32)
    snw = sb("snw", [128, 512], F32)
    tri = sb("tri", [128, 128], F32)
    triT = sb("triT", [128, 128], F32)
    mf = sb("mf", [128, 128], F32)
    mb = sb("mb", [128, 128], F32)
    dt, lndt, ad, acs, tot, eacs, wst, biasL = [sb(f"d_{nm}", [128, NT, 16], F32) for nm in
                                                ("dt", "lndt", "ad", "acs", "tot", "eacs", "wst", "biasL")]
    tmpd, dec = dt, tot
    CT = sb("CT", [128, 2, T], BF16)
    bt2 = [sb(f"bt2_{i}", [128, 2, 128], BF16) for i in range(2)]
    XB = sb("XB", [128, NT, 768], BF16)
    SbE = sb("SbE", [128, NT, 512], BF16)
    xin6 = [sb(f"xin6_{i}", [128, 6, 128], BF16) for i in range(2)]
    zts = [sb(f"zt{i}", [128, 512], BF16) for i in range(2)]
    Sst = [sb("Sf", [128, 512], F32), sb("Sb", [128, 512], F32)]
    SfE = [sb(f"SfE{i}", [128, 512], BF16) for i in range(2)]
    xw = [sb(f"xw{i}", [128, 512], BF16) for i in range(2)]
    CBs = sb("CBs", [128, 2, 128], F32)
    Lr = sb("Lr", [128, 8, 128], F32)
    Lm = Lr
    MT = [sb(f"MT{i}", [128, 8, 128], BF16) for i in range(2)]
    Y1, Y2, YZ = [sb(f"Y{i}", [128, 512], F32) for i in range(3)]
    junk = sb("djunk", [128, 512], BF16)
    s2 = sb("ds2", [128, 2], F32)
    usb = sb("usb", [128, 512], BF16)
    usT = [sb(f"usTs{i}", [128, 4, 512], BF16) for i in range(2)]

    dma_in(K, par.h[:, 0:16], I.a_log[l].partition_broadcast(128), [par.b])
    dma_in(K, par.h[:, 16:32], I.dt_bias[l].partition_broadcast(128), [par.b])
    dma_in(K, par.h[:, 32:40], I.d_skip[l].partition_broadcast(128), [par.b])
    dma_in(K, snw.h, I.ssd_norm_w[l].partition_broadcast(128), [snw.b])
    for t_, src in ((tri, I.tri), (triT, I.triT), (mf, I.mask_f), (mb, I.mask_b)):
        dma_in(K, t_.h, src, [t_.b])
    xv = S.xbcT.rearrange("(g p) t -> p g t", p=128)
    dma_in(K, CT.h, xv[:, 6:8, :], [CT.b])

    def bc16(ap):
        return ap.unsqueeze(1).to_broadcast([128, NT, 16])

    def bc64(ap):
        return ap.unsqueeze(2).to_broadcast([128, 8, 64])

    def v3(ap):
        return ap.rearrange("p (h q) -> p h q", h=8)

    P.op("vector", lambda e: e.tensor_tensor(out=dt.h, in0=K.dtraw.h, in1=bc16(par.h[:, 16:32]), op=ALU.add),
         reads=[K.dtraw.b, par.b], writes=[dt.b])
    P.op("scalar", lambda e: e.activation(out=dt.h, in_=dt.h, func=AF.Exp), reads=[dt.b], writes=[dt.b])
    P.op("scalar", lambda e: e.activation(out=dt.h, in_=dt.h, func=AF.Ln, bias=1.0), reads=[dt.b], writes=[dt.b])
    P.op("scalar", lambda e: e.activation(out=lndt.h, in_=dt.h, func=AF.Ln), reads=[dt.b], writes=[lndt.b])
    P.op("scalar", lambda e: e.activation(out=par.h[:, 0:16], in_=par.h[:, 0:16], func=AF.Exp), reads=[par.b], writes=[par.b])
    P.op("vector", lambda e: e.scalar_tensor_tensor(out=ad.h, in0=dt.h, scalar=-1.0, in1=bc16(par.h[:, 0:16]),
                                                    op0=ALU.mult, op1=ALU.mult), reads=[dt.b, par.b], writes=[ad.b])
    for c in range(NT):
        bank = K.ps[c // 16]
        r0 = (c % 16) * 32
        mm(K, bank.h[:, r0:r0 + 8], [(tri.h, ad.h[:, c, 0:8])], reads=[tri.b, ad.b], writes=[bank.b])
        mm(K, bank.h[:, r0 + 8:r0 + 16], [(triT.h, ad.h[:, c, 8:16])], reads=[triT.b, ad.b], writes=[bank.b])
        mm(K, bank.h[:, r0 + 16:r0 + 32], [(K.ones_f.h, ad.h[:, c, :])], reads=[K.ones_f.b, ad.b], writes=[bank.b])
    for bi_, (c0, ncz) in enumerate(((0, 16), (16, 16), (32, 2))):
        bank = K.ps[bi_]
        bv = bank.h[:, 0:ncz * 32].rearrange("p (a b) -> p a b", a=ncz, b=32)
        P.op("vector", lambda e, bv=bv, c0=c0, ncz=ncz: e.tensor_copy(out=acs.h[:, c0:c0 + ncz, :], in_=bv[:, :, 0:16]),
             reads=[bank.b], writes=[acs.b])
        P.op("vector", lambda e, bv=bv, c0=c0, ncz=ncz: e.tensor_copy(out=tot.h[:, c0:c0 + ncz, :], in_=bv[:, :, 16:32]),
             reads=[bank.b], writes=[tot.b])
    P.op("scalar", lambda e: e.activation(out=eacs.h, in_=acs.h, func=AF.Exp), reads=[acs.b], writes=[eacs.b])
    P.op("vector", lambda e: e.tensor_tensor(out=biasL.h, in0=lndt.h, in1=acs.h, op=ALU.subtract),
         reads=[lndt.b, acs.b], writes=[biasL.b])
    P.op("vector", lambda e: e.tensor_tensor(out=tmpd.h, in0=tot.h, in1=biasL.h, op=ALU.add),
         reads=[tot.b, biasL.b], writes=[tmpd.b])
    P.op("scalar", lambda e: e.activation(out=wst.h, in_=tmpd.h, func=AF.Exp), reads=[tmpd.b], writes=[wst.b])
    P.op("scalar", lambda e: e.activation(out=dec.h, in_=tot.h, func=AF.Exp), reads=[tot.b], writes=[dec.b])
    for c in range(NT):
        xi = xin6[c % 2]
        bank = K.ps[3 + c % 2]
        pb = K.psb[3 + c % 2]
        dma_in(K, xi.h, xv[:, 0:6, c * 128:(c + 1) * 128], [xi.b])

        def tr(e, xi=xi, pb=pb):
            ins = None
            for j in range(6):
                ins = e.transpose(out=pb[:, j * 128:(j + 1) * 128], in_=xi.h[:, j, :], identity=K.ident.h)
            return ins
        P.op("tensor", tr, reads=[xi.b, K.ident.b], writes=[bank.b])
        P.op("scalar", lambda e, pb=pb, c=c: e.copy(out=XB.h[:, c, :], in_=pb[:, 0:768]), reads=[bank.b], writes=[XB.b])

    psSt = K.ps[3]

    def state_update(c, d):
        Sd = Sst[d]
        w = xw[d]
        P.op("vector", lambda e: e.tensor_tensor(out=v3(w.h), in0=v3(XB.h[:, c, 0:512]),
                                                 in1=bc64(wst.h[:, c, d * 8:(d + 1) * 8]), op=ALU.mult),
             reads=[XB.b, wst.b], writes=[w.b])
        for g in range(2):
            mm(K, psSt.h[:, g * 256:(g + 1) * 256], [(XB.h[:, c, 512 + g * 128:512 + (g + 1) * 128], w.h[:, g * 256:(g + 1) * 256])],
               reads=[XB.b, w.b], writes=[psSt.b])
        P.op("vector", lambda e: e.tensor_tensor(out=v3(Sd.h), in0=v3(Sd.h), in1=bc64(dec.h[:, c, d * 8:(d + 1) * 8]), op=ALU.mult),
             reads=[Sd.b, dec.b], writes=[Sd.b])
        P.op("vector", lambda e: e.tensor_tensor(out=Sd.h, in0=Sd.h, in1=psSt.h[:, :], op=ALU.add),
             reads=[Sd.b, psSt.b], writes=[Sd.b])

    P.op("gpsimd", lambda e: e.memset(Sst[0].h, 0.0), writes=[Sst[0].b])
    P.op("gpsimd", lambda e: e.memset(Sst[1].h, 0.0), writes=[Sst[1].b])
    for c in [1, 0] + list(range(NT - 1, 1, -1)):
        P.op("gpsimd", lambda e, c=c: e.tensor_copy(out=SbE.h[:, c, :], in_=Sst[1].h), reads=[Sst[1].b], writes=[SbE.b])
        state_update(c, 1)
    zv = S.z_tm
    usv = S.usT.rearrange("(g p) t -> p g t", p=128)
    psY, psCB, psT = K.ps[7], K.ps[4], K.ps[2]
    psR = [K.ps[5], K.ps[6]]
    psYo = [K.ps[0], K.ps[1]]
    ui = 0
    for c in range(NT):
        ctx = c < 2
        sfe = SfE[c % 2]
        P.op("gpsimd", lambda e, sfe=sfe: e.tensor_copy(out=sfe.h, in_=Sst[0].h), reads=[Sst[0].b], writes=[sfe.b])
        if not (ctx and last):
            zt = zts[c % 2]
            dma_in(K, zt.h, zv[c * 128:(c + 1) * 128, :], [zt.b])
            bt = bt2[c % 2]
            dma_in(K, bt.h, xv[:, 4:6, c * 128:(c + 1) * 128], [bt.b])
            for g in range(2):
                mm(K, psCB.h[:, g * 128:(g + 1) * 128], [(bt.h[:, g, :], CT.h[:, g, c * 128:(c + 1) * 128])],
                   reads=[bt.b, CT.b], writes=[psCB.b])
            P.op("scalar", lambda e: e.copy(out=CBs.h.rearrange("p a b -> p (a b)"), in_=psCB.h[:, 0:256]),
                 reads=[psCB.b], writes=[CBs.b])
            for d in range(2):
                trd = tri if d == 0 else triT
                msk = mf if d == 0 else mb
                mt = MT[d]
                for h in range(8):
                    bk = psR[h // 4]
                    mm(K, bk.h[:, (h % 4) * 128:(h % 4 + 1) * 128],
                       [(ad.h[:, c, d * 8 + h:d * 8 + h + 1].to_broadcast([128, 128]), trd.h)],
                       reads=[ad.b, trd.b], writes=[bk.b])
                for h in range(8):
                    bk = psR[h // 4]
                    P.op("scalar", lambda e, bk=bk, h=h, c=c, d=d: e.activation(
                        out=Lr.h[:, h, :], in_=bk.h[:, (h % 4) * 128:(h % 4 + 1) * 128], func=AF.Exp,
                        bias=biasL.h[:, c, d * 8 + h:d * 8 + h + 1]), reads=[bk.b, biasL.b], writes=[Lr.b])
                P.op("vector", lambda e, msk=msk: e.scalar_tensor_tensor(
                    out=Lm.h, in0=Lr.h, scalar=1e30, in1=msk.h.unsqueeze(1).to_broadcast([128, 8, 128]),
                    op0=ALU.min, op1=ALU.mult), reads=[Lr.b, msk.b], writes=[Lm.b])
                for g in range(2):
                    P.op("vector", lambda e, g=g, mt=mt: e.tensor_tensor(
                        out=mt.h[:, g * 4:(g + 1) * 4, :], in0=Lm.h[:, g * 4:(g + 1) * 4, :],
                        in1=CBs.h[:, g, :].unsqueeze(1).to_broadcast([128, 4, 128]), op=ALU.mult),
                        reads=[Lm.b, CBs.b], writes=[mt.b])
                se = sfe.h if d == 0 else SbE.h[:, c, :]
                seb = sfe.b if d == 0 else SbE.b
                for g in range(2):
                    mm(K, psYo[d].h[:, g * 256:(g + 1) * 256], [(CT.h[:, g, c * 128:(c + 1) * 128], se[:, g * 256:(g + 1) * 256])],
                       reads=[CT.b, seb], writes=[psYo[d].b])
            for h in range(8):
                mm(K, psY.h[:, h * 64:(h + 1) * 64], [(MT[0].h[:, h, :], XB.h[:, c, h * 64:(h + 1) * 64]),
                                                      (MT[1].h[:, h, :], XB.h[:, c, h * 64:(h + 1) * 64])],
                   reads=[MT[0].b, MT[1].b, XB.b], writes=[psY.b])
            P.op("vector", lambda e, c=c: e.tensor_tensor(out=v3(Y1.h), in0=v3(psYo[0].h[:, :]), in1=bc64(eacs.h[:, c, 0:8]), op=ALU.mult),
                 reads=[psYo[0].b, eacs.b], writes=[Y1.b])
            P.op("vector", lambda e, c=c: e.tensor_tensor(out=v3(Y2.h), in0=v3(psYo[1].h[:, :]), in1=bc64(eacs.h[:, c, 8:16]), op=ALU.mult),
                 reads=[psYo[1].b, eacs.b], writes=[Y2.b])
            P.op("gpsimd", lambda e: e.tensor_tensor(out=Y1.h, in0=Y1.h, in1=Y2.h, op=ALU.add), reads=[Y1.b, Y2.b], writes=[Y1.b])
            P.op("gpsimd", lambda e, c=c: e.tensor_tensor(out=v3(Y2.h), in0=v3(XB.h[:, c, 0:512]), in1=bc64(par.h[:, 32:40]), op=ALU.mult),
                 reads=[XB.b, par.b, Y1.b], writes=[Y2.b])
            P.op("gpsimd", lambda e: e.tensor_tensor(out=Y1.h, in0=Y1.h, in1=Y2.h, op=ALU.add), reads=[Y1.b, Y2.b], writes=[Y1.b])
            P.op("vector", lambda e: e.tensor_tensor(out=Y1.h, in0=Y1.h, in1=psY.h[:, :], op=ALU.add), reads=[Y1.b, psY.b], writes=[Y1.b])
            P.op("vector", lambda e, zt=zt: e.tensor_tensor(out=YZ.h, in0=Y1.h, in1=zt.h, op=ALU.mult), reads=[Y1.b, zt.b], writes=[YZ.b])
            P.op("scalar", lambda e: e.activation(out=junk.h, in_=YZ.h, func=AF.Square, accum_out=s2.h[:, 0:1]),
                 reads=[YZ.b], writes=[junk.b, s2.b])
            P.op("vector", lambda e: e.tensor_scalar(out=s2.h[:, 1:2], in0=s2.h[:, 0:1], scalar1=1.0 / 512, scalar2=EPS,
                                                     op0=ALU.mult, op1=ALU.add), reads=[s2.b], writes=[s2.b])
            P.op("scalar", lambda e: e.sqrt(out=s2.h[:, 1:2], in_=s2.h[:, 1:2]), reads=[s2.b], writes=[s2.b])
            P.op("vector", lambda e: e.reciprocal(out=s2.h[:, 1:2], in_=s2.h[:, 1:2]), reads=[s2.b], writes=[s2.b])
            P.op("vector", lambda e: e.scalar_tensor_tensor(out=usb.h, in0=YZ.h, scalar=s2.h[:, 1:2], in1=snw.h,
                                                            op0=ALU.mult, op1=ALU.mult), reads=[YZ.b, s2.b, snw.b], writes=[usb.b])
            pbT = K.psb[2]

            def tr2(e, pbT=pbT):
                ins = None
                for j in range(4):
                    ins = e.transpose(out=pbT[:, j * 128:(j + 1) * 128], in_=usb.h[:, j * 128:(j + 1) * 128], identity=K.ident.h)
                return ins
            P.op("tensor", tr2, reads=[usb.b, K.ident.b], writes=[psT.b])
            if ctx:
                grp0, slot, glen = 0, c, 2
            else:
                grp0 = 2 + ((c - 2) // 4) * 4
                slot, glen = c - grp0, 4
            ut = usT[(0 if ctx else 1 + (c - 2) // 4) % 2]
            P.op("scalar", lambda e, ut=ut, slot=slot, pbT=pbT: e.copy(
                out=ut.h[:, :, slot * 128:(slot + 1) * 128], in_=pbT[:, 0:512].rearrange("p (g m) -> p g m", g=4)),
                reads=[psT.b], writes=[ut.b])
            if slot == glen - 1:
                dma_out(K, usv[:, :, grp0 * 128:(grp0 + glen) * 128], ut.h[:, :, 0:glen * 128], reads=[ut.b])
        state_update(c, 0)


def phase_e(K, l, last):
    P, I, S, sb = K.P, K.I, K.S, K.sb
    K.off = K.base
    wbr = [sb(f"wbr{i}", [128, 4, D], BF16) for i in range(3)]
    wo = sb("wo", [128, 8, D], BF16)
    wstage = [sb(f"wstage{i}", [128, 4, D], F32) for i in range(1)]
    uts = [sb(f"ut{i}", [128, 3, 4, 512], BF16) for i in range(2)]
    gt = sb("gt", [128, 24, 512], BF16)
    yTs = [sb(f"yT{i}", [128, 8, 512], BF16) for i in range(2)]
    M = [sb(f"M{i}", [128, 512], F32) for i in range(3)]
    xts = [sb(f"ext{i}", [128, D], F32) for i in range(2)]
    xns = [sb(f"exn{i}", [128, D], F32) for i in range(2)]
    tmp = [sb(f"etmp{i}", [128, 512], F32) for i in range(2)]
    srcs = [(I.w_of[l], wbr[0].h), (I.w_oa[l], wbr[1].h), (I.w_os[l], wbr[2].h),
            (I.w_out[l, 0:512, :], wo.h[:, 0:4, :]), (I.w_out[l, 512:1024, :], wo.h[:, 4:8, :])]
    wbufs = [wbr[0].b, wbr[1].b, wbr[2].b, wo.b, wo.b]
    for i, (src, dst) in enumerate(srcs):
        ws = wstage[0]
        dma_in(K, ws.h, src.rearrange("(kc p) n -> p kc n", p=128), [ws.b])
        P.op("vector", lambda e, ws=ws, dst=dst: e.tensor_copy(out=dst[:, 0:2, :], in_=ws.h[:, 0:2, :]), reads=[ws.b], writes=[wbufs[i]])
        P.op("gpsimd", lambda e, ws=ws, dst=dst: e.tensor_copy(out=dst[:, 2:4, :], in_=ws.h[:, 2:4, :]), reads=[ws.b], writes=[wbufs[i]])
    if last:
        nfB = sb("nfB", [128, D], F32)
        s2 = sb("es2", [128, 2], F32)
        junk = sb("ejunk", [128, D], BF16)
        dma_in(K, nfB.h, I.norm_f.partition_broadcast(128), [nfB.b])
    uviews = [t_.rearrange("(g p) t -> p g t", p=128) for t_ in (S.ufT, S.uaT, S.usT)]
    gv = S.gT.rearrange("(c p) t -> p c t", p=128)
    xsrc = I.xin if l == 0 else S.xs
    xi = 0
    for si, (tok0, n, ctx) in enumerate(tok_tiles()):
        if ctx and last:
            continue
        ut = uts[si % 2]
        yT = yTs[si % 2]
        gA = K.mod["g_c" if ctx else "g_l"]
        for br in range(3):
            dma_in(K, ut.h[:, br, :, 0:n], uviews[br][:, :, tok0:tok0 + n], [ut.b])
        for c0 in range(0, 24, 8):
            dma_in(K, gt.h[:, c0:c0 + 8, 0:n], gv[:, c0:c0 + 8, tok0:tok0 + n], [gt.b])
        for oc in range(8):
            banks = [K.ps[br + 3 * (oc % 2)] for br in range(3)]
            for br in range(3):
                mm(K, banks[br].h[:, 0:n], [(wbr[br].h[:, kc, oc * 128:(oc + 1) * 128], ut.h[:, br, kc, 0:n]) for kc in range(4)],
                   reads=[wbr[br].b, ut.b], writes=[banks[br].b])
            for br in range(3):
                P.op("vector", lambda e, br=br, oc=oc, n=n, bank=banks[br]: e.tensor_tensor(
                    out=M[br].h[:, 0:n], in0=bank.h[:, 0:n], in1=gt.h[:, br * 8 + oc, 0:n], op=ALU.mult),
                    reads=[banks[br].b, gt.b], writes=[M[br].b])
            P.op("gpsimd", lambda e, n=n: e.tensor_tensor(out=M[0].h[:, 0:n], in0=M[0].h[:, 0:n], in1=M[1].h[:, 0:n], op=ALU.add),
                 reads=[M[0].b, M[1].b], writes=[M[0].b])
            P.op("gpsimd", lambda e, n=n, oc=oc, yT=yT: e.tensor_tensor(out=yT.h[:, oc, 0:n], in0=M[0].h[:, 0:n], in1=M[2].h[:, 0:n], op=ALU.add),
                 reads=[M[0].b, M[2].b], writes=[yT.b])
        for tt in range(n // 128):
            r0 = tok0 + tt * 128
            xt, xn = xts[xi % 2], xns[xi % 2]
            xi += 1
            dma_in(K, xt.h, xsrc[r0:r0 + 128, :], [xt.b])
            for half in range(2):
                bank = K.ps[6 + half]
                tp = tmp[half]
                hs = slice(half * 512, (half + 1) * 512)
                mm(K, bank.h[:, :], [(yT.h[:, kc, tt * 128:(tt + 1) * 128], wo.h[:, kc, hs]) for kc in range(8)],
                   reads=[yT.b, wo.b], writes=[bank.b])
                P.op("vector", lambda e, bank=bank, tp=tp, hs=hs, gA=gA: e.tensor_tensor(out=tp.h, in0=bank.h[:, :], in1=gA.h[:, hs], op=ALU.mult),
                     reads=[bank.b, gA.b], writes=[tp.b])
                P.op("gpsimd", lambda e, tp=tp, xt=xt, xn=xn, hs=hs: e.tensor_tensor(out=xn.h[:, hs], in0=tp.h, in1=xt.h[:, hs], op=ALU.add),
                     reads=[tp.b, xt.b], writes=[xn.b])
            if not last:
                dma_out(K, S.xs[r0:r0 + 128, :], xn.h, reads=[xn.b])
            else:
                P.op("scalar", lambda e, xn=xn: e.activation(out=junk.h, in_=xn.h, func=AF.Square, accum_out=s2.h[:, 0:1]),
                     reads=[xn.b], writes=[junk.b, s2.b])
                P.op("vector", lambda e: e.tensor_scalar(out=s2.h[:, 1:2], in0=s2.h[:, 0:1], scalar1=1.0 / D, scalar2=EPS,
                                                         op0=ALU.mult, op1=ALU.add), reads=[s2.b], writes=[s2.b])
                P.op("scalar", lambda e: e.sqrt(out=s2.h[:, 1:2], in_=s2.h[:, 1:2]), reads=[s2.b], writes=[s2.b])
                P.op("vector", lambda e: e.reciprocal(out=s2.h[:, 1:2], in_=s2.h[:, 1:2]), reads=[s2.b], writes=[s2.b])
                P.op("vector", lambda e, xn=xn, xt=xt: e.scalar_tensor_tensor(out=xt.h, in0=xn.h, scalar=s2.h[:, 1:2], in1=nfB.h,
                                                                               op0=ALU.mult, op1=ALU.mult),
                     reads=[xn.b, s2.b, nfB.b], writes=[xt.b])
                dma_out(K, K.out[r0 - NCTX:r0 - NCTX + 128, :], xt.h, reads=[xt.b])
```

```python
import math
from contextlib import ExitStack

import numpy as np
import ml_dtypes

import concourse.bass as bass
import concourse.mybir as mybir
from concourse.bass_utils import run_bass_kernel_spmd

F32 = mybir.dt.float32
BF16 = mybir.dt.bfloat16
AF = mybir.ActivationFunctionType
ALU = mybir.AluOpType
AX = mybir.AxisListType

D = 1024
SEQ = 4096
NCTX = 256
T = SEQ + NCTX
NT = T // 128
DEPTH = 4
EPS = 1e-6
IN_W = 7696
C_FU, C_FG, C_Q, C_K, C_V, C_AG, C_Z, C_XBC, C_DT, C_GT = 0, 512, 1024, 1536, 2048, 2560, 3072, 3584, 4608, 4624

COMPUTE = ("tensor", "vector", "scalar", "gpsimd")
STREAMS = ("tensor", "vector", "scalar", "gpsimd", "sync")
NS = 16
EPOCH = 20000


class Buf:
    __slots__ = ("name", "w", "r")

    def __init__(self, name=""):
        self.name = name
        self.w = None
        self.r = []


class Op:
    __slots__ = ("stream", "fn", "is_dma", "signal", "waits", "n", "slot", "seq", "clock", "semi", "semv")


class Prog:
    def __init__(self):
        self.streams = {s: [] for s in STREAMS}
        self.known = {s: {c: 0 for c in COMPUTE} for s in STREAMS}
        self.known_dma = {s: {} for s in STREAMS}
        self.nseq = {c: 0 for c in COMPUTE}
        self.dma_ops = {s: [] for s in STREAMS}
        self.last = {c: None for c in COMPUTE}
        self.pending = {s: [] for s in STREAMS}

    def op(self, stream, fn, reads=(), writes=(), dma=False):
        o = Op()
        o.stream = stream
        o.fn = fn
        o.is_dma = dma
        o.signal = False
        o.waits = []
        deps = []
        for b in reads:
            if b.w is not None:
                deps.append(b.w)
        for b in writes:
            if b.w is not None:
                deps.append(b.w)
            deps.extend(b.r)
        if self.pending[stream]:
            deps.extend(self.pending[stream])
            self.pending[stream] = []
        if dma:
            n = len(self.dma_ops[stream])
            o.n = n
            o.slot = n % NS
            if n >= NS:
                deps.append(self.dma_ops[stream][n - NS])
            self.dma_ops[stream].append(o)
        else:
            self.nseq[stream] += 1
            o.seq = self.nseq[stream]
            self.last[stream] = o
        kn = self.known[stream]
        kd = self.known_dma[stream]
        for d in deps:
            if d.is_dma:
                key = (d.stream, d.slot)
                if kd.get(key, -1) < d.n:
                    kd[key] = d.n
                    o.waits.append(d)
                    for c, v in d.clock.items():
                        if kn[c] < v:
                            kn[c] = v
            else:
                if d.stream == "tensor" and stream == "tensor" and not dma:
                    continue
                if kn[d.stream] < d.seq:
                    o.waits.append(d)
                    d.signal = True
                    kn[d.stream] = d.seq
                    for c, v in d.clock.items():
                        if kn[c] < v:
                            kn[c] = v
        o.clock = dict(kn)
        for b in reads:
            b.r.append(o)
        for b in writes:
            b.w = o
            b.r = []
        self.streams[stream].append(o)
        return o

    def barrier(self):
        ops = []
        for c in COMPUTE:
            if self.last[c] is not None:
                ops.append(self.last[c])
        for s in STREAMS:
            ops.extend(self.dma_ops[s][-NS:])
        for s in STREAMS:
            self.pending[s] = list(ops)

    def emit(self, nc):
        self.barrier()
        nsem = {}
        for c in COMPUTE:
            cnt = 0
            for o in self.streams[c]:
                if o.is_dma:
                    continue
                if o.signal:
                    o.semi = cnt // EPOCH
                    o.semv = cnt % EPOCH + 1
                    cnt += 1
            nsem[c] = max(1, (cnt + EPOCH - 1) // EPOCH)
        with ExitStack() as es:
            csem = {c: [es.enter_context(nc.semaphore(f"c_{c}_{i}")) for i in range(nsem[c])] for c in COMPUTE}
            dsem = {s: [es.enter_context(nc.semaphore(f"d_{s}_{i}")) for i in range(NS)]
                    for s in STREAMS if self.dma_ops[s]}

            def wait(e, d):
                if d.is_dma:
                    e.wait_ge(dsem[d.stream][d.slot], 16 * (d.n // NS + 1))
                else:
                    e.wait_ge(csem[d.stream][d.semi], d.semv)

            def body_for(stream):
                def body(e):
                    kn = self.known[stream]
                    kd = self.known_dma[stream]
                    for o in self.streams[stream]:
                        for d in o.waits:
                            wait(e, d)
                        ins = o.fn(e)
                        if o.is_dma:
                            ins.then_inc(dsem[stream][o.slot], 16)
                        elif o.signal:
                            ins.then_inc(csem[stream][o.semi], 1)
                    for d in self.pending[stream]:
                        if d.is_dma:
                            if kd.get((d.stream, d.slot), -1) < d.n:
                                wait(e, d)
                        elif d.signal and not (d.stream == stream):
                            if kn[d.stream] < d.seq:
                                wait(e, d)
                return body

            for c in COMPUTE:
                pass
            with nc.Block() as block:
                block.sync(body_for("sync"))
                block.tensor(body_for("tensor"))
                block.vector(body_for("vector"))
                block.scalar(body_for("scalar"))
                block.gpsimd(body_for("gpsimd"))


class Tl:
    __slots__ = ("h", "b")

    def __init__(self, h, name=""):
        self.h = h
        self.b = Buf(name)


class Ctx:
    pass


def build_program(n_layers=DEPTH, stop_after=None, dbg=()):
    nc = bass.Bass("TRN2", target_bir_lowering=False)
    P = Prog()
    K = Ctx()
    K.nc, K.P = nc, P
    K.dbgset = set(dbg)

    def din(name, shape, dt=F32):
        return nc.dram_tensor(name, list(shape), dt, kind="ExternalInput").ap()

    def dscr(name, shape, dt=BF16):
        kind = "ExternalOutput" if name in dbg else "Internal"
        return nc.dram_tensor(name, list(shape), dt, kind=kind).ap()

    I = Ctx()
    I.xin = din("xin", [T, D])
    I.c_t = din("c_t", [128, 8])
    I.cctx_t = din("cctx_t", [128, 8])
    I.w_mod = din("w_mod", [DEPTH, D, 3 * D])
    I.b_mod = din("b_mod", [DEPTH, 3 * D])
    I.norm_w = din("norm_w", [DEPTH, D])
    I.w_in = din("w_in", [DEPTH, D, IN_W])
    I.convw_t = din("convw_t", [DEPTH, 128, 8, 5])
    I.convb_t = din("convb_t", [DEPTH, 128, 8])
    I.a_log = din("a_log", [DEPTH, 16])
    I.dt_bias = din("dt_bias", [DEPTH, 16])
    I.d_skip = din("d_skip", [DEPTH, 8])
    I.ssd_norm_w = din("ssd_norm_w", [DEPTH, 512])
    I.lam = din("lam", [DEPTH, 256])
    I.subln_w = din("subln_w", [DEPTH, 128])
    I.w_of = din("w_of", [DEPTH, 512, D])
    I.w_oa = din("w_oa", [DEPTH, 512, D])
    I.w_os = din("w_os", [DEPTH, 512, D])
    I.w_out = din("w_out", [DEPTH, D, D])
    I.norm_f = din("norm_f", [D])
    I.ident = din("ident", [128, 128], BF16)
    I.cosT = din("cosT", [128, SEQ])
    I.sinT = din("sinT", [128, SEQ])
    I.dftB = din("dftB", [128, 256], BF16)
    I.C4 = din("C4", [SEQ, SEQ], BF16)
    I.S4 = din("S4", [SEQ, SEQ], BF16)
    I.C2 = din("C2", [NCTX, NCTX], BF16)
    I.S2 = din("S2", [NCTX, NCTX], BF16)
    I.tri = din("tri", [128, 128])
    I.triT = din("triT", [128, 128])
    I.mask_f = din("mask_f", [128, 128])
    I.mask_b = din("mask_b", [128, 128])
    K.I = I
    out = nc.dram_tensor("out", [SEQ, D], F32, kind="ExternalOutput").ap()
    K.out = out

    S = Ctx()
    S.xs = dscr("xs", [T, D], F32)
    S.fuT = dscr("fuT", [512, T])
    S.fgT = dscr("fgT", [512, T])
    S.qT = dscr("qT", [512, T])
    S.kT = dscr("kT", [512, T])
    S.v_tm = dscr("v_tm", [T, 512])
    S.agT = dscr("agT", [512, T])
    S.z_tm = dscr("z_tm", [T, 512])
    S.xbcT = dscr("xbcT", [1024, T])
    S.gT = dscr("gT", [3072, T])
    S.ufT = dscr("ufT", [512, T])
    S.uaT = dscr("uaT", [512, T])
    S.usT = dscr("usT", [512, T])
    S.dbg = dscr("dbg", [128, 8192], F32)
    K.S = S

    with ExitStack() as top:
        ARENA = 52800
        arena = top.enter_context(nc.sbuf_tensor("arena", [128, ARENA], F32))
        K.base = 0
        K.off = 0

        def sb(name, shape, dt=F32):
            n = 1
            for v in shape[1:]:
                n *= v
            words = n if dt == F32 else (n + 1) // 2
            words = (words + 7) // 8 * 8
            assert K.off + words <= ARENA, (name, K.off, words)
            ap = arena[:, K.off:K.off + words]
            K.off += words
            if dt != F32:
                ap = ap.bitcast(dt)
            ap = ap[:, 0:n]
            if len(shape) == 3:
                ap = ap.rearrange("p (a b) -> p a b", a=shape[1], b=shape[2])
            elif len(shape) == 4:
                ap = ap.rearrange("p (a b c) -> p a b c", a=shape[1], b=shape[2], c=shape[3])
            return Tl(ap, name)
        K.sb = sb
        K.psall = top.enter_context(nc.psum_tensor("psall", [128, 4096], F32))
        K.ps = [Tl(K.psall[:, i * 512:(i + 1) * 512], f"ps{i}") for i in range(8)]
        K.psb = [K.psall[:, i * 512:(i + 1) * 512].bitcast(BF16) for i in range(8)]
        K.ident = sb("ident", [128, 128], BF16)
        K.ones_bf = sb("ones_bf", [128, 128], BF16)
        K.ones_f = sb("ones_f", [128, 128], F32)
        K.dtraw = sb("dtraw", [128, NT, 16], F32)
        K.mod = {k: sb("mod_" + k, [128, D], F32) for k in ("sc_l", "sh_l", "g_l", "sc_c", "sh_c", "g_c")}
        K.base = K.off
        P.op("sync", lambda e: e.dma_start(out=K.ident.h, in_=I.ident), writes=[K.ident.b], dma=True)
        P.op("vector", lambda e: e.memset(K.ones_bf.h, 1.0), writes=[K.ones_bf.b])
        P.op("vector", lambda e: e.memset(K.ones_f.h, 1.0), writes=[K.ones_f.b])
        phases = [phase_mod, phase_a1, phase_a2, phase_b, phase_c, phase_d, phase_e]
        done = False
        for l in range(n_layers):
            for ph in phases:
                ph(K, l, l == DEPTH - 1)
                P.barrier()
                if stop_after == (l, ph.__name__):
                    done = True
                    break
            if done:
                break
        P.emit(nc)
    return nc


def dma_in(K, out_ap, in_ap, writes, reads=(), q="sync"):
    return K.P.op(q, lambda e: e.dma_start(out=out_ap, in_=in_ap), reads=reads, writes=writes, dma=True)


def dma_out(K, out_ap, in_ap, reads, q="gpsimd"):
    return K.P.op(q, lambda e: e.dma_start(out=out_ap, in_=in_ap), reads=reads, writes=(), dma=True)


def mm(K, out_ap, pairs, reads, writes, start=True, stop=True):
    def fn(e):
        n = len(pairs)
        ins = None
        for i, (l, r) in enumerate(pairs):
            ins = e.matmul(out_ap, lhsT=l, rhs=r, start=(start and i == 0), stop=(stop and i == n - 1))
        return ins
    return K.P.op("tensor", fn, reads=reads, writes=writes)


def tok_tiles():
    r = [(0, NCTX, True)]
    for s in range(SEQ // 512):
        r.append((NCTX + s * 512, 512, False))
    return r


def phase_mod(K, l, last):
    P, I, sb = K.P, K.I, K.sb
    K.off = K.base
    cs = sb("cs", [128, 16], F32)
    lh = sb("lh", [128, 16, 128], F32)
    bB = sb("bB", [128, 3 * D], F32)
    nwB = sb("nwB", [128, D], F32)
    tmp = sb("mtmp", [128, 512], F32)
    wt = [sb(f"wmod{i}", [128, 1536], F32) for i in range(2)]
    dma_in(K, cs.h[:, 0:8], I.c_t, [cs.b])
    dma_in(K, cs.h[:, 8:16], I.cctx_t, [cs.b])
    dma_in(K, bB.h, I.b_mod[l].partition_broadcast(128), [bB.b])
    dma_in(K, nwB.h, I.norm_w[l].partition_broadcast(128), [nwB.b])
    P.op("scalar", lambda e: e.activation(out=cs.h, in_=cs.h, func=AF.Silu), reads=[cs.b], writes=[cs.b])
    P.op("vector", lambda e: e.tensor_copy(out=lh.h, in_=cs.h.unsqueeze(2).to_broadcast([128, 16, 128])),
         reads=[cs.b], writes=[lh.b])
    names = (("sh_l", "sc_l", "g_l"), ("sh_c", "sc_c", "g_c"))
    it = 0
    for half in range(2):
        for kc in range(8):
            w = wt[it % 2]
            it += 1
            dma_in(K, w.h, I.w_mod[l, kc * 128:(kc + 1) * 128, half * 1536:(half + 1) * 1536], [w.b])
            for who in range(2):
                for j in range(3):
                    bank = K.ps[who * 3 + j]
                    mm(K, bank.h[:, :], [(lh.h[:, who * 8 + kc, :], w.h[:, j * 512:(j + 1) * 512])],
                       reads=[lh.b, w.b], writes=[bank.b], start=(kc == 0), stop=(kc == 7))
        for who in range(2):
            for j in range(3):
                jb = half * 3 + j
                kind, off = jb // 2, (jb % 2) * 512
                bank = K.ps[who * 3 + j]
                dst = K.mod[names[who][kind]]
                bsl = bB.h[:, jb * 512:(jb + 1) * 512]
                if kind == 1:
                    P.op("vector", lambda e, bank=bank, bsl=bsl: e.tensor_tensor(out=tmp.h, in0=bank.h[:, :], in1=bsl, op=ALU.add),
                         reads=[bank.b, bB.b], writes=[tmp.b])
                    P.op("vector", lambda e, dst=dst, off=off: e.scalar_tensor_tensor(
                        out=dst.h[:, off:off + 512], in0=tmp.h, scalar=1.0, in1=nwB.h[:, off:off + 512],
                        op0=ALU.add, op1=ALU.mult), reads=[tmp.b, nwB.b], writes=[dst.b])
                else:
                    P.op("vector", lambda e, bank=bank, bsl=bsl, dst=dst, off=off: e.tensor_tensor(
                        out=dst.h[:, off:off + 512], in0=bank.h[:, :], in1=bsl, op=ALU.add),
                        reads=[bank.b, bB.b], writes=[dst.b])


def phase_a1(K, l, last):
    P, I, S, sb = K.P, K.I, K.S, K.sb
    K.off = K.base
    K.hT = sb("hT_all", [128, 8, T], BF16)
    K.a2_base = K.off
    NB1 = 4
    xt = [sb(f"xt{i}", [128, D], F32) for i in range(NB1)]
    junk = sb("junk", [128, D], BF16)
    t1 = [sb(f"t1{i}", [128, D], F32) for i in range(NB1)]
    hb = [sb(f"hb{i}", [128, D], BF16) for i in range(NB1)]
    st = [sb(f"st{i}", [128, 2], F32) for i in range(NB1)]
    src = I.xin if l == 0 else S.xs

    def bufs(t):
        return xt[t % NB1], t1[t % NB1], hb[t % NB1], st[t % NB1]

    def stage1(t):
        x, tt, h, s2 = bufs(t)
        dma_in(K, x.h, src[t * 128:(t + 1) * 128, :], [x.b])
        P.op("scalar", lambda e, x=x, s2=s2: e.activation(out=junk.h, in_=x.h, func=AF.Square, accum_out=s2.h[:, 0:1]),
             reads=[x.b], writes=[junk.b, s2.b])
        P.op("scalar", lambda e, s2=s2: e.activation(out=s2.h[:, 1:2], in_=s2.h[:, 0:1], func=AF.Ln, scale=1.0 / D, bias=EPS),
             reads=[s2.b], writes=[s2.b])
        P.op("scalar", lambda e, s2=s2: e.activation(out=s2.h[:, 1:2], in_=s2.h[:, 1:2], func=AF.Exp, scale=-0.5),
             reads=[s2.b], writes=[s2.b])

    def stage2(t):
        x, tt, h, s2 = bufs(t)
        ctx = t < 2
        sc = K.mod["sc_c" if ctx else "sc_l"]
        sh = K.mod["sh_c" if ctx else "sh_l"]
        P.op("vector", lambda e, x=x, tt=tt, s2=s2, sc=sc: e.scalar_tensor_tensor(
            out=tt.h, in0=x.h, scalar=s2.h[:, 1:2], in1=sc.h, op0=ALU.mult, op1=ALU.mult),
            reads=[x.b, s2.b, sc.b], writes=[tt.b])
        P.op("gpsimd", lambda e, tt=tt, h=h, sh=sh: e.tensor_tensor(out=h.h, in0=tt.h, in1=sh.h, op=ALU.add),
             reads=[tt.b, sh.b], writes=[h.b])

    def stage3(t):
        x, tt, h, s2 = bufs(t)
        bank = K.ps[t % 2]
        pb = K.psb[t % 2]

        def tr(e, h=h, pb=pb):
            ins = None
            for kc in range(8):
                ins = e.transpose(out=pb[:, kc * 128:(kc + 1) * 128], in_=h.h[:, kc * 128:(kc + 1) * 128],
                                  identity=K.ident.h)
            return ins
        P.op("tensor", tr, reads=[h.b, K.ident.b], writes=[bank.b])
        P.op("scalar", lambda e, pb=pb, t=t: e.copy(out=K.hT.h[:, :, t * 128:(t + 1) * 128],
                                                   in_=pb.rearrange("p (a b) -> p a b", a=8, b=128)),
             reads=[bank.b], writes=[K.hT.b])

    for step in range(NT + 2):
        if step < NT:
            stage1(step)
        if 0 <= step - 1 < NT:
            stage2(step - 1)
        if 0 <= step - 2 < NT:
            stage3(step - 2)


def phase_a2(K, l, last):
    P, I, S, sb = K.P, K.I, K.S, K.sb
    K.off = K.a2_base
    Wv = I.w_in[l].rearrange("(kc p) n -> p kc n", p=128)
    wf = sb("wf", [128, 8, 512], F32)
    wbs = [sb(f"wb{i}", [128, 8, 512], BF16) for i in range(2)]
    wrs = [sb(f"wr{i}", [128, 8, 512], BF16) for i in range(2)]
    stg = [sb(f"stg{i}", [128, T], BF16) for i in range(2)]
    sub_base = K.off
    cnt = {"s": 0, "b": 0}
    tiles = tok_tiles()

    groups = [("fm", C_FU, 512, None, S.fuT), ("fm", C_FG, 512, AF.Silu, S.fgT),
              ("rope", C_Q, 512, None, S.qT), ("rope", C_K, 512, None, S.kT),
              ("tm", C_V, 512, None, S.v_tm), ("fm", C_AG, 512, AF.Silu, S.agT), ("tm", C_Z, 512, AF.Silu, S.z_tm),
              ("conv", C_XBC, 512, 0, None), ("conv", C_XBC + 512, 512, 1, None), ("dt", C_DT, 16, None, None)]
    for g in range(6):
        groups.append(("fm", C_GT + g * 512, 512, AF.Sigmoid, S.gT[g * 512:(g + 1) * 512, :]))
    started, loaded = set(), {}

    def start_load(i):
        if i >= len(groups) or i in started:
            return
        started.add(i)
        _, c0, ncols, _, _ = groups[i]
        dma_in(K, wf.h[:, :, 0:ncols], Wv[:, :, c0:c0 + ncols], [wf.b])

    def finish_load(i):
        if i >= len(groups) or i in loaded:
            return
        start_load(i)
        kind, c0, ncols, _, _ = groups[i]
        wb, wr = wbs[i % 2], wrs[i % 2]
        P.op("vector", lambda e: e.tensor_copy(out=wb.h[:, 0:5, 0:ncols], in_=wf.h[:, 0:5, 0:ncols]),
             reads=[wf.b], writes=[wb.b])
        P.op("gpsimd", lambda e: e.tensor_copy(out=wb.h[:, 5:8, 0:ncols], in_=wf.h[:, 5:8, 0:ncols]),
             reads=[wf.b], writes=[wb.b])
        if kind == "rope":
            def r1(e):
                ins = None
                for kc in range(8):
                    src = wf.h[:, kc, :].rearrange("p (n h s) -> p n h s", h=2, s=16)
                    dst = wr.h[:, kc, :].rearrange("p (n h s) -> p n h s", h=2, s=16)
                    ins = e.mul(out=dst[:, :, 0, :], in_=src[:, :, 1, :], mul=-1.0)
                return ins

            def r2(e):
                ins = None
                for kc in range(8):
                    src = wf.h[:, kc, :].rearrange("p (n h s) -> p n h s", h=2, s=16)
                    dst = wr.h[:, kc, :].rearrange("p (n h s) -> p n h s", h=2, s=16)
                    ins = e.tensor_copy(out=dst[:, :, 1, :], in_=src[:, :, 0, :])
                return ins
            P.op("scalar", r1, reads=[wf.b], writes=[wr.b])
            P.op("vector", r2, reads=[wf.b], writes=[wr.b])
        loaded[i] = (wb, wr)

    def next_bank():
        b = K.ps[2 + cnt["b"] % 4]
        cnt["b"] += 1
        return b

    def proj(bank, wt, j, tok0, n):
        mm(K, bank.h[:, 0:n], [(wt.h[:, kc, j * 128:(j + 1) * 128], K.hT.h[:, kc, tok0:tok0 + n]) for kc in range(8)],
           reads=[wt.b, K.hT.b], writes=[bank.b])

    def fm_group(wb, func, dest, mid):
        for j in range(4):
            s = stg[cnt["s"] % 2]
            cnt["s"] += 1
            for si, (tok0, n, ctx) in enumerate(tiles):
                bank = next_bank()
                proj(bank, wb, j, tok0, n)
                if func is None and si % 2 == 1:
                    P.op("vector", lambda e, s=s, bank=bank, tok0=tok0, n=n: e.tensor_copy(
                        out=s.h[:, tok0:tok0 + n], in_=bank.h[:, 0:n]), reads=[bank.b], writes=[s.b])
                else:
                    f = AF.Copy if func is None else func
                    P.op("scalar", lambda e, s=s, bank=bank, tok0=tok0, n=n, f=f: e.activation(
                        out=s.h[:, tok0:tok0 + n], in_=bank.h[:, 0:n], func=f), reads=[bank.b], writes=[s.b])
            dma_out(K, dest[j * 128:(j + 1) * 128, :], s.h, reads=[s.b])
            if j == 1:
                mid()

    def tm_group(wb, func, dest, mid, tms):
        dv = dest.rearrange("(a p) n -> p a n", p=128)
        for t in range(NT):
            tm = tms[(t // 4) % 2]
            bank = next_bank()
            mm(K, bank.h[:, :], [(K.hT.h[:, kc, t * 128:(t + 1) * 128], wb.h[:, kc, :]) for kc in range(8)],
               reads=[wb.b, K.hT.b], writes=[bank.b])
            f = AF.Copy if func is None else func
            P.op("scalar", lambda e, tm=tm, bank=bank, t=t, f=f: e.activation(out=tm.h[:, t % 4, :], in_=bank.h[:, :], func=f),
                 reads=[bank.b], writes=[tm.b])
            if t % 4 == 3 or t == NT - 1:
                t0 = (t // 4) * 4
                na = t - t0 + 1
                dma_out(K, dv[:, t0:t0 + na, :], tm.h[:, 0:na, :], reads=[tm.b])
            if t == NT // 2:
                mid()

    def rope_group(wb, wr, dest, mid, rs):
        tabs, rt1, rt2, ro = rs
        rc = 0
        for si, (tok0, n, ctx) in enumerate(tiles):
            if not ctx:
                tab = tabs[si % 2]
                p0 = tok0 - NCTX
                dma_in(K, tab.h[:, 0, :], I.cosT[:, p0:p0 + 512], [tab.b])
                dma_in(K, tab.h[:, 1, :], I.sinT[:, p0:p0 + 512], [tab.b])
            for j in range(4):
                o = ro[rc % 4]
                a, b2 = rt1[rc % 2], rt2[rc % 2]
                rc += 1
                bank = next_bank()
                proj(bank, wb, j, tok0, n)
                if ctx:
                    P.op("scalar", lambda e, o=o, bank=bank, n=n: e.copy(out=o.h[:, 0:n], in_=bank.h[:, 0:n]),
                         reads=[bank.b], writes=[o.b])
                else:
                    bank2 = next_bank()
                    proj(bank2, wr, j, tok0, n)
                    P.op("vector", lambda e, a=a, bank=bank, tab=tab: e.tensor_tensor(
                        out=a.h, in0=bank.h[:, :], in1=tab.h[:, 0, :], op=ALU.mult), reads=[bank.b, tab.b], writes=[a.b])
                    P.op("vector", lambda e, b2=b2, bank2=bank2, tab=tab: e.tensor_tensor(
                        out=b2.h, in0=bank2.h[:, :], in1=tab.h[:, 1, :], op=ALU.mult), reads=[bank2.b, tab.b], writes=[b2.b])
                    P.op("gpsimd", lambda e, o=o, a=a, b2=b2: e.tensor_tensor(out=o.h, in0=a.h, in1=b2.h, op=ALU.add),
                         reads=[a.b, b2.b], writes=[o.b])
                dma_out(K, dest[j * 128:(j + 1) * 128, tok0:tok0 + n], o.h[:, 0:n], reads=[o.b])
            if si == 4:
                mid()

    def conv_group(wb, g, mid, cs):
        cst, acc, cw, cb = cs
        ci = 0
        for j in range(4):
            cc = g * 4 + j
            s = stg[cnt["s"] % 2]
            cnt["s"] += 1
            for si, (tok0, n, ctx) in enumerate(tiles):
                bank = next_bank()
                proj(bank, wb, j, tok0, n)
                col = tok0 + (2 if ctx else 6)
                P.op("scalar", lambda e, bank=bank, col=col, n=n: e.copy(out=cst.h[:, col:col + n], in_=bank.h[:, 0:n]),
                     reads=[bank.b], writes=[cst.b])
            for si, (tok0, n, ctx) in enumerate(tiles):
                col = tok0 + (2 if ctx else 6)
                a = acc[ci % 2]
                ci += 1
                P.op("vector", lambda e, a=a, col=col, n=n, cc=cc: e.tensor_scalar_mul(
                    out=a.h[:, 0:n], in0=cst.h[:, col - 2:col - 2 + n], scalar1=cw.h[:, cc, 0:1]),
                    reads=[cst.b, cw.b], writes=[a.b])
                for k in range(1, 5):
                    P.op("vector", lambda e, a=a, col=col, n=n, cc=cc, k=k: e.scalar_tensor_tensor(
                        out=a.h[:, 0:n], in0=cst.h[:, col - 2 + k:col - 2 + k + n], scalar=cw.h[:, cc, k:k + 1],
                        in1=a.h[:, 0:n], op0=ALU.mult, op1=ALU.add), reads=[cst.b, cw.b, a.b], writes=[a.b])
                P.op("scalar", lambda e, s=s, a=a, tok0=tok0, n=n, cc=cc: e.activation(
                    out=s.h[:, tok0:tok0 + n], in_=a.h[:, 0:n], func=AF.Silu, bias=cb.h[:, cc:cc + 1]),
                    reads=[a.b, cb.b], writes=[s.b])
            dma_out(K, S.xbcT[cc * 128:(cc + 1) * 128, :], s.h, reads=[s.b])
            if j == 1:
                mid()

    def dt_group(wb, mid):
        for t in range(NT):
            bank = K.ps[6] if t < 32 else K.ps[7]
            r0 = (t % 32) * 16
            mm(K, bank.h[:, r0:r0 + 16], [(K.hT.h[:, kc, t * 128:(t + 1) * 128], wb.h[:, kc, 0:16]) for kc in range(8)],
               reads=[wb.b, K.hT.b], writes=[bank.b])
        mid()
        P.op("vector", lambda e: e.tensor_copy(out=K.dtraw.h[:, 0:32, :],
                                               in_=K.ps[6].h[:, :].rearrange("p (a b) -> p a b", a=32, b=16)),
             reads=[K.ps[6].b], writes=[K.dtraw.b])
        P.op("vector", lambda e: e.tensor_copy(out=K.dtraw.h[:, 32:34, :],
                                               in_=K.ps[7].h[:, 0:32].rearrange("p (a b) -> p a b", a=2, b=16)),
             reads=[K.ps[7].b], writes=[K.dtraw.b])
        if "dbg" in K.dbgset:
            dma_out(K, S.dbg[:, 0:NT * 16], K.dtraw.h.rearrange("p a b -> p (a b)"), reads=[K.dtraw.b])

    finish_load(0)
    rs = cs = tms = None
    for i, (kind, c0, ncols, arg, dest) in enumerate(groups):
        wb, wr = loaded[i]
        start_load(i + 1)
        mid = (lambda i=i: finish_load(i + 1))
        if kind == "rope" and rs is None:
            rs = ([sb(f"rtab{k}", [128, 2, 512], F32) for k in range(2)],
                  [sb(f"rt1_{k}", [128, 512], F32) for k in range(2)],
                  [sb(f"rt2_{k}", [128, 512], F32) for k in range(2)],
                  [sb(f"ro{k}", [128, 512], BF16) for k in range(4)])
        if kind == "tm" and tms is None:
            P.barrier()
            K.off = sub_base
            tms = [sb(f"tm_{k}", [128, 4, 512], BF16) for k in range(2)]
            CW = T + 8
            cst = sb("cst", [128, CW], F32)
            acc = [sb(f"cacc{k}", [128, 512], F32) for k in range(2)]
            cw = sb("cw", [128, 8, 5], F32)
            cb = sb("cb", [128, 8], F32)
            dma_in(K, cw.h, I.convw_t[l], [cw.b])
            dma_in(K, cb.h, I.convb_t[l], [cb.b])
            P.op("gpsimd", lambda e: e.memset(cst.h, 0.0), writes=[cst.b])
            cs = (cst, acc, cw, cb)
        if kind == "fm":
            fm_group(wb, arg, dest, mid)
        elif kind == "rope":
            rope_group(wb, wr, dest, mid, rs)
        elif kind == "tm":
            tm_group(wb, arg, dest, mid, tms)
        elif kind == "conv":
            conv_group(wb, arg, mid, cs)
        elif kind == "dt":
            dt_group(wb, mid)
        finish_load(i + 1)


def phase_b(K, l, last):
    P, I, S, sb = K.P, K.I, K.S, K.sb
    K.off = K.base
    U = sb("U", [128, NT, 2, 512], BF16)
    dB = sb("dB", [128, 256], BF16)
    fin = [sb(f"fin{i}", [128, 4, 512], BF16) for i in range(2)]
    pieces = [(sb(f"Cp{i}", [128, 8, 512], BF16), sb(f"Sp{i}", [128, 8, 512], BF16)) for i in range(3)]
    fgb = [sb(f"fgb{i}", [128, 4, 512], BF16) for i in range(2)]
    uo = [sb(f"uo{i}", [128, 4, 512], BF16) for i in range(2)]
    fuv = S.fuT.rearrange("(g p) t -> p g t", p=128)
    fgv = S.fgT.rearrange("(g p) t -> p g t", p=128)
    ufv = S.ufT.rearrange("(g p) t -> p g t", p=128)
    dma_in(K, dB.h, I.dftB, [dB.b])
    tiles = tok_tiles()
    bi = 0
    for si, (tok0, n, ctx) in enumerate(tiles):
        if ctx and last:
            continue
        f = fin[si % 2]
        dma_in(K, f.h[:, :, 0:n], fuv[:, :, tok0:tok0 + n], [f.b])
        for tt in range(n // 128):
            t = tok0 // 128 + tt
            for b2 in range(2):
                bank = K.ps[(bi % 2) * 2 + b2]
                for gg in range(2):
                    g = b2 * 2 + gg
                    mm(K, bank.h[:, gg * 256:(gg + 1) * 256], [(f.h[:, g, tt * 128:(tt + 1) * 128], dB.h)],
                       reads=[f.b, dB.b], writes=[bank.b])
                bv = bank.h[:, :].rearrange("p (g c m) -> p g c m", g=2, c=2, m=128)
                P.op("scalar", lambda e, bv=bv, t=t, b2=b2: e.copy(
                    out=U.h[:, t, 0, b2 * 256:(b2 + 1) * 256].rearrange("p (g m) -> p g m", g=2), in_=bv[:, :, 0, :]),
                    reads=[bank.b], writes=[U.b])
                P.op("vector", lambda e, bv=bv, t=t, b2=b2: e.tensor_copy(
                    out=U.h[:, t, 1, b2 * 256:(b2 + 1) * 256].rearrange("p (g m) -> p g m", g=2), in_=bv[:, :, 1, :]),
                    reads=[bank.b], writes=[U.b])
            bi += 1
    C4v = I.C4.rearrange("(nt p) k -> p nt k", p=128)
    S4v = I.S4.rearrange("(nt p) k -> p nt k", p=128)
    pi = 0
    for kb in range(SEQ // 512):
        tok0 = NCTX + kb * 512
        fg = fgb[kb % 2]
        o = uo[kb % 2]
        dma_in(K, fg.h, fgv[:, :, tok0:tok0 + 512], [fg.b])
        banks = [K.ps[(kb % 2) * 4 + ch] for ch in range(4)]
        for pc in range(4):
            Cp, Sp = pieces[pi % 3]
            pi += 1
            dma_in(K, Cp.h, C4v[:, pc * 8:(pc + 1) * 8, kb * 512:(kb + 1) * 512], [Cp.b])
            dma_in(K, Sp.h, S4v[:, pc * 8:(pc + 1) * 8, kb * 512:(kb + 1) * 512], [Sp.b])
            for ch in range(4):
                pairs = []
                for nt in range(8):
                    t = 2 + pc * 8 + nt
                    pairs.append((U.h[:, t, 0, ch * 128:(ch + 1) * 128], Cp.h[:, nt, :]))
                    pairs.append((U.h[:, t, 1, ch * 128:(ch + 1) * 128], Sp.h[:, nt, :]))
                mm(K, banks[ch].h[:, :], pairs, reads=[U.b, Cp.b, Sp.b], writes=[banks[ch].b],
                   start=(pc == 0), stop=(pc == 3))
        for ch in range(4):
            P.op("vector", lambda e, o=o, fg=fg, ch=ch, bank=banks[ch]: e.tensor_tensor(
                out=o.h[:, ch, :], in0=bank.h[:, :], in1=fg.h[:, ch, :], op=ALU.mult),
                reads=[banks[ch].b, fg.b], writes=[o.b])
        dma_out(K, ufv[:, :, tok0:tok0 + 512], o.h, reads=[o.b])
    if not last:
        c2 = sb("c2", [128, 2, 2, 256], BF16)
        dma_in(K, c2.h[:, 0, :, :], I.C2.rearrange("(nt p) k -> p nt k", p=128), [c2.b])
        dma_in(K, c2.h[:, 1, :, :], I.S2.rearrange("(nt p) k -> p nt k", p=128), [c2.b])
        fg = fgb[0]
        o = uo[0]
        dma_in(K, fg.h[:, :, 0:NCTX], fgv[:, :, 0:NCTX], [fg.b])
        for ch in range(4):
            bank = K.ps[ch]
            pairs = []
            for nt in range(2):
                pairs.append((U.h[:, nt, 0, ch * 128:(ch + 1) * 128], c2.h[:, 0, nt, :]))
                pairs.append((U.h[:, nt, 1, ch * 128:(ch + 1) * 128], c2.h[:, 1, nt, :]))
            mm(K, bank.h[:, 0:NCTX], pairs, reads=[U.b, c2.b], writes=[bank.b])
            P.op("vector", lambda e, o=o, fg=fg, ch=ch, bank=bank: e.tensor_tensor(
                out=o.h[:, ch, 0:NCTX], in0=bank.h[:, 0:NCTX], in1=fg.h[:, ch, 0:NCTX], op=ALU.mult),
                reads=[bank.b, fg.b], writes=[o.b])
        dma_out(K, ufv[:, :, 0:NCTX], o.h[:, :, 0:NCTX], reads=[o.b])


def phase_c(K, l, last):
    P, I, S, sb = K.P, K.I, K.S, K.sb
    K.off = K.base
    lam_init = 0.8 - 0.6 * math.exp(-0.3 * l)
    kTs = sb("kTs", [128, 4, T], BF16)
    vs = sb("vs", [128, NT, 512], BF16)
    lamt = sb("lamt", [128, 4, 64], F32)
    lam2 = sb("lam2", [128, 2, 64], F32)
    ls = sb("ls", [128, 4], F32)
    wsub = sb("wsub", [128, 1], F32)
    qts = [sb(f"qt{i}", [128, 4, 512], BF16) for i in range(2)]
    ags = [sb(f"agt{i}", [128, 4, 512], BF16) for i in range(2)]
    uos = [sb(f"uao{i}", [128, 4, 512], BF16) for i in range(2)]
    Et = [sb(f"E{i}", [128, 512], BF16) for i in range(6)]
    R0, R1, T0, T1, A, RS, O = [sb(f"ep{i}", [128, 512], F32) for i in range(7)]
    SQ = sb("sq", [128, 512], BF16)
    ZA = [sb(f"za{i}", [128, 512], F32) for i in range(2)]
    OC = [sb(f"oc{i}", [128, 512], F32) for i in range(2)]
    kv = S.kT.rearrange("(h p) t -> p h t", p=128)
    qv = S.qT.rearrange("(h p) t -> p h t", p=128)
    agv = S.agT.rearrange("(h p) t -> p h t", p=128)
    uav = S.uaT.rearrange("(h p) t -> p h t", p=128)
    for h in range(4):
        dma_in(K, kTs.h[:, h, :], kv[:, h, :], [kTs.b])
    vv = S.v_tm.rearrange("(a p) n -> p a n", p=128)
    for a0 in range(0, NT, 8):
        a1 = min(NT, a0 + 8)
        dma_in(K, vs.h[:, a0:a1, :], vv[:, a0:a1, :], [vs.b])
    dma_in(K, lamt.h.rearrange("p a b -> p (a b)"), I.lam[l].partition_broadcast(128), [lamt.b])
    dma_in(K, wsub.h, I.subln_w[l].rearrange("(p o) -> p o", o=1), [wsub.b])
    P.op("vector", lambda e: e.tensor_tensor(out=lam2.h[:, 0, :], in0=lamt.h[:, 0, :], in1=lamt.h[:, 1, :], op=ALU.mult),
         reads=[lamt.b], writes=[lam2.b])
    P.op("vector", lambda e: e.tensor_tensor(out=lam2.h[:, 1, :], in0=lamt.h[:, 2, :], in1=lamt.h[:, 3, :], op=ALU.mult),
         reads=[lamt.b], writes=[lam2.b])
    P.op("vector", lambda e: e.reduce_sum(out=ls.h[:, 0:2], in_=lam2.h, axis=AX.X), reads=[lam2.b], writes=[ls.b])
    P.op("scalar", lambda e: e.activation(out=ls.h[:, 0:2], in_=ls.h[:, 0:2], func=AF.Exp), reads=[ls.b], writes=[ls.b])
    P.op("vector", lambda e: e.tensor_tensor(out=ls.h[:, 2:3], in0=ls.h[:, 1:2], in1=ls.h[:, 0:1], op=ALU.subtract),
         reads=[ls.b], writes=[ls.b])
    P.op("vector", lambda e: e.tensor_scalar_add(out=ls.h[:, 3:4], in0=ls.h[:, 2:3], scalar1=-lam_init),
         reads=[ls.b], writes=[ls.b])
    P.op("scalar", lambda e: e.mul(out=wsub.h, in_=wsub.h, mul=(1.0 - lam_init)), reads=[wsub.b], writes=[wsub.b])
    psO = [K.ps[4], K.ps[5]]
    psZ = [K.ps[6], K.ps[7]]
    ei = 0
    for bi, (tok0, n, ctx) in enumerate(tok_tiles()):
        if ctx and last:
            continue
        ktiles = [0, 1] if ctx else list(range(NT))
        qt, agt, uo = qts[bi % 2], ags[bi % 2], uos[bi % 2]
        dma_in(K, qt.h[:, :, 0:n], qv[:, :, tok0:tok0 + n], [qt.b])
        dma_in(K, agt.h[:, :, 0:n], agv[:, :, tok0:tok0 + n], [agt.b])
        for h in range(4):
            nk = len(ktiles)

            def issue_S(ki, h=h, n=n, qt=qt):
                kt = ktiles[ki]
                for m in range(2):
                    bS = K.ps[(ki % 2) * 2 + m]
                    mm(K, bS.h[:, 0:n], [(kTs.h[m * 64:(m + 1) * 64, h, kt * 128:(kt + 1) * 128],
                                          qt.h[m * 64:(m + 1) * 64, h, 0:n])], reads=[kTs.b, qt.b], writes=[bS.b])
            issue_S(0)
            for ki, kt in enumerate(ktiles):
                if ki + 1 < nk:
                    issue_S(ki + 1)
                b0 = (ki % 2) * 2
                for m in range(2):
                    E = Et[ei % 6]
                    ei += 1
                    Em = E.h[:, 0:n]
                    P.op("scalar", lambda e, Em=Em, b0=b0, m=m, n=n: e.activation(
                        out=Em, in_=K.ps[b0 + m].h[:, 0:n], func=AF.Exp, scale=0.125),
                        reads=[K.ps[b0 + m].b], writes=[E.b])
                    mm(K, psO[m].h[:, 0:n], [(vs.h[:, kt, h * 128:(h + 1) * 128], Em)],
                       reads=[vs.b, E.b], writes=[psO[m].b], start=(ki == 0), stop=(ki == nk - 1))
                    if m == 0:
                        mm(K, psZ[0].h[:, 0:n], [(K.ones_bf.h, Em)], reads=[K.ones_bf.b, E.b], writes=[psZ[0].b],
                           start=(ki == 0), stop=(ki == nk - 1))
                    elif ki == 0:
                        P.op("vector", lambda e, Em=Em, n=n: e.tensor_copy(out=ZA[1].h[:, 0:n], in_=Em),
                             reads=[E.b], writes=[ZA[1].b])
                    else:
                        P.op("vector", lambda e, Em=Em, n=n: e.tensor_tensor(out=ZA[1].h[:, 0:n], in0=ZA[1].h[:, 0:n], in1=Em, op=ALU.add),
                             reads=[E.b, ZA[1].b], writes=[ZA[1].b])
            mm(K, psZ[1].h[:, 0:n], [(K.ones_f.h, ZA[1].h[:, 0:n])], reads=[K.ones_f.b, ZA[1].b], writes=[psZ[1].b])
            P.op("vector", lambda e, n=n: e.tensor_copy(out=OC[0].h[:, 0:n], in_=psO[0].h[:, 0:n]), reads=[psO[0].b], writes=[OC[0].b])
            P.op("vector", lambda e, n=n: e.tensor_copy(out=OC[1].h[:, 0:n], in_=psO[1].h[:, 0:n]), reads=[psO[1].b], writes=[OC[1].b])
            P.op("vector", lambda e, n=n: e.tensor_copy(out=R0.h[:, 0:n], in_=psZ[0].h[:, 0:n]), reads=[psZ[0].b], writes=[R0.b])
            P.op("vector", lambda e, n=n: e.reciprocal(out=R0.h[:, 0:n], in_=R0.h[:, 0:n]), reads=[R0.b], writes=[R0.b])
            P.op("vector", lambda e, n=n: e.reciprocal(out=R1.h[:, 0:n], in_=psZ[1].h[:, 0:n]), reads=[psZ[1].b], writes=[R1.b])
            P.op("gpsimd", lambda e, n=n: e.tensor_tensor(out=T0.h[:, 0:n], in0=OC[0].h[:, 0:n], in1=R0.h[:, 0:n], op=ALU.mult),
                 reads=[OC[0].b, R0.b], writes=[T0.b])
            P.op("vector", lambda e, n=n: e.tensor_tensor(out=T1.h[:, 0:n], in0=OC[1].h[:, 0:n], in1=R1.h[:, 0:n], op=ALU.mult),
                 reads=[OC[1].b, R1.b], writes=[T1.b])
            P.op("vector", lambda e, n=n: e.scalar_tensor_tensor(out=A.h[:, 0:n], in0=T1.h[:, 0:n], scalar=ls.h[:, 3:4],
                                                                in1=T0.h[:, 0:n], op0=ALU.mult, op1=ALU.add),
                 reads=[T0.b, T1.b, ls.b], writes=[A.b])
            P.op("gpsimd", lambda e, n=n: e.tensor_tensor(out=SQ.h[:, 0:n], in0=A.h[:, 0:n], in1=A.h[:, 0:n], op=ALU.mult),
                 reads=[A.b], writes=[SQ.b])
            bq = psZ[1]
            mm(K, bq.h[:, 0:n], [(K.ones_bf.h, SQ.h[:, 0:n])], reads=[K.ones_bf.b, SQ.b], writes=[bq.b])
            P.op("scalar", lambda e, n=n, bq=bq: e.activation(out=RS.h[:, 0:n], in_=bq.h[:, 0:n], func=AF.Ln, scale=1.0 / 128, bias=EPS),
                 reads=[bq.b], writes=[RS.b])
            P.op("scalar", lambda e, n=n: e.activation(out=RS.h[:, 0:n], in_=RS.h[:, 0:n], func=AF.Exp, scale=-0.5),
                 reads=[RS.b], writes=[RS.b])
            P.op("gpsimd", lambda e, n=n: e.tensor_tensor(out=O.h[:, 0:n], in0=A.h[:, 0:n], in1=RS.h[:, 0:n], op=ALU.mult),
                 reads=[A.b, RS.b], writes=[O.b])
            P.op("vector", lambda e, n=n, h=h, uo=uo, agt=agt: e.scalar_tensor_tensor(
                out=uo.h[:, h, 0:n], in0=O.h[:, 0:n], scalar=wsub.h[:, 0:1], in1=agt.h[:, h, 0:n], op0=ALU.mult, op1=ALU.mult),
                reads=[O.b, wsub.b, agt.b], writes=[uo.b])
        dma_out(K, uav[:, :, tok0:tok0 + n], uo.h[:, :, 0:n], reads=[uo.b])


def phase_d(K, l, last):
    P, I, S, sb = K.P, K.I, K.S, K.sb
    K.off = K.base
    par = sb("par", [128, 40], F32)
    snw = sb("snw", [128, 512], F32)
    tri = sb("tri", [128, 128], F32)
    triT = sb("triT", [128, 128], F32)
    mf = sb("mf", [128, 128], F32)
    mb = sb("mb", [128, 128], F32)
    dt, lndt, ad, acs, tot, eacs, wst, biasL = [sb(f"d_{nm}", [128, NT, 16], F32) for nm in
                                                ("dt", "lndt", "ad", "acs", "tot", "eacs", "wst", "biasL")]
    tmpd, dec = dt, tot
    CT = sb("CT", [128, 2, T], BF16)
    bt2 = [sb(f"bt2_{i}", [128, 2, 128], BF16) for i in range(2)]
    XB = sb("XB", [128, NT, 768], BF16)
    SbE = sb("SbE", [128, NT, 512], BF16)
    xin6 = [sb(f"xin6_{i}", [128, 6, 128], BF16) for i in range(2)]
    zts = [sb(f"zt{i}", [128, 512], BF16) for i in range(2)]
    Sst = [sb("Sf", [128, 512], F32), sb("Sb", [128, 512], F32)]
    SfE = [sb(f"SfE{i}", [128, 512], BF16) for i in range(2)]
    xw = [sb(f"xw{i}", [128, 512], BF16) for i in range(2)]
    CBs = sb("CBs", [128, 2, 128], F32)
    Lr = sb("Lr", [128, 8, 128], F32)
    Lm = Lr
    MT = [sb(f"MT{i}", [128, 8, 128], BF16) for i in range(2)]
    Y1, Y2, YZ = [sb(f"Y{i}", [128, 512], F32) for i in range(3)]
    junk = sb("djunk", [128, 512], BF16)
    s2 = sb("ds2", [128, 2], F32)
    usb = sb("usb", [128, 512], BF16)
    usT = [sb(f"usTs{i}", [128, 4, 512], BF16) for i in range(2)]

    dma_in(K, par.h[:, 0:16], I.a_log[l].partition_broadcast(128), [par.b])
    dma_in(K, par.h[:, 16:32], I.dt_bias[l].partition_broadcast(128), [par.b])
    dma_in(K, par.h[:, 32:40], I.d_skip[l].partition_broadcast(128), [par.b])
    dma_in(K, snw.h, I.ssd_norm_w[l].partition_broadcast(128), [snw.b])
    for t_, src in ((tri, I.tri), (triT, I.triT), (mf, I.mask_f), (mb, I.mask_b)):
        dma_in(K, t_.h, src, [t_.b])
    xv = S.xbcT.rearrange("(g p) t -> p g t", p=128)
    dma_in(K, CT.h, xv[:, 6:8, :], [CT.b])

    def bc16(ap):
        return ap.unsqueeze(1).to_broadcast([128, NT, 16])

    def bc64(ap):
        return ap.unsqueeze(2).to_broadcast([128, 8, 64])

    def v3(ap):
        return ap.rearrange("p (h q) -> p h q", h=8)

    P.op("vector", lambda e: e.tensor_tensor(out=dt.h, in0=K.dtraw.h, in1=bc16(par.h[:, 16:32]), op=ALU.add),
         reads=[K.dtraw.b, par.b], writes=[dt.b])
    P.op("scalar", lambda e: e.activation(out=dt.h, in_=dt.h, func=AF.Exp), reads=[dt.b], writes=[dt.b])
    P.op("scalar", lambda e: e.activation(out=dt.h, in_=dt.h, func=AF.Ln, bias=1.0), reads=[dt.b], writes=[dt.b])
    P.op("scalar", lambda e: e.activation(out=lndt.h, in_=dt.h, func=AF.Ln), reads=[dt.b], writes=[lndt.b])
    P.op("scalar", lambda e: e.activation(out=par.h[:, 0:16], in_=par.h[:, 0:16], func=AF.Exp), reads=[par.b], writes=[par.b])
    P.op("vector", lambda e: e.scalar_tensor_tensor(out=ad.h, in0=dt.h, scalar=-1.0, in1=bc16(par.h[:, 0:16]),
                                                    op0=ALU.mult, op1=ALU.mult), reads=[dt.b, par.b], writes=[ad.b])
    for c in range(NT):
        bank = K.ps[c // 16]
        r0 = (c % 16) * 32
        mm(K, bank.h[:, r0:r0 + 8], [(tri.h, ad.h[:, c, 0:8])], reads=[tri.b, ad.b], writes=[bank.b])
        mm(K, bank.h[:, r0 + 8:r0 + 16], [(triT.h, ad.h[:, c, 8:16])], reads=[triT.b, ad.b], writes=[bank.b])
        mm(K, bank.h[:, r0 + 16:r0 + 32], [(K.ones_f.h, ad.h[:, c, :])], reads=[K.ones_f.b, ad.b], writes=[bank.b])
    for bi_, (c0, ncz) in enumerate(((0, 16), (16, 16), (32, 2))):
        bank = K.ps[bi_]
        bv = bank.h[:, 0:ncz * 32].rearrange("p (a b) -> p a b", a=ncz, b=32)
        P.op("vector", lambda e, bv=bv, c0=c0, ncz=ncz: e.tensor_copy(out=acs.h[:, c0:c0 + ncz, :], in_=bv[:, :, 0:16]),
             reads=[bank.b], writes=[acs.b])
        P.op("vector", lambda e, bv=bv, c0=c0, ncz=ncz: e.tensor_copy(out=tot.h[:, c0:c0 + ncz, :], in_=bv[:, :, 16:32]),
             reads=[bank.b], writes=[tot.b])
    P.op("scalar", lambda e: e.activation(out=eacs.h, in_=acs.h, func=AF.Exp), reads=[acs.b], writes=[eacs.b])
    P.op("vector", lambda e: e.tensor_tensor(out=biasL.h, in0=lndt.h, in1=acs.h, op=ALU.subtract),
         reads=[lndt.b, acs.b], writes=[biasL.b])
    P.op("vector", lambda e: e.tensor_tensor(out=tmpd.h, in0=tot.h, in1=biasL.h, op=ALU.add),
         reads=[tot.b, biasL.b], writes=[tmpd.b])
    P.op("scalar", lambda e: e.activation(out=wst.h, in_=tmpd.h, func=AF.Exp), reads=[tmpd.b], writes=[wst.b])
    P.op("scalar", lambda e: e.activation(out=dec.h, in_=tot.h, func=AF.Exp), reads=[tot.b], writes=[dec.b])
    for c in range(NT):
        xi = xin6[c % 2]
        bank = K.ps[3 + c % 2]
        pb = K.psb[3 + c % 2]
        dma_in(K, xi.h, xv[:, 0:6, c * 128:(c + 1) * 128], [xi.b])

        def tr(e, xi=xi, pb=pb):
            ins = None
            for j in range(6):
                ins = e.transpose(out=pb[:, j * 128:(j + 1) * 128], in_=xi.h[:, j, :], identity=K.ident.h)
            return ins
        P.op("tensor", tr, reads=[xi.b, K.ident.b], writes=[bank.b])
        P.op("scalar", lambda e, pb=pb, c=c: e.copy(out=XB.h[:, c, :], in_=pb[:, 0:768]), reads=[bank.b], writes=[XB.b])

    psSt = K.ps[3]

    def state_update(c, d):
        Sd = Sst[d]
        w = xw[d]
        P.op("vector", lambda e: e.tensor_tensor(out=v3(w.h), in0=v3(XB.h[:, c, 0:512]),
                                                 in1=bc64(wst.h[:, c, d * 8:(d + 1) * 8]), op=ALU.mult),
             reads=[XB.b, wst.b], writes=[w.b])
        for g in range(2):
            mm(K, psSt.h[:, g * 256:(g + 1) * 256], [(XB.h[:, c, 512 + g * 128:512 + (g + 1) * 128], w.h[:, g * 256:(g + 1) * 256])],
               reads=[XB.b, w.b], writes=[psSt.b])
        P.op("vector", lambda e: e.tensor_tensor(out=v3(Sd.h), in0=v3(Sd.h), in1=bc64(dec.h[:, c, d * 8:(d + 1) * 8]), op=ALU.mult),
             reads=[Sd.b, dec.b], writes=[Sd.b])
        P.op("vector", lambda e: e.tensor_tensor(out=Sd.h, in0=Sd.h, in1=psSt.h[:, :], op=ALU.add),
             reads=[Sd.b, psSt.b], writes=[Sd.b])

    P.op("gpsimd", lambda e: e.memset(Sst[0].h, 0.0), writes=[Sst[0].b])
    P.op("gpsimd", lambda e: e.memset(Sst[1].h, 0.0), writes=[Sst[1].b])
    for c in [1, 0] + list(range(NT - 1, 1, -1)):
        P.op("gpsimd", lambda e, c=c: e.tensor_copy(out=SbE.h[:, c, :], in_=Sst[1].h), reads=[Sst[1].b], writes=[SbE.b])
        state_update(c, 1)
    zv = S.z_tm
    usv = S.usT.rearrange("(g p) t -> p g t", p=128)
    psY, psCB, psT = K.ps[7], K.ps[4], K.ps[2]
    psR = [K.ps[5], K.ps[6]]
    psYo = [K.ps[0], K.ps[1]]
    ui = 0
    for c in range(NT):
        ctx = c < 2
        sfe = SfE[c % 2]
        P.op("gpsimd", lambda e, sfe=sfe: e.tensor_copy(out=sfe.h, in_=Sst[0].h), reads=[Sst[0].b], writes=[sfe.b])
        if not (ctx and last):
            zt = zts[c % 2]
            dma_in(K, zt.h, zv[c * 128:(c + 1) * 128, :], [zt.b])
            bt = bt2[c % 2]
            dma_in(K, bt.h, xv[:, 4:6, c * 128:(c + 1) * 128], [bt.b])
            for g in range(2):
                mm(K, psCB.h[:, g * 128:(g + 1) * 128], [(bt.h[:, g, :], CT.h[:, g, c * 128:(c + 1) * 128])],
                   reads=[bt.b, CT.b], writes=[psCB.b])
            P.op("scalar", lambda e: e.copy(out=CBs.h.rearrange("p a b -> p (a b)"), in_=psCB.h[:, 0:256]),
                 reads=[psCB.b], writes=[CBs.b])
            for d in range(2):
                trd = tri if d == 0 else triT
                msk = mf if d == 0 else mb
                mt = MT[d]
                for h in range(8):
                    bk = psR[h // 4]
                    mm(K, bk.h[:, (h % 4) * 128:(h % 4 + 1) * 128],
                       [(ad.h[:, c, d * 8 + h:d * 8 + h + 1].to_broadcast([128, 128]), trd.h)],
                       reads=[ad.b, trd.b], writes=[bk.b])
                for h in range(8):
                    bk = psR[h // 4]
                    P.op("scalar", lambda e, bk=bk, h=h, c=c, d=d: e.activation(
                        out=Lr.h[:, h, :], in_=bk.h[:, (h % 4) * 128:(h % 4 + 1) * 128], func=AF.Exp,
                        bias=biasL.h[:, c, d * 8 + h:d * 8 + h + 1]), reads=[bk.b, biasL.b], writes=[Lr.b])
                P.op("vector", lambda e, msk=msk: e.scalar_tensor_tensor(
                    out=Lm.h, in0=Lr.h, scalar=1e30, in1=msk.h.unsqueeze(1).to_broadcast([128, 8, 128]),
                    op0=ALU.min, op1=ALU.mult), reads=[Lr.b, msk.b], writes=[Lm.b])
                for g in range(2):
                    P.op("vector", lambda e, g=g, mt=mt: e.tensor_tensor(
                        out=mt.h[:, g * 4:(g + 1) * 4, :], in0=Lm.h[:, g * 4:(g + 1) * 4, :],
                        in1=CBs.h[:, g, :].unsqueeze(1).to_broadcast([128, 4, 128]), op=ALU.mult),
                        reads=[Lm.b, CBs.b], writes=[mt.b])
                se = sfe.h if d == 0 else SbE.h[:, c, :]
                seb = sfe.b if d == 0 else SbE.b
                for g in range(2):
                    mm(K, psYo[d].h[:, g * 256:(g + 1) * 256], [(CT.h[:, g, c * 128:(c + 1) * 128], se[:, g * 256:(g + 1) * 256])],
                       reads=[CT.b, seb], writes=[psYo[d].b])
            for h in range(8):
                mm(K, psY.h[:, h * 64:(h + 1) * 64], [(MT[0].h[:, h, :], XB.h[:, c, h * 64:(h + 1) * 64]),
                                                      (MT[1].h[:, h, :], XB.h[:, c, h * 64:(h + 1) * 64])],
                   reads=[MT[0].b, MT[1].b, XB.b], writes=[psY.b])
            P.op("vector", lambda e, c=c: e.tensor_tensor(out=v3(Y1.h), in0=v3(psYo[0].h[:, :]), in1=bc64(eacs.h[:, c, 0:8]), op=ALU.mult),
                 reads=[psYo[0].b, eacs.b], writes=[Y1.b])
            P.op("vector", lambda e, c=c: e.tensor_tensor(out=v3(Y2.h), in0=v3(psYo[1].h[:, :]), in1=bc64(eacs.h[:, c, 8:16]), op=ALU.mult),
                 reads=[psYo[1].b, eacs.b], writes=[Y2.b])
            P.op("gpsimd", lambda e: e.tensor_tensor(out=Y1.h, in0=Y1.h, in1=Y2.h, op=ALU.add), reads=[Y1.b, Y2.b], writes=[Y1.b])
            P.op("gpsimd", lambda e, c=c: e.tensor_tensor(out=v3(Y2.h), in0=v3(XB.h[:, c, 0:512]), in1=bc64(par.h[:, 32:40]), op=ALU.mult),
                 reads=[XB.b, par.b, Y1.b], writes=[Y2.b])
            P.op("gpsimd", lambda e: e.tensor_tensor(out=Y1.h, in0=Y1.h, in1=Y2.h, op=ALU.add), reads=[Y1.b, Y2.b], writes=[Y1.b])
            P.op("vector", lambda e: e.tensor_tensor(out=Y1.h, in0=Y1.h, in1=psY.h[:, :], op=ALU.add), reads=[Y1.b, psY.b], writes=[Y1.b])
            P.op("vector", lambda e, zt=zt: e.tensor_tensor(out=YZ.h, in0=Y1.h, in1=zt.h, op=ALU.mult), reads=[Y1.b, zt.b], writes=[YZ.b])
            P.op("scalar", lambda e: e.activation(out=junk.h, in_=YZ.h, func=AF.Square, accum_out=s2.h[:, 0:1]),
                 reads=[YZ.b], writes=[junk.b, s2.b])
            P.op("scalar", lambda e: e.activation(out=s2.h[:, 1:2], in_=s2.h[:, 0:1], func=AF.Ln, scale=1.0 / 512, bias=EPS),
                 reads=[s2.b], writes=[s2.b])
            P.op("scalar", lambda e: e.activation(out=s2.h[:, 1:2], in_=s2.h[:, 1:2], func=AF.Exp, scale=-0.5),
                 reads=[s2.b], writes=[s2.b])
            P.op("vector", lambda e: e.scalar_tensor_tensor(out=usb.h, in0=YZ.h, scalar=s2.h[:, 1:2], in1=snw.h,
                                                            op0=ALU.mult, op1=ALU.mult), reads=[YZ.b, s2.b, snw.b], writes=[usb.b])
            pbT = K.psb[2]

            def tr2(e, pbT=pbT):
                ins = None
                for j in range(4):
                    ins = e.transpose(out=pbT[:, j * 128:(j + 1) * 128], in_=usb.h[:, j * 128:(j + 1) * 128], identity=K.ident.h)
                return ins
            P.op("tensor", tr2, reads=[usb.b, K.ident.b], writes=[psT.b])
            if ctx:
                grp0, slot, glen = 0, c, 2
            else:
                grp0 = 2 + ((c - 2) // 4) * 4
                slot, glen = c - grp0, 4
            ut = usT[(0 if ctx else 1 + (c - 2) // 4) % 2]
            P.op("scalar", lambda e, ut=ut, slot=slot, pbT=pbT: e.copy(
                out=ut.h[:, :, slot * 128:(slot + 1) * 128], in_=pbT[:, 0:512].rearrange("p (g m) -> p g m", g=4)),
                reads=[psT.b], writes=[ut.b])
            if slot == glen - 1:
                dma_out(K, usv[:, :, grp0 * 128:(grp0 + glen) * 128], ut.h[:, :, 0:glen * 128], reads=[ut.b])
        state_update(c, 0)


def phase_e(K, l, last):
    P, I, S, sb = K.P, K.I, K.S, K.sb
    K.off = K.base
    wbr = [sb(f"wbr{i}", [128, 4, D], BF16) for i in range(3)]
    wo = sb("wo", [128, 8, D], BF16)
    wstage = [sb(f"wstage{i}", [128, 4, D], F32) for i in range(1)]
    uts = [sb(f"ut{i}", [128, 3, 4, 512], BF16) for i in range(2)]
    gts = [sb(f"gt{i}", [128, 3, 4, 512], BF16) for i in range(3)]
    yTs = [sb(f"yT{i}", [128, 8, 512], BF16) for i in range(2)]
    M2 = [[sb(f"M{j}_{i}", [128, 512], F32) for i in range(3)] for j in range(2)]
    xts = [sb(f"ext{i}", [128, D], F32) for i in range(2)]
    xns = [sb(f"exn{i}", [128, D], F32) for i in range(2)]
    tmp = [sb(f"etmp{i}", [128, 512], F32) for i in range(2)]
    srcs = [(I.w_of[l], wbr[0].h), (I.w_oa[l], wbr[1].h), (I.w_os[l], wbr[2].h),
            (I.w_out[l, 0:512, :], wo.h[:, 0:4, :]), (I.w_out[l, 512:1024, :], wo.h[:, 4:8, :])]
    wbufs = [wbr[0].b, wbr[1].b, wbr[2].b, wo.b, wo.b]
    for i, (src, dst) in enumerate(srcs):
        ws = wstage[0]
        dma_in(K, ws.h, src.rearrange("(kc p) n -> p kc n", p=128), [ws.b])
        P.op("vector", lambda e, ws=ws, dst=dst: e.tensor_copy(out=dst[:, 0:2, :], in_=ws.h[:, 0:2, :]), reads=[ws.b], writes=[wbufs[i]])
        P.op("gpsimd", lambda e, ws=ws, dst=dst: e.tensor_copy(out=dst[:, 2:4, :], in_=ws.h[:, 2:4, :]), reads=[ws.b], writes=[wbufs[i]])
    if last:
        nfB = sb("nfB", [128, D], F32)
        s2 = sb("es2", [128, 2], F32)
        junk = sb("ejunk", [128, D], BF16)
        dma_in(K, nfB.h, I.norm_f.partition_broadcast(128), [nfB.b])
    uviews = [t_.rearrange("(g p) t -> p g t", p=128) for t_ in (S.ufT, S.uaT, S.usT)]
    gv = S.gT.rearrange("(c p) t -> p c t", p=128)
    xsrc = I.xin if l == 0 else S.xs
    state = {"xi": 0, "gi": 0}
    stiles = [tl for tl in tok_tiles() if not (tl[2] and last)]

    def merge(si):
        tok0, n, ctx = stiles[si]
        ut = uts[si % 2]
        yT = yTs[si % 2]
        for br in range(3):
            dma_in(K, ut.h[:, br, :, 0:n], uviews[br][:, :, tok0:tok0 + n], [ut.b])
        for oc in range(8):
            if oc % 4 == 0:
                gth = gts[state["gi"] % 3]
                state["gi"] += 1
                for br in range(3):
                    dma_in(K, gth.h[:, br, :, 0:n], gv[:, br * 8 + oc:br * 8 + oc + 4, tok0:tok0 + n], [gth.b])
            M = M2[oc % 2]
            banks = [K.ps[br + 3 * (oc % 2)] for br in range(3)]
            for br in range(3):
                mm(K, banks[br].h[:, 0:n], [(wbr[br].h[:, kc, oc * 128:(oc + 1) * 128], ut.h[:, br, kc, 0:n]) for kc in range(4)],
                   reads=[wbr[br].b, ut.b], writes=[banks[br].b])
            for br in range(3):
                P.op("vector", lambda e, br=br, oc=oc, n=n, bank=banks[br], M=M, gth=gth: e.tensor_tensor(
                    out=M[br].h[:, 0:n], in0=bank.h[:, 0:n], in1=gth.h[:, br, oc % 4, 0:n], op=ALU.mult),
                    reads=[banks[br].b, gth.b], writes=[M[br].b])
            P.op("gpsimd", lambda e, n=n, M=M: e.tensor_tensor(out=M[0].h[:, 0:n], in0=M[0].h[:, 0:n], in1=M[1].h[:, 0:n], op=ALU.add),
                 reads=[M[0].b, M[1].b], writes=[M[0].b])
            P.op("gpsimd", lambda e, n=n, oc=oc, yT=yT, M=M: e.tensor_tensor(out=yT.h[:, oc, 0:n], in0=M[0].h[:, 0:n], in1=M[2].h[:, 0:n], op=ALU.add),
                 reads=[M[0].b, M[2].b], writes=[yT.b])

    def outproj(si):
        tok0, n, ctx = stiles[si]
        yT = yTs[si % 2]
        gA = K.mod["g_c" if ctx else "g_l"]
        for tt in range(n // 128):
            r0 = tok0 + tt * 128
            xt, xn = xts[state["xi"] % 2], xns[state["xi"] % 2]
            state["xi"] += 1
            dma_in(K, xt.h, xsrc[r0:r0 + 128, :], [xt.b])
            for half in range(2):
                bank = K.ps[6 + half]
                tp = tmp[half]
                hs = slice(half * 512, (half + 1) * 512)
                mm(K, bank.h[:, :], [(yT.h[:, kc, tt * 128:(tt + 1) * 128], wo.h[:, kc, hs]) for kc in range(8)],
                   reads=[yT.b, wo.b], writes=[bank.b])
                P.op("vector", lambda e, bank=bank, tp=tp, hs=hs, gA=gA: e.tensor_tensor(out=tp.h, in0=bank.h[:, :], in1=gA.h[:, hs], op=ALU.mult),
                     reads=[bank.b, gA.b], writes=[tp.b])
                P.op("vector", lambda e, tp=tp, xt=xt, xn=xn, hs=hs: e.tensor_tensor(out=xn.h[:, hs], in0=tp.h, in1=xt.h[:, hs], op=ALU.add),
                     reads=[tp.b, xt.b], writes=[xn.b])
            if not last:
                dma_out(K, S.xs[r0:r0 + 128, :], xn.h, reads=[xn.b])
            else:
                P.op("scalar", lambda e, xn=xn: e.activation(out=junk.h, in_=xn.h, func=AF.Square, accum_out=s2.h[:, 0:1]),
                     reads=[xn.b], writes=[junk.b, s2.b])
                P.op("scalar", lambda e: e.activation(out=s2.h[:, 1:2], in_=s2.h[:, 0:1], func=AF.Ln, scale=1.0 / D, bias=EPS),
                     reads=[s2.b], writes=[s2.b])
                P.op("scalar", lambda e: e.activation(out=s2.h[:, 1:2], in_=s2.h[:, 1:2], func=AF.Exp, scale=-0.5),
                     reads=[s2.b], writes=[s2.b])
                P.op("vector", lambda e, xn=xn, xt=xt: e.scalar_tensor_tensor(out=xt.h, in0=xn.h, scalar=s2.h[:, 1:2], in1=nfB.h,
                                                                               op0=ALU.mult, op1=ALU.mult),
                     reads=[xn.b, s2.b, nfB.b], writes=[xt.b])
                dma_out(K, K.out[r0 - NCTX:r0 - NCTX + 128, :], xt.h, reads=[xt.b])

    merge(0)
    for si in range(len(stiles)):
        if si + 1 < len(stiles):
            merge(si + 1)
        outproj(si)


_CONST = {}


def _constants():
    if _CONST:
        return _CONST
    bf = ml_dtypes.bfloat16
    c = _CONST
    c["ident"] = np.eye(128, dtype=np.float32).astype(bf)
    rows = np.repeat(np.arange(SEQ // 64, dtype=np.float32), 64)
    cols = np.tile(np.arange(64, dtype=np.float32), SEQ // 64)
    freqs = (np.float32(10000.0) ** (-np.arange(0, 32, 2, dtype=np.float32) / np.float32(32))).astype(np.float32)
    ang_r = rows[:, None] * freqs
    ang_c = cols[:, None] * freqs
    ang = np.concatenate([ang_r, ang_r, ang_c, ang_c], axis=-1).astype(np.float32)
    cosT = np.cos(ang).astype(np.float32).T
    sinT = np.sin(ang).astype(np.float32).T
    c["cosT"] = np.ascontiguousarray(np.concatenate([cosT, cosT], axis=0))
    c["sinT"] = np.ascontiguousarray(np.concatenate([sinT, sinT], axis=0))
    j = np.arange(128, dtype=np.float64)
    angB = 2 * np.pi * np.outer(j, j) / 128.0
    c["dftB"] = (np.concatenate([np.cos(angB), -np.sin(angB)], axis=1) / np.sqrt(128.0)).astype(np.float32).astype(bf)
    for key, n in (("4", SEQ), ("2", NCTX)):
        idx = np.arange(n, dtype=np.int64)
        ph = (np.outer(idx, idx) % n).astype(np.float64) * (2 * np.pi / n)
        sc = 1.0 / np.sqrt(float(n))
        c["C" + key] = (np.cos(ph) * sc).astype(np.float32).astype(bf)
        c["S" + key] = (np.sin(ph) * sc).astype(np.float32).astype(bf)
    s_ = np.arange(128)[:, None]
    l_ = np.arange(128)[None, :]
    c["tri"] = (s_ <= l_).astype(np.float32)
    c["triT"] = (s_ >= l_).astype(np.float32)
    c["mask_f"] = (l_ >= s_).astype(np.float32)
    c["mask_b"] = (l_ <= s_).astype(np.float32)
    return c


def make_in_maps(inputs):
    f = lambda a: np.ascontiguousarray(np.asarray(a, dtype=np.float32))
    x, c, ctx = f(inputs["x"]), f(inputs["c"]), f(inputs["ctx"])
    shared = {
        "cctx_t": np.ascontiguousarray(f(inputs["c_ctx"]).reshape(8, 128).T),
        "w_mod": f(inputs["w_mod"]), "b_mod": f(inputs["b_mod"]), "norm_w": f(inputs["norm_w"]),
        "w_in": f(inputs["w_in"]),
        "convw_t": np.ascontiguousarray(f(inputs["conv_w"]).reshape(DEPTH, 5, 8, 128).transpose(0, 3, 2, 1)),
        "convb_t": np.ascontiguousarray(f(inputs["conv_b"]).reshape(DEPTH, 8, 128).transpose(0, 2, 1)),
        "a_log": f(inputs["a_log"]).reshape(DEPTH, 16), "dt_bias": f(inputs["dt_bias"]).reshape(DEPTH, 16),
        "d_skip": f(inputs["d_skip"]), "ssd_norm_w": f(inputs["ssd_norm_w"]),
        "lam": f(inputs["lam"]).reshape(DEPTH, 256), "subln_w": f(inputs["subln_w"]),
        "w_of": f(inputs["w_of"]), "w_oa": f(inputs["w_oa"]), "w_os": f(inputs["w_os"]), "w_out": f(inputs["w_out"]),
        "norm_f": f(inputs["norm_f"]),
    }
    shared.update(_constants())
    maps = []
    for b in range(x.shape[0]):
        m = dict(shared)
        m["xin"] = np.ascontiguousarray(np.concatenate([ctx[b], x[b]], axis=0))
        m["c_t"] = np.ascontiguousarray(c[b].reshape(8, 128).T)
        maps.append(m)
    return maps


_NC_CACHE = {}


def kernel(**inputs):
    if "nc" not in _NC_CACHE:
        _NC_CACHE["nc"] = build_program()
    nc = _NC_CACHE["nc"]
    maps = make_in_maps(inputs)
    res = run_bass_kernel_spmd(nc, maps, core_ids=list(range(len(maps))))
    return np.stack([np.asarray(r["out"], dtype=np.float32) for r in res.results], axis=0)
```

```python
import math
from contextlib import ExitStack

import numpy as np
import ml_dtypes

import concourse.bass as bass
import concourse.mybir as mybir
from concourse.bass_utils import run_bass_kernel_spmd

F32 = mybir.dt.float32
BF16 = mybir.dt.bfloat16
AF = mybir.ActivationFunctionType
ALU = mybir.AluOpType
AX = mybir.AxisListType

D = 1024
SEQ = 4096
NCTX = 256
T = SEQ + NCTX
NT = T // 128
DEPTH = 4
EPS = 1e-6
IN_W = 7696
C_FU, C_FG, C_Q, C_K, C_V, C_AG, C_Z, C_XBC, C_DT, C_GT = 0, 512, 1024, 1536, 2048, 2560, 3072, 3584, 4608, 4624

COMPUTE = ("tensor", "vector", "scalar", "gpsimd")
STREAMS = ("tensor", "vector", "scalar", "gpsimd", "sync")
NS = 16
EPOCH = 20000


class Buf:
    __slots__ = ("name", "w", "r")

    def __init__(self, name=""):
        self.name = name
        self.w = None
        self.r = []


class Op:
    __slots__ = ("stream", "fn", "is_dma", "signal", "waits", "n", "slot", "seq", "clock", "semi", "semv")


class Prog:
    def __init__(self):
        self.streams = {s: [] for s in STREAMS}
        self.known = {s: {c: 0 for c in COMPUTE} for s in STREAMS}
        self.known_dma = {s: {} for s in STREAMS}
        self.nseq = {c: 0 for c in COMPUTE}
        self.dma_ops = {s: [] for s in STREAMS}
        self.last = {c: None for c in COMPUTE}
        self.pending = {s: [] for s in STREAMS}

    def op(self, stream, fn, reads=(), writes=(), dma=False):
        o = Op()
        o.stream = stream
        o.fn = fn
        o.is_dma = dma
        o.signal = False
        o.waits = []
        deps = []
        for b in reads:
            if b.w is not None:
                deps.append(b.w)
        for b in writes:
            if b.w is not None:
                deps.append(b.w)
            deps.extend(b.r)
        if self.pending[stream]:
            deps.extend(self.pending[stream])
            self.pending[stream] = []
        if dma:
            n = len(self.dma_ops[stream])
            o.n = n
            o.slot = n % NS
            if n >= NS:
                deps.append(self.dma_ops[stream][n - NS])
            self.dma_ops[stream].append(o)
        else:
            self.nseq[stream] += 1
            o.seq = self.nseq[stream]
            self.last[stream] = o
        kn = self.known[stream]
        kd = self.known_dma[stream]
        for d in deps:
            if d.is_dma:
                key = (d.stream, d.slot)
                if kd.get(key, -1) < d.n:
                    kd[key] = d.n
                    o.waits.append(d)
                    for c, v in d.clock.items():
                        if kn[c] < v:
                            kn[c] = v
            else:
                if d.stream == "tensor" and stream == "tensor" and not dma:
                    continue
                if kn[d.stream] < d.seq:
                    o.waits.append(d)
                    d.signal = True
                    kn[d.stream] = d.seq
                    for c, v in d.clock.items():
                        if kn[c] < v:
                            kn[c] = v
        o.clock = dict(kn)
        for b in reads:
            b.r.append(o)
        for b in writes:
            b.w = o
            b.r = []
        self.streams[stream].append(o)
        return o

    def barrier(self):
        ops = []
        for c in COMPUTE:
            if self.last[c] is not None:
                ops.append(self.last[c])
        for s in STREAMS:
            ops.extend(self.dma_ops[s][-NS:])
        for s in STREAMS:
            self.pending[s] = list(ops)

    def emit(self, nc):
        self.barrier()
        nsem = {}
        for c in COMPUTE:
            cnt = 0
            for o in self.streams[c]:
                if o.is_dma:
                    continue
                if o.signal:
                    o.semi = cnt // EPOCH
                    o.semv = cnt % EPOCH + 1
                    cnt += 1
            nsem[c] = max(1, (cnt + EPOCH - 1) // EPOCH)
        with ExitStack() as es:
            csem = {c: [es.enter_context(nc.semaphore(f"c_{c}_{i}")) for i in range(nsem[c])] for c in COMPUTE}
            dsem = {s: [es.enter_context(nc.semaphore(f"d_{s}_{i}")) for i in range(NS)]
                    for s in STREAMS if self.dma_ops[s]}

            def wait(e, d):
                if d.is_dma:
                    e.wait_ge(dsem[d.stream][d.slot], 16 * (d.n // NS + 1))
                else:
                    e.wait_ge(csem[d.stream][d.semi], d.semv)

            def body_for(stream):
                def body(e):
                    kn = self.known[stream]
                    kd = self.known_dma[stream]
                    for o in self.streams[stream]:
                        for d in o.waits:
                            wait(e, d)
                        ins = o.fn(e)
                        if o.is_dma:
                            ins.then_inc(dsem[stream][o.slot], 16)
                        elif o.signal:
                            ins.then_inc(csem[stream][o.semi], 1)
                    for d in self.pending[stream]:
                        if d.is_dma:
                            if kd.get((d.stream, d.slot), -1) < d.n:
                                wait(e, d)
                        elif d.signal and not (d.stream == stream):
                            if kn[d.stream] < d.seq:
                                wait(e, d)
                return body

            for c in COMPUTE:
                pass
            with nc.Block() as block:
                block.sync(body_for("sync"))
                block.tensor(body_for("tensor"))
                block.vector(body_for("vector"))
                block.scalar(body_for("scalar"))
                block.gpsimd(body_for("gpsimd"))


class Tl:
    __slots__ = ("h", "b")

    def __init__(self, h, name=""):
        self.h = h
        self.b = Buf(name)


class Ctx:
    pass


def build_program(n_layers=DEPTH, stop_after=None, dbg=()):
    nc = bass.Bass("TRN2", target_bir_lowering=False)
    P = Prog()
    K = Ctx()
    K.nc, K.P = nc, P
    K.dbgset = set(dbg)

    def din(name, shape, dt=F32):
        return nc.dram_tensor(name, list(shape), dt, kind="ExternalInput").ap()

    def dscr(name, shape, dt=BF16):
        kind = "ExternalOutput" if name in dbg else "Internal"
        return nc.dram_tensor(name, list(shape), dt, kind=kind).ap()

    I = Ctx()
    I.xin = din("xin", [T, D])
    I.c_t = din("c_t", [128, 8])
    I.cctx_t = din("cctx_t", [128, 8])
    I.w_mod = din("w_mod", [DEPTH, D, 3 * D])
    I.b_mod = din("b_mod", [DEPTH, 3 * D])
    I.norm_w = din("norm_w", [DEPTH, D])
    I.w_in = din("w_in", [DEPTH, D, IN_W])
    I.convw_t = din("convw_t", [DEPTH, 128, 8, 5])
    I.convb_t = din("convb_t", [DEPTH, 128, 8])
    I.a_log = din("a_log", [DEPTH, 16])
    I.dt_bias = din("dt_bias", [DEPTH, 16])
    I.d_skip = din("d_skip", [DEPTH, 8])
    I.ssd_norm_w = din("ssd_norm_w", [DEPTH, 512])
    I.lam = din("lam", [DEPTH, 256])
    I.subln_w = din("subln_w", [DEPTH, 128])
    I.w_of = din("w_of", [DEPTH, 512, D])
    I.w_oa = din("w_oa", [DEPTH, 512, D])
    I.w_os = din("w_os", [DEPTH, 512, D])
    I.w_out = din("w_out", [DEPTH, D, D])
    I.norm_f = din("norm_f", [D])
    I.ident = din("ident", [128, 128], BF16)
    I.cosT = din("cosT", [128, SEQ])
    I.sinT = din("sinT", [128, SEQ])
    I.dftB = din("dftB", [128, 256], BF16)
    I.C4 = din("C4", [SEQ, SEQ], BF16)
    I.S4 = din("S4", [SEQ, SEQ], BF16)
    I.C2 = din("C2", [NCTX, NCTX], BF16)
    I.S2 = din("S2", [NCTX, NCTX], BF16)
    I.tri = din("tri", [128, 128])
    I.triT = din("triT", [128, 128])
    I.mask_f = din("mask_f", [128, 128])
    I.mask_b = din("mask_b", [128, 128])
    K.I = I
    out = nc.dram_tensor("out", [SEQ, D], F32, kind="ExternalOutput").ap()
    K.out = out

    S = Ctx()
    S.xs = dscr("xs", [T, D], F32)
    S.fuT = dscr("fuT", [512, T])
    S.fgT = dscr("fgT", [512, T])
    S.qT = dscr("qT", [512, T])
    S.kT = dscr("kT", [512, T])
    S.v_tm = dscr("v_tm", [T, 512])
    S.agT = dscr("agT", [512, T])
    S.z_tm = dscr("z_tm", [T, 512])
    S.xbcT = dscr("xbcT", [1024, T])
    S.gT = dscr("gT", [3072, T])
    S.ufT = dscr("ufT", [512, T])
    S.uaT = dscr("uaT", [512, T])
    S.usT = dscr("usT", [512, T])
    S.dbg = dscr("dbg", [128, 8192], F32)
    K.S = S

    with ExitStack() as top:
        ARENA = 52800
        arena = top.enter_context(nc.sbuf_tensor("arena", [128, ARENA], F32))
        K.base = 0
        K.off = 0

        def sb(name, shape, dt=F32):
            n = 1
            for v in shape[1:]:
                n *= v
            words = n if dt == F32 else (n + 1) // 2
            words = (words + 7) // 8 * 8
            assert K.off + words <= ARENA, (name, K.off, words)
            ap = arena[:, K.off:K.off + words]
            K.off += words
            if dt != F32:
                ap = ap.bitcast(dt)
            ap = ap[:, 0:n]
            if len(shape) == 3:
                ap = ap.rearrange("p (a b) -> p a b", a=shape[1], b=shape[2])
            elif len(shape) == 4:
                ap = ap.rearrange("p (a b c) -> p a b c", a=shape[1], b=shape[2], c=shape[3])
            return Tl(ap, name)
        K.sb = sb
        K.psall = top.enter_context(nc.psum_tensor("psall", [128, 4096], F32))
        K.ps = [Tl(K.psall[:, i * 512:(i + 1) * 512], f"ps{i}") for i in range(8)]
        K.psb = [K.psall[:, i * 512:(i + 1) * 512].bitcast(BF16) for i in range(8)]
        K.ident = sb("ident", [128, 128], BF16)
        K.ones_bf = sb("ones_bf", [128, 128], BF16)
        K.ones_f = sb("ones_f", [128, 128], F32)
        K.dtraw = sb("dtraw", [128, NT, 16], F32)
        K.mod = {k: sb("mod_" + k, [128, D], F32) for k in ("sc_l", "sh_l", "g_l", "sc_c", "sh_c", "g_c")}
        K.base = K.off
        P.op("sync", lambda e: e.dma_start(out=K.ident.h, in_=I.ident), writes=[K.ident.b], dma=True)
        P.op("vector", lambda e: e.memset(K.ones_bf.h, 1.0), writes=[K.ones_bf.b])
        P.op("vector", lambda e: e.memset(K.ones_f.h, 1.0), writes=[K.ones_f.b])
        phases = [phase_mod, phase_a1, phase_a2, phase_b, phase_c, phase_d, phase_e]
        done = False
        for l in range(n_layers):
            for ph in phases:
                ph(K, l, l == DEPTH - 1)
                P.barrier()
                if stop_after == (l, ph.__name__):
                    done = True
                    break
            if done:
                break
        P.emit(nc)
    return nc


def dma_in(K, out_ap, in_ap, writes, reads=(), q="sync"):
    return K.P.op(q, lambda e: e.dma_start(out=out_ap, in_=in_ap), reads=reads, writes=writes, dma=True)


def dma_out(K, out_ap, in_ap, reads, q="gpsimd"):
    return K.P.op(q, lambda e: e.dma_start(out=out_ap, in_=in_ap), reads=reads, writes=(), dma=True)


def mm(K, out_ap, pairs, reads, writes, start=True, stop=True):
    def fn(e):
        n = len(pairs)
        ins = None
        for i, (l, r) in enumerate(pairs):
            ins = e.matmul(out_ap, lhsT=l, rhs=r, start=(start and i == 0), stop=(stop and i == n - 1))
        return ins
    return K.P.op("tensor", fn, reads=reads, writes=writes)


def tok_tiles():
    r = [(0, NCTX, True)]
    for s in range(SEQ // 512):
        r.append((NCTX + s * 512, 512, False))
    return r


def phase_mod(K, l, last):
    P, I, sb = K.P, K.I, K.sb
    K.off = K.base
    cs = sb("cs", [128, 16], F32)
    lh = sb("lh", [128, 16, 128], F32)
    bB = sb("bB", [128, 3 * D], F32)
    nwB = sb("nwB", [128, D], F32)
    tmp = sb("mtmp", [128, 512], F32)
    wt = [sb(f"wmod{i}", [128, 1536], F32) for i in range(2)]
    dma_in(K, cs.h[:, 0:8], I.c_t, [cs.b])
    dma_in(K, cs.h[:, 8:16], I.cctx_t, [cs.b])
    dma_in(K, bB.h, I.b_mod[l].partition_broadcast(128), [bB.b])
    dma_in(K, nwB.h, I.norm_w[l].partition_broadcast(128), [nwB.b])
    P.op("scalar", lambda e: e.activation(out=cs.h, in_=cs.h, func=AF.Silu), reads=[cs.b], writes=[cs.b])
    P.op("vector", lambda e: e.tensor_copy(out=lh.h, in_=cs.h.unsqueeze(2).to_broadcast([128, 16, 128])),
         reads=[cs.b], writes=[lh.b])
    names = (("sh_l", "sc_l", "g_l"), ("sh_c", "sc_c", "g_c"))
    it = 0
    for half in range(2):
        for kc in range(8):
            w = wt[it % 2]
            it += 1
            dma_in(K, w.h, I.w_mod[l, kc * 128:(kc + 1) * 128, half * 1536:(half + 1) * 1536], [w.b])
            for who in range(2):
                for j in range(3):
                    bank = K.ps[who * 3 + j]
                    mm(K, bank.h[:, :], [(lh.h[:, who * 8 + kc, :], w.h[:, j * 512:(j + 1) * 512])],
                       reads=[lh.b, w.b], writes=[bank.b], start=(kc == 0), stop=(kc == 7))
        for who in range(2):
            for j in range(3):
                jb = half * 3 + j
                kind, off = jb // 2, (jb % 2) * 512
                bank = K.ps[who * 3 + j]
                dst = K.mod[names[who][kind]]
                bsl = bB.h[:, jb * 512:(jb + 1) * 512]
                if kind == 1:
                    P.op("vector", lambda e, bank=bank, bsl=bsl: e.tensor_tensor(out=tmp.h, in0=bank.h[:, :], in1=bsl, op=ALU.add),
                         reads=[bank.b, bB.b], writes=[tmp.b])
                    P.op("vector", lambda e, dst=dst, off=off: e.scalar_tensor_tensor(
                        out=dst.h[:, off:off + 512], in0=tmp.h, scalar=1.0, in1=nwB.h[:, off:off + 512],
                        op0=ALU.add, op1=ALU.mult), reads=[tmp.b, nwB.b], writes=[dst.b])
                else:
                    P.op("vector", lambda e, bank=bank, bsl=bsl, dst=dst, off=off: e.tensor_tensor(
                        out=dst.h[:, off:off + 512], in0=bank.h[:, :], in1=bsl, op=ALU.add),
                        reads=[bank.b, bB.b], writes=[dst.b])


def phase_a1(K, l, last):
    P, I, S, sb = K.P, K.I, K.S, K.sb
    K.off = K.base
    K.hT = sb("hT_all", [128, 8, T], BF16)
    K.a2_base = K.off
    NB1 = 4
    xt = [sb(f"xt{i}", [128, D], F32) for i in range(NB1)]
    junk = sb("junk", [128, D], BF16)
    t1 = [sb(f"t1{i}", [128, D], F32) for i in range(NB1)]
    hb = [sb(f"hb{i}", [128, D], BF16) for i in range(NB1)]
    st = [sb(f"st{i}", [128, 2], F32) for i in range(NB1)]
    src = I.xin if l == 0 else S.xs

    def bufs(t):
        return xt[t % NB1], t1[t % NB1], hb[t % NB1], st[t % NB1]

    def stage1(t):
        x, tt, h, s2 = bufs(t)
        dma_in(K, x.h, src[t * 128:(t + 1) * 128, :], [x.b])
        P.op("scalar", lambda e, x=x, s2=s2: e.activation(out=junk.h, in_=x.h, func=AF.Square, accum_out=s2.h[:, 0:1]),
             reads=[x.b], writes=[junk.b, s2.b])
        P.op("scalar", lambda e, s2=s2: e.activation(out=s2.h[:, 1:2], in_=s2.h[:, 0:1], func=AF.Ln, scale=1.0 / D, bias=EPS),
             reads=[s2.b], writes=[s2.b])
        P.op("scalar", lambda e, s2=s2: e.activation(out=s2.h[:, 1:2], in_=s2.h[:, 1:2], func=AF.Exp, scale=-0.5),
             reads=[s2.b], writes=[s2.b])

    def stage2(t):
        x, tt, h, s2 = bufs(t)
        ctx = t < 2
        sc = K.mod["sc_c" if ctx else "sc_l"]
        sh = K.mod["sh_c" if ctx else "sh_l"]
        P.op("vector", lambda e, x=x, tt=tt, s2=s2, sc=sc: e.scalar_tensor_tensor(
            out=tt.h, in0=x.h, scalar=s2.h[:, 1:2], in1=sc.h, op0=ALU.mult, op1=ALU.mult),
            reads=[x.b, s2.b, sc.b], writes=[tt.b])
        P.op("gpsimd", lambda e, tt=tt, h=h, sh=sh: e.tensor_tensor(out=h.h, in0=tt.h, in1=sh.h, op=ALU.add),
             reads=[tt.b, sh.b], writes=[h.b])

    def stage3(t):
        x, tt, h, s2 = bufs(t)
        bank = K.ps[t % 2]
        pb = K.psb[t % 2]

        def tr(e, h=h, pb=pb):
            ins = None
            for kc in range(8):
                ins = e.transpose(out=pb[:, kc * 128:(kc + 1) * 128], in_=h.h[:, kc * 128:(kc + 1) * 128],
                                  identity=K.ident.h)
            return ins
        P.op("tensor", tr, reads=[h.b, K.ident.b], writes=[bank.b])
        P.op("scalar", lambda e, pb=pb, t=t: e.copy(out=K.hT.h[:, :, t * 128:(t + 1) * 128],
                                                   in_=pb.rearrange("p (a b) -> p a b", a=8, b=128)),
             reads=[bank.b], writes=[K.hT.b])

    for step in range(NT + 2):
        if step < NT:
            stage1(step)
        if 0 <= step - 1 < NT:
            stage2(step - 1)
        if 0 <= step - 2 < NT:
            stage3(step - 2)


def phase_a2(K, l, last):
    P, I, S, sb = K.P, K.I, K.S, K.sb
    K.off = K.a2_base
    Wv = I.w_in[l].rearrange("(kc p) n -> p kc n", p=128)
    wf = sb("wf", [128, 8, 512], F32)
    wbs = [sb(f"wb{i}", [128, 8, 512], BF16) for i in range(2)]
    wrs = [sb(f"wr{i}", [128, 8, 512], BF16) for i in range(2)]
    stg = [sb(f"stg{i}", [128, T], BF16) for i in range(2)]
    sub_base = K.off
    cnt = {"s": 0, "b": 0}
    tiles = tok_tiles()

    groups = [("fm", C_FU, 512, None, S.fuT), ("fm", C_FG, 512, AF.Silu, S.fgT),
              ("rope", C_Q, 512, None, S.qT), ("rope", C_K, 512, None, S.kT),
              ("tm", C_V, 512, None, S.v_tm), ("fm", C_AG, 512, AF.Silu, S.agT), ("tm", C_Z, 512, AF.Silu, S.z_tm),
              ("conv", C_XBC, 512, 0, None), ("conv", C_XBC + 512, 512, 1, None), ("dt", C_DT, 16, None, None)]
    for g in range(6):
        groups.append(("fm", C_GT + g * 512, 512, AF.Sigmoid, S.gT[g * 512:(g + 1) * 512, :]))
    started, loaded = set(), {}

    def start_load(i):
        if i >= len(groups) or i in started:
            return
        started.add(i)
        _, c0, ncols, _, _ = groups[i]
        dma_in(K, wf.h[:, :, 0:ncols], Wv[:, :, c0:c0 + ncols], [wf.b])

    def finish_load(i):
        if i >= len(groups) or i in loaded:
            return
        start_load(i)
        kind, c0, ncols, _, _ = groups[i]
        wb, wr = wbs[i % 2], wrs[i % 2]
        P.op("vector", lambda e: e.tensor_copy(out=wb.h[:, 0:5, 0:ncols], in_=wf.h[:, 0:5, 0:ncols]),
             reads=[wf.b], writes=[wb.b])
        P.op("gpsimd", lambda e: e.tensor_copy(out=wb.h[:, 5:8, 0:ncols], in_=wf.h[:, 5:8, 0:ncols]),
             reads=[wf.b], writes=[wb.b])
        if kind == "rope":
            def r1(e):
                ins = None
                for kc in range(8):
                    src = wf.h[:, kc, :].rearrange("p (n h s) -> p n h s", h=2, s=16)
                    dst = wr.h[:, kc, :].rearrange("p (n h s) -> p n h s", h=2, s=16)
                    ins = e.mul(out=dst[:, :, 0, :], in_=src[:, :, 1, :], mul=-1.0)
                return ins

            def r2(e):
                ins = None
                for kc in range(8):
                    src = wf.h[:, kc, :].rearrange("p (n h s) -> p n h s", h=2, s=16)
                    dst = wr.h[:, kc, :].rearrange("p (n h s) -> p n h s", h=2, s=16)
                    ins = e.tensor_copy(out=dst[:, :, 1, :], in_=src[:, :, 0, :])
                return ins
            P.op("scalar", r1, reads=[wf.b], writes=[wr.b])
            P.op("vector", r2, reads=[wf.b], writes=[wr.b])
        loaded[i] = (wb, wr)

    def next_bank():
        b = K.ps[2 + cnt["b"] % 4]
        cnt["b"] += 1
        return b

    def proj(bank, wt, j, tok0, n):
        mm(K, bank.h[:, 0:n], [(wt.h[:, kc, j * 128:(j + 1) * 128], K.hT.h[:, kc, tok0:tok0 + n]) for kc in range(8)],
           reads=[wt.b, K.hT.b], writes=[bank.b])

    def fm_group(wb, func, dest, mid):
        for j in range(4):
            s = stg[cnt["s"] % 2]
            cnt["s"] += 1
            for si, (tok0, n, ctx) in enumerate(tiles):
                bank = next_bank()
                proj(bank, wb, j, tok0, n)
                if func is None and si % 2 == 1:
                    P.op("vector", lambda e, s=s, bank=bank, tok0=tok0, n=n: e.tensor_copy(
                        out=s.h[:, tok0:tok0 + n], in_=bank.h[:, 0:n]), reads=[bank.b], writes=[s.b])
                else:
                    f = AF.Copy if func is None else func
                    P.op("scalar", lambda e, s=s, bank=bank, tok0=tok0, n=n, f=f: e.activation(
                        out=s.h[:, tok0:tok0 + n], in_=bank.h[:, 0:n], func=f), reads=[bank.b], writes=[s.b])
            dma_out(K, dest[j * 128:(j + 1) * 128, :], s.h, reads=[s.b])
            if j == 1:
                mid()

    def tm_group(wb, func, dest, mid, tms):
        dv = dest.rearrange("(a p) n -> p a n", p=128)
        for t in range(NT):
            tm = tms[(t // 4) % 2]
            bank = next_bank()
            mm(K, bank.h[:, :], [(K.hT.h[:, kc, t * 128:(t + 1) * 128], wb.h[:, kc, :]) for kc in range(8)],
               reads=[wb.b, K.hT.b], writes=[bank.b])
            f = AF.Copy if func is None else func
            P.op("scalar", lambda e, tm=tm, bank=bank, t=t, f=f: e.activation(out=tm.h[:, t % 4, :], in_=bank.h[:, :], func=f),
                 reads=[bank.b], writes=[tm.b])
            if t % 4 == 3 or t == NT - 1:
                t0 = (t // 4) * 4
                na = t - t0 + 1
                dma_out(K, dv[:, t0:t0 + na, :], tm.h[:, 0:na, :], reads=[tm.b])
            if t == NT // 2:
                mid()

    def rope_group(wb, wr, dest, mid, rs):
        tabs, rt1, rt2, ro = rs
        rc = 0
        for si, (tok0, n, ctx) in enumerate(tiles):
            if not ctx:
                tab = tabs[si % 2]
                p0 = tok0 - NCTX
                dma_in(K, tab.h[:, 0, :], I.cosT[:, p0:p0 + 512], [tab.b])
                dma_in(K, tab.h[:, 1, :], I.sinT[:, p0:p0 + 512], [tab.b])
            for j in range(4):
                o = ro[rc % 4]
                a, b2 = rt1[rc % 2], rt2[rc % 2]
                rc += 1
                bank = next_bank()
                proj(bank, wb, j, tok0, n)
                if ctx:
                    P.op("scalar", lambda e, o=o, bank=bank, n=n: e.copy(out=o.h[:, 0:n], in_=bank.h[:, 0:n]),
                         reads=[bank.b], writes=[o.b])
                else:
                    bank2 = next_bank()
                    proj(bank2, wr, j, tok0, n)
                    P.op("vector", lambda e, a=a, bank=bank, tab=tab: e.tensor_tensor(
                        out=a.h, in0=bank.h[:, :], in1=tab.h[:, 0, :], op=ALU.mult), reads=[bank.b, tab.b], writes=[a.b])
                    P.op("vector", lambda e, b2=b2, bank2=bank2, tab=tab: e.tensor_tensor(
                        out=b2.h, in0=bank2.h[:, :], in1=tab.h[:, 1, :], op=ALU.mult), reads=[bank2.b, tab.b], writes=[b2.b])
                    P.op("gpsimd", lambda e, o=o, a=a, b2=b2: e.tensor_tensor(out=o.h, in0=a.h, in1=b2.h, op=ALU.add),
                         reads=[a.b, b2.b], writes=[o.b])
                dma_out(K, dest[j * 128:(j + 1) * 128, tok0:tok0 + n], o.h[:, 0:n], reads=[o.b])
            if si == 4:
                mid()

    def conv_group(wb, g, mid, cs):
        cst, acc, cw, cb = cs
        ci = 0
        for j in range(4):
            cc = g * 4 + j
            s = stg[cnt["s"] % 2]
            cnt["s"] += 1
            for si, (tok0, n, ctx) in enumerate(tiles):
                bank = next_bank()
                proj(bank, wb, j, tok0, n)
                col = tok0 + (2 if ctx else 6)
                P.op("scalar", lambda e, bank=bank, col=col, n=n: e.copy(out=cst.h[:, col:col + n], in_=bank.h[:, 0:n]),
                     reads=[bank.b], writes=[cst.b])
            for si, (tok0, n, ctx) in enumerate(tiles):
                col = tok0 + (2 if ctx else 6)
                a = acc[ci % 2]
                ci += 1
                P.op("vector", lambda e, a=a, col=col, n=n, cc=cc: e.tensor_scalar_mul(
                    out=a.h[:, 0:n], in0=cst.h[:, col - 2:col - 2 + n], scalar1=cw.h[:, cc, 0:1]),
                    reads=[cst.b, cw.b], writes=[a.b])
                for k in range(1, 5):
                    P.op("vector", lambda e, a=a, col=col, n=n, cc=cc, k=k: e.scalar_tensor_tensor(
                        out=a.h[:, 0:n], in0=cst.h[:, col - 2 + k:col - 2 + k + n], scalar=cw.h[:, cc, k:k + 1],
                        in1=a.h[:, 0:n], op0=ALU.mult, op1=ALU.add), reads=[cst.b, cw.b, a.b], writes=[a.b])
                P.op("scalar", lambda e, s=s, a=a, tok0=tok0, n=n, cc=cc: e.activation(
                    out=s.h[:, tok0:tok0 + n], in_=a.h[:, 0:n], func=AF.Silu, bias=cb.h[:, cc:cc + 1]),
                    reads=[a.b, cb.b], writes=[s.b])
            dma_out(K, S.xbcT[cc * 128:(cc + 1) * 128, :], s.h, reads=[s.b])
            if j == 1:
                mid()

    def dt_group(wb, mid):
        for t in range(NT):
            bank = K.ps[6] if t < 32 else K.ps[7]
            r0 = (t % 32) * 16
            mm(K, bank.h[:, r0:r0 + 16], [(K.hT.h[:, kc, t * 128:(t + 1) * 128], wb.h[:, kc, 0:16]) for kc in range(8)],
               reads=[wb.b, K.hT.b], writes=[bank.b])
        mid()
        P.op("vector", lambda e: e.tensor_copy(out=K.dtraw.h[:, 0:32, :],
                                               in_=K.ps[6].h[:, :].rearrange("p (a b) -> p a b", a=32, b=16)),
             reads=[K.ps[6].b], writes=[K.dtraw.b])
        P.op("vector", lambda e: e.tensor_copy(out=K.dtraw.h[:, 32:34, :],
                                               in_=K.ps[7].h[:, 0:32].rearrange("p (a b) -> p a b", a=2, b=16)),
             reads=[K.ps[7].b], writes=[K.dtraw.b])
        if "dbg" in K.dbgset:
            dma_out(K, S.dbg[:, 0:NT * 16], K.dtraw.h.rearrange("p a b -> p (a b)"), reads=[K.dtraw.b])

    finish_load(0)
    rs = cs = tms = None
    for i, (kind, c0, ncols, arg, dest) in enumerate(groups):
        wb, wr = loaded[i]
        start_load(i + 1)
        mid = (lambda i=i: finish_load(i + 1))
        if kind == "rope" and rs is None:
            rs = ([sb(f"rtab{k}", [128, 2, 512], F32) for k in range(2)],
                  [sb(f"rt1_{k}", [128, 512], F32) for k in range(2)],
                  [sb(f"rt2_{k}", [128, 512], F32) for k in range(2)],
                  [sb(f"ro{k}", [128, 512], BF16) for k in range(4)])
        if kind == "tm" and tms is None:
            P.barrier()
            K.off = sub_base
            tms = [sb(f"tm_{k}", [128, 4, 512], BF16) for k in range(2)]
            CW = T + 8
            cst = sb("cst", [128, CW], F32)
            acc = [sb(f"cacc{k}", [128, 512], F32) for k in range(2)]
            cw = sb("cw", [128, 8, 5], F32)
            cb = sb("cb", [128, 8], F32)
            dma_in(K, cw.h, I.convw_t[l], [cw.b])
            dma_in(K, cb.h, I.convb_t[l], [cb.b])
            P.op("gpsimd", lambda e: e.memset(cst.h, 0.0), writes=[cst.b])
            cs = (cst, acc, cw, cb)
        if kind == "fm":
            fm_group(wb, arg, dest, mid)
        elif kind == "rope":
            rope_group(wb, wr, dest, mid, rs)
        elif kind == "tm":
            tm_group(wb, arg, dest, mid, tms)
        elif kind == "conv":
            conv_group(wb, arg, mid, cs)
        elif kind == "dt":
            dt_group(wb, mid)
        finish_load(i + 1)


def phase_b(K, l, last):
    P, I, S, sb = K.P, K.I, K.S, K.sb
    K.off = K.base
    U = sb("U", [128, NT, 2, 512], BF16)
    dB = sb("dB", [128, 256], BF16)
    fin = [sb(f"fin{i}", [128, 4, 512], BF16) for i in range(2)]
    pieces = [(sb(f"Cp{i}", [128, 8, 512], BF16), sb(f"Sp{i}", [128, 8, 512], BF16)) for i in range(3)]
    fgb = [sb(f"fgb{i}", [128, 4, 512], BF16) for i in range(2)]
    uo = [sb(f"uo{i}", [128, 4, 512], BF16) for i in range(2)]
    fuv = S.fuT.rearrange("(g p) t -> p g t", p=128)
    fgv = S.fgT.rearrange("(g p) t -> p g t", p=128)
    ufv = S.ufT.rearrange("(g p) t -> p g t", p=128)
    dma_in(K, dB.h, I.dftB, [dB.b])
    tiles = tok_tiles()
    bi = 0
    for si, (tok0, n, ctx) in enumerate(tiles):
        if ctx and last:
            continue
        f = fin[si % 2]
        dma_in(K, f.h[:, :, 0:n], fuv[:, :, tok0:tok0 + n], [f.b])
        for tt in range(n // 128):
            t = tok0 // 128 + tt
            for b2 in range(2):
                bank = K.ps[(bi % 2) * 2 + b2]
                for gg in range(2):
                    g = b2 * 2 + gg
                    mm(K, bank.h[:, gg * 256:(gg + 1) * 256], [(f.h[:, g, tt * 128:(tt + 1) * 128], dB.h)],
                       reads=[f.b, dB.b], writes=[bank.b])
                bv = bank.h[:, :].rearrange("p (g c m) -> p g c m", g=2, c=2, m=128)
                P.op("scalar", lambda e, bv=bv, t=t, b2=b2: e.copy(
                    out=U.h[:, t, 0, b2 * 256:(b2 + 1) * 256].rearrange("p (g m) -> p g m", g=2), in_=bv[:, :, 0, :]),
                    reads=[bank.b], writes=[U.b])
                P.op("vector", lambda e, bv=bv, t=t, b2=b2: e.tensor_copy(
                    out=U.h[:, t, 1, b2 * 256:(b2 + 1) * 256].rearrange("p (g m) -> p g m", g=2), in_=bv[:, :, 1, :]),
                    reads=[bank.b], writes=[U.b])
            bi += 1
    C4v = I.C4.rearrange("(nt p) k -> p nt k", p=128)
    S4v = I.S4.rearrange("(nt p) k -> p nt k", p=128)
    pi = 0
    for kb in range(SEQ // 512):
        tok0 = NCTX + kb * 512
        fg = fgb[kb % 2]
        o = uo[kb % 2]
        dma_in(K, fg.h, fgv[:, :, tok0:tok0 + 512], [fg.b])
        banks = [K.ps[(kb % 2) * 4 + ch] for ch in range(4)]
        for pc in range(4):
            Cp, Sp = pieces[pi % 3]
            pi += 1
            dma_in(K, Cp.h, C4v[:, pc * 8:(pc + 1) * 8, kb * 512:(kb + 1) * 512], [Cp.b])
            dma_in(K, Sp.h, S4v[:, pc * 8:(pc + 1) * 8, kb * 512:(kb + 1) * 512], [Sp.b])
            for ch in range(4):
                pairs = []
                for nt in range(8):
                    t = 2 + pc * 8 + nt
                    pairs.append((U.h[:, t, 0, ch * 128:(ch + 1) * 128], Cp.h[:, nt, :]))
                    pairs.append((U.h[:, t, 1, ch * 128:(ch + 1) * 128], Sp.h[:, nt, :]))
                mm(K, banks[ch].h[:, :], pairs, reads=[U.b, Cp.b, Sp.b], writes=[banks[ch].b],
                   start=(pc == 0), stop=(pc == 3))
        for ch in range(4):
            P.op("vector", lambda e, o=o, fg=fg, ch=ch, bank=banks[ch]: e.tensor_tensor(
                out=o.h[:, ch, :], in0=bank.h[:, :], in1=fg.h[:, ch, :], op=ALU.mult),
                reads=[banks[ch].b, fg.b], writes=[o.b])
        dma_out(K, ufv[:, :, tok0:tok0 + 512], o.h, reads=[o.b])
    if not last:
        c2 = sb("c2", [128, 2, 2, 256], BF16)
        dma_in(K, c2.h[:, 0, :, :], I.C2.rearrange("(nt p) k -> p nt k", p=128), [c2.b])
        dma_in(K, c2.h[:, 1, :, :], I.S2.rearrange("(nt p) k -> p nt k", p=128), [c2.b])
        fg = fgb[0]
        o = uo[0]
        dma_in(K, fg.h[:, :, 0:NCTX], fgv[:, :, 0:NCTX], [fg.b])
        for ch in range(4):
            bank = K.ps[ch]
            pairs = []
            for nt in range(2):
                pairs.append((U.h[:, nt, 0, ch * 128:(ch + 1) * 128], c2.h[:, 0, nt, :]))
                pairs.append((U.h[:, nt, 1, ch * 128:(ch + 1) * 128], c2.h[:, 1, nt, :]))
            mm(K, bank.h[:, 0:NCTX], pairs, reads=[U.b, c2.b], writes=[bank.b])
            P.op("vector", lambda e, o=o, fg=fg, ch=ch, bank=bank: e.tensor_tensor(
                out=o.h[:, ch, 0:NCTX], in0=bank.h[:, 0:NCTX], in1=fg.h[:, ch, 0:NCTX], op=ALU.mult),
                reads=[bank.b, fg.b], writes=[o.b])
        dma_out(K, ufv[:, :, 0:NCTX], o.h[:, :, 0:NCTX], reads=[o.b])


def phase_c(K, l, last):
    P, I, S, sb = K.P, K.I, K.S, K.sb
    K.off = K.base
    lam_init = 0.8 - 0.6 * math.exp(-0.3 * l)
    kTs = sb("kTs", [128, 4, T], BF16)
    vs = sb("vs", [128, NT, 512], BF16)
    lamt = sb("lamt", [128, 4, 64], F32)
    lam2 = sb("lam2", [128, 2, 64], F32)
    ls = sb("ls", [128, 4], F32)
    wsub = sb("wsub", [128, 1], F32)
    qts = [sb(f"qt{i}", [128, 4, 512], BF16) for i in range(2)]
    ags = [sb(f"agt{i}", [128, 4, 512], BF16) for i in range(2)]
    uos = [sb(f"uao{i}", [128, 4, 512], BF16) for i in range(2)]
    Et = [sb(f"E{i}", [128, 512], BF16) for i in range(6)]
    R0, R1, T0, T1, A, RS, O = [sb(f"ep{i}", [128, 512], F32) for i in range(7)]
    SQ = sb("sq", [128, 512], BF16)
    ZA = [sb(f"za{i}", [128, 512], F32) for i in range(2)]
    OC = [sb(f"oc{i}", [128, 512], F32) for i in range(2)]
    kv = S.kT.rearrange("(h p) t -> p h t", p=128)
    qv = S.qT.rearrange("(h p) t -> p h t", p=128)
    agv = S.agT.rearrange("(h p) t -> p h t", p=128)
    uav = S.uaT.rearrange("(h p) t -> p h t", p=128)
    for h in range(4):
        dma_in(K, kTs.h[:, h, :], kv[:, h, :], [kTs.b])
    vv = S.v_tm.rearrange("(a p) n -> p a n", p=128)
    for a0 in range(0, NT, 8):
        a1 = min(NT, a0 + 8)
        dma_in(K, vs.h[:, a0:a1, :], vv[:, a0:a1, :], [vs.b])
    dma_in(K, lamt.h.rearrange("p a b -> p (a b)"), I.lam[l].partition_broadcast(128), [lamt.b])
    dma_in(K, wsub.h, I.subln_w[l].rearrange("(p o) -> p o", o=1), [wsub.b])
    P.op("vector", lambda e: e.tensor_tensor(out=lam2.h[:, 0, :], in0=lamt.h[:, 0, :], in1=lamt.h[:, 1, :], op=ALU.mult),
         reads=[lamt.b], writes=[lam2.b])
    P.op("vector", lambda e: e.tensor_tensor(out=lam2.h[:, 1, :], in0=lamt.h[:, 2, :], in1=lamt.h[:, 3, :], op=ALU.mult),
         reads=[lamt.b], writes=[lam2.b])
    P.op("vector", lambda e: e.reduce_sum(out=ls.h[:, 0:2], in_=lam2.h, axis=AX.X), reads=[lam2.b], writes=[ls.b])
    P.op("scalar", lambda e: e.activation(out=ls.h[:, 0:2], in_=ls.h[:, 0:2], func=AF.Exp), reads=[ls.b], writes=[ls.b])
    P.op("vector", lambda e: e.tensor_tensor(out=ls.h[:, 2:3], in0=ls.h[:, 1:2], in1=ls.h[:, 0:1], op=ALU.subtract),
         reads=[ls.b], writes=[ls.b])
    P.op("vector", lambda e: e.tensor_scalar_add(out=ls.h[:, 3:4], in0=ls.h[:, 2:3], scalar1=-lam_init),
         reads=[ls.b], writes=[ls.b])
    P.op("scalar", lambda e: e.mul(out=wsub.h, in_=wsub.h, mul=(1.0 - lam_init)), reads=[wsub.b], writes=[wsub.b])
    psO = [K.ps[4], K.ps[5]]
    psZ = [K.ps[6], K.ps[7]]
    ei = 0
    deferred = []
    for bi, (tok0, n, ctx) in enumerate(tok_tiles()):
        if ctx and last:
            continue
        ktiles = [0, 1] if ctx else list(range(NT))
        qt, agt, uo = qts[bi % 2], ags[bi % 2], uos[bi % 2]
        dma_in(K, qt.h[:, :, 0:n], qv[:, :, tok0:tok0 + n], [qt.b])
        dma_in(K, agt.h[:, :, 0:n], agv[:, :, tok0:tok0 + n], [agt.b])
        for h in range(4):
            nk = len(ktiles)

            def issue_S(ki, h=h, n=n, qt=qt):
                kt = ktiles[ki]
                for m in range(2):
                    bS = K.ps[(ki % 2) * 2 + m]
                    mm(K, bS.h[:, 0:n], [(kTs.h[m * 64:(m + 1) * 64, h, kt * 128:(kt + 1) * 128],
                                          qt.h[m * 64:(m + 1) * 64, h, 0:n])], reads=[kTs.b, qt.b], writes=[bS.b])
            issue_S(0)
            for ki, kt in enumerate(ktiles):
                if ki + 1 < nk:
                    issue_S(ki + 1)
                if ki == min(2, nk - 1) and deferred:
                    deferred.pop(0)()
                b0 = (ki % 2) * 2
                for m in range(2):
                    E = Et[ei % 6]
                    ei += 1
                    Em = E.h[:, 0:n]
                    P.op("scalar", lambda e, Em=Em, b0=b0, m=m, n=n: e.activation(
                        out=Em, in_=K.ps[b0 + m].h[:, 0:n], func=AF.Exp, scale=0.125),
                        reads=[K.ps[b0 + m].b], writes=[E.b])
                    mm(K, psO[m].h[:, 0:n], [(vs.h[:, kt, h * 128:(h + 1) * 128], Em)],
                       reads=[vs.b, E.b], writes=[psO[m].b], start=(ki == 0), stop=(ki == nk - 1))
                    if m == 0:
                        mm(K, psZ[0].h[:, 0:n], [(K.ones_bf.h, Em)], reads=[K.ones_bf.b, E.b], writes=[psZ[0].b],
                           start=(ki == 0), stop=(ki == nk - 1))
                    elif ki == 0:
                        P.op("vector", lambda e, Em=Em, n=n: e.tensor_copy(out=ZA[1].h[:, 0:n], in_=Em),
                             reads=[E.b], writes=[ZA[1].b])
                    else:
                        P.op("vector", lambda e, Em=Em, n=n: e.tensor_tensor(out=ZA[1].h[:, 0:n], in0=ZA[1].h[:, 0:n], in1=Em, op=ALU.add),
                             reads=[E.b, ZA[1].b], writes=[ZA[1].b])
            mm(K, psZ[1].h[:, 0:n], [(K.ones_f.h, ZA[1].h[:, 0:n])], reads=[K.ones_f.b, ZA[1].b], writes=[psZ[1].b])
            P.op("vector", lambda e, n=n: e.tensor_copy(out=OC[0].h[:, 0:n], in_=psO[0].h[:, 0:n]), reads=[psO[0].b], writes=[OC[0].b])
            P.op("vector", lambda e, n=n: e.tensor_copy(out=OC[1].h[:, 0:n], in_=psO[1].h[:, 0:n]), reads=[psO[1].b], writes=[OC[1].b])
            P.op("vector", lambda e, n=n: e.tensor_copy(out=R0.h[:, 0:n], in_=psZ[0].h[:, 0:n]), reads=[psZ[0].b], writes=[R0.b])
            P.op("vector", lambda e, n=n: e.reciprocal(out=R0.h[:, 0:n], in_=R0.h[:, 0:n]), reads=[R0.b], writes=[R0.b])
            P.op("vector", lambda e, n=n: e.reciprocal(out=R1.h[:, 0:n], in_=psZ[1].h[:, 0:n]), reads=[psZ[1].b], writes=[R1.b])
            P.op("gpsimd", lambda e, n=n: e.tensor_tensor(out=T0.h[:, 0:n], in0=OC[0].h[:, 0:n], in1=R0.h[:, 0:n], op=ALU.mult),
                 reads=[OC[0].b, R0.b], writes=[T0.b])
            P.op("vector", lambda e, n=n: e.tensor_tensor(out=T1.h[:, 0:n], in0=OC[1].h[:, 0:n], in1=R1.h[:, 0:n], op=ALU.mult),
                 reads=[OC[1].b, R1.b], writes=[T1.b])
            P.op("vector", lambda e, n=n: e.scalar_tensor_tensor(out=A.h[:, 0:n], in0=T1.h[:, 0:n], scalar=ls.h[:, 3:4],
                                                                in1=T0.h[:, 0:n], op0=ALU.mult, op1=ALU.add),
                 reads=[T0.b, T1.b, ls.b], writes=[A.b])
            P.op("gpsimd", lambda e, n=n: e.tensor_tensor(out=SQ.h[:, 0:n], in0=A.h[:, 0:n], in1=A.h[:, 0:n], op=ALU.mult),
                 reads=[A.b], writes=[SQ.b])
            def part2(n=n, h=h, uo=uo, agt=agt, tok0=tok0):
                bq = psZ[1]
                mm(K, bq.h[:, 0:n], [(K.ones_bf.h, SQ.h[:, 0:n])], reads=[K.ones_bf.b, SQ.b], writes=[bq.b])
                P.op("scalar", lambda e, n=n, bq=bq: e.activation(out=RS.h[:, 0:n], in_=bq.h[:, 0:n], func=AF.Ln, scale=1.0 / 128, bias=EPS),
                     reads=[bq.b], writes=[RS.b])
                P.op("scalar", lambda e, n=n: e.activation(out=RS.h[:, 0:n], in_=RS.h[:, 0:n], func=AF.Exp, scale=-0.5),
                     reads=[RS.b], writes=[RS.b])
                P.op("gpsimd", lambda e, n=n: e.tensor_tensor(out=O.h[:, 0:n], in0=A.h[:, 0:n], in1=RS.h[:, 0:n], op=ALU.mult),
                     reads=[A.b, RS.b], writes=[O.b])
                P.op("vector", lambda e, n=n, h=h, uo=uo, agt=agt: e.scalar_tensor_tensor(
                    out=uo.h[:, h, 0:n], in0=O.h[:, 0:n], scalar=wsub.h[:, 0:1], in1=agt.h[:, h, 0:n], op0=ALU.mult, op1=ALU.mult),
                    reads=[O.b, wsub.b, agt.b], writes=[uo.b])
                if h == 3:
                    dma_out(K, uav[:, :, tok0:tok0 + n], uo.h[:, :, 0:n], reads=[uo.b])
            deferred.append(part2)
    while deferred:
        deferred.pop(0)()


def phase_d(K, l, last):
    P, I, S, sb = K.P, K.I, K.S, K.sb
    K.off = K.base
    par = sb("par", [128, 40], F32)
    snw = sb("snw", [128, 512], F32)
    tri = sb("tri", [128, 128], F32)
    triT = sb("triT", [128, 128], F32)
    mf = sb("mf", [128, 128], F32)
    mb = sb("mb", [128, 128], F32)
    dt, lndt, ad, acs, tot, eacs, wst, biasL = [sb(f"d_{nm}", [128, NT, 16], F32) for nm in
                                                ("dt", "lndt", "ad", "acs", "tot", "eacs", "wst", "biasL")]
    tmpd, dec = dt, tot
    CT = sb("CT", [128, 2, T], BF16)
    bt2 = [sb(f"bt2_{i}", [128, 2, 128], BF16) for i in range(2)]
    XB = sb("XB", [128, NT, 768], BF16)
    SbE = sb("SbE", [128, NT, 512], BF16)
    xin6 = [sb(f"xin6_{i}", [128, 6, 128], BF16) for i in range(2)]
    zts = [sb(f"zt{i}", [128, 512], BF16) for i in range(2)]
    Sst = [sb("Sf", [128, 512], F32), sb("Sb", [128, 512], F32)]
    SfE = [sb(f"SfE{i}", [128, 512], BF16) for i in range(2)]
    xw = [sb(f"xw{i}", [128, 512], BF16) for i in range(2)]
    CBs = sb("CBs", [128, 2, 128], F32)
    Lrs = [sb(f"Lr{i}", [128, 8, 128], F32) for i in range(2)]
    MT4 = [[sb(f"MT{j}_{i}", [128, 8, 128], BF16) for i in range(2)] for j in range(2)]
    Y1, Y2, YZ = [sb(f"Y{i}", [128, 512], F32) for i in range(3)]
    junk = sb("djunk", [128, 512], BF16)
    s2 = sb("ds2", [128, 2], F32)
    usb = sb("usb", [128, 512], BF16)
    usT = [sb(f"usTs{i}", [128, 4, 512], BF16) for i in range(2)]

    dma_in(K, par.h[:, 0:16], I.a_log[l].partition_broadcast(128), [par.b])
    dma_in(K, par.h[:, 16:32], I.dt_bias[l].partition_broadcast(128), [par.b])
    dma_in(K, par.h[:, 32:40], I.d_skip[l].partition_broadcast(128), [par.b])
    dma_in(K, snw.h, I.ssd_norm_w[l].partition_broadcast(128), [snw.b])
    for t_, src in ((tri, I.tri), (triT, I.triT), (mf, I.mask_f), (mb, I.mask_b)):
        dma_in(K, t_.h, src, [t_.b])
    xv = S.xbcT.rearrange("(g p) t -> p g t", p=128)
    dma_in(K, CT.h, xv[:, 6:8, :], [CT.b])

    def bc16(ap):
        return ap.unsqueeze(1).to_broadcast([128, NT, 16])

    def bc64(ap):
        return ap.unsqueeze(2).to_broadcast([128, 8, 64])

    def v3(ap):
        return ap.rearrange("p (h q) -> p h q", h=8)

    P.op("vector", lambda e: e.tensor_tensor(out=dt.h, in0=K.dtraw.h, in1=bc16(par.h[:, 16:32]), op=ALU.add),
         reads=[K.dtraw.b, par.b], writes=[dt.b])
    P.op("scalar", lambda e: e.activation(out=dt.h, in_=dt.h, func=AF.Exp), reads=[dt.b], writes=[dt.b])
    P.op("scalar", lambda e: e.activation(out=dt.h, in_=dt.h, func=AF.Ln, bias=1.0), reads=[dt.b], writes=[dt.b])
    P.op("scalar", lambda e: e.activation(out=lndt.h, in_=dt.h, func=AF.Ln), reads=[dt.b], writes=[lndt.b])
    P.op("scalar", lambda e: e.activation(out=par.h[:, 0:16], in_=par.h[:, 0:16], func=AF.Exp), reads=[par.b], writes=[par.b])
    P.op("vector", lambda e: e.scalar_tensor_tensor(out=ad.h, in0=dt.h, scalar=-1.0, in1=bc16(par.h[:, 0:16]),
                                                    op0=ALU.mult, op1=ALU.mult), reads=[dt.b, par.b], writes=[ad.b])
    for c in range(NT):
        bank = K.ps[c // 16]
        r0 = (c % 16) * 32
        mm(K, bank.h[:, r0:r0 + 8], [(tri.h, ad.h[:, c, 0:8])], reads=[tri.b, ad.b], writes=[bank.b])
        mm(K, bank.h[:, r0 + 8:r0 + 16], [(triT.h, ad.h[:, c, 8:16])], reads=[triT.b, ad.b], writes=[bank.b])
        mm(K, bank.h[:, r0 + 16:r0 + 32], [(K.ones_f.h, ad.h[:, c, :])], reads=[K.ones_f.b, ad.b], writes=[bank.b])
    for bi_, (c0, ncz) in enumerate(((0, 16), (16, 16), (32, 2))):
        bank = K.ps[bi_]
        bv = bank.h[:, 0:ncz * 32].rearrange("p (a b) -> p a b", a=ncz, b=32)
        P.op("vector", lambda e, bv=bv, c0=c0, ncz=ncz: e.tensor_copy(out=acs.h[:, c0:c0 + ncz, :], in_=bv[:, :, 0:16]),
             reads=[bank.b], writes=[acs.b])
        P.op("vector", lambda e, bv=bv, c0=c0, ncz=ncz: e.tensor_copy(out=tot.h[:, c0:c0 + ncz, :], in_=bv[:, :, 16:32]),
             reads=[bank.b], writes=[tot.b])
    P.op("scalar", lambda e: e.activation(out=eacs.h, in_=acs.h, func=AF.Exp), reads=[acs.b], writes=[eacs.b])
    P.op("vector", lambda e: e.tensor_tensor(out=biasL.h, in0=lndt.h, in1=acs.h, op=ALU.subtract),
         reads=[lndt.b, acs.b], writes=[biasL.b])
    P.op("vector", lambda e: e.tensor_tensor(out=tmpd.h, in0=tot.h, in1=biasL.h, op=ALU.add),
         reads=[tot.b, biasL.b], writes=[tmpd.b])
    P.op("scalar", lambda e: e.activation(out=wst.h, in_=tmpd.h, func=AF.Exp), reads=[tmpd.b], writes=[wst.b])
    P.op("scalar", lambda e: e.activation(out=dec.h, in_=tot.h, func=AF.Exp), reads=[tot.b], writes=[dec.b])
    for c in range(NT):
        xi = xin6[c % 2]
        bank = K.ps[3 + c % 2]
        pb = K.psb[3 + c % 2]
        dma_in(K, xi.h, xv[:, 0:6, c * 128:(c + 1) * 128], [xi.b])

        def tr(e, xi=xi, pb=pb):
            ins = None
            for j in range(6):
                ins = e.transpose(out=pb[:, j * 128:(j + 1) * 128], in_=xi.h[:, j, :], identity=K.ident.h)
            return ins
        P.op("tensor", tr, reads=[xi.b, K.ident.b], writes=[bank.b])
        P.op("scalar", lambda e, pb=pb, c=c: e.copy(out=XB.h[:, c, :], in_=pb[:, 0:768]), reads=[bank.b], writes=[XB.b])

    psSt = K.ps[3]

    def make_xw(c, d, w):
        P.op("vector", lambda e: e.tensor_tensor(out=v3(w.h), in0=v3(XB.h[:, c, 0:512]),
                                                 in1=bc64(wst.h[:, c, d * 8:(d + 1) * 8]), op=ALU.mult),
             reads=[XB.b, wst.b], writes=[w.b])

    def apply_update(c, d, w):
        Sd = Sst[d]
        for g in range(2):
            mm(K, psSt.h[:, g * 256:(g + 1) * 256], [(XB.h[:, c, 512 + g * 128:512 + (g + 1) * 128], w.h[:, g * 256:(g + 1) * 256])],
               reads=[XB.b, w.b], writes=[psSt.b])
        P.op("vector", lambda e: e.tensor_tensor(out=v3(Sd.h), in0=v3(Sd.h), in1=bc64(dec.h[:, c, d * 8:(d + 1) * 8]), op=ALU.mult),
             reads=[Sd.b, dec.b], writes=[Sd.b])
        P.op("vector", lambda e: e.tensor_tensor(out=Sd.h, in0=Sd.h, in1=psSt.h[:, :], op=ALU.add),
             reads=[Sd.b, psSt.b], writes=[Sd.b])

    P.op("gpsimd", lambda e: e.memset(Sst[0].h, 0.0), writes=[Sst[0].b])
    P.op("gpsimd", lambda e: e.memset(Sst[1].h, 0.0), writes=[Sst[1].b])
    border = [1, 0] + list(range(NT - 1, 1, -1))
    make_xw(border[0], 1, xw[0])
    for bi2, c in enumerate(border):
        if bi2 + 1 < len(border):
            make_xw(border[bi2 + 1], 1, xw[(bi2 + 1) % 2])
        P.op("scalar", lambda e, c=c: e.copy(out=SbE.h[:, c, :], in_=Sst[1].h), reads=[Sst[1].b], writes=[SbE.b])
        apply_update(c, 1, xw[bi2 % 2])

    zv = S.z_tm
    usv = S.usT.rearrange("(g p) t -> p g t", p=128)
    psY, psCB, psT = K.ps[7], K.ps[4], K.ps[2]
    psR = [K.ps[5], K.ps[6]]
    psYo = [K.ps[0], K.ps[1]]
    need = [not (c < 2 and last) for c in range(NT)]

    def prep(c):
        if not need[c]:
            return
        zt = zts[c % 2]
        dma_in(K, zt.h, zv[c * 128:(c + 1) * 128, :], [zt.b])
        bt = bt2[c % 2]
        dma_in(K, bt.h, xv[:, 4:6, c * 128:(c + 1) * 128], [bt.b])
        for g in range(2):
            mm(K, psCB.h[:, g * 128:(g + 1) * 128], [(bt.h[:, g, :], CT.h[:, g, c * 128:(c + 1) * 128])],
               reads=[bt.b, CT.b], writes=[psCB.b])
        P.op("scalar", lambda e: e.copy(out=CBs.h.rearrange("p a b -> p (a b)"), in_=psCB.h[:, 0:256]),
             reads=[psCB.b], writes=[CBs.b])
        for d in range(2):
            trd = tri if d == 0 else triT
            msk = mf if d == 0 else mb
            mt = MT4[c % 2][d]
            L = Lrs[d]
            for h in range(8):
                bk = psR[h // 4]
                mm(K, bk.h[:, (h % 4) * 128:(h % 4 + 1) * 128],
                   [(ad.h[:, c, d * 8 + h:d * 8 + h + 1].to_broadcast([128, 128]), trd.h)],
                   reads=[ad.b, trd.b], writes=[bk.b])
            for h in range(8):
                bk = psR[h // 4]
                P.op("scalar", lambda e, bk=bk, h=h, c=c, d=d, L=L: e.activation(
                    out=L.h[:, h, :], in_=bk.h[:, (h % 4) * 128:(h % 4 + 1) * 128], func=AF.Exp,
                    bias=biasL.h[:, c, d * 8 + h:d * 8 + h + 1]), reads=[bk.b, biasL.b], writes=[L.b])
            P.op("vector", lambda e, msk=msk, L=L: e.scalar_tensor_tensor(
                out=L.h, in0=L.h, scalar=1e30, in1=msk.h.unsqueeze(1).to_broadcast([128, 8, 128]),
                op0=ALU.min, op1=ALU.mult), reads=[L.b, msk.b], writes=[L.b])
            for g in range(2):
                P.op("vector", lambda e, g=g, mt=mt, L=L: e.tensor_tensor(
                    out=mt.h[:, g * 4:(g + 1) * 4, :], in0=L.h[:, g * 4:(g + 1) * 4, :],
                    in1=CBs.h[:, g, :].unsqueeze(1).to_broadcast([128, 4, 128]), op=ALU.mult),
                    reads=[L.b, CBs.b], writes=[mt.b])

    def body(c):
        ctx = c < 2
        sfe = SfE[c % 2]
        wf_ = xw[c % 2]
        make_xw(c, 0, wf_)
        if need[c]:
            zt = zts[c % 2]
            mts = MT4[c % 2]
            P.op("scalar", lambda e, sfe=sfe: e.copy(out=sfe.h, in_=Sst[0].h), reads=[Sst[0].b], writes=[sfe.b])
            for h in range(8):
                mm(K, psY.h[:, h * 64:(h + 1) * 64], [(mts[0].h[:, h, :], XB.h[:, c, h * 64:(h + 1) * 64]),
                                                      (mts[1].h[:, h, :], XB.h[:, c, h * 64:(h + 1) * 64])],
                   reads=[mts[0].b, mts[1].b, XB.b], writes=[psY.b])
            for d in range(2):
                se = sfe.h if d == 0 else SbE.h[:, c, :]
                seb = sfe.b if d == 0 else SbE.b
                for g in range(2):
                    mm(K, psYo[d].h[:, g * 256:(g + 1) * 256], [(CT.h[:, g, c * 128:(c + 1) * 128], se[:, g * 256:(g + 1) * 256])],
                       reads=[CT.b, seb], writes=[psYo[d].b])
        apply_update(c, 0, wf_)
        if not need[c]:
            return
        P.op("vector", lambda e, c=c: e.tensor_tensor(out=v3(Y1.h), in0=v3(psYo[0].h[:, :]), in1=bc64(eacs.h[:, c, 0:8]), op=ALU.mult),
             reads=[psYo[0].b, eacs.b], writes=[Y1.b])
        P.op("vector", lambda e, c=c: e.tensor_tensor(out=v3(Y2.h), in0=v3(psYo[1].h[:, :]), in1=bc64(eacs.h[:, c, 8:16]), op=ALU.mult),
             reads=[psYo[1].b, eacs.b], writes=[Y2.b])
        P.op("gpsimd", lambda e: e.tensor_tensor(out=Y1.h, in0=Y1.h, in1=Y2.h, op=ALU.add), reads=[Y1.b, Y2.b], writes=[Y1.b])
        P.op("gpsimd", lambda e, c=c: e.tensor_tensor(out=v3(Y2.h), in0=v3(XB.h[:, c, 0:512]), in1=bc64(par.h[:, 32:40]), op=ALU.mult),
             reads=[XB.b, par.b, Y1.b], writes=[Y2.b])
        P.op("gpsimd", lambda e: e.tensor_tensor(out=Y1.h, in0=Y1.h, in1=Y2.h, op=ALU.add), reads=[Y1.b, Y2.b], writes=[Y1.b])
        P.op("vector", lambda e: e.tensor_tensor(out=Y1.h, in0=Y1.h, in1=psY.h[:, :], op=ALU.add), reads=[Y1.b, psY.b], writes=[Y1.b])
        P.op("vector", lambda e, zt=zt: e.tensor_tensor(out=YZ.h, in0=Y1.h, in1=zt.h, op=ALU.mult), reads=[Y1.b, zt.b], writes=[YZ.b])
        P.op("scalar", lambda e: e.activation(out=junk.h, in_=YZ.h, func=AF.Square, accum_out=s2.h[:, 0:1]),
             reads=[YZ.b], writes=[junk.b, s2.b])
        P.op("scalar", lambda e: e.activation(out=s2.h[:, 1:2], in_=s2.h[:, 0:1], func=AF.Ln, scale=1.0 / 512, bias=EPS),
             reads=[s2.b], writes=[s2.b])
        P.op("scalar", lambda e: e.activation(out=s2.h[:, 1:2], in_=s2.h[:, 1:2], func=AF.Exp, scale=-0.5),
             reads=[s2.b], writes=[s2.b])
        P.op("vector", lambda e: e.scalar_tensor_tensor(out=usb.h, in0=YZ.h, scalar=s2.h[:, 1:2], in1=snw.h,
                                                        op0=ALU.mult, op1=ALU.mult), reads=[YZ.b, s2.b, snw.b], writes=[usb.b])
        pbT = K.psb[2]

        def tr2(e, pbT=pbT):
            ins = None
            for j in range(4):
                ins = e.transpose(out=pbT[:, j * 128:(j + 1) * 128], in_=usb.h[:, j * 128:(j + 1) * 128], identity=K.ident.h)
            return ins
        P.op("tensor", tr2, reads=[usb.b, K.ident.b], writes=[psT.b])
        if ctx:
            grp0, slot, glen = 0, c, 2
        else:
            grp0 = 2 + ((c - 2) // 4) * 4
            slot, glen = c - grp0, 4
        ut = usT[(0 if ctx else 1 + (c - 2) // 4) % 2]
        P.op("scalar", lambda e, ut=ut, slot=slot, pbT=pbT: e.copy(
            out=ut.h[:, :, slot * 128:(slot + 1) * 128], in_=pbT[:, 0:512].rearrange("p (g m) -> p g m", g=4)),
            reads=[psT.b], writes=[ut.b])
        if slot == glen - 1:
            dma_out(K, usv[:, :, grp0 * 128:(grp0 + glen) * 128], ut.h[:, :, 0:glen * 128], reads=[ut.b])

    prep(0)
    for c in range(NT):
        if c + 1 < NT:
            prep(c + 1)
        body(c)


def phase_e(K, l, last):
    P, I, S, sb = K.P, K.I, K.S, K.sb
    K.off = K.base
    wbr = [sb(f"wbr{i}", [128, 4, D], BF16) for i in range(3)]
    wo = sb("wo", [128, 8, D], BF16)
    wstage = [sb(f"wstage{i}", [128, 4, D], F32) for i in range(1)]
    uts = [sb(f"ut{i}", [128, 3, 4, 512], BF16) for i in range(2)]
    gts = [sb(f"gt{i}", [128, 3, 4, 512], BF16) for i in range(3)]
    yTs = [sb(f"yT{i}", [128, 8, 512], BF16) for i in range(2)]
    M2 = [[sb(f"M{j}_{i}", [128, 512], F32) for i in range(3)] for j in range(2)]
    xts = [sb(f"ext{i}", [128, D], F32) for i in range(2)]
    xns = [sb(f"exn{i}", [128, D], F32) for i in range(2)]
    tmp = [sb(f"etmp{i}", [128, 512], F32) for i in range(2)]
    srcs = [(I.w_of[l], wbr[0].h), (I.w_oa[l], wbr[1].h), (I.w_os[l], wbr[2].h),
            (I.w_out[l, 0:512, :], wo.h[:, 0:4, :]), (I.w_out[l, 512:1024, :], wo.h[:, 4:8, :])]
    wbufs = [wbr[0].b, wbr[1].b, wbr[2].b, wo.b, wo.b]
    for i, (src, dst) in enumerate(srcs):
        ws = wstage[0]
        dma_in(K, ws.h, src.rearrange("(kc p) n -> p kc n", p=128), [ws.b])
        P.op("vector", lambda e, ws=ws, dst=dst: e.tensor_copy(out=dst[:, 0:2, :], in_=ws.h[:, 0:2, :]), reads=[ws.b], writes=[wbufs[i]])
        P.op("gpsimd", lambda e, ws=ws, dst=dst: e.tensor_copy(out=dst[:, 2:4, :], in_=ws.h[:, 2:4, :]), reads=[ws.b], writes=[wbufs[i]])
    if last:
        nfB = sb("nfB", [128, D], F32)
        s2 = sb("es2", [128, 2], F32)
        junk = sb("ejunk", [128, D], BF16)
        dma_in(K, nfB.h, I.norm_f.partition_broadcast(128), [nfB.b])
    uviews = [t_.rearrange("(g p) t -> p g t", p=128) for t_ in (S.ufT, S.uaT, S.usT)]
    gv = S.gT.rearrange("(c p) t -> p c t", p=128)
    xsrc = I.xin if l == 0 else S.xs
    state = {"xi": 0, "gi": 0}
    stiles = [tl for tl in tok_tiles() if not (tl[2] and last)]

    def merge(si):
        tok0, n, ctx = stiles[si]
        ut = uts[si % 2]
        yT = yTs[si % 2]
        for br in range(3):
            dma_in(K, ut.h[:, br, :, 0:n], uviews[br][:, :, tok0:tok0 + n], [ut.b])
        for oc in range(8):
            if oc % 4 == 0:
                gth = gts[state["gi"] % 3]
                state["gi"] += 1
                for br in range(3):
                    dma_in(K, gth.h[:, br, :, 0:n], gv[:, br * 8 + oc:br * 8 + oc + 4, tok0:tok0 + n], [gth.b])
            M = M2[oc % 2]
            banks = [K.ps[br + 3 * (oc % 2)] for br in range(3)]
            for br in range(3):
                mm(K, banks[br].h[:, 0:n], [(wbr[br].h[:, kc, oc * 128:(oc + 1) * 128], ut.h[:, br, kc, 0:n]) for kc in range(4)],
                   reads=[wbr[br].b, ut.b], writes=[banks[br].b])
            for br in range(3):
                P.op("vector", lambda e, br=br, oc=oc, n=n, bank=banks[br], M=M, gth=gth: e.tensor_tensor(
                    out=M[br].h[:, 0:n], in0=bank.h[:, 0:n], in1=gth.h[:, br, oc % 4, 0:n], op=ALU.mult),
                    reads=[banks[br].b, gth.b], writes=[M[br].b])
            P.op("gpsimd", lambda e, n=n, M=M: e.tensor_tensor(out=M[0].h[:, 0:n], in0=M[0].h[:, 0:n], in1=M[1].h[:, 0:n], op=ALU.add),
                 reads=[M[0].b, M[1].b], writes=[M[0].b])
            P.op("gpsimd", lambda e, n=n, oc=oc, yT=yT, M=M: e.tensor_tensor(out=yT.h[:, oc, 0:n], in0=M[0].h[:, 0:n], in1=M[2].h[:, 0:n], op=ALU.add),
                 reads=[M[0].b, M[2].b], writes=[yT.b])

    def outproj(si):
        tok0, n, ctx = stiles[si]
        yT = yTs[si % 2]
        gA = K.mod["g_c" if ctx else "g_l"]
        for tt in range(n // 128):
            r0 = tok0 + tt * 128
            xt, xn = xts[state["xi"] % 2], xns[state["xi"] % 2]
            state["xi"] += 1
            dma_in(K, xt.h, xsrc[r0:r0 + 128, :], [xt.b])
            for half in range(2):
                bank = K.ps[6 + half]
                tp = tmp[half]
                hs = slice(half * 512, (half + 1) * 512)
                mm(K, bank.h[:, :], [(yT.h[:, kc, tt * 128:(tt + 1) * 128], wo.h[:, kc, hs]) for kc in range(8)],
                   reads=[yT.b, wo.b], writes=[bank.b])
                P.op("vector", lambda e, bank=bank, tp=tp, hs=hs, gA=gA: e.tensor_tensor(out=tp.h, in0=bank.h[:, :], in1=gA.h[:, hs], op=ALU.mult),
                     reads=[bank.b, gA.b], writes=[tp.b])
                P.op("vector", lambda e, tp=tp, xt=xt, xn=xn, hs=hs: e.tensor_tensor(out=xn.h[:, hs], in0=tp.h, in1=xt.h[:, hs], op=ALU.add),
                     reads=[tp.b, xt.b], writes=[xn.b])
            if not last:
                dma_out(K, S.xs[r0:r0 + 128, :], xn.h, reads=[xn.b])
            else:
                P.op("scalar", lambda e, xn=xn: e.activation(out=junk.h, in_=xn.h, func=AF.Square, accum_out=s2.h[:, 0:1]),
                     reads=[xn.b], writes=[junk.b, s2.b])
                P.op("scalar", lambda e: e.activation(out=s2.h[:, 1:2], in_=s2.h[:, 0:1], func=AF.Ln, scale=1.0 / D, bias=EPS),
                     reads=[s2.b], writes=[s2.b])
                P.op("scalar", lambda e: e.activation(out=s2.h[:, 1:2], in_=s2.h[:, 1:2], func=AF.Exp, scale=-0.5),
                     reads=[s2.b], writes=[s2.b])
                P.op("vector", lambda e, xn=xn, xt=xt: e.scalar_tensor_tensor(out=xt.h, in0=xn.h, scalar=s2.h[:, 1:2], in1=nfB.h,
                                                                               op0=ALU.mult, op1=ALU.mult),
                     reads=[xn.b, s2.b, nfB.b], writes=[xt.b])
                dma_out(K, K.out[r0 - NCTX:r0 - NCTX + 128, :], xt.h, reads=[xt.b])

    merge(0)
    for si in range(len(stiles)):
        if si + 1 < len(stiles):
            merge(si + 1)
        outproj(si)


_CONST = {}


def _constants():
    if _CONST:
        return _CONST
    bf = ml_dtypes.bfloat16
    c = _CONST
    c["ident"] = np.eye(128, dtype=np.float32).astype(bf)
    rows = np.repeat(np.arange(SEQ // 64, dtype=np.float32), 64)
    cols = np.tile(np.arange(64, dtype=np.float32), SEQ // 64)
    freqs = (np.float32(10000.0) ** (-np.arange(0, 32, 2, dtype=np.float32) / np.float32(32))).astype(np.float32)
    ang_r = rows[:, None] * freqs
    ang_c = cols[:, None] * freqs
    ang = np.concatenate([ang_r, ang_r, ang_c, ang_c], axis=-1).astype(np.float32)
    cosT = np.cos(ang).astype(np.float32).T
    sinT = np.sin(ang).astype(np.float32).T
    c["cosT"] = np.ascontiguousarray(np.concatenate([cosT, cosT], axis=0))
    c["sinT"] = np.ascontiguousarray(np.concatenate([sinT, sinT], axis=0))
    j = np.arange(128, dtype=np.float64)
    angB = 2 * np.pi * np.outer(j, j) / 128.0
    c["dftB"] = (np.concatenate([np.cos(angB), -np.sin(angB)], axis=1) / np.sqrt(128.0)).astype(np.float32).astype(bf)
    for key, n in (("4", SEQ), ("2", NCTX)):
        idx = np.arange(n, dtype=np.int64)
        ph = (np.outer(idx, idx) % n).astype(np.float64) * (2 * np.pi / n)
        sc = 1.0 / np.sqrt(float(n))
        c["C" + key] = (np.cos(ph) * sc).astype(np.float32).astype(bf)
        c["S" + key] = (np.sin(ph) * sc).astype(np.float32).astype(bf)
    s_ = np.arange(128)[:, None]
    l_ = np.arange(128)[None, :]
    c["tri"] = (s_ <= l_).astype(np.float32)
    c["triT"] = (s_ >= l_).astype(np.float32)
    c["mask_f"] = (l_ >= s_).astype(np.float32)
    c["mask_b"] = (l_ <= s_).astype(np.float32)
    return c


def make_in_maps(inputs):
    f = lambda a: np.ascontiguousarray(np.asarray(a, dtype=np.float32))
    x, c, ctx = f(inputs["x"]), f(inputs["c"]), f(inputs["ctx"])
    shared = {
        "cctx_t": np.ascontiguousarray(f(inputs["c_ctx"]).reshape(8, 128).T),
        "w_mod": f(inputs["w_mod"]), "b_mod": f(inputs["b_mod"]), "norm_w": f(inputs["norm_w"]),
        "w_in": f(inputs["w_in"]),
        "convw_t": np.ascontiguousarray(f(inputs["conv_w"]).reshape(DEPTH, 5, 8, 128).transpose(0, 3, 2, 1)),
        "convb_t": np.ascontiguousarray(f(inputs["conv_b"]).reshape(DEPTH, 8, 128).transpose(0, 2, 1)),
        "a_log": f(inputs["a_log"]).reshape(DEPTH, 16), "dt_bias": f(inputs["dt_bias"]).reshape(DEPTH, 16),
        "d_skip": f(inputs["d_skip"]), "ssd_norm_w": f(inputs["ssd_norm_w"]),
        "lam": f(inputs["lam"]).reshape(DEPTH, 256), "subln_w": f(inputs["subln_w"]),
        "w_of": f(inputs["w_of"]), "w_oa": f(inputs["w_oa"]), "w_os": f(inputs["w_os"]), "w_out": f(inputs["w_out"]),
        "norm_f": f(inputs["norm_f"]),
    }
    shared.update(_constants())
    maps = []
    for b in range(x.shape[0]):
        m = dict(shared)
        m["xin"] = np.ascontiguousarray(np.concatenate([ctx[b], x[b]], axis=0))
        m["c_t"] = np.ascontiguousarray(c[b].reshape(8, 128).T)
        maps.append(m)
    return maps


_NC_CACHE = {}


def kernel(**inputs):
    if "nc" not in _NC_CACHE:
        _NC_CACHE["nc"] = build_program()
    nc = _NC_CACHE["nc"]
    maps = make_in_maps(inputs)
    res = run_bass_kernel_spmd(nc, maps, core_ids=list(range(len(maps))))
    return np.stack([np.asarray(r["out"], dtype=np.float32) for r in res.results], axis=0)
```

```python
import math
from contextlib import ExitStack

import numpy as np
import ml_dtypes

import concourse.bass as bass
import concourse.mybir as mybir
from concourse.bass_utils import run_bass_kernel_spmd

F32 = mybir.dt.float32
BF16 = mybir.dt.bfloat16
AF = mybir.ActivationFunctionType
ALU = mybir.AluOpType
AX = mybir.AxisListType

D = 1024
SEQ = 4096
NCTX = 256
T = SEQ + NCTX
NT = T // 128
DEPTH = 4
EPS = 1e-6
IN_W = 7696
C_FU, C_FG, C_Q, C_K, C_V, C_AG, C_Z, C_XBC, C_DT, C_GT = 0, 512, 1024, 1536, 2048, 2560, 3072, 3584, 4608, 4624

COMPUTE = ("tensor", "vector", "scalar", "gpsimd")
STREAMS = ("tensor", "vector", "scalar", "gpsimd", "sync")
NS = 16
EPOCH = 20000


class Buf:
    __slots__ = ("name", "w", "r")

    def __init__(self, name=""):
        self.name = name
        self.w = None
        self.r = []


class Op:
    __slots__ = ("stream", "fn", "is_dma", "signal", "waits", "n", "slot", "seq", "clock", "semi", "semv")


class Prog:
    def __init__(self):
        self.streams = {s: [] for s in STREAMS}
        self.known = {s: {c: 0 for c in COMPUTE} for s in STREAMS}
        self.known_dma = {s: {} for s in STREAMS}
        self.nseq = {c: 0 for c in COMPUTE}
        self.dma_ops = {s: [] for s in STREAMS}
        self.last = {c: None for c in COMPUTE}
        self.pending = {s: [] for s in STREAMS}

    def op(self, stream, fn, reads=(), writes=(), dma=False):
        o = Op()
        o.stream = stream
        o.fn = fn
        o.is_dma = dma
        o.signal = False
        o.waits = []
        deps = []
        for b in reads:
            if b.w is not None:
                deps.append(b.w)
        for b in writes:
            if b.w is not None:
                deps.append(b.w)
            deps.extend(b.r)
        if self.pending[stream]:
            deps.extend(self.pending[stream])
            self.pending[stream] = []
        if dma:
            n = len(self.dma_ops[stream])
            o.n = n
            o.slot = n % NS
            if n >= NS:
                deps.append(self.dma_ops[stream][n - NS])
            self.dma_ops[stream].append(o)
        else:
            self.nseq[stream] += 1
            o.seq = self.nseq[stream]
            self.last[stream] = o
        kn = self.known[stream]
        kd = self.known_dma[stream]
        for d in deps:
            if d.is_dma:
                key = (d.stream, d.slot)
                if kd.get(key, -1) < d.n:
                    kd[key] = d.n
                    o.waits.append(d)
                    for c, v in d.clock.items():
                        if kn[c] < v:
                            kn[c] = v
            else:
                if d.stream == "tensor" and stream == "tensor" and not dma:
                    continue
                if kn[d.stream] < d.seq:
                    o.waits.append(d)
                    d.signal = True
                    kn[d.stream] = d.seq
                    for c, v in d.clock.items():
                        if kn[c] < v:
                            kn[c] = v
        o.clock = dict(kn)
        for b in reads:
            b.r.append(o)
        for b in writes:
            b.w = o
            b.r = []
        self.streams[stream].append(o)
        return o

    def barrier(self):
        ops = []
        for c in COMPUTE:
            if self.last[c] is not None:
                ops.append(self.last[c])
        for s in STREAMS:
            ops.extend(self.dma_ops[s][-NS:])
        for s in STREAMS:
            self.pending[s] = list(ops)

    def emit(self, nc):
        self.barrier()
        nsem = {}
        for c in COMPUTE:
            cnt = 0
            for o in self.streams[c]:
                if o.is_dma:
                    continue
                if o.signal:
                    o.semi = cnt // EPOCH
                    o.semv = cnt % EPOCH + 1
                    cnt += 1
            nsem[c] = max(1, (cnt + EPOCH - 1) // EPOCH)
        with ExitStack() as es:
            csem = {c: [es.enter_context(nc.semaphore(f"c_{c}_{i}")) for i in range(nsem[c])] for c in COMPUTE}
            dsem = {s: [es.enter_context(nc.semaphore(f"d_{s}_{i}")) for i in range(NS)]
                    for s in STREAMS if self.dma_ops[s]}

            def wait(e, d):
                if d.is_dma:
                    e.wait_ge(dsem[d.stream][d.slot], 16 * (d.n // NS + 1))
                else:
                    e.wait_ge(csem[d.stream][d.semi], d.semv)

            def body_for(stream):
                def body(e):
                    kn = self.known[stream]
                    kd = self.known_dma[stream]
                    for o in self.streams[stream]:
                        for d in o.waits:
                            wait(e, d)
                        ins = o.fn(e)
                        if o.is_dma:
                            ins.then_inc(dsem[stream][o.slot], 16)
                        elif o.signal:
                            ins.then_inc(csem[stream][o.semi], 1)
                    for d in self.pending[stream]:
                        if d.is_dma:
                            if kd.get((d.stream, d.slot), -1) < d.n:
                                wait(e, d)
                        elif d.signal and not (d.stream == stream):
                            if kn[d.stream] < d.seq:
                                wait(e, d)
                return body

            for c in COMPUTE:
                pass
            with nc.Block() as block:
                block.sync(body_for("sync"))
                block.tensor(body_for("tensor"))
                block.vector(body_for("vector"))
                block.scalar(body_for("scalar"))
                block.gpsimd(body_for("gpsimd"))


class Tl:
    __slots__ = ("h", "b")

    def __init__(self, h, name=""):
        self.h = h
        self.b = Buf(name)


class Ctx:
    pass


def build_program(n_layers=DEPTH, stop_after=None, dbg=()):
    nc = bass.Bass("TRN2", target_bir_lowering=False)
    P = Prog()
    K = Ctx()
    K.nc, K.P = nc, P
    K.dbgset = set(dbg)

    def din(name, shape, dt=F32):
        return nc.dram_tensor(name, list(shape), dt, kind="ExternalInput").ap()

    def dscr(name, shape, dt=BF16):
        kind = "ExternalOutput" if name in dbg else "Internal"
        return nc.dram_tensor(name, list(shape), dt, kind=kind).ap()

    I = Ctx()
    I.xin = din("xin", [T, D])
    I.c_t = din("c_t", [128, 8])
    I.cctx_t = din("cctx_t", [128, 8])
    I.w_mod = din("w_mod", [DEPTH, D, 3 * D])
    I.b_mod = din("b_mod", [DEPTH, 3 * D])
    I.norm_w = din("norm_w", [DEPTH, D])
    I.w_in = din("w_in", [DEPTH, D, IN_W])
    I.convw_t = din("convw_t", [DEPTH, 128, 8, 5])
    I.convb_t = din("convb_t", [DEPTH, 128, 8])
    I.a_log = din("a_log", [DEPTH, 16])
    I.dt_bias = din("dt_bias", [DEPTH, 16])
    I.d_skip = din("d_skip", [DEPTH, 8])
    I.ssd_norm_w = din("ssd_norm_w", [DEPTH, 512])
    I.lam = din("lam", [DEPTH, 256])
    I.subln_w = din("subln_w", [DEPTH, 128])
    I.w_of = din("w_of", [DEPTH, 512, D])
    I.w_oa = din("w_oa", [DEPTH, 512, D])
    I.w_os = din("w_os", [DEPTH, 512, D])
    I.w_out = din("w_out", [DEPTH, D, D])
    I.norm_f = din("norm_f", [D])
    I.ident = din("ident", [128, 128], BF16)
    I.cosT = din("cosT", [128, SEQ])
    I.sinT = din("sinT", [128, SEQ])
    I.dftB = din("dftB", [128, 256], BF16)
    I.C4 = din("C4", [SEQ, SEQ], BF16)
    I.S4 = din("S4", [SEQ, SEQ], BF16)
    I.C2 = din("C2", [NCTX, NCTX], BF16)
    I.S2 = din("S2", [NCTX, NCTX], BF16)
    I.tri = din("tri", [128, 128])
    I.triT = din("triT", [128, 128])
    I.mask_f = din("mask_f", [128, 128])
    I.mask_b = din("mask_b", [128, 128])
    K.I = I
    out = nc.dram_tensor("out", [SEQ, D], F32, kind="ExternalOutput").ap()
    K.out = out

    S = Ctx()
    S.xs = dscr("xs", [T, D], F32)
    S.fuT = dscr("fuT", [512, T])
    S.fgT = dscr("fgT", [512, T])
    S.qT = dscr("qT", [512, T])
    S.kT = dscr("kT", [512, T])
    S.v_tm = dscr("v_tm", [T, 512])
    S.agT = dscr("agT", [512, T])
    S.z_tm = dscr("z_tm", [T, 512])
    S.xbcT = dscr("xbcT", [1024, T])
    S.gT = dscr("gT", [3072, T])
    S.ufT = dscr("ufT", [512, T])
    S.uaT = dscr("uaT", [512, T])
    S.usT = dscr("usT", [512, T])
    S.dbg = dscr("dbg", [128, 8192], F32)
    K.S = S

    with ExitStack() as top:
        ARENA = 52800
        arena = top.enter_context(nc.sbuf_tensor("arena", [128, ARENA], F32))
        K.base = 0
        K.off = 0

        def sb(name, shape, dt=F32):
            n = 1
            for v in shape[1:]:
                n *= v
            words = n if dt == F32 else (n + 1) // 2
            words = (words + 7) // 8 * 8
            assert K.off + words <= ARENA, (name, K.off, words)
            ap = arena[:, K.off:K.off + words]
            K.off += words
            if dt != F32:
                ap = ap.bitcast(dt)
            ap = ap[:, 0:n]
            if len(shape) == 3:
                ap = ap.rearrange("p (a b) -> p a b", a=shape[1], b=shape[2])
            elif len(shape) == 4:
                ap = ap.rearrange("p (a b c) -> p a b c", a=shape[1], b=shape[2], c=shape[3])
            return Tl(ap, name)
        K.sb = sb
        K.psall = top.enter_context(nc.psum_tensor("psall", [128, 4096], F32))
        K.ps = [Tl(K.psall[:, i * 512:(i + 1) * 512], f"ps{i}") for i in range(8)]
        K.psb = [K.psall[:, i * 512:(i + 1) * 512].bitcast(BF16) for i in range(8)]
        K.ident = sb("ident", [128, 128], BF16)
        K.ones_bf = sb("ones_bf", [128, 128], BF16)
        K.ones_f = sb("ones_f", [128, 128], F32)
        K.dtraw = sb("dtraw", [128, NT, 16], F32)
        K.mod = {k: sb("mod_" + k, [128, D], F32) for k in ("sc_l", "sh_l", "g_l", "sc_c", "sh_c", "g_c")}
        K.base = K.off
        P.op("sync", lambda e: e.dma_start(out=K.ident.h, in_=I.ident), writes=[K.ident.b], dma=True)
        P.op("vector", lambda e: e.memset(K.ones_bf.h, 1.0), writes=[K.ones_bf.b])
        P.op("vector", lambda e: e.memset(K.ones_f.h, 1.0), writes=[K.ones_f.b])
        phases = [phase_mod, phase_a1, phase_a2, phase_b, phase_c, phase_d, phase_e]
        done = False
        for l in range(n_layers):
            for ph in phases:
                ph(K, l, l == DEPTH - 1)
                P.barrier()
                if stop_after == (l, ph.__name__):
                    done = True
                    break
            if done:
                break
        P.emit(nc)
    return nc


def dma_in(K, out_ap, in_ap, writes, reads=(), q="sync"):
    return K.P.op(q, lambda e: e.dma_start(out=out_ap, in_=in_ap), reads=reads, writes=writes, dma=True)


def dma_out(K, out_ap, in_ap, reads, q="gpsimd"):
    return K.P.op(q, lambda e: e.dma_start(out=out_ap, in_=in_ap), reads=reads, writes=(), dma=True)


def mm(K, out_ap, pairs, reads, writes, start=True, stop=True):
    def fn(e):
        n = len(pairs)
        ins = None
        for i, (l, r) in enumerate(pairs):
            ins = e.matmul(out_ap, lhsT=l, rhs=r, start=(start and i == 0), stop=(stop and i == n - 1))
        return ins
    return K.P.op("tensor", fn, reads=reads, writes=writes)


def tok_tiles():
    r = [(0, NCTX, True)]
    for s in range(SEQ // 512):
        r.append((NCTX + s * 512, 512, False))
    return r


def phase_mod(K, l, last):
    P, I, sb = K.P, K.I, K.sb
    K.off = K.base
    cs = sb("cs", [128, 16], F32)
    lh = sb("lh", [128, 16, 128], F32)
    bB = sb("bB", [128, 3 * D], F32)
    nwB = sb("nwB", [128, D], F32)
    tmp = sb("mtmp", [128, 512], F32)
    wt = [sb(f"wmod{i}", [128, 1536], F32) for i in range(2)]
    dma_in(K, cs.h[:, 0:8], I.c_t, [cs.b])
    dma_in(K, cs.h[:, 8:16], I.cctx_t, [cs.b])
    dma_in(K, bB.h, I.b_mod[l].partition_broadcast(128), [bB.b])
    dma_in(K, nwB.h, I.norm_w[l].partition_broadcast(128), [nwB.b])
    P.op("scalar", lambda e: e.activation(out=cs.h, in_=cs.h, func=AF.Silu), reads=[cs.b], writes=[cs.b])
    P.op("vector", lambda e: e.tensor_copy(out=lh.h, in_=cs.h.unsqueeze(2).to_broadcast([128, 16, 128])),
         reads=[cs.b], writes=[lh.b])
    names = (("sh_l", "sc_l", "g_l"), ("sh_c", "sc_c", "g_c"))
    it = 0
    for half in range(2):
        for kc in range(8):
            w = wt[it % 2]
            it += 1
            dma_in(K, w.h, I.w_mod[l, kc * 128:(kc + 1) * 128, half * 1536:(half + 1) * 1536], [w.b])
            for who in range(2):
                for j in range(3):
                    bank = K.ps[who * 3 + j]
                    mm(K, bank.h[:, :], [(lh.h[:, who * 8 + kc, :], w.h[:, j * 512:(j + 1) * 512])],
                       reads=[lh.b, w.b], writes=[bank.b], start=(kc == 0), stop=(kc == 7))
        for who in range(2):
            for j in range(3):
                jb = half * 3 + j
                kind, off = jb // 2, (jb % 2) * 512
                bank = K.ps[who * 3 + j]
                dst = K.mod[names[who][kind]]
                bsl = bB.h[:, jb * 512:(jb + 1) * 512]
                if kind == 1:
                    P.op("vector", lambda e, bank=bank, bsl=bsl: e.tensor_tensor(out=tmp.h, in0=bank.h[:, :], in1=bsl, op=ALU.add),
                         reads=[bank.b, bB.b], writes=[tmp.b])
                    P.op("vector", lambda e, dst=dst, off=off: e.scalar_tensor_tensor(
                        out=dst.h[:, off:off + 512], in0=tmp.h, scalar=1.0, in1=nwB.h[:, off:off + 512],
                        op0=ALU.add, op1=ALU.mult), reads=[tmp.b, nwB.b], writes=[dst.b])
                else:
                    P.op("vector", lambda e, bank=bank, bsl=bsl, dst=dst, off=off: e.tensor_tensor(
                        out=dst.h[:, off:off + 512], in0=bank.h[:, :], in1=bsl, op=ALU.add),
                        reads=[bank.b, bB.b], writes=[dst.b])


def phase_a1(K, l, last):
    P, I, S, sb = K.P, K.I, K.S, K.sb
    K.off = K.base
    K.hT = sb("hT_all", [128, 8, T], BF16)
    K.a2_base = K.off
    NB1 = 4
    xt = [sb(f"xt{i}", [128, D], F32) for i in range(NB1)]
    junk = sb("junk", [128, D], BF16)
    t1 = [sb(f"t1{i}", [128, D], F32) for i in range(NB1)]
    hb = [sb(f"hb{i}", [128, D], BF16) for i in range(NB1)]
    st = [sb(f"st{i}", [128, 2], F32) for i in range(NB1)]
    src = I.xin if l == 0 else S.xs

    def bufs(t):
        return xt[t % NB1], t1[t % NB1], hb[t % NB1], st[t % NB1]

    def stage1(t):
        x, tt, h, s2 = bufs(t)
        dma_in(K, x.h, src[t * 128:(t + 1) * 128, :], [x.b])
        P.op("scalar", lambda e, x=x, s2=s2: e.activation(out=junk.h, in_=x.h, func=AF.Square, accum_out=s2.h[:, 0:1]),
             reads=[x.b], writes=[junk.b, s2.b])
        P.op("scalar", lambda e, s2=s2: e.activation(out=s2.h[:, 1:2], in_=s2.h[:, 0:1], func=AF.Ln, scale=1.0 / D, bias=EPS),
             reads=[s2.b], writes=[s2.b])
        P.op("scalar", lambda e, s2=s2: e.activation(out=s2.h[:, 1:2], in_=s2.h[:, 1:2], func=AF.Exp, scale=-0.5),
             reads=[s2.b], writes=[s2.b])

    def stage2(t):
        x, tt, h, s2 = bufs(t)
        ctx = t < 2
        sc = K.mod["sc_c" if ctx else "sc_l"]
        sh = K.mod["sh_c" if ctx else "sh_l"]
        P.op("vector", lambda e, x=x, tt=tt, s2=s2, sc=sc: e.scalar_tensor_tensor(
            out=tt.h, in0=x.h, scalar=s2.h[:, 1:2], in1=sc.h, op0=ALU.mult, op1=ALU.mult),
            reads=[x.b, s2.b, sc.b], writes=[tt.b])
        P.op("gpsimd", lambda e, tt=tt, h=h, sh=sh: e.tensor_tensor(out=h.h, in0=tt.h, in1=sh.h, op=ALU.add),
             reads=[tt.b, sh.b], writes=[h.b])

    def stage3(t):
        x, tt, h, s2 = bufs(t)
        bank = K.ps[t % 2]
        pb = K.psb[t % 2]

        def tr(e, h=h, pb=pb):
            ins = None
            for kc in range(8):
                ins = e.transpose(out=pb[:, kc * 128:(kc + 1) * 128], in_=h.h[:, kc * 128:(kc + 1) * 128],
                                  identity=K.ident.h)
            return ins
        P.op("tensor", tr, reads=[h.b, K.ident.b], writes=[bank.b])
        P.op("scalar", lambda e, pb=pb, t=t: e.copy(out=K.hT.h[:, :, t * 128:(t + 1) * 128],
                                                   in_=pb.rearrange("p (a b) -> p a b", a=8, b=128)),
             reads=[bank.b], writes=[K.hT.b])

    for step in range(NT + 2):
        if step < NT:
            stage1(step)
        if 0 <= step - 1 < NT:
            stage2(step - 1)
        if 0 <= step - 2 < NT:
            stage3(step - 2)


def phase_a2(K, l, last):
    P, I, S, sb = K.P, K.I, K.S, K.sb
    K.off = K.a2_base
    Wv = I.w_in[l].rearrange("(kc p) n -> p kc n", p=128)
    wf = sb("wf", [128, 8, 512], F32)
    wbs = [sb(f"wb{i}", [128, 8, 512], BF16) for i in range(2)]
    wrs = [sb(f"wr{i}", [128, 8, 512], BF16) for i in range(2)]
    stg = [sb(f"stg{i}", [128, T], BF16) for i in range(2)]
    sub_base = K.off
    cnt = {"s": 0, "b": 0}
    tiles = tok_tiles()

    groups = [("fm", C_FU, 512, None, S.fuT), ("fm", C_FG, 512, AF.Silu, S.fgT),
              ("rope", C_Q, 512, None, S.qT), ("rope", C_K, 512, None, S.kT),
              ("tm", C_V, 512, None, S.v_tm), ("fm", C_AG, 512, AF.Silu, S.agT), ("tm", C_Z, 512, AF.Silu, S.z_tm),
              ("conv", C_XBC, 512, 0, None), ("conv", C_XBC + 512, 512, 1, None), ("dt", C_DT, 16, None, None)]
    for g in range(6):
        groups.append(("fm", C_GT + g * 512, 512, AF.Sigmoid, S.gT[g * 512:(g + 1) * 512, :]))
    started, loaded = set(), {}

    def start_load(i):
        if i >= len(groups) or i in started:
            return
        started.add(i)
        _, c0, ncols, _, _ = groups[i]
        dma_in(K, wf.h[:, :, 0:ncols], Wv[:, :, c0:c0 + ncols], [wf.b])

    def finish_load(i):
        if i >= len(groups) or i in loaded:
            return
        start_load(i)
        kind, c0, ncols, _, _ = groups[i]
        wb, wr = wbs[i % 2], wrs[i % 2]
        P.op("vector", lambda e: e.tensor_copy(out=wb.h[:, 0:5, 0:ncols], in_=wf.h[:, 0:5, 0:ncols]),
             reads=[wf.b], writes=[wb.b])
        P.op("gpsimd", lambda e: e.tensor_copy(out=wb.h[:, 5:8, 0:ncols], in_=wf.h[:, 5:8, 0:ncols]),
             reads=[wf.b], writes=[wb.b])
        if kind == "rope":
            def r1(e):
                ins = None
                for kc in range(8):
                    src = wf.h[:, kc, :].rearrange("p (n h s) -> p n h s", h=2, s=16)
                    dst = wr.h[:, kc, :].rearrange("p (n h s) -> p n h s", h=2, s=16)
                    ins = e.mul(out=dst[:, :, 0, :], in_=src[:, :, 1, :], mul=-1.0)
                return ins

            def r2(e):
                ins = None
                for kc in range(8):
                    src = wf.h[:, kc, :].rearrange("p (n h s) -> p n h s", h=2, s=16)
                    dst = wr.h[:, kc, :].rearrange("p (n h s) -> p n h s", h=2, s=16)
                    ins = e.tensor_copy(out=dst[:, :, 1, :], in_=src[:, :, 0, :])
                return ins
            P.op("scalar", r1, reads=[wf.b], writes=[wr.b])
            P.op("vector", r2, reads=[wf.b], writes=[wr.b])
        loaded[i] = (wb, wr)

    def next_bank():
        b = K.ps[2 + cnt["b"] % 4]
        cnt["b"] += 1
        return b

    def proj(bank, wt, j, tok0, n):
        mm(K, bank.h[:, 0:n], [(wt.h[:, kc, j * 128:(j + 1) * 128], K.hT.h[:, kc, tok0:tok0 + n]) for kc in range(8)],
           reads=[wt.b, K.hT.b], writes=[bank.b])

    def fm_group(wb, func, dest, mid):
        for j in range(4):
            s = stg[cnt["s"] % 2]
            cnt["s"] += 1
            for si, (tok0, n, ctx) in enumerate(tiles):
                bank = next_bank()
                proj(bank, wb, j, tok0, n)
                if func is None and si % 2 == 1:
                    P.op("vector", lambda e, s=s, bank=bank, tok0=tok0, n=n: e.tensor_copy(
                        out=s.h[:, tok0:tok0 + n], in_=bank.h[:, 0:n]), reads=[bank.b], writes=[s.b])
                else:
                    f = AF.Copy if func is None else func
                    P.op("scalar", lambda e, s=s, bank=bank, tok0=tok0, n=n, f=f: e.activation(
                        out=s.h[:, tok0:tok0 + n], in_=bank.h[:, 0:n], func=f), reads=[bank.b], writes=[s.b])
            dma_out(K, dest[j * 128:(j + 1) * 128, :], s.h, reads=[s.b])
            if j == 1:
                mid()

    def tm_group(wb, func, dest, mid, tms):
        dv = dest.rearrange("(a p) n -> p a n", p=128)
        for t in range(NT):
            tm = tms[(t // 4) % 2]
            bank = next_bank()
            mm(K, bank.h[:, :], [(K.hT.h[:, kc, t * 128:(t + 1) * 128], wb.h[:, kc, :]) for kc in range(8)],
               reads=[wb.b, K.hT.b], writes=[bank.b])
            f = AF.Copy if func is None else func
            P.op("scalar", lambda e, tm=tm, bank=bank, t=t, f=f: e.activation(out=tm.h[:, t % 4, :], in_=bank.h[:, :], func=f),
                 reads=[bank.b], writes=[tm.b])
            if t % 4 == 3 or t == NT - 1:
                t0 = (t // 4) * 4
                na = t - t0 + 1
                dma_out(K, dv[:, t0:t0 + na, :], tm.h[:, 0:na, :], reads=[tm.b])
            if t == NT // 2:
                mid()

    def rope_group(wb, wr, dest, mid, rs):
        tabs, rt1, rt2, ro = rs
        rc = 0
        for si, (tok0, n, ctx) in enumerate(tiles):
            if not ctx:
                tab = tabs[si % 2]
                p0 = tok0 - NCTX
                dma_in(K, tab.h[:, 0, :], I.cosT[:, p0:p0 + 512], [tab.b])
                dma_in(K, tab.h[:, 1, :], I.sinT[:, p0:p0 + 512], [tab.b])
            for j in range(4):
                o = ro[rc % 4]
                a, b2 = rt1[rc % 2], rt2[rc % 2]
                rc += 1
                bank = next_bank()
                proj(bank, wb, j, tok0, n)
                if ctx:
                    P.op("scalar", lambda e, o=o, bank=bank, n=n: e.copy(out=o.h[:, 0:n], in_=bank.h[:, 0:n]),
                         reads=[bank.b], writes=[o.b])
                else:
                    bank2 = next_bank()
                    proj(bank2, wr, j, tok0, n)
                    P.op("vector", lambda e, a=a, bank=bank, tab=tab: e.tensor_tensor(
                        out=a.h, in0=bank.h[:, :], in1=tab.h[:, 0, :], op=ALU.mult), reads=[bank.b, tab.b], writes=[a.b])
                    P.op("vector", lambda e, b2=b2, bank2=bank2, tab=tab: e.tensor_tensor(
                        out=b2.h, in0=bank2.h[:, :], in1=tab.h[:, 1, :], op=ALU.mult), reads=[bank2.b, tab.b], writes=[b2.b])
                    P.op("gpsimd", lambda e, o=o, a=a, b2=b2: e.tensor_tensor(out=o.h, in0=a.h, in1=b2.h, op=ALU.add),
                         reads=[a.b, b2.b], writes=[o.b])
                dma_out(K, dest[j * 128:(j + 1) * 128, tok0:tok0 + n], o.h[:, 0:n], reads=[o.b])
            if si == 4:
                mid()

    def conv_group(wb, g, mid, cs):
        cst, acc, cw, cb = cs
        ci = 0
        for j in range(4):
            cc = g * 4 + j
            s = stg[cnt["s"] % 2]
            cnt["s"] += 1
            for si, (tok0, n, ctx) in enumerate(tiles):
                bank = next_bank()
                proj(bank, wb, j, tok0, n)
                col = tok0 + (2 if ctx else 6)
                P.op("scalar", lambda e, bank=bank, col=col, n=n: e.copy(out=cst.h[:, col:col + n], in_=bank.h[:, 0:n]),
                     reads=[bank.b], writes=[cst.b])
            for si, (tok0, n, ctx) in enumerate(tiles):
                col = tok0 + (2 if ctx else 6)
                a = acc[ci % 2]
                ci += 1
                P.op("vector", lambda e, a=a, col=col, n=n, cc=cc: e.tensor_scalar_mul(
                    out=a.h[:, 0:n], in0=cst.h[:, col - 2:col - 2 + n], scalar1=cw.h[:, cc, 0:1]),
                    reads=[cst.b, cw.b], writes=[a.b])
                for k in range(1, 5):
                    P.op("vector", lambda e, a=a, col=col, n=n, cc=cc, k=k: e.scalar_tensor_tensor(
                        out=a.h[:, 0:n], in0=cst.h[:, col - 2 + k:col - 2 + k + n], scalar=cw.h[:, cc, k:k + 1],
                        in1=a.h[:, 0:n], op0=ALU.mult, op1=ALU.add), reads=[cst.b, cw.b, a.b], writes=[a.b])
                P.op("scalar", lambda e, s=s, a=a, tok0=tok0, n=n, cc=cc: e.activation(
                    out=s.h[:, tok0:tok0 + n], in_=a.h[:, 0:n], func=AF.Silu, bias=cb.h[:, cc:cc + 1]),
                    reads=[a.b, cb.b], writes=[s.b])
            dma_out(K, S.xbcT[cc * 128:(cc + 1) * 128, :], s.h, reads=[s.b])
            if j == 1:
                mid()

    def dt_group(wb, mid):
        for t in range(NT):
            bank = K.ps[6] if t < 32 else K.ps[7]
            r0 = (t % 32) * 16
            mm(K, bank.h[:, r0:r0 + 16], [(K.hT.h[:, kc, t * 128:(t + 1) * 128], wb.h[:, kc, 0:16]) for kc in range(8)],
               reads=[wb.b, K.hT.b], writes=[bank.b])
        mid()
        P.op("vector", lambda e: e.tensor_copy(out=K.dtraw.h[:, 0:32, :],
                                               in_=K.ps[6].h[:, :].rearrange("p (a b) -> p a b", a=32, b=16)),
             reads=[K.ps[6].b], writes=[K.dtraw.b])
        P.op("vector", lambda e: e.tensor_copy(out=K.dtraw.h[:, 32:34, :],
                                               in_=K.ps[7].h[:, 0:32].rearrange("p (a b) -> p a b", a=2, b=16)),
             reads=[K.ps[7].b], writes=[K.dtraw.b])
        if "dbg" in K.dbgset:
            dma_out(K, S.dbg[:, 0:NT * 16], K.dtraw.h.rearrange("p a b -> p (a b)"), reads=[K.dtraw.b])

    finish_load(0)
    rs = cs = tms = None
    for i, (kind, c0, ncols, arg, dest) in enumerate(groups):
        wb, wr = loaded[i]
        start_load(i + 1)
        mid = (lambda i=i: finish_load(i + 1))
        if kind == "rope" and rs is None:
            rs = ([sb(f"rtab{k}", [128, 2, 512], F32) for k in range(2)],
                  [sb(f"rt1_{k}", [128, 512], F32) for k in range(2)],
                  [sb(f"rt2_{k}", [128, 512], F32) for k in range(2)],
                  [sb(f"ro{k}", [128, 512], BF16) for k in range(4)])
        if kind == "tm" and tms is None:
            P.barrier()
            K.off = sub_base
            tms = [sb(f"tm_{k}", [128, 4, 512], BF16) for k in range(2)]
            CW = T + 8
            cst = sb("cst", [128, CW], F32)
            acc = [sb(f"cacc{k}", [128, 512], F32) for k in range(2)]
            cw = sb("cw", [128, 8, 5], F32)
            cb = sb("cb", [128, 8], F32)
            dma_in(K, cw.h, I.convw_t[l], [cw.b])
            dma_in(K, cb.h, I.convb_t[l], [cb.b])
            P.op("gpsimd", lambda e: e.memset(cst.h, 0.0), writes=[cst.b])
            cs = (cst, acc, cw, cb)
        if kind == "fm":
            fm_group(wb, arg, dest, mid)
        elif kind == "rope":
            rope_group(wb, wr, dest, mid, rs)
        elif kind == "tm":
            tm_group(wb, arg, dest, mid, tms)
        elif kind == "conv":
            conv_group(wb, arg, mid, cs)
        elif kind == "dt":
            dt_group(wb, mid)
        finish_load(i + 1)


def phase_b(K, l, last):
    P, I, S, sb = K.P, K.I, K.S, K.sb
    K.off = K.base
    U = sb("U", [128, NT, 2, 512], BF16)
    dB = sb("dB", [128, 256], BF16)
    fin = [sb(f"fin{i}", [128, 4, 512], BF16) for i in range(2)]
    pieces = [(sb(f"Cp{i}", [128, 8, 512], BF16), sb(f"Sp{i}", [128, 8, 512], BF16)) for i in range(3)]
    fgb = [sb(f"fgb{i}", [128, 4, 512], BF16) for i in range(2)]
    uo = [sb(f"uo{i}", [128, 4, 512], BF16) for i in range(2)]
    fuv = S.fuT.rearrange("(g p) t -> p g t", p=128)
    fgv = S.fgT.rearrange("(g p) t -> p g t", p=128)
    ufv = S.ufT.rearrange("(g p) t -> p g t", p=128)
    dma_in(K, dB.h, I.dftB, [dB.b])
    tiles = tok_tiles()
    bi = 0
    for si, (tok0, n, ctx) in enumerate(tiles):
        if ctx and last:
            continue
        f = fin[si % 2]
        dma_in(K, f.h[:, :, 0:n], fuv[:, :, tok0:tok0 + n], [f.b])
        for tt in range(n // 128):
            t = tok0 // 128 + tt
            for b2 in range(2):
                bank = K.ps[(bi % 2) * 2 + b2]
                for gg in range(2):
                    g = b2 * 2 + gg
                    mm(K, bank.h[:, gg * 256:(gg + 1) * 256], [(f.h[:, g, tt * 128:(tt + 1) * 128], dB.h)],
                       reads=[f.b, dB.b], writes=[bank.b])
                bv = bank.h[:, :].rearrange("p (g c m) -> p g c m", g=2, c=2, m=128)
                P.op("scalar", lambda e, bv=bv, t=t, b2=b2: e.copy(
                    out=U.h[:, t, 0, b2 * 256:(b2 + 1) * 256].rearrange("p (g m) -> p g m", g=2), in_=bv[:, :, 0, :]),
                    reads=[bank.b], writes=[U.b])
                P.op("vector", lambda e, bv=bv, t=t, b2=b2: e.tensor_copy(
                    out=U.h[:, t, 1, b2 * 256:(b2 + 1) * 256].rearrange("p (g m) -> p g m", g=2), in_=bv[:, :, 1, :]),
                    reads=[bank.b], writes=[U.b])
            bi += 1
    C4v = I.C4.rearrange("(nt p) k -> p nt k", p=128)
    S4v = I.S4.rearrange("(nt p) k -> p nt k", p=128)
    pi = 0
    for kb in range(SEQ // 512):
        tok0 = NCTX + kb * 512
        fg = fgb[kb % 2]
        o = uo[kb % 2]
        dma_in(K, fg.h, fgv[:, :, tok0:tok0 + 512], [fg.b])
        banks = [K.ps[(kb % 2) * 4 + ch] for ch in range(4)]
        for pc in range(4):
            Cp, Sp = pieces[pi % 3]
            pi += 1
            dma_in(K, Cp.h, C4v[:, pc * 8:(pc + 1) * 8, kb * 512:(kb + 1) * 512], [Cp.b])
            dma_in(K, Sp.h, S4v[:, pc * 8:(pc + 1) * 8, kb * 512:(kb + 1) * 512], [Sp.b])
            for ch in range(4):
                pairs = []
                for nt in range(8):
                    t = 2 + pc * 8 + nt
                    pairs.append((U.h[:, t, 0, ch * 128:(ch + 1) * 128], Cp.h[:, nt, :]))
                    pairs.append((U.h[:, t, 1, ch * 128:(ch + 1) * 128], Sp.h[:, nt, :]))
                mm(K, banks[ch].h[:, :], pairs, reads=[U.b, Cp.b, Sp.b], writes=[banks[ch].b],
                   start=(pc == 0), stop=(pc == 3))
        for ch in range(4):
            P.op("vector", lambda e, o=o, fg=fg, ch=ch, bank=banks[ch]: e.tensor_tensor(
                out=o.h[:, ch, :], in0=bank.h[:, :], in1=fg.h[:, ch, :], op=ALU.mult),
                reads=[banks[ch].b, fg.b], writes=[o.b])
        dma_out(K, ufv[:, :, tok0:tok0 + 512], o.h, reads=[o.b])
    if not last:
        c2 = sb("c2", [128, 2, 2, 256], BF16)
        dma_in(K, c2.h[:, 0, :, :], I.C2.rearrange("(nt p) k -> p nt k", p=128), [c2.b])
        dma_in(K, c2.h[:, 1, :, :], I.S2.rearrange("(nt p) k -> p nt k", p=128), [c2.b])
        fg = fgb[0]
        o = uo[0]
        dma_in(K, fg.h[:, :, 0:NCTX], fgv[:, :, 0:NCTX], [fg.b])
        for ch in range(4):
            bank = K.ps[ch]
            pairs = []
            for nt in range(2):
                pairs.append((U.h[:, nt, 0, ch * 128:(ch + 1) * 128], c2.h[:, 0, nt, :]))
                pairs.append((U.h[:, nt, 1, ch * 128:(ch + 1) * 128], c2.h[:, 1, nt, :]))
            mm(K, bank.h[:, 0:NCTX], pairs, reads=[U.b, c2.b], writes=[bank.b])
            P.op("vector", lambda e, o=o, fg=fg, ch=ch, bank=bank: e.tensor_tensor(
                out=o.h[:, ch, 0:NCTX], in0=bank.h[:, 0:NCTX], in1=fg.h[:, ch, 0:NCTX], op=ALU.mult),
                reads=[bank.b, fg.b], writes=[o.b])
        dma_out(K, ufv[:, :, 0:NCTX], o.h[:, :, 0:NCTX], reads=[o.b])


def phase_c(K, l, last):
    P, I, S, sb = K.P, K.I, K.S, K.sb
    K.off = K.base
    lam_init = 0.8 - 0.6 * math.exp(-0.3 * l)
    kTs = sb("kTs", [128, 4, T], BF16)
    vs = sb("vs", [128, NT, 512], BF16)
    lamt = sb("lamt", [128, 4, 64], F32)
    lam2 = sb("lam2", [128, 2, 64], F32)
    ls = sb("ls", [128, 4], F32)
    wsub = sb("wsub", [128, 1], F32)
    qts = [sb(f"qt{i}", [128, 4, 512], BF16) for i in range(2)]
    ags = [sb(f"agt{i}", [128, 4, 512], BF16) for i in range(2)]
    uos = [sb(f"uao{i}", [128, 4, 512], BF16) for i in range(2)]
    Et = [sb(f"E{i}", [128, 512], BF16) for i in range(6)]
    R0, R1, T0, T1, A, RS, O = [sb(f"ep{i}", [128, 512], F32) for i in range(7)]
    SQ = sb("sq", [128, 512], BF16)
    ZA = [sb(f"za{i}", [128, 512], F32) for i in range(2)]
    OC = [sb(f"oc{i}", [128, 512], F32) for i in range(2)]
    kv = S.kT.rearrange("(h p) t -> p h t", p=128)
    qv = S.qT.rearrange("(h p) t -> p h t", p=128)
    agv = S.agT.rearrange("(h p) t -> p h t", p=128)
    uav = S.uaT.rearrange("(h p) t -> p h t", p=128)
    for h in range(4):
        dma_in(K, kTs.h[:, h, :], kv[:, h, :], [kTs.b])
    vv = S.v_tm.rearrange("(a p) n -> p a n", p=128)
    for a0 in range(0, NT, 8):
        a1 = min(NT, a0 + 8)
        dma_in(K, vs.h[:, a0:a1, :], vv[:, a0:a1, :], [vs.b])
    dma_in(K, lamt.h.rearrange("p a b -> p (a b)"), I.lam[l].partition_broadcast(128), [lamt.b])
    dma_in(K, wsub.h, I.subln_w[l].rearrange("(p o) -> p o", o=1), [wsub.b])
    P.op("vector", lambda e: e.tensor_tensor(out=lam2.h[:, 0, :], in0=lamt.h[:, 0, :], in1=lamt.h[:, 1, :], op=ALU.mult),
         reads=[lamt.b], writes=[lam2.b])
    P.op("vector", lambda e: e.tensor_tensor(out=lam2.h[:, 1, :], in0=lamt.h[:, 2, :], in1=lamt.h[:, 3, :], op=ALU.mult),
         reads=[lamt.b], writes=[lam2.b])
    P.op("vector", lambda e: e.reduce_sum(out=ls.h[:, 0:2], in_=lam2.h, axis=AX.X), reads=[lam2.b], writes=[ls.b])
    P.op("scalar", lambda e: e.activation(out=ls.h[:, 0:2], in_=ls.h[:, 0:2], func=AF.Exp), reads=[ls.b], writes=[ls.b])
    P.op("vector", lambda e: e.tensor_tensor(out=ls.h[:, 2:3], in0=ls.h[:, 1:2], in1=ls.h[:, 0:1], op=ALU.subtract),
         reads=[ls.b], writes=[ls.b])
    P.op("vector", lambda e: e.tensor_scalar_add(out=ls.h[:, 3:4], in0=ls.h[:, 2:3], scalar1=-lam_init),
         reads=[ls.b], writes=[ls.b])
    P.op("scalar", lambda e: e.mul(out=wsub.h, in_=wsub.h, mul=(1.0 - lam_init)), reads=[wsub.b], writes=[wsub.b])
    psO = [K.ps[4], K.ps[5]]
    psZ = [K.ps[6], K.ps[7]]
    ei = 0
    deferred = []
    for bi, (tok0, n, ctx) in enumerate(tok_tiles()):
        if ctx and last:
            continue
        ktiles = [0, 1] if ctx else list(range(NT))
        qt, agt, uo = qts[bi % 2], ags[bi % 2], uos[bi % 2]
        dma_in(K, qt.h[:, :, 0:n], qv[:, :, tok0:tok0 + n], [qt.b])
        dma_in(K, agt.h[:, :, 0:n], agv[:, :, tok0:tok0 + n], [agt.b])
        for h in range(4):
            nk = len(ktiles)

            def issue_S(ki, h=h, n=n, qt=qt):
                kt = ktiles[ki]
                for m in range(2):
                    bS = K.ps[(ki % 2) * 2 + m]
                    mm(K, bS.h[:, 0:n], [(kTs.h[m * 64:(m + 1) * 64, h, kt * 128:(kt + 1) * 128],
                                          qt.h[m * 64:(m + 1) * 64, h, 0:n])], reads=[kTs.b, qt.b], writes=[bS.b])
            issue_S(0)
            for ki, kt in enumerate(ktiles):
                if ki + 1 < nk:
                    issue_S(ki + 1)
                if ki == min(2, nk - 1) and deferred:
                    deferred.pop(0)()
                b0 = (ki % 2) * 2
                for m in range(2):
                    E = Et[ei % 6]
                    ei += 1
                    Em = E.h[:, 0:n]
                    P.op("scalar", lambda e, Em=Em, b0=b0, m=m, n=n: e.activation(
                        out=Em, in_=K.ps[b0 + m].h[:, 0:n], func=AF.Exp, scale=0.125),
                        reads=[K.ps[b0 + m].b], writes=[E.b])
                    mm(K, psO[m].h[:, 0:n], [(vs.h[:, kt, h * 128:(h + 1) * 128], Em)],
                       reads=[vs.b, E.b], writes=[psO[m].b], start=(ki == 0), stop=(ki == nk - 1))
                    if m == 0:
                        mm(K, psZ[0].h[:, 0:n], [(K.ones_bf.h, Em)], reads=[K.ones_bf.b, E.b], writes=[psZ[0].b],
                           start=(ki == 0), stop=(ki == nk - 1))
                    elif ki == 0:
                        P.op("vector", lambda e, Em=Em, n=n: e.tensor_copy(out=ZA[1].h[:, 0:n], in_=Em),
                             reads=[E.b], writes=[ZA[1].b])
                    else:
                        P.op("vector", lambda e, Em=Em, n=n: e.tensor_tensor(out=ZA[1].h[:, 0:n], in0=ZA[1].h[:, 0:n], in1=Em, op=ALU.add),
                             reads=[E.b, ZA[1].b], writes=[ZA[1].b])
            mm(K, psZ[1].h[:, 0:n], [(K.ones_f.h, ZA[1].h[:, 0:n])], reads=[K.ones_f.b, ZA[1].b], writes=[psZ[1].b])
            P.op("vector", lambda e, n=n: e.tensor_copy(out=OC[0].h[:, 0:n], in_=psO[0].h[:, 0:n]), reads=[psO[0].b], writes=[OC[0].b])
            P.op("vector", lambda e, n=n: e.tensor_copy(out=OC[1].h[:, 0:n], in_=psO[1].h[:, 0:n]), reads=[psO[1].b], writes=[OC[1].b])
            P.op("vector", lambda e, n=n: e.tensor_copy(out=R0.h[:, 0:n], in_=psZ[0].h[:, 0:n]), reads=[psZ[0].b], writes=[R0.b])
            P.op("vector", lambda e, n=n: e.reciprocal(out=R0.h[:, 0:n], in_=R0.h[:, 0:n]), reads=[R0.b], writes=[R0.b])
            P.op("vector", lambda e, n=n: e.reciprocal(out=R1.h[:, 0:n], in_=psZ[1].h[:, 0:n]), reads=[psZ[1].b], writes=[R1.b])
            P.op("gpsimd", lambda e, n=n: e.tensor_tensor(out=T0.h[:, 0:n], in0=OC[0].h[:, 0:n], in1=R0.h[:, 0:n], op=ALU.mult),
                 reads=[OC[0].b, R0.b], writes=[T0.b])
            P.op("vector", lambda e, n=n: e.tensor_tensor(out=T1.h[:, 0:n], in0=OC[1].h[:, 0:n], in1=R1.h[:, 0:n], op=ALU.mult),
                 reads=[OC[1].b, R1.b], writes=[T1.b])
            P.op("vector", lambda e, n=n: e.scalar_tensor_tensor(out=A.h[:, 0:n], in0=T1.h[:, 0:n], scalar=ls.h[:, 3:4],
                                                                in1=T0.h[:, 0:n], op0=ALU.mult, op1=ALU.add),
                 reads=[T0.b, T1.b, ls.b], writes=[A.b])
            P.op("gpsimd", lambda e, n=n: e.tensor_tensor(out=SQ.h[:, 0:n], in0=A.h[:, 0:n], in1=A.h[:, 0:n], op=ALU.mult),
                 reads=[A.b], writes=[SQ.b])
            def part2(n=n, h=h, uo=uo, agt=agt, tok0=tok0):
                bq = psZ[1]
                mm(K, bq.h[:, 0:n], [(K.ones_bf.h, SQ.h[:, 0:n])], reads=[K.ones_bf.b, SQ.b], writes=[bq.b])
                P.op("scalar", lambda e, n=n, bq=bq: e.activation(out=RS.h[:, 0:n], in_=bq.h[:, 0:n], func=AF.Ln, scale=1.0 / 128, bias=EPS),
                     reads=[bq.b], writes=[RS.b])
                P.op("scalar", lambda e, n=n: e.activation(out=RS.h[:, 0:n], in_=RS.h[:, 0:n], func=AF.Exp, scale=-0.5),
                     reads=[RS.b], writes=[RS.b])
                P.op("gpsimd", lambda e, n=n: e.tensor_tensor(out=O.h[:, 0:n], in0=A.h[:, 0:n], in1=RS.h[:, 0:n], op=ALU.mult),
                     reads=[A.b, RS.b], writes=[O.b])
                P.op("vector", lambda e, n=n, h=h, uo=uo, agt=agt: e.scalar_tensor_tensor(
                    out=uo.h[:, h, 0:n], in0=O.h[:, 0:n], scalar=wsub.h[:, 0:1], in1=agt.h[:, h, 0:n], op0=ALU.mult, op1=ALU.mult),
                    reads=[O.b, wsub.b, agt.b], writes=[uo.b])
                if h == 3:
                    dma_out(K, uav[:, :, tok0:tok0 + n], uo.h[:, :, 0:n], reads=[uo.b])
            deferred.append(part2)
    while deferred:
        deferred.pop(0)()


def phase_d(K, l, last):
    P, I, S, sb = K.P, K.I, K.S, K.sb
    K.off = K.base
    par = sb("par", [128, 40], F32)
    snw = sb("snw", [128, 512], F32)
    tri = sb("tri", [128, 128], F32)
    triT = sb("triT", [128, 128], F32)
    mf = sb("mf", [128, 128], F32)
    mb = sb("mb", [128, 128], F32)
    dt, lndt, ad, acs, tot, eacs, wst, biasL = [sb(f"d_{nm}", [128, NT, 16], F32) for nm in
                                                ("dt", "lndt", "ad", "acs", "tot", "eacs", "wst", "biasL")]
    tmpd, dec = dt, tot
    CT = sb("CT", [128, 2, T], BF16)
    bt2 = [sb(f"bt2_{i}", [128, 2, 128], BF16) for i in range(3)]
    DS = [sb(f"DS{i}", [128, 512], F32) for i in range(2)]
    XB = sb("XB", [128, NT, 768], BF16)
    SbE = sb("SbE", [128, NT, 512], BF16)
    xin6 = [sb(f"xin6_{i}", [128, 6, 128], BF16) for i in range(2)]
    zts = [sb(f"zt{i}", [128, 512], BF16) for i in range(4)]
    Sst = [sb("Sf", [128, 512], F32), sb("Sb", [128, 512], F32)]
    SfE = [sb(f"SfE{i}", [128, 512], BF16) for i in range(2)]
    xw = [sb(f"xw{i}", [128, 512], BF16) for i in range(2)]
    CBs = sb("CBs", [128, 2, 128], F32)
    Lrs = [sb(f"Lr{i}", [128, 8, 128], F32) for i in range(2)]
    MT4 = [[sb(f"MT{j}_{i}", [128, 8, 128], BF16) for i in range(2)] for j in range(2)]
    Y1, Y2, YZ = [sb(f"Y{i}", [128, 512], F32) for i in range(3)]
    junk = sb("djunk", [128, 512], BF16)
    s2 = sb("ds2", [128, 2], F32)
    usb = sb("usb", [128, 512], BF16)
    usT = [sb(f"usTs{i}", [128, 4, 512], BF16) for i in range(2)]

    dma_in(K, par.h[:, 0:16], I.a_log[l].partition_broadcast(128), [par.b])
    dma_in(K, par.h[:, 16:32], I.dt_bias[l].partition_broadcast(128), [par.b])
    dma_in(K, par.h[:, 32:40], I.d_skip[l].partition_broadcast(128), [par.b])
    dma_in(K, snw.h, I.ssd_norm_w[l].partition_broadcast(128), [snw.b])
    for t_, src in ((tri, I.tri), (triT, I.triT), (mf, I.mask_f), (mb, I.mask_b)):
        dma_in(K, t_.h, src, [t_.b])
    xv = S.xbcT.rearrange("(g p) t -> p g t", p=128)
    dma_in(K, CT.h, xv[:, 6:8, :], [CT.b])

    def bc16(ap):
        return ap.unsqueeze(1).to_broadcast([128, NT, 16])

    def bc64(ap):
        return ap.unsqueeze(2).to_broadcast([128, 8, 64])

    def v3(ap):
        return ap.rearrange("p (h q) -> p h q", h=8)

    P.op("vector", lambda e: e.tensor_tensor(out=dt.h, in0=K.dtraw.h, in1=bc16(par.h[:, 16:32]), op=ALU.add),
         reads=[K.dtraw.b, par.b], writes=[dt.b])
    P.op("scalar", lambda e: e.activation(out=dt.h, in_=dt.h, func=AF.Exp), reads=[dt.b], writes=[dt.b])
    P.op("scalar", lambda e: e.activation(out=dt.h, in_=dt.h, func=AF.Ln, bias=1.0), reads=[dt.b], writes=[dt.b])
    P.op("scalar", lambda e: e.activation(out=lndt.h, in_=dt.h, func=AF.Ln), reads=[dt.b], writes=[lndt.b])
    P.op("scalar", lambda e: e.activation(out=par.h[:, 0:16], in_=par.h[:, 0:16], func=AF.Exp), reads=[par.b], writes=[par.b])
    P.op("vector", lambda e: e.scalar_tensor_tensor(out=ad.h, in0=dt.h, scalar=-1.0, in1=bc16(par.h[:, 0:16]),
                                                    op0=ALU.mult, op1=ALU.mult), reads=[dt.b, par.b], writes=[ad.b])
    for c in range(NT):
        bank = K.ps[c // 16]
        r0 = (c % 16) * 32
        mm(K, bank.h[:, r0:r0 + 8], [(tri.h, ad.h[:, c, 0:8])], reads=[tri.b, ad.b], writes=[bank.b])
        mm(K, bank.h[:, r0 + 8:r0 + 16], [(triT.h, ad.h[:, c, 8:16])], reads=[triT.b, ad.b], writes=[bank.b])
        mm(K, bank.h[:, r0 + 16:r0 + 32], [(K.ones_f.h, ad.h[:, c, :])], reads=[K.ones_f.b, ad.b], writes=[bank.b])
    for bi_, (c0, ncz) in enumerate(((0, 16), (16, 16), (32, 2))):
        bank = K.ps[bi_]
        bv = bank.h[:, 0:ncz * 32].rearrange("p (a b) -> p a b", a=ncz, b=32)
        P.op("vector", lambda e, bv=bv, c0=c0, ncz=ncz: e.tensor_copy(out=acs.h[:, c0:c0 + ncz, :], in_=bv[:, :, 0:16]),
             reads=[bank.b], writes=[acs.b])
        P.op("vector", lambda e, bv=bv, c0=c0, ncz=ncz: e.tensor_copy(out=tot.h[:, c0:c0 + ncz, :], in_=bv[:, :, 16:32]),
             reads=[bank.b], writes=[tot.b])
    P.op("scalar", lambda e: e.activation(out=eacs.h, in_=acs.h, func=AF.Exp), reads=[acs.b], writes=[eacs.b])
    P.op("vector", lambda e: e.tensor_tensor(out=biasL.h, in0=lndt.h, in1=acs.h, op=ALU.subtract),
         reads=[lndt.b, acs.b], writes=[biasL.b])
    P.op("vector", lambda e: e.tensor_tensor(out=tmpd.h, in0=tot.h, in1=biasL.h, op=ALU.add),
         reads=[tot.b, biasL.b], writes=[tmpd.b])
    P.op("scalar", lambda e: e.activation(out=wst.h, in_=tmpd.h, func=AF.Exp), reads=[tmpd.b], writes=[wst.b])
    P.op("scalar", lambda e: e.activation(out=dec.h, in_=tot.h, func=AF.Exp), reads=[tot.b], writes=[dec.b])
    for c in range(NT):
        xi = xin6[c % 2]
        bank = K.ps[3 + c % 2]
        pb = K.psb[3 + c % 2]
        dma_in(K, xi.h, xv[:, 0:6, c * 128:(c + 1) * 128], [xi.b])

        def tr(e, xi=xi, pb=pb):
            ins = None
            for j in range(6):
                ins = e.transpose(out=pb[:, j * 128:(j + 1) * 128], in_=xi.h[:, j, :], identity=K.ident.h)
            return ins
        P.op("tensor", tr, reads=[xi.b, K.ident.b], writes=[bank.b])
        P.op("scalar", lambda e, pb=pb, c=c: e.copy(out=XB.h[:, c, :], in_=pb[:, 0:768]), reads=[bank.b], writes=[XB.b])

    psSt = K.ps[3]

    def make_xw(c, d, w):
        P.op("vector", lambda e: e.tensor_tensor(out=v3(w.h), in0=v3(XB.h[:, c, 0:512]),
                                                 in1=bc64(wst.h[:, c, d * 8:(d + 1) * 8]), op=ALU.mult),
             reads=[XB.b, wst.b], writes=[w.b])

    def apply_update(c, d, w):
        Sd = Sst[d]
        for g in range(2):
            mm(K, psSt.h[:, g * 256:(g + 1) * 256], [(XB.h[:, c, 512 + g * 128:512 + (g + 1) * 128], w.h[:, g * 256:(g + 1) * 256])],
               reads=[XB.b, w.b], writes=[psSt.b])
        P.op("vector", lambda e: e.tensor_tensor(out=v3(Sd.h), in0=v3(Sd.h), in1=bc64(dec.h[:, c, d * 8:(d + 1) * 8]), op=ALU.mult),
             reads=[Sd.b, dec.b], writes=[Sd.b])
        P.op("vector", lambda e: e.tensor_tensor(out=Sd.h, in0=Sd.h, in1=psSt.h[:, :], op=ALU.add),
             reads=[Sd.b, psSt.b], writes=[Sd.b])

    P.op("gpsimd", lambda e: e.memset(Sst[0].h, 0.0), writes=[Sst[0].b])
    P.op("gpsimd", lambda e: e.memset(Sst[1].h, 0.0), writes=[Sst[1].b])
    border = [1, 0] + list(range(NT - 1, 1, -1))
    make_xw(border[0], 1, xw[0])
    for bi2, c in enumerate(border):
        if bi2 + 1 < len(border):
            make_xw(border[bi2 + 1], 1, xw[(bi2 + 1) % 2])
        P.op("scalar", lambda e, c=c: e.copy(out=SbE.h[:, c, :], in_=Sst[1].h), reads=[Sst[1].b], writes=[SbE.b])
        apply_update(c, 1, xw[bi2 % 2])

    zv = S.z_tm
    usv = S.usT.rearrange("(g p) t -> p g t", p=128)
    psY, psCB, psT = K.ps[7], K.ps[4], K.ps[2]
    psR = [K.ps[5], K.ps[6]]
    psYo = [K.ps[0], K.ps[1]]
    need = [not (c < 2 and last) for c in range(NT)]
    tails = []

    def prep(c):
        if not need[c]:
            return
        bt = bt2[c % 3]
        dma_in(K, bt.h, xv[:, 4:6, c * 128:(c + 1) * 128], [bt.b])
        zt = zts[c % 4]
        dma_in(K, zt.h, zv[c * 128:(c + 1) * 128, :], [zt.b])
        ds = DS[c % 2]
        P.op("gpsimd", lambda e, c=c, ds=ds: e.tensor_tensor(out=v3(ds.h), in0=v3(XB.h[:, c, 0:512]), in1=bc64(par.h[:, 32:40]), op=ALU.mult),
             reads=[XB.b, par.b], writes=[ds.b])
        for g in range(2):
            mm(K, psCB.h[:, g * 128:(g + 1) * 128], [(bt.h[:, g, :], CT.h[:, g, c * 128:(c + 1) * 128])],
               reads=[bt.b, CT.b], writes=[psCB.b])
        P.op("scalar", lambda e: e.copy(out=CBs.h.rearrange("p a b -> p (a b)"), in_=psCB.h[:, 0:256]),
             reads=[psCB.b], writes=[CBs.b])
        for d in range(2):
            trd = tri if d == 0 else triT
            msk = mf if d == 0 else mb
            mt = MT4[c % 2][d]
            L = Lrs[d]
            for h in range(8):
                bk = psR[h // 4]
                mm(K, bk.h[:, (h % 4) * 128:(h % 4 + 1) * 128],
                   [(ad.h[:, c, d * 8 + h:d * 8 + h + 1].to_broadcast([128, 128]), trd.h)],
                   reads=[ad.b, trd.b], writes=[bk.b])
            for h in range(8):
                bk = psR[h // 4]
                P.op("scalar", lambda e, bk=bk, h=h, c=c, d=d, L=L: e.activation(
                    out=L.h[:, h, :], in_=bk.h[:, (h % 4) * 128:(h % 4 + 1) * 128], func=AF.Exp,
                    bias=biasL.h[:, c, d * 8 + h:d * 8 + h + 1]), reads=[bk.b, biasL.b], writes=[L.b])
            P.op("vector", lambda e, msk=msk, L=L: e.scalar_tensor_tensor(
                out=L.h, in0=L.h, scalar=1e30, in1=msk.h.unsqueeze(1).to_broadcast([128, 8, 128]),
                op0=ALU.min, op1=ALU.mult), reads=[L.b, msk.b], writes=[L.b])
            for g in range(2):
                P.op("vector", lambda e, g=g, mt=mt, L=L: e.tensor_tensor(
                    out=mt.h[:, g * 4:(g + 1) * 4, :], in0=L.h[:, g * 4:(g + 1) * 4, :],
                    in1=CBs.h[:, g, :].unsqueeze(1).to_broadcast([128, 4, 128]), op=ALU.mult),
                    reads=[L.b, CBs.b], writes=[mt.b])

    def body(c):
        ctx = c < 2
        sfe = SfE[c % 2]
        wf_ = xw[c % 2]
        make_xw(c, 0, wf_)
        if need[c]:
            zt = zts[c % 4]
            ds = DS[c % 2]
            mts = MT4[c % 2]
            P.op("scalar", lambda e, sfe=sfe: e.copy(out=sfe.h, in_=Sst[0].h), reads=[Sst[0].b], writes=[sfe.b])
            for h in range(8):
                mm(K, psY.h[:, h * 64:(h + 1) * 64], [(mts[0].h[:, h, :], XB.h[:, c, h * 64:(h + 1) * 64]),
                                                      (mts[1].h[:, h, :], XB.h[:, c, h * 64:(h + 1) * 64])],
                   reads=[mts[0].b, mts[1].b, XB.b], writes=[psY.b])
            for d in range(2):
                se = sfe.h if d == 0 else SbE.h[:, c, :]
                seb = sfe.b if d == 0 else SbE.b
                for g in range(2):
                    mm(K, psYo[d].h[:, g * 256:(g + 1) * 256], [(CT.h[:, g, c * 128:(c + 1) * 128], se[:, g * 256:(g + 1) * 256])],
                       reads=[CT.b, seb], writes=[psYo[d].b])
        apply_update(c, 0, wf_)
        if tails:
            tails.pop(0)()
        if not need[c]:
            return
        P.op("vector", lambda e, c=c: e.tensor_tensor(out=v3(Y1.h), in0=v3(psYo[0].h[:, :]), in1=bc64(eacs.h[:, c, 0:8]), op=ALU.mult),
             reads=[psYo[0].b, eacs.b], writes=[Y1.b])
        P.op("vector", lambda e, c=c: e.tensor_tensor(out=v3(Y2.h), in0=v3(psYo[1].h[:, :]), in1=bc64(eacs.h[:, c, 8:16]), op=ALU.mult),
             reads=[psYo[1].b, eacs.b], writes=[Y2.b])
        P.op("vector", lambda e: e.tensor_tensor(out=Y1.h, in0=Y1.h, in1=Y2.h, op=ALU.add), reads=[Y1.b, Y2.b], writes=[Y1.b])
        P.op("vector", lambda e, ds=ds: e.tensor_tensor(out=Y1.h, in0=Y1.h, in1=ds.h, op=ALU.add), reads=[Y1.b, ds.b], writes=[Y1.b])
        P.op("vector", lambda e: e.tensor_tensor(out=Y1.h, in0=Y1.h, in1=psY.h[:, :], op=ALU.add), reads=[Y1.b, psY.b], writes=[Y1.b])
        P.op("vector", lambda e, zt=zt: e.tensor_tensor(out=YZ.h, in0=Y1.h, in1=zt.h, op=ALU.mult), reads=[Y1.b, zt.b], writes=[YZ.b])
        P.op("scalar", lambda e: e.activation(out=junk.h, in_=YZ.h, func=AF.Square, accum_out=s2.h[:, 0:1]),
             reads=[YZ.b], writes=[junk.b, s2.b])
        P.op("scalar", lambda e: e.activation(out=s2.h[:, 1:2], in_=s2.h[:, 0:1], func=AF.Ln, scale=1.0 / 512, bias=EPS),
             reads=[s2.b], writes=[s2.b])
        P.op("scalar", lambda e: e.activation(out=s2.h[:, 1:2], in_=s2.h[:, 1:2], func=AF.Exp, scale=-0.5),
             reads=[s2.b], writes=[s2.b])
        P.op("vector", lambda e: e.scalar_tensor_tensor(out=usb.h, in0=YZ.h, scalar=s2.h[:, 1:2], in1=snw.h,
                                                        op0=ALU.mult, op1=ALU.mult), reads=[YZ.b, s2.b, snw.b], writes=[usb.b])
        def tail(c=c, ctx=ctx):
            pbT = K.psb[2]

            def tr2(e, pbT=pbT):
                ins = None
                for j in range(4):
                    ins = e.transpose(out=pbT[:, j * 128:(j + 1) * 128], in_=usb.h[:, j * 128:(j + 1) * 128], identity=K.ident.h)
                return ins
            P.op("tensor", tr2, reads=[usb.b, K.ident.b], writes=[psT.b])
            if ctx:
                grp0, slot, glen = 0, c, 2
            else:
                grp0 = 2 + ((c - 2) // 4) * 4
                slot, glen = c - grp0, 4
            ut = usT[(0 if ctx else 1 + (c - 2) // 4) % 2]
            P.op("scalar", lambda e, ut=ut, slot=slot, pbT=pbT: e.copy(
                out=ut.h[:, :, slot * 128:(slot + 1) * 128], in_=pbT[:, 0:512].rearrange("p (g m) -> p g m", g=4)),
                reads=[psT.b], writes=[ut.b])
            if slot == glen - 1:
                dma_out(K, usv[:, :, grp0 * 128:(grp0 + glen) * 128], ut.h[:, :, 0:glen * 128], reads=[ut.b])
        tails.append(tail)

    prep(0)
    for c in range(NT):
        if c + 1 < NT:
            prep(c + 1)
        body(c)
    while tails:
        tails.pop(0)()


def phase_e(K, l, last):
    P, I, S, sb = K.P, K.I, K.S, K.sb
    K.off = K.base
    wbr = [sb(f"wbr{i}", [128, 4, D], BF16) for i in range(3)]
    wo = sb("wo", [128, 8, D], BF16)
    wstage = [sb(f"wstage{i}", [128, 4, D], F32) for i in range(1)]
    uts = [sb(f"ut{i}", [128, 3, 4, 512], BF16) for i in range(2)]
    gts = [sb(f"gt{i}", [128, 3, 4, 512], BF16) for i in range(3)]
    yTs = [sb(f"yT{i}", [128, 8, 512], BF16) for i in range(2)]
    M2 = [[sb(f"M{j}_{i}", [128, 512], F32) for i in range(3)] for j in range(2)]
    xts = [sb(f"ext{i}", [128, D], F32) for i in range(2)]
    xns = [sb(f"exn{i}", [128, D], F32) for i in range(2)]
    tmp = [sb(f"etmp{i}", [128, 512], F32) for i in range(2)]
    srcs = [(I.w_of[l], wbr[0].h), (I.w_oa[l], wbr[1].h), (I.w_os[l], wbr[2].h),
            (I.w_out[l, 0:512, :], wo.h[:, 0:4, :]), (I.w_out[l, 512:1024, :], wo.h[:, 4:8, :])]
    wbufs = [wbr[0].b, wbr[1].b, wbr[2].b, wo.b, wo.b]
    for i, (src, dst) in enumerate(srcs):
        ws = wstage[0]
        dma_in(K, ws.h, src.rearrange("(kc p) n -> p kc n", p=128), [ws.b])
        P.op("vector", lambda e, ws=ws, dst=dst: e.tensor_copy(out=dst[:, 0:2, :], in_=ws.h[:, 0:2, :]), reads=[ws.b], writes=[wbufs[i]])
        P.op("gpsimd", lambda e, ws=ws, dst=dst: e.tensor_copy(out=dst[:, 2:4, :], in_=ws.h[:, 2:4, :]), reads=[ws.b], writes=[wbufs[i]])
    if last:
        nfB = sb("nfB", [128, D], F32)
        s2 = sb("es2", [128, 2], F32)
        junk = sb("ejunk", [128, D], BF16)
        dma_in(K, nfB.h, I.norm_f.partition_broadcast(128), [nfB.b])
    uviews = [t_.rearrange("(g p) t -> p g t", p=128) for t_ in (S.ufT, S.uaT, S.usT)]
    gv = S.gT.rearrange("(c p) t -> p c t", p=128)
    xsrc = I.xin if l == 0 else S.xs
    state = {"xi": 0, "gi": 0}
    stiles = [tl for tl in tok_tiles() if not (tl[2] and last)]

    def merge(si):
        tok0, n, ctx = stiles[si]
        ut = uts[si % 2]
        yT = yTs[si % 2]
        for br in range(3):
            dma_in(K, ut.h[:, br, :, 0:n], uviews[br][:, :, tok0:tok0 + n], [ut.b])
        for oc in range(8):
            if oc % 4 == 0:
                gth = gts[state["gi"] % 3]
                state["gi"] += 1
                for br in range(3):
                    dma_in(K, gth.h[:, br, :, 0:n], gv[:, br * 8 + oc:br * 8 + oc + 4, tok0:tok0 + n], [gth.b])
            M = M2[oc % 2]
            banks = [K.ps[br + 3 * (oc % 2)] for br in range(3)]
            for br in range(3):
                mm(K, banks[br].h[:, 0:n], [(wbr[br].h[:, kc, oc * 128:(oc + 1) * 128], ut.h[:, br, kc, 0:n]) for kc in range(4)],
                   reads=[wbr[br].b, ut.b], writes=[banks[br].b])
            for br in range(3):
                P.op("vector", lambda e, br=br, oc=oc, n=n, bank=banks[br], M=M, gth=gth: e.tensor_tensor(
                    out=M[br].h[:, 0:n], in0=bank.h[:, 0:n], in1=gth.h[:, br, oc % 4, 0:n], op=ALU.mult),
                    reads=[banks[br].b, gth.b], writes=[M[br].b])
            P.op("gpsimd", lambda e, n=n, M=M: e.tensor_tensor(out=M[0].h[:, 0:n], in0=M[0].h[:, 0:n], in1=M[1].h[:, 0:n], op=ALU.add),
                 reads=[M[0].b, M[1].b], writes=[M[0].b])
            P.op("gpsimd", lambda e, n=n, oc=oc, yT=yT, M=M: e.tensor_tensor(out=yT.h[:, oc, 0:n], in0=M[0].h[:, 0:n], in1=M[2].h[:, 0:n], op=ALU.add),
                 reads=[M[0].b, M[2].b], writes=[yT.b])

    def outproj(si):
        tok0, n, ctx = stiles[si]
        yT = yTs[si % 2]
        gA = K.mod["g_c" if ctx else "g_l"]
        for tt in range(n // 128):
            r0 = tok0 + tt * 128
            xt, xn = xts[state["xi"] % 2], xns[state["xi"] % 2]
            state["xi"] += 1
            dma_in(K, xt.h, xsrc[r0:r0 + 128, :], [xt.b])
            for half in range(2):
                bank = K.ps[6 + half]
                tp = tmp[half]
                hs = slice(half * 512, (half + 1) * 512)
                mm(K, bank.h[:, :], [(yT.h[:, kc, tt * 128:(tt + 1) * 128], wo.h[:, kc, hs]) for kc in range(8)],
                   reads=[yT.b, wo.b], writes=[bank.b])
                P.op("vector", lambda e, bank=bank, tp=tp, hs=hs, gA=gA: e.tensor_tensor(out=tp.h, in0=bank.h[:, :], in1=gA.h[:, hs], op=ALU.mult),
                     reads=[bank.b, gA.b], writes=[tp.b])
                P.op("vector", lambda e, tp=tp, xt=xt, xn=xn, hs=hs: e.tensor_tensor(out=xn.h[:, hs], in0=tp.h, in1=xt.h[:, hs], op=ALU.add),
                     reads=[tp.b, xt.b], writes=[xn.b])
            if not last:
                dma_out(K, S.xs[r0:r0 + 128, :], xn.h, reads=[xn.b])
            else:
                P.op("scalar", lambda e, xn=xn: e.activation(out=junk.h, in_=xn.h, func=AF.Square, accum_out=s2.h[:, 0:1]),
                     reads=[xn.b], writes=[junk.b, s2.b])
                P.op("scalar", lambda e: e.activation(out=s2.h[:, 1:2], in_=s2.h[:, 0:1], func=AF.Ln, scale=1.0 / D, bias=EPS),
                     reads=[s2.b], writes=[s2.b])
                P.op("scalar", lambda e: e.activation(out=s2.h[:, 1:2], in_=s2.h[:, 1:2], func=AF.Exp, scale=-0.5),
                     reads=[s2.b], writes=[s2.b])
                P.op("vector", lambda e, xn=xn, xt=xt: e.scalar_tensor_tensor(out=xt.h, in0=xn.h, scalar=s2.h[:, 1:2], in1=nfB.h,
                                                                               op0=ALU.mult, op1=ALU.mult),
                     reads=[xn.b, s2.b, nfB.b], writes=[xt.b])
                dma_out(K, K.out[r0 - NCTX:r0 - NCTX + 128, :], xt.h, reads=[xt.b])

    merge(0)
    for si in range(len(stiles)):
        if si + 1 < len(stiles):
            merge(si + 1)
        outproj(si)


_CONST = {}


def _constants():
    if _CONST:
        return _CONST
    bf = ml_dtypes.bfloat16
    c = _CONST
    c["ident"] = np.eye(128, dtype=np.float32).astype(bf)
    rows = np.repeat(np.arange(SEQ // 64, dtype=np.float32), 64)
    cols = np.tile(np.arange(64, dtype=np.float32), SEQ // 64)
    freqs = (np.float32(10000.0) ** (-np.arange(0, 32, 2, dtype=np.float32) / np.float32(32))).astype(np.float32)
    ang_r = rows[:, None] * freqs
    ang_c = cols[:, None] * freqs
    ang = np.concatenate([ang_r, ang_r, ang_c, ang_c], axis=-1).astype(np.float32)
    cosT = np.cos(ang).astype(np.float32).T
    sinT = np.sin(ang).astype(np.float32).T
    c["cosT"] = np.ascontiguousarray(np.concatenate([cosT, cosT], axis=0))
    c["sinT"] = np.ascontiguousarray(np.concatenate([sinT, sinT], axis=0))
    j = np.arange(128, dtype=np.float64)
    angB = 2 * np.pi * np.outer(j, j) / 128.0
    c["dftB"] = (np.concatenate([np.cos(angB), -np.sin(angB)], axis=1) / np.sqrt(128.0)).astype(np.float32).astype(bf)
    for key, n in (("4", SEQ), ("2", NCTX)):
        idx = np.arange(n, dtype=np.int64)
        ph = (np.outer(idx, idx) % n).astype(np.float64) * (2 * np.pi / n)
        sc = 1.0 / np.sqrt(float(n))
        c["C" + key] = (np.cos(ph) * sc).astype(np.float32).astype(bf)
        c["S" + key] = (np.sin(ph) * sc).astype(np.float32).astype(bf)
    s_ = np.arange(128)[:, None]
    l_ = np.arange(128)[None, :]
    c["tri"] = (s_ <= l_).astype(np.float32)
    c["triT"] = (s_ >= l_).astype(np.float32)
    c["mask_f"] = (l_ >= s_).astype(np.float32)
    c["mask_b"] = (l_ <= s_).astype(np.float32)
    return c


def make_in_maps(inputs):
    f = lambda a: np.ascontiguousarray(np.asarray(a, dtype=np.float32))
    x, c, ctx = f(inputs["x"]), f(inputs["c"]), f(inputs["ctx"])
    shared = {
        "cctx_t": np.ascontiguousarray(f(inputs["c_ctx"]).reshape(8, 128).T),
        "w_mod": f(inputs["w_mod"]), "b_mod": f(inputs["b_mod"]), "norm_w": f(inputs["norm_w"]),
        "w_in": f(inputs["w_in"]),
        "convw_t": np.ascontiguousarray(f(inputs["conv_w"]).reshape(DEPTH, 5, 8, 128).transpose(0, 3, 2, 1)),
        "convb_t": np.ascontiguousarray(f(inputs["conv_b"]).reshape(DEPTH, 8, 128).transpose(0, 2, 1)),
        "a_log": f(inputs["a_log"]).reshape(DEPTH, 16), "dt_bias": f(inputs["dt_bias"]).reshape(DEPTH, 16),
        "d_skip": f(inputs["d_skip"]), "ssd_norm_w": f(inputs["ssd_norm_w"]),
        "lam": f(inputs["lam"]).reshape(DEPTH, 256), "subln_w": f(inputs["subln_w"]),
        "w_of": f(inputs["w_of"]), "w_oa": f(inputs["w_oa"]), "w_os": f(inputs["w_os"]), "w_out": f(inputs["w_out"]),
        "norm_f": f(inputs["norm_f"]),
    }
    shared.update(_constants())
    maps = []
    for b in range(x.shape[0]):
        m = dict(shared)
        m["xin"] = np.ascontiguousarray(np.concatenate([ctx[b], x[b]], axis=0))
        m["c_t"] = np.ascontiguousarray(c[b].reshape(8, 128).T)
        maps.append(m)
    return maps


_NC_CACHE = {}


def kernel(**inputs):
    if "nc" not in _NC_CACHE:
        _NC_CACHE["nc"] = build_program()
    nc = _NC_CACHE["nc"]
    maps = make_in_maps(inputs)
    res = run_bass_kernel_spmd(nc, maps, core_ids=list(range(len(maps))))
    return np.stack([np.asarray(r["out"], dtype=np.float32) for r in res.results], axis=0)
```

```python
import math
from contextlib import ExitStack

import numpy as np
import ml_dtypes

import concourse.bass as bass
import concourse.mybir as mybir
from concourse.bass_utils import run_bass_kernel_spmd

F32 = mybir.dt.float32
BF16 = mybir.dt.bfloat16
AF = mybir.ActivationFunctionType
ALU = mybir.AluOpType
AX = mybir.AxisListType

D = 1024
SEQ = 4096
NCTX = 256
T = SEQ + NCTX
NT = T // 128
DEPTH = 4
EPS = 1e-6
IN_W = 7696
C_FU, C_FG, C_Q, C_K, C_V, C_AG, C_Z, C_XBC, C_DT, C_GT = 0, 512, 1024, 1536, 2048, 2560, 3072, 3584, 4608, 4624

COMPUTE = ("tensor", "vector", "scalar", "gpsimd")
STREAMS = ("tensor", "vector", "scalar", "gpsimd", "sync")
NS = 16
EPOCH = 20000


class Buf:
    __slots__ = ("name", "w", "r")

    def __init__(self, name=""):
        self.name = name
        self.w = None
        self.r = []


class Op:
    __slots__ = ("stream", "fn", "is_dma", "signal", "waits", "n", "slot", "seq", "clock", "semi", "semv")


class Prog:
    def __init__(self):
        self.streams = {s: [] for s in STREAMS}
        self.known = {s: {c: 0 for c in COMPUTE} for s in STREAMS}
        self.known_dma = {s: {} for s in STREAMS}
        self.nseq = {c: 0 for c in COMPUTE}
        self.dma_ops = {s: [] for s in STREAMS}
        self.last = {c: None for c in COMPUTE}
        self.pending = {s: [] for s in STREAMS}

    def op(self, stream, fn, reads=(), writes=(), dma=False):
        o = Op()
        o.stream = stream
        o.fn = fn
        o.is_dma = dma
        o.signal = False
        o.waits = []
        deps = []
        for b in reads:
            if b.w is not None:
                deps.append(b.w)
        for b in writes:
            if b.w is not None:
                deps.append(b.w)
            deps.extend(b.r)
        if self.pending[stream]:
            deps.extend(self.pending[stream])
            self.pending[stream] = []
        if dma:
            n = len(self.dma_ops[stream])
            o.n = n
            o.slot = n % NS
            if n >= NS:
                deps.append(self.dma_ops[stream][n - NS])
            self.dma_ops[stream].append(o)
        else:
            self.nseq[stream] += 1
            o.seq = self.nseq[stream]
            self.last[stream] = o
        kn = self.known[stream]
        kd = self.known_dma[stream]
        for d in deps:
            if d.is_dma:
                key = (d.stream, d.slot)
                if kd.get(key, -1) < d.n:
                    kd[key] = d.n
                    o.waits.append(d)
                    for c, v in d.clock.items():
                        if kn[c] < v:
                            kn[c] = v
            else:
                if d.stream == "tensor" and stream == "tensor" and not dma:
                    continue
                if kn[d.stream] < d.seq:
                    o.waits.append(d)
                    d.signal = True
                    kn[d.stream] = d.seq
                    for c, v in d.clock.items():
                        if kn[c] < v:
                            kn[c] = v
        o.clock = dict(kn)
        for b in reads:
            b.r.append(o)
        for b in writes:
            b.w = o
            b.r = []
        self.streams[stream].append(o)
        return o

    def barrier(self):
        ops = []
        for c in COMPUTE:
            if self.last[c] is not None:
                ops.append(self.last[c])
        for s in STREAMS:
            ops.extend(self.dma_ops[s][-NS:])
        for s in STREAMS:
            self.pending[s] = list(ops)

    def emit(self, nc):
        self.barrier()
        nsem = {}
        for c in COMPUTE:
            cnt = 0
            for o in self.streams[c]:
                if o.is_dma:
                    continue
                if o.signal:
                    o.semi = cnt // EPOCH
                    o.semv = cnt % EPOCH + 1
                    cnt += 1
            nsem[c] = max(1, (cnt + EPOCH - 1) // EPOCH)
        with ExitStack() as es:
            csem = {c: [es.enter_context(nc.semaphore(f"c_{c}_{i}")) for i in range(nsem[c])] for c in COMPUTE}
            dsem = {s: [es.enter_context(nc.semaphore(f"d_{s}_{i}")) for i in range(NS)]
                    for s in STREAMS if self.dma_ops[s]}

            def wait(e, d):
                if d.is_dma:
                    e.wait_ge(dsem[d.stream][d.slot], 16 * (d.n // NS + 1))
                else:
                    e.wait_ge(csem[d.stream][d.semi], d.semv)

            def body_for(stream):
                def body(e):
                    kn = self.known[stream]
                    kd = self.known_dma[stream]
                    for o in self.streams[stream]:
                        for d in o.waits:
                            wait(e, d)
                        ins = o.fn(e)
                        if o.is_dma:
                            ins.then_inc(dsem[stream][o.slot], 16)
                        elif o.signal:
                            ins.then_inc(csem[stream][o.semi], 1)
                    for d in self.pending[stream]:
                        if d.is_dma:
                            if kd.get((d.stream, d.slot), -1) < d.n:
                                wait(e, d)
                        elif d.signal and not (d.stream == stream):
                            if kn[d.stream] < d.seq:
                                wait(e, d)
                return body

            for c in COMPUTE:
                pass
            with nc.Block() as block:
                block.sync(body_for("sync"))
                block.tensor(body_for("tensor"))
                block.vector(body_for("vector"))
                block.scalar(body_for("scalar"))
                block.gpsimd(body_for("gpsimd"))


class Tl:
    __slots__ = ("h", "b")

    def __init__(self, h, name=""):
        self.h = h
        self.b = Buf(name)


class Ctx:
    pass


def build_program(n_layers=DEPTH, stop_after=None, dbg=()):
    nc = bass.Bass("TRN2", target_bir_lowering=False)
    P = Prog()
    K = Ctx()
    K.nc, K.P = nc, P
    K.dbgset = set(dbg)

    def din(name, shape, dt=F32):
        return nc.dram_tensor(name, list(shape), dt, kind="ExternalInput").ap()

    def dscr(name, shape, dt=BF16):
        kind = "ExternalOutput" if name in dbg else "Internal"
        return nc.dram_tensor(name, list(shape), dt, kind=kind).ap()

    I = Ctx()
    I.xin = din("xin", [T, D])
    I.c_t = din("c_t", [128, 8])
    I.cctx_t = din("cctx_t", [128, 8])
    I.w_mod = din("w_mod", [DEPTH, D, 3 * D])
    I.b_mod = din("b_mod", [DEPTH, 3 * D])
    I.norm_w = din("norm_w", [DEPTH, D])
    I.w_in = din("w_in", [DEPTH, D, IN_W])
    I.convw_t = din("convw_t", [DEPTH, 128, 8, 5])
    I.convb_t = din("convb_t", [DEPTH, 128, 8])
    I.a_log = din("a_log", [DEPTH, 16])
    I.dt_bias = din("dt_bias", [DEPTH, 16])
    I.d_skip = din("d_skip", [DEPTH, 8])
    I.ssd_norm_w = din("ssd_norm_w", [DEPTH, 512])
    I.lam = din("lam", [DEPTH, 256])
    I.subln_w = din("subln_w", [DEPTH, 128])
    I.w_of = din("w_of", [DEPTH, 512, D])
    I.w_oa = din("w_oa", [DEPTH, 512, D])
    I.w_os = din("w_os", [DEPTH, 512, D])
    I.w_out = din("w_out", [DEPTH, D, D])
    I.norm_f = din("norm_f", [D])
    I.ident = din("ident", [128, 128], BF16)
    I.cosT = din("cosT", [128, SEQ])
    I.sinT = din("sinT", [128, SEQ])
    I.dftB = din("dftB", [128, 256], BF16)
    I.C4 = din("C4", [SEQ, SEQ], BF16)
    I.S4 = din("S4", [SEQ, SEQ], BF16)
    I.C2 = din("C2", [NCTX, NCTX], BF16)
    I.S2 = din("S2", [NCTX, NCTX], BF16)
    I.tri = din("tri", [128, 128])
    I.triT = din("triT", [128, 128])
    I.mask_f = din("mask_f", [128, 128])
    I.mask_b = din("mask_b", [128, 128])
    K.I = I
    out = nc.dram_tensor("out", [SEQ, D], F32, kind="ExternalOutput").ap()
    K.out = out

    S = Ctx()
    S.xs = dscr("xs", [T, D], F32)
    S.fuT = dscr("fuT", [512, T])
    S.fgT = dscr("fgT", [512, T])
    S.qT = dscr("qT", [512, T])
    S.kT = dscr("kT", [512, T])
    S.v_tm = dscr("v_tm", [T, 512])
    S.agT = dscr("agT", [512, T])
    S.z_tm = dscr("z_tm", [T, 512])
    S.xbcT = dscr("xbcT", [1024, T])
    S.gT = dscr("gT", [3072, T])
    S.ufT = dscr("ufT", [512, T])
    S.uaT = dscr("uaT", [512, T])
    S.usT = dscr("usT", [512, T])
    S.dbg = dscr("dbg", [128, 8192], F32)
    K.S = S

    with ExitStack() as top:
        ARENA = 52800
        arena = top.enter_context(nc.sbuf_tensor("arena", [128, ARENA], F32))
        K.base = 0
        K.off = 0

        def sb(name, shape, dt=F32):
            n = 1
            for v in shape[1:]:
                n *= v
            words = n if dt == F32 else (n + 1) // 2
            words = (words + 7) // 8 * 8
            assert K.off + words <= ARENA, (name, K.off, words)
            ap = arena[:, K.off:K.off + words]
            K.off += words
            if dt != F32:
                ap = ap.bitcast(dt)
            ap = ap[:, 0:n]
            if len(shape) == 3:
                ap = ap.rearrange("p (a b) -> p a b", a=shape[1], b=shape[2])
            elif len(shape) == 4:
                ap = ap.rearrange("p (a b c) -> p a b c", a=shape[1], b=shape[2], c=shape[3])
            return Tl(ap, name)
        K.sb = sb
        K.psall = top.enter_context(nc.psum_tensor("psall", [128, 4096], F32))
        K.ps = [Tl(K.psall[:, i * 512:(i + 1) * 512], f"ps{i}") for i in range(8)]
        K.psb = [K.psall[:, i * 512:(i + 1) * 512].bitcast(BF16) for i in range(8)]
        K.ident = sb("ident", [128, 128], BF16)
        K.ones_bf = sb("ones_bf", [128, 128], BF16)
        K.ones_f = sb("ones_f", [128, 128], F32)
        K.dtraw = sb("dtraw", [128, NT, 16], F32)
        K.mod = {k: sb("mod_" + k, [128, D], F32) for k in ("sc_l", "sh_l", "g_l", "sc_c", "sh_c", "g_c")}
        K.base = K.off
        P.op("sync", lambda e: e.dma_start(out=K.ident.h, in_=I.ident), writes=[K.ident.b], dma=True)
        P.op("vector", lambda e: e.memset(K.ones_bf.h, 1.0), writes=[K.ones_bf.b])
        P.op("vector", lambda e: e.memset(K.ones_f.h, 1.0), writes=[K.ones_f.b])
        phases = [phase_mod, phase_a1, phase_a2, phase_b, phase_c, phase_d, phase_e]
        done = False
        for l in range(n_layers):
            for ph in phases:
                ph(K, l, l == DEPTH - 1)
                P.barrier()
                if stop_after == (l, ph.__name__):
                    done = True
                    break
            if done:
                break
        P.emit(nc)
    return nc


def dma_in(K, out_ap, in_ap, writes, reads=(), q="sync"):
    return K.P.op(q, lambda e: e.dma_start(out=out_ap, in_=in_ap), reads=reads, writes=writes, dma=True)


def dma_out(K, out_ap, in_ap, reads, q="gpsimd"):
    return K.P.op(q, lambda e: e.dma_start(out=out_ap, in_=in_ap), reads=reads, writes=(), dma=True)


def mm(K, out_ap, pairs, reads, writes, start=True, stop=True):
    def fn(e):
        n = len(pairs)
        ins = None
        for i, (l, r) in enumerate(pairs):
            ins = e.matmul(out_ap, lhsT=l, rhs=r, start=(start and i == 0), stop=(stop and i == n - 1))
        return ins
    return K.P.op("tensor", fn, reads=reads, writes=writes)


def tok_tiles():
    r = [(0, NCTX, True)]
    for s in range(SEQ // 512):
        r.append((NCTX + s * 512, 512, False))
    return r


def phase_mod(K, l, last):
    P, I, sb = K.P, K.I, K.sb
    K.off = K.base
    cs = sb("cs", [128, 16], F32)
    lh = sb("lh", [128, 16, 128], F32)
    bB = sb("bB", [128, 3 * D], F32)
    nwB = sb("nwB", [128, D], F32)
    tmp = sb("mtmp", [128, 512], F32)
    wt = [sb(f"wmod{i}", [128, 1536], F32) for i in range(2)]
    dma_in(K, cs.h[:, 0:8], I.c_t, [cs.b])
    dma_in(K, cs.h[:, 8:16], I.cctx_t, [cs.b])
    dma_in(K, bB.h, I.b_mod[l].partition_broadcast(128), [bB.b])
    dma_in(K, nwB.h, I.norm_w[l].partition_broadcast(128), [nwB.b])
    P.op("scalar", lambda e: e.activation(out=cs.h, in_=cs.h, func=AF.Silu), reads=[cs.b], writes=[cs.b])
    P.op("vector", lambda e: e.tensor_copy(out=lh.h, in_=cs.h.unsqueeze(2).to_broadcast([128, 16, 128])),
         reads=[cs.b], writes=[lh.b])
    names = (("sh_l", "sc_l", "g_l"), ("sh_c", "sc_c", "g_c"))
    it = 0
    for half in range(2):
        for kc in range(8):
            w = wt[it % 2]
            it += 1
            dma_in(K, w.h, I.w_mod[l, kc * 128:(kc + 1) * 128, half * 1536:(half + 1) * 1536], [w.b])
            for who in range(2):
                for j in range(3):
                    bank = K.ps[who * 3 + j]
                    mm(K, bank.h[:, :], [(lh.h[:, who * 8 + kc, :], w.h[:, j * 512:(j + 1) * 512])],
                       reads=[lh.b, w.b], writes=[bank.b], start=(kc == 0), stop=(kc == 7))
        for who in range(2):
            for j in range(3):
                jb = half * 3 + j
                kind, off = jb // 2, (jb % 2) * 512
                bank = K.ps[who * 3 + j]
                dst = K.mod[names[who][kind]]
                bsl = bB.h[:, jb * 512:(jb + 1) * 512]
                if kind == 1:
                    P.op("vector", lambda e, bank=bank, bsl=bsl: e.tensor_tensor(out=tmp.h, in0=bank.h[:, :], in1=bsl, op=ALU.add),
                         reads=[bank.b, bB.b], writes=[tmp.b])
                    P.op("vector", lambda e, dst=dst, off=off: e.scalar_tensor_tensor(
                        out=dst.h[:, off:off + 512], in0=tmp.h, scalar=1.0, in1=nwB.h[:, off:off + 512],
                        op0=ALU.add, op1=ALU.mult), reads=[tmp.b, nwB.b], writes=[dst.b])
                else:
                    P.op("vector", lambda e, bank=bank, bsl=bsl, dst=dst, off=off: e.tensor_tensor(
                        out=dst.h[:, off:off + 512], in0=bank.h[:, :], in1=bsl, op=ALU.add),
                        reads=[bank.b, bB.b], writes=[dst.b])


def phase_a1(K, l, last):
    P, I, S, sb = K.P, K.I, K.S, K.sb
    K.off = K.base
    K.hT = sb("hT_all", [128, 8, T], BF16)
    K.a2_base = K.off
    NB1 = 4
    xt = [sb(f"xt{i}", [128, D], F32) for i in range(NB1)]
    junk = sb("junk", [128, D], BF16)
    t1 = [sb(f"t1{i}", [128, D], F32) for i in range(NB1)]
    hb = [sb(f"hb{i}", [128, D], BF16) for i in range(NB1)]
    st = [sb(f"st{i}", [128, 2], F32) for i in range(NB1)]
    src = I.xin if l == 0 else S.xs

    def bufs(t):
        return xt[t % NB1], t1[t % NB1], hb[t % NB1], st[t % NB1]

    def stage1(t):
        x, tt, h, s2 = bufs(t)
        dma_in(K, x.h, src[t * 128:(t + 1) * 128, :], [x.b])
        P.op("scalar", lambda e, x=x, s2=s2: e.activation(out=junk.h, in_=x.h, func=AF.Square, accum_out=s2.h[:, 0:1]),
             reads=[x.b], writes=[junk.b, s2.b])
        P.op("scalar", lambda e, s2=s2: e.activation(out=s2.h[:, 1:2], in_=s2.h[:, 0:1], func=AF.Ln, scale=1.0 / D, bias=EPS),
             reads=[s2.b], writes=[s2.b])
        P.op("scalar", lambda e, s2=s2: e.activation(out=s2.h[:, 1:2], in_=s2.h[:, 1:2], func=AF.Exp, scale=-0.5),
             reads=[s2.b], writes=[s2.b])

    def stage2(t):
        x, tt, h, s2 = bufs(t)
        ctx = t < 2
        sc = K.mod["sc_c" if ctx else "sc_l"]
        sh = K.mod["sh_c" if ctx else "sh_l"]
        P.op("vector", lambda e, x=x, tt=tt, s2=s2, sc=sc: e.scalar_tensor_tensor(
            out=tt.h, in0=x.h, scalar=s2.h[:, 1:2], in1=sc.h, op0=ALU.mult, op1=ALU.mult),
            reads=[x.b, s2.b, sc.b], writes=[tt.b])
        P.op("gpsimd", lambda e, tt=tt, h=h, sh=sh: e.tensor_tensor(out=h.h, in0=tt.h, in1=sh.h, op=ALU.add),
             reads=[tt.b, sh.b], writes=[h.b])

    def stage3(t):
        x, tt, h, s2 = bufs(t)
        bank = K.ps[t % 2]
        pb = K.psb[t % 2]

        def tr(e, h=h, pb=pb):
            ins = None
            for kc in range(8):
                ins = e.transpose(out=pb[:, kc * 128:(kc + 1) * 128], in_=h.h[:, kc * 128:(kc + 1) * 128],
                                  identity=K.ident.h)
            return ins
        P.op("tensor", tr, reads=[h.b, K.ident.b], writes=[bank.b])
        P.op("scalar", lambda e, pb=pb, t=t: e.copy(out=K.hT.h[:, :, t * 128:(t + 1) * 128],
                                                   in_=pb.rearrange("p (a b) -> p a b", a=8, b=128)),
             reads=[bank.b], writes=[K.hT.b])

    for step in range(NT + 2):
        if step < NT:
            stage1(step)
        if 0 <= step - 1 < NT:
            stage2(step - 1)
        if 0 <= step - 2 < NT:
            stage3(step - 2)


def phase_a2(K, l, last):
    P, I, S, sb = K.P, K.I, K.S, K.sb
    K.off = K.a2_base
    Wv = I.w_in[l].rearrange("(kc p) n -> p kc n", p=128)
    wf = sb("wf", [128, 8, 512], F32)
    wbs = [sb(f"wb{i}", [128, 8, 512], BF16) for i in range(2)]
    wrs = [sb(f"wr{i}", [128, 8, 512], BF16) for i in range(2)]
    stg = [sb(f"stg{i}", [128, T], BF16) for i in range(2)]
    sub_base = K.off
    cnt = {"s": 0, "b": 0}
    tiles = tok_tiles()

    groups = [("fm", C_FU, 512, None, S.fuT), ("fm", C_FG, 512, AF.Silu, S.fgT),
              ("rope", C_Q, 512, None, S.qT), ("rope", C_K, 512, None, S.kT),
              ("tm", C_V, 512, None, S.v_tm), ("fm", C_AG, 512, AF.Silu, S.agT), ("tm", C_Z, 512, AF.Silu, S.z_tm),
              ("conv", C_XBC, 512, 0, None), ("conv", C_XBC + 512, 512, 1, None), ("dt", C_DT, 16, None, None)]
    for g in range(6):
        groups.append(("fm", C_GT + g * 512, 512, AF.Sigmoid, S.gT[g * 512:(g + 1) * 512, :]))
    started, loaded = set(), {}

    def start_load(i):
        if i >= len(groups) or i in started:
            return
        started.add(i)
        _, c0, ncols, _, _ = groups[i]
        dma_in(K, wf.h[:, :, 0:ncols], Wv[:, :, c0:c0 + ncols], [wf.b])

    def finish_load(i):
        if i >= len(groups) or i in loaded:
            return
        start_load(i)
        kind, c0, ncols, _, _ = groups[i]
        wb, wr = wbs[i % 2], wrs[i % 2]
        P.op("vector", lambda e: e.tensor_copy(out=wb.h[:, 0:5, 0:ncols], in_=wf.h[:, 0:5, 0:ncols]),
             reads=[wf.b], writes=[wb.b])
        P.op("gpsimd", lambda e: e.tensor_copy(out=wb.h[:, 5:8, 0:ncols], in_=wf.h[:, 5:8, 0:ncols]),
             reads=[wf.b], writes=[wb.b])
        if kind == "rope":
            def r1(e):
                ins = None
                for kc in range(8):
                    src = wf.h[:, kc, :].rearrange("p (n h s) -> p n h s", h=2, s=16)
                    dst = wr.h[:, kc, :].rearrange("p (n h s) -> p n h s", h=2, s=16)
                    ins = e.mul(out=dst[:, :, 0, :], in_=src[:, :, 1, :], mul=-1.0)
                return ins

            def r2(e):
                ins = None
                for kc in range(8):
                    src = wf.h[:, kc, :].rearrange("p (n h s) -> p n h s", h=2, s=16)
                    dst = wr.h[:, kc, :].rearrange("p (n h s) -> p n h s", h=2, s=16)
                    ins = e.tensor_copy(out=dst[:, :, 1, :], in_=src[:, :, 0, :])
                return ins
            P.op("scalar", r1, reads=[wf.b], writes=[wr.b])
            P.op("vector", r2, reads=[wf.b], writes=[wr.b])
        loaded[i] = (wb, wr)

    def next_bank():
        b = K.ps[cnt["b"] % 6]
        cnt["b"] += 1
        return b

    def proj(bank, wt, j, tok0, n):
        mm(K, bank.h[:, 0:n], [(wt.h[:, kc, j * 128:(j + 1) * 128], K.hT.h[:, kc, tok0:tok0 + n]) for kc in range(8)],
           reads=[wt.b, K.hT.b], writes=[bank.b])

    def fm_group(wb, func, dest, mid):
        for j in range(4):
            s = stg[cnt["s"] % 2]
            cnt["s"] += 1
            for si, (tok0, n, ctx) in enumerate(tiles):
                bank = next_bank()
                proj(bank, wb, j, tok0, n)
                if func is None and si % 2 == 1:
                    P.op("vector", lambda e, s=s, bank=bank, tok0=tok0, n=n: e.tensor_copy(
                        out=s.h[:, tok0:tok0 + n], in_=bank.h[:, 0:n]), reads=[bank.b], writes=[s.b])
                else:
                    f = AF.Copy if func is None else func
                    P.op("scalar", lambda e, s=s, bank=bank, tok0=tok0, n=n, f=f: e.activation(
                        out=s.h[:, tok0:tok0 + n], in_=bank.h[:, 0:n], func=f), reads=[bank.b], writes=[s.b])
            dma_out(K, dest[j * 128:(j + 1) * 128, :], s.h, reads=[s.b])
            if j == 1:
                mid()

    def tm_group(wb, func, dest, mid, tms):
        dv = dest.rearrange("(a p) n -> p a n", p=128)
        for t in range(NT):
            tm = tms[(t // 4) % 2]
            bank = next_bank()
            mm(K, bank.h[:, :], [(K.hT.h[:, kc, t * 128:(t + 1) * 128], wb.h[:, kc, :]) for kc in range(8)],
               reads=[wb.b, K.hT.b], writes=[bank.b])
            f = AF.Copy if func is None else func
            P.op("scalar", lambda e, tm=tm, bank=bank, t=t, f=f: e.activation(out=tm.h[:, t % 4, :], in_=bank.h[:, :], func=f),
                 reads=[bank.b], writes=[tm.b])
            if t % 4 == 3 or t == NT - 1:
                t0 = (t // 4) * 4
                na = t - t0 + 1
                dma_out(K, dv[:, t0:t0 + na, :], tm.h[:, 0:na, :], reads=[tm.b])
            if t == NT // 2:
                mid()

    def rope_group(wb, wr, dest, mid, rs):
        tabs, rt1, rt2, ro = rs
        rc = 0
        for si, (tok0, n, ctx) in enumerate(tiles):
            if not ctx:
                tab = tabs[si % 2]
                p0 = tok0 - NCTX
                dma_in(K, tab.h[:, 0, :], I.cosT[:, p0:p0 + 512], [tab.b])
                dma_in(K, tab.h[:, 1, :], I.sinT[:, p0:p0 + 512], [tab.b])
            for j in range(4):
                o = ro[rc % 4]
                a, b2 = rt1[rc % 2], rt2[rc % 2]
                rc += 1
                bank = next_bank()
                proj(bank, wb, j, tok0, n)
                if ctx:
                    P.op("scalar", lambda e, o=o, bank=bank, n=n: e.copy(out=o.h[:, 0:n], in_=bank.h[:, 0:n]),
                         reads=[bank.b], writes=[o.b])
                else:
                    bank2 = next_bank()
                    proj(bank2, wr, j, tok0, n)
                    P.op("vector", lambda e, a=a, bank=bank, tab=tab: e.tensor_tensor(
                        out=a.h, in0=bank.h[:, :], in1=tab.h[:, 0, :], op=ALU.mult), reads=[bank.b, tab.b], writes=[a.b])
                    P.op("vector", lambda e, b2=b2, bank2=bank2, tab=tab: e.tensor_tensor(
                        out=b2.h, in0=bank2.h[:, :], in1=tab.h[:, 1, :], op=ALU.mult), reads=[bank2.b, tab.b], writes=[b2.b])
                    P.op("gpsimd", lambda e, o=o, a=a, b2=b2: e.tensor_tensor(out=o.h, in0=a.h, in1=b2.h, op=ALU.add),
                         reads=[a.b, b2.b], writes=[o.b])
                dma_out(K, dest[j * 128:(j + 1) * 128, tok0:tok0 + n], o.h[:, 0:n], reads=[o.b])
            if si == 4:
                mid()

    def conv_group(wb, g, mid, cs):
        cst, acc, cw, cb = cs
        ci = 0
        for j in range(4):
            cc = g * 4 + j
            s = stg[cnt["s"] % 2]
            cnt["s"] += 1
            for si, (tok0, n, ctx) in enumerate(tiles):
                bank = next_bank()
                proj(bank, wb, j, tok0, n)
                col = tok0 + (2 if ctx else 6)
                P.op("scalar", lambda e, bank=bank, col=col, n=n: e.copy(out=cst.h[:, col:col + n], in_=bank.h[:, 0:n]),
                     reads=[bank.b], writes=[cst.b])
            for si, (tok0, n, ctx) in enumerate(tiles):
                col = tok0 + (2 if ctx else 6)
                a = acc[ci % 2]
                ci += 1
                P.op("vector", lambda e, a=a, col=col, n=n, cc=cc: e.tensor_scalar_mul(
                    out=a.h[:, 0:n], in0=cst.h[:, col - 2:col - 2 + n], scalar1=cw.h[:, cc, 0:1]),
                    reads=[cst.b, cw.b], writes=[a.b])
                for k in range(1, 5):
                    P.op("vector", lambda e, a=a, col=col, n=n, cc=cc, k=k: e.scalar_tensor_tensor(
                        out=a.h[:, 0:n], in0=cst.h[:, col - 2 + k:col - 2 + k + n], scalar=cw.h[:, cc, k:k + 1],
                        in1=a.h[:, 0:n], op0=ALU.mult, op1=ALU.add), reads=[cst.b, cw.b, a.b], writes=[a.b])
                P.op("scalar", lambda e, s=s, a=a, tok0=tok0, n=n, cc=cc: e.activation(
                    out=s.h[:, tok0:tok0 + n], in_=a.h[:, 0:n], func=AF.Silu, bias=cb.h[:, cc:cc + 1]),
                    reads=[a.b, cb.b], writes=[s.b])
            dma_out(K, S.xbcT[cc * 128:(cc + 1) * 128, :], s.h, reads=[s.b])
            if j == 1:
                mid()

    def dt_group(wb, mid):
        for t in range(NT):
            bank = K.ps[6] if t < 32 else K.ps[7]
            r0 = (t % 32) * 16
            mm(K, bank.h[:, r0:r0 + 16], [(K.hT.h[:, kc, t * 128:(t + 1) * 128], wb.h[:, kc, 0:16]) for kc in range(8)],
               reads=[wb.b, K.hT.b], writes=[bank.b])
        mid()
        P.op("vector", lambda e: e.tensor_copy(out=K.dtraw.h[:, 0:32, :],
                                               in_=K.ps[6].h[:, :].rearrange("p (a b) -> p a b", a=32, b=16)),
             reads=[K.ps[6].b], writes=[K.dtraw.b])
        P.op("vector", lambda e: e.tensor_copy(out=K.dtraw.h[:, 32:34, :],
                                               in_=K.ps[7].h[:, 0:32].rearrange("p (a b) -> p a b", a=2, b=16)),
             reads=[K.ps[7].b], writes=[K.dtraw.b])
        if "dbg" in K.dbgset:
            dma_out(K, S.dbg[:, 0:NT * 16], K.dtraw.h.rearrange("p a b -> p (a b)"), reads=[K.dtraw.b])

    finish_load(0)
    rs = cs = tms = None
    for i, (kind, c0, ncols, arg, dest) in enumerate(groups):
        wb, wr = loaded[i]
        start_load(i + 1)
        mid = (lambda i=i: finish_load(i + 1))
        if kind == "rope" and rs is None:
            rs = ([sb(f"rtab{k}", [128, 2, 512], F32) for k in range(2)],
                  [sb(f"rt1_{k}", [128, 512], F32) for k in range(2)],
                  [sb(f"rt2_{k}", [128, 512], F32) for k in range(2)],
                  [sb(f"ro{k}", [128, 512], BF16) for k in range(4)])
        if kind == "tm" and tms is None:
            P.barrier()
            K.off = sub_base
            tms = [sb(f"tm_{k}", [128, 4, 512], BF16) for k in range(2)]
            CW = T + 8
            cst = sb("cst", [128, CW], F32)
            acc = [sb(f"cacc{k}", [128, 512], F32) for k in range(2)]
            cw = sb("cw", [128, 8, 5], F32)
            cb = sb("cb", [128, 8], F32)
            dma_in(K, cw.h, I.convw_t[l], [cw.b])
            dma_in(K, cb.h, I.convb_t[l], [cb.b])
            P.op("gpsimd", lambda e: e.memset(cst.h, 0.0), writes=[cst.b])
            cs = (cst, acc, cw, cb)
        if kind == "fm":
            fm_group(wb, arg, dest, mid)
        elif kind == "rope":
            rope_group(wb, wr, dest, mid, rs)
        elif kind == "tm":
            tm_group(wb, arg, dest, mid, tms)
        elif kind == "conv":
            conv_group(wb, arg, mid, cs)
        elif kind == "dt":
            dt_group(wb, mid)
        finish_load(i + 1)


def phase_b(K, l, last):
    P, I, S, sb = K.P, K.I, K.S, K.sb
    K.off = K.base
    U = sb("U", [128, NT, 2, 512], BF16)
    dB = sb("dB", [128, 256], BF16)
    fin = [sb(f"fin{i}", [128, 4, 512], BF16) for i in range(2)]
    pieces = [(sb(f"Cp{i}", [128, 8, 512], BF16), sb(f"Sp{i}", [128, 8, 512], BF16)) for i in range(3)]
    fgb = [sb(f"fgb{i}", [128, 4, 512], BF16) for i in range(2)]
    uo = [sb(f"uo{i}", [128, 4, 512], BF16) for i in range(2)]
    fuv = S.fuT.rearrange("(g p) t -> p g t", p=128)
    fgv = S.fgT.rearrange("(g p) t -> p g t", p=128)
    ufv = S.ufT.rearrange("(g p) t -> p g t", p=128)
    dma_in(K, dB.h, I.dftB, [dB.b])
    tiles = tok_tiles()
    bi = 0
    for si, (tok0, n, ctx) in enumerate(tiles):
        if ctx and last:
            continue
        f = fin[si % 2]
        dma_in(K, f.h[:, :, 0:n], fuv[:, :, tok0:tok0 + n], [f.b])
        for tt in range(n // 128):
            t = tok0 // 128 + tt
            for b2 in range(2):
                bank = K.ps[(bi % 2) * 2 + b2]
                for gg in range(2):
                    g = b2 * 2 + gg
                    mm(K, bank.h[:, gg * 256:(gg + 1) * 256], [(f.h[:, g, tt * 128:(tt + 1) * 128], dB.h)],
                       reads=[f.b, dB.b], writes=[bank.b])
                bv = bank.h[:, :].rearrange("p (g c m) -> p g c m", g=2, c=2, m=128)
                P.op("scalar", lambda e, bv=bv, t=t, b2=b2: e.copy(
                    out=U.h[:, t, 0, b2 * 256:(b2 + 1) * 256].rearrange("p (g m) -> p g m", g=2), in_=bv[:, :, 0, :]),
                    reads=[bank.b], writes=[U.b])
                P.op("vector", lambda e, bv=bv, t=t, b2=b2: e.tensor_copy(
                    out=U.h[:, t, 1, b2 * 256:(b2 + 1) * 256].rearrange("p (g m) -> p g m", g=2), in_=bv[:, :, 1, :]),
                    reads=[bank.b], writes=[U.b])
            bi += 1
    C4v = I.C4.rearrange("(nt p) k -> p nt k", p=128)
    S4v = I.S4.rearrange("(nt p) k -> p nt k", p=128)
    pi = 0
    for kb in range(SEQ // 512):
        tok0 = NCTX + kb * 512
        fg = fgb[kb % 2]
        o = uo[kb % 2]
        dma_in(K, fg.h, fgv[:, :, tok0:tok0 + 512], [fg.b])
        banks = [K.ps[(kb % 2) * 4 + ch] for ch in range(4)]
        for pc in range(4):
            Cp, Sp = pieces[pi % 3]
            pi += 1
            dma_in(K, Cp.h, C4v[:, pc * 8:(pc + 1) * 8, kb * 512:(kb + 1) * 512], [Cp.b])
            dma_in(K, Sp.h, S4v[:, pc * 8:(pc + 1) * 8, kb * 512:(kb + 1) * 512], [Sp.b])
            for ch in range(4):
                pairs = []
                for nt in range(8):
                    t = 2 + pc * 8 + nt
                    pairs.append((U.h[:, t, 0, ch * 128:(ch + 1) * 128], Cp.h[:, nt, :]))
                    pairs.append((U.h[:, t, 1, ch * 128:(ch + 1) * 128], Sp.h[:, nt, :]))
                mm(K, banks[ch].h[:, :], pairs, reads=[U.b, Cp.b, Sp.b], writes=[banks[ch].b],
                   start=(pc == 0), stop=(pc == 3))
        for ch in range(4):
            P.op("vector", lambda e, o=o, fg=fg, ch=ch, bank=banks[ch]: e.tensor_tensor(
                out=o.h[:, ch, :], in0=bank.h[:, :], in1=fg.h[:, ch, :], op=ALU.mult),
                reads=[banks[ch].b, fg.b], writes=[o.b])
        dma_out(K, ufv[:, :, tok0:tok0 + 512], o.h, reads=[o.b])
    if not last:
        c2 = sb("c2", [128, 2, 2, 256], BF16)
        dma_in(K, c2.h[:, 0, :, :], I.C2.rearrange("(nt p) k -> p nt k", p=128), [c2.b])
        dma_in(K, c2.h[:, 1, :, :], I.S2.rearrange("(nt p) k -> p nt k", p=128), [c2.b])
        fg = fgb[0]
        o = uo[0]
        dma_in(K, fg.h[:, :, 0:NCTX], fgv[:, :, 0:NCTX], [fg.b])
        for ch in range(4):
            bank = K.ps[ch]
            pairs = []
            for nt in range(2):
                pairs.append((U.h[:, nt, 0, ch * 128:(ch + 1) * 128], c2.h[:, 0, nt, :]))
                pairs.append((U.h[:, nt, 1, ch * 128:(ch + 1) * 128], c2.h[:, 1, nt, :]))
            mm(K, bank.h[:, 0:NCTX], pairs, reads=[U.b, c2.b], writes=[bank.b])
            P.op("vector", lambda e, o=o, fg=fg, ch=ch, bank=bank: e.tensor_tensor(
                out=o.h[:, ch, 0:NCTX], in0=bank.h[:, 0:NCTX], in1=fg.h[:, ch, 0:NCTX], op=ALU.mult),
                reads=[bank.b, fg.b], writes=[o.b])
        dma_out(K, ufv[:, :, 0:NCTX], o.h[:, :, 0:NCTX], reads=[o.b])


def phase_c(K, l, last):
    P, I, S, sb = K.P, K.I, K.S, K.sb
    K.off = K.base
    lam_init = 0.8 - 0.6 * math.exp(-0.3 * l)
    kTs = sb("kTs", [128, 4, T], BF16)
    vs = sb("vs", [128, NT, 512], BF16)
    lamt = sb("lamt", [128, 4, 64], F32)
    lam2 = sb("lam2", [128, 2, 64], F32)
    ls = sb("ls", [128, 4], F32)
    wsub = sb("wsub", [128, 1], F32)
    qts = [sb(f"qt{i}", [128, 4, 512], BF16) for i in range(2)]
    ags = [sb(f"agt{i}", [128, 4, 512], BF16) for i in range(2)]
    uos = [sb(f"uao{i}", [128, 4, 512], BF16) for i in range(2)]
    Et = [sb(f"E{i}", [128, 512], BF16) for i in range(6)]
    R0, R1, T0, T1, A, RS, O = [sb(f"ep{i}", [128, 512], F32) for i in range(7)]
    SQ = sb("sq", [128, 512], BF16)
    ZA = [sb(f"za{i}", [128, 512], F32) for i in range(2)]
    OC = [sb(f"oc{i}", [128, 512], F32) for i in range(2)]
    kv = S.kT.rearrange("(h p) t -> p h t", p=128)
    qv = S.qT.rearrange("(h p) t -> p h t", p=128)
    agv = S.agT.rearrange("(h p) t -> p h t", p=128)
    uav = S.uaT.rearrange("(h p) t -> p h t", p=128)
    for h in range(4):
        dma_in(K, kTs.h[:, h, :], kv[:, h, :], [kTs.b])
    vv = S.v_tm.rearrange("(a p) n -> p a n", p=128)
    for a0 in range(0, NT, 8):
        a1 = min(NT, a0 + 8)
        dma_in(K, vs.h[:, a0:a1, :], vv[:, a0:a1, :], [vs.b])
    dma_in(K, lamt.h.rearrange("p a b -> p (a b)"), I.lam[l].partition_broadcast(128), [lamt.b])
    dma_in(K, wsub.h, I.subln_w[l].rearrange("(p o) -> p o", o=1), [wsub.b])
    P.op("vector", lambda e: e.tensor_tensor(out=lam2.h[:, 0, :], in0=lamt.h[:, 0, :], in1=lamt.h[:, 1, :], op=ALU.mult),
         reads=[lamt.b], writes=[lam2.b])
    P.op("vector", lambda e: e.tensor_tensor(out=lam2.h[:, 1, :], in0=lamt.h[:, 2, :], in1=lamt.h[:, 3, :], op=ALU.mult),
         reads=[lamt.b], writes=[lam2.b])
    P.op("vector", lambda e: e.reduce_sum(out=ls.h[:, 0:2], in_=lam2.h, axis=AX.X), reads=[lam2.b], writes=[ls.b])
    P.op("scalar", lambda e: e.activation(out=ls.h[:, 0:2], in_=ls.h[:, 0:2], func=AF.Exp), reads=[ls.b], writes=[ls.b])
    P.op("vector", lambda e: e.tensor_tensor(out=ls.h[:, 2:3], in0=ls.h[:, 1:2], in1=ls.h[:, 0:1], op=ALU.subtract),
         reads=[ls.b], writes=[ls.b])
    P.op("vector", lambda e: e.tensor_scalar_add(out=ls.h[:, 3:4], in0=ls.h[:, 2:3], scalar1=-lam_init),
         reads=[ls.b], writes=[ls.b])
    P.op("scalar", lambda e: e.mul(out=wsub.h, in_=wsub.h, mul=(1.0 - lam_init)), reads=[wsub.b], writes=[wsub.b])
    psO = [K.ps[4], K.ps[5]]
    psZ = [K.ps[6], K.ps[7]]
    ei = 0
    deferred = []
    for bi, (tok0, n, ctx) in enumerate(tok_tiles()):
        if ctx and last:
            continue
        ktiles = [0, 1] if ctx else list(range(NT))
        qt, agt, uo = qts[bi % 2], ags[bi % 2], uos[bi % 2]
        dma_in(K, qt.h[:, :, 0:n], qv[:, :, tok0:tok0 + n], [qt.b])
        dma_in(K, agt.h[:, :, 0:n], agv[:, :, tok0:tok0 + n], [agt.b])
        for h in range(4):
            nk = len(ktiles)

            def issue_S(ki, h=h, n=n, qt=qt):
                kt = ktiles[ki]
                for m in range(2):
                    bS = K.ps[(ki % 2) * 2 + m]
                    mm(K, bS.h[:, 0:n], [(kTs.h[m * 64:(m + 1) * 64, h, kt * 128:(kt + 1) * 128],
                                          qt.h[m * 64:(m + 1) * 64, h, 0:n])], reads=[kTs.b, qt.b], writes=[bS.b])
            issue_S(0)
            for ki, kt in enumerate(ktiles):
                if ki + 1 < nk:
                    issue_S(ki + 1)
                if ki == min(2, nk - 1) and deferred:
                    deferred.pop(0)()
                b0 = (ki % 2) * 2
                for m in range(2):
                    E = Et[ei % 6]
                    ei += 1
                    Em = E.h[:, 0:n]
                    P.op("scalar", lambda e, Em=Em, b0=b0, m=m, n=n: e.activation(
                        out=Em, in_=K.ps[b0 + m].h[:, 0:n], func=AF.Exp, scale=0.125),
                        reads=[K.ps[b0 + m].b], writes=[E.b])
                    mm(K, psO[m].h[:, 0:n], [(vs.h[:, kt, h * 128:(h + 1) * 128], Em)],
                       reads=[vs.b, E.b], writes=[psO[m].b], start=(ki == 0), stop=(ki == nk - 1))
                    if m == 0:
                        mm(K, psZ[0].h[:, 0:n], [(K.ones_bf.h, Em)], reads=[K.ones_bf.b, E.b], writes=[psZ[0].b],
                           start=(ki == 0), stop=(ki == nk - 1))
                    elif ki == 0:
                        P.op("vector", lambda e, Em=Em, n=n: e.tensor_copy(out=ZA[1].h[:, 0:n], in_=Em),
                             reads=[E.b], writes=[ZA[1].b])
                    else:
                        P.op("vector", lambda e, Em=Em, n=n: e.tensor_tensor(out=ZA[1].h[:, 0:n], in0=ZA[1].h[:, 0:n], in1=Em, op=ALU.add),
                             reads=[E.b, ZA[1].b], writes=[ZA[1].b])
            mm(K, psZ[1].h[:, 0:n], [(K.ones_f.h, ZA[1].h[:, 0:n])], reads=[K.ones_f.b, ZA[1].b], writes=[psZ[1].b])
            P.op("vector", lambda e, n=n: e.tensor_copy(out=OC[0].h[:, 0:n], in_=psO[0].h[:, 0:n]), reads=[psO[0].b], writes=[OC[0].b])
            P.op("vector", lambda e, n=n: e.tensor_copy(out=OC[1].h[:, 0:n], in_=psO[1].h[:, 0:n]), reads=[psO[1].b], writes=[OC[1].b])
            P.op("vector", lambda e, n=n: e.tensor_copy(out=R0.h[:, 0:n], in_=psZ[0].h[:, 0:n]), reads=[psZ[0].b], writes=[R0.b])
            P.op("vector", lambda e, n=n: e.reciprocal(out=R0.h[:, 0:n], in_=R0.h[:, 0:n]), reads=[R0.b], writes=[R0.b])
            P.op("vector", lambda e, n=n: e.reciprocal(out=R1.h[:, 0:n], in_=psZ[1].h[:, 0:n]), reads=[psZ[1].b], writes=[R1.b])
            P.op("gpsimd", lambda e, n=n: e.tensor_tensor(out=T0.h[:, 0:n], in0=OC[0].h[:, 0:n], in1=R0.h[:, 0:n], op=ALU.mult),
                 reads=[OC[0].b, R0.b], writes=[T0.b])
            P.op("vector", lambda e, n=n: e.tensor_tensor(out=T1.h[:, 0:n], in0=OC[1].h[:, 0:n], in1=R1.h[:, 0:n], op=ALU.mult),
                 reads=[OC[1].b, R1.b], writes=[T1.b])
            P.op("vector", lambda e, n=n: e.scalar_tensor_tensor(out=A.h[:, 0:n], in0=T1.h[:, 0:n], scalar=ls.h[:, 3:4],
                                                                in1=T0.h[:, 0:n], op0=ALU.mult, op1=ALU.add),
                 reads=[T0.b, T1.b, ls.b], writes=[A.b])
            P.op("gpsimd", lambda e, n=n: e.tensor_tensor(out=SQ.h[:, 0:n], in0=A.h[:, 0:n], in1=A.h[:, 0:n], op=ALU.mult),
                 reads=[A.b], writes=[SQ.b])
            def part2(n=n, h=h, uo=uo, agt=agt, tok0=tok0):
                bq = psZ[1]
                mm(K, bq.h[:, 0:n], [(K.ones_bf.h, SQ.h[:, 0:n])], reads=[K.ones_bf.b, SQ.b], writes=[bq.b])
                P.op("scalar", lambda e, n=n, bq=bq: e.activation(out=RS.h[:, 0:n], in_=bq.h[:, 0:n], func=AF.Ln, scale=1.0 / 128, bias=EPS),
                     reads=[bq.b], writes=[RS.b])
                P.op("scalar", lambda e, n=n: e.activation(out=RS.h[:, 0:n], in_=RS.h[:, 0:n], func=AF.Exp, scale=-0.5),
                     reads=[RS.b], writes=[RS.b])
                P.op("gpsimd", lambda e, n=n: e.tensor_tensor(out=O.h[:, 0:n], in0=A.h[:, 0:n], in1=RS.h[:, 0:n], op=ALU.mult),
                     reads=[A.b, RS.b], writes=[O.b])
                P.op("vector", lambda e, n=n, h=h, uo=uo, agt=agt: e.scalar_tensor_tensor(
                    out=uo.h[:, h, 0:n], in0=O.h[:, 0:n], scalar=wsub.h[:, 0:1], in1=agt.h[:, h, 0:n], op0=ALU.mult, op1=ALU.mult),
                    reads=[O.b, wsub.b, agt.b], writes=[uo.b])
                if h == 3:
                    dma_out(K, uav[:, :, tok0:tok0 + n], uo.h[:, :, 0:n], reads=[uo.b])
            deferred.append(part2)
    while deferred:
        deferred.pop(0)()


def phase_d(K, l, last):
    P, I, S, sb = K.P, K.I, K.S, K.sb
    K.off = K.base
    par = sb("par", [128, 40], F32)
    snw = sb("snw", [128, 512], F32)
    tri = sb("tri", [128, 128], F32)
    triT = sb("triT", [128, 128], F32)
    mf = sb("mf", [128, 128], F32)
    mb = sb("mb", [128, 128], F32)
    dt, lndt, ad, acs, tot, eacs, wst, biasL = [sb(f"d_{nm}", [128, NT, 16], F32) for nm in
                                                ("dt", "lndt", "ad", "acs", "tot", "eacs", "wst", "biasL")]
    tmpd, dec = dt, tot
    CT = sb("CT", [128, 2, T], BF16)
    bt2 = [sb(f"bt2_{i}", [128, 2, 128], BF16) for i in range(3)]
    DS = [sb(f"DS{i}", [128, 512], F32) for i in range(2)]
    XB = sb("XB", [128, NT, 768], BF16)
    SbE = sb("SbE", [128, NT, 512], BF16)
    xin6 = [sb(f"xin6_{i}", [128, 6, 128], BF16) for i in range(2)]
    zts = [sb(f"zt{i}", [128, 512], BF16) for i in range(4)]
    Sst = [sb("Sf", [128, 512], F32), sb("Sb", [128, 512], F32)]
    SfE = [sb(f"SfE{i}", [128, 512], BF16) for i in range(2)]
    xw = [sb(f"xw{i}", [128, 512], BF16) for i in range(2)]
    CBs = sb("CBs", [128, 2, 128], F32)
    Lrs = [sb(f"Lr{i}", [128, 8, 128], F32) for i in range(2)]
    MT4 = [[sb(f"MT{j}_{i}", [128, 8, 128], BF16) for i in range(2)] for j in range(2)]
    Y1, Y2, YZ = [sb(f"Y{i}", [128, 512], F32) for i in range(3)]
    junk = sb("djunk", [128, 512], BF16)
    s2 = sb("ds2", [128, 2], F32)
    usb = sb("usb", [128, 512], BF16)
    usT = [sb(f"usTs{i}", [128, 4, 512], BF16) for i in range(2)]

    dma_in(K, par.h[:, 0:16], I.a_log[l].partition_broadcast(128), [par.b])
    dma_in(K, par.h[:, 16:32], I.dt_bias[l].partition_broadcast(128), [par.b])
    dma_in(K, par.h[:, 32:40], I.d_skip[l].partition_broadcast(128), [par.b])
    dma_in(K, snw.h, I.ssd_norm_w[l].partition_broadcast(128), [snw.b])
    for t_, src in ((tri, I.tri), (triT, I.triT), (mf, I.mask_f), (mb, I.mask_b)):
        dma_in(K, t_.h, src, [t_.b])
    xv = S.xbcT.rearrange("(g p) t -> p g t", p=128)
    dma_in(K, CT.h, xv[:, 6:8, :], [CT.b])

    def bc16(ap):
        return ap.unsqueeze(1).to_broadcast([128, NT, 16])

    def bc64(ap):
        return ap.unsqueeze(2).to_broadcast([128, 8, 64])

    def v3(ap):
        return ap.rearrange("p (h q) -> p h q", h=8)

    P.op("vector", lambda e: e.tensor_tensor(out=dt.h, in0=K.dtraw.h, in1=bc16(par.h[:, 16:32]), op=ALU.add),
         reads=[K.dtraw.b, par.b], writes=[dt.b])
    P.op("scalar", lambda e: e.activation(out=dt.h, in_=dt.h, func=AF.Exp), reads=[dt.b], writes=[dt.b])
    P.op("scalar", lambda e: e.activation(out=dt.h, in_=dt.h, func=AF.Ln, bias=1.0), reads=[dt.b], writes=[dt.b])
    P.op("scalar", lambda e: e.activation(out=lndt.h, in_=dt.h, func=AF.Ln), reads=[dt.b], writes=[lndt.b])
    P.op("scalar", lambda e: e.activation(out=par.h[:, 0:16], in_=par.h[:, 0:16], func=AF.Exp), reads=[par.b], writes=[par.b])
    P.op("vector", lambda e: e.scalar_tensor_tensor(out=ad.h, in0=dt.h, scalar=-1.0, in1=bc16(par.h[:, 0:16]),
                                                    op0=ALU.mult, op1=ALU.mult), reads=[dt.b, par.b], writes=[ad.b])
    for c in range(NT):
        bank = K.ps[c // 16]
        r0 = (c % 16) * 32
        mm(K, bank.h[:, r0:r0 + 8], [(tri.h, ad.h[:, c, 0:8])], reads=[tri.b, ad.b], writes=[bank.b])
        mm(K, bank.h[:, r0 + 8:r0 + 16], [(triT.h, ad.h[:, c, 8:16])], reads=[triT.b, ad.b], writes=[bank.b])
        mm(K, bank.h[:, r0 + 16:r0 + 32], [(K.ones_f.h, ad.h[:, c, :])], reads=[K.ones_f.b, ad.b], writes=[bank.b])
    for bi_, (c0, ncz) in enumerate(((0, 16), (16, 16), (32, 2))):
        bank = K.ps[bi_]
        bv = bank.h[:, 0:ncz * 32].rearrange("p (a b) -> p a b", a=ncz, b=32)
        P.op("vector", lambda e, bv=bv, c0=c0, ncz=ncz: e.tensor_copy(out=acs.h[:, c0:c0 + ncz, :], in_=bv[:, :, 0:16]),
             reads=[bank.b], writes=[acs.b])
        P.op("vector", lambda e, bv=bv, c0=c0, ncz=ncz: e.tensor_copy(out=tot.h[:, c0:c0 + ncz, :], in_=bv[:, :, 16:32]),
             reads=[bank.b], writes=[tot.b])
    P.op("scalar", lambda e: e.activation(out=eacs.h, in_=acs.h, func=AF.Exp), reads=[acs.b], writes=[eacs.b])
    P.op("vector", lambda e: e.tensor_tensor(out=biasL.h, in0=lndt.h, in1=acs.h, op=ALU.subtract),
         reads=[lndt.b, acs.b], writes=[biasL.b])
    P.op("vector", lambda e: e.tensor_tensor(out=tmpd.h, in0=tot.h, in1=biasL.h, op=ALU.add),
         reads=[tot.b, biasL.b], writes=[tmpd.b])
    P.op("scalar", lambda e: e.activation(out=wst.h, in_=tmpd.h, func=AF.Exp), reads=[tmpd.b], writes=[wst.b])
    P.op("scalar", lambda e: e.activation(out=dec.h, in_=tot.h, func=AF.Exp), reads=[tot.b], writes=[dec.b])
    for c in range(NT):
        xi = xin6[c % 2]
        bank = K.ps[3 + c % 2]
        pb = K.psb[3 + c % 2]
        dma_in(K, xi.h, xv[:, 0:6, c * 128:(c + 1) * 128], [xi.b])

        def tr(e, xi=xi, pb=pb):
            ins = None
            for j in range(6):
                ins = e.transpose(out=pb[:, j * 128:(j + 1) * 128], in_=xi.h[:, j, :], identity=K.ident.h)
            return ins
        P.op("tensor", tr, reads=[xi.b, K.ident.b], writes=[bank.b])
        P.op("scalar", lambda e, pb=pb, c=c: e.copy(out=XB.h[:, c, :], in_=pb[:, 0:768]), reads=[bank.b], writes=[XB.b])

    psSt = K.ps[3]

    def make_xw(c, d, w):
        P.op("vector", lambda e: e.tensor_tensor(out=v3(w.h), in0=v3(XB.h[:, c, 0:512]),
                                                 in1=bc64(wst.h[:, c, d * 8:(d + 1) * 8]), op=ALU.mult),
             reads=[XB.b, wst.b], writes=[w.b])

    def apply_update(c, d, w):
        Sd = Sst[d]
        for g in range(2):
            mm(K, psSt.h[:, g * 256:(g + 1) * 256], [(XB.h[:, c, 512 + g * 128:512 + (g + 1) * 128], w.h[:, g * 256:(g + 1) * 256])],
               reads=[XB.b, w.b], writes=[psSt.b])
        P.op("vector", lambda e: e.tensor_tensor(out=v3(Sd.h), in0=v3(Sd.h), in1=bc64(dec.h[:, c, d * 8:(d + 1) * 8]), op=ALU.mult),
             reads=[Sd.b, dec.b], writes=[Sd.b])
        P.op("vector", lambda e: e.tensor_tensor(out=Sd.h, in0=Sd.h, in1=psSt.h[:, :], op=ALU.add),
             reads=[Sd.b, psSt.b], writes=[Sd.b])

    P.op("gpsimd", lambda e: e.memset(Sst[0].h, 0.0), writes=[Sst[0].b])
    P.op("gpsimd", lambda e: e.memset(Sst[1].h, 0.0), writes=[Sst[1].b])
    border = [1, 0] + list(range(NT - 1, 1, -1))
    make_xw(border[0], 1, xw[0])
    for bi2, c in enumerate(border):
        if bi2 + 1 < len(border):
            make_xw(border[bi2 + 1], 1, xw[(bi2 + 1) % 2])
        P.op("scalar", lambda e, c=c: e.copy(out=SbE.h[:, c, :], in_=Sst[1].h), reads=[Sst[1].b], writes=[SbE.b])
        apply_update(c, 1, xw[bi2 % 2])

    zv = S.z_tm
    usv = S.usT.rearrange("(g p) t -> p g t", p=128)
    psY, psCB, psT = K.ps[7], K.ps[4], K.ps[2]
    psR = [K.ps[5], K.ps[6]]
    psYo = [K.ps[0], K.ps[1]]
    need = [not (c < 2 and last) for c in range(NT)]
    tails = []

    def prep(c):
        if not need[c]:
            return
        bt = bt2[c % 3]
        dma_in(K, bt.h, xv[:, 4:6, c * 128:(c + 1) * 128], [bt.b])
        zt = zts[c % 4]
        dma_in(K, zt.h, zv[c * 128:(c + 1) * 128, :], [zt.b])
        ds = DS[c % 2]
        P.op("gpsimd", lambda e, c=c, ds=ds: e.tensor_tensor(out=v3(ds.h), in0=v3(XB.h[:, c, 0:512]), in1=bc64(par.h[:, 32:40]), op=ALU.mult),
             reads=[XB.b, par.b], writes=[ds.b])
        for g in range(2):
            mm(K, psCB.h[:, g * 128:(g + 1) * 128], [(bt.h[:, g, :], CT.h[:, g, c * 128:(c + 1) * 128])],
               reads=[bt.b, CT.b], writes=[psCB.b])
        P.op("scalar", lambda e: e.copy(out=CBs.h.rearrange("p a b -> p (a b)"), in_=psCB.h[:, 0:256]),
             reads=[psCB.b], writes=[CBs.b])
        for d in range(2):
            trd = tri if d == 0 else triT
            msk = mf if d == 0 else mb
            mt = MT4[c % 2][d]
            L = Lrs[d]
            for h in range(8):
                bk = psR[h // 4]
                mm(K, bk.h[:, (h % 4) * 128:(h % 4 + 1) * 128],
                   [(ad.h[:, c, d * 8 + h:d * 8 + h + 1].to_broadcast([128, 128]), trd.h)],
                   reads=[ad.b, trd.b], writes=[bk.b])
            for h in range(8):
                bk = psR[h // 4]
                P.op("scalar", lambda e, bk=bk, h=h, c=c, d=d, L=L: e.activation(
                    out=L.h[:, h, :], in_=bk.h[:, (h % 4) * 128:(h % 4 + 1) * 128], func=AF.Exp,
                    bias=biasL.h[:, c, d * 8 + h:d * 8 + h + 1]), reads=[bk.b, biasL.b], writes=[L.b])
            P.op("vector", lambda e, msk=msk, L=L: e.scalar_tensor_tensor(
                out=L.h, in0=L.h, scalar=1e30, in1=msk.h.unsqueeze(1).to_broadcast([128, 8, 128]),
                op0=ALU.min, op1=ALU.mult), reads=[L.b, msk.b], writes=[L.b])
            for g in range(2):
                P.op("vector", lambda e, g=g, mt=mt, L=L: e.tensor_tensor(
                    out=mt.h[:, g * 4:(g + 1) * 4, :], in0=L.h[:, g * 4:(g + 1) * 4, :],
                    in1=CBs.h[:, g, :].unsqueeze(1).to_broadcast([128, 4, 128]), op=ALU.mult),
                    reads=[L.b, CBs.b], writes=[mt.b])

    def body(c):
        ctx = c < 2
        sfe = SfE[c % 2]
        wf_ = xw[c % 2]
        make_xw(c, 0, wf_)
        if need[c]:
            zt = zts[c % 4]
            ds = DS[c % 2]
            mts = MT4[c % 2]
            P.op("scalar", lambda e, sfe=sfe: e.copy(out=sfe.h, in_=Sst[0].h), reads=[Sst[0].b], writes=[sfe.b])
            for h in range(8):
                mm(K, psY.h[:, h * 64:(h + 1) * 64], [(mts[0].h[:, h, :], XB.h[:, c, h * 64:(h + 1) * 64]),
                                                      (mts[1].h[:, h, :], XB.h[:, c, h * 64:(h + 1) * 64])],
                   reads=[mts[0].b, mts[1].b, XB.b], writes=[psY.b])
            for d in range(2):
                se = sfe.h if d == 0 else SbE.h[:, c, :]
                seb = sfe.b if d == 0 else SbE.b
                for g in range(2):
                    mm(K, psYo[d].h[:, g * 256:(g + 1) * 256], [(CT.h[:, g, c * 128:(c + 1) * 128], se[:, g * 256:(g + 1) * 256])],
                       reads=[CT.b, seb], writes=[psYo[d].b])
        apply_update(c, 0, wf_)
        if tails:
            tails.pop(0)()
        if not need[c]:
            return
        P.op("vector", lambda e, c=c: e.tensor_tensor(out=v3(Y1.h), in0=v3(psYo[0].h[:, :]), in1=bc64(eacs.h[:, c, 0:8]), op=ALU.mult),
             reads=[psYo[0].b, eacs.b], writes=[Y1.b])
        P.op("vector", lambda e, c=c: e.tensor_tensor(out=v3(Y2.h), in0=v3(psYo[1].h[:, :]), in1=bc64(eacs.h[:, c, 8:16]), op=ALU.mult),
             reads=[psYo[1].b, eacs.b], writes=[Y2.b])
        P.op("vector", lambda e: e.tensor_tensor(out=Y1.h, in0=Y1.h, in1=Y2.h, op=ALU.add), reads=[Y1.b, Y2.b], writes=[Y1.b])
        P.op("vector", lambda e, ds=ds: e.tensor_tensor(out=Y1.h, in0=Y1.h, in1=ds.h, op=ALU.add), reads=[Y1.b, ds.b], writes=[Y1.b])
        P.op("vector", lambda e: e.tensor_tensor(out=Y1.h, in0=Y1.h, in1=psY.h[:, :], op=ALU.add), reads=[Y1.b, psY.b], writes=[Y1.b])
        P.op("vector", lambda e, zt=zt: e.tensor_tensor(out=YZ.h, in0=Y1.h, in1=zt.h, op=ALU.mult), reads=[Y1.b, zt.b], writes=[YZ.b])
        P.op("scalar", lambda e: e.activation(out=junk.h, in_=YZ.h, func=AF.Square, accum_out=s2.h[:, 0:1]),
             reads=[YZ.b], writes=[junk.b, s2.b])
        P.op("scalar", lambda e: e.activation(out=s2.h[:, 1:2], in_=s2.h[:, 0:1], func=AF.Ln, scale=1.0 / 512, bias=EPS),
             reads=[s2.b], writes=[s2.b])
        P.op("scalar", lambda e: e.activation(out=s2.h[:, 1:2], in_=s2.h[:, 1:2], func=AF.Exp, scale=-0.5),
             reads=[s2.b], writes=[s2.b])
        P.op("vector", lambda e: e.scalar_tensor_tensor(out=usb.h, in0=YZ.h, scalar=s2.h[:, 1:2], in1=snw.h,
                                                        op0=ALU.mult, op1=ALU.mult), reads=[YZ.b, s2.b, snw.b], writes=[usb.b])
        def tail(c=c, ctx=ctx):
            pbT = K.psb[2]

            def tr2(e, pbT=pbT):
                ins = None
                for j in range(4):
                    ins = e.transpose(out=pbT[:, j * 128:(j + 1) * 128], in_=usb.h[:, j * 128:(j + 1) * 128], identity=K.ident.h)
                return ins
            P.op("tensor", tr2, reads=[usb.b, K.ident.b], writes=[psT.b])
            if ctx:
                grp0, slot, glen = 0, c, 2
            else:
                grp0 = 2 + ((c - 2) // 4) * 4
                slot, glen = c - grp0, 4
            ut = usT[(0 if ctx else 1 + (c - 2) // 4) % 2]
            P.op("scalar", lambda e, ut=ut, slot=slot, pbT=pbT: e.copy(
                out=ut.h[:, :, slot * 128:(slot + 1) * 128], in_=pbT[:, 0:512].rearrange("p (g m) -> p g m", g=4)),
                reads=[psT.b], writes=[ut.b])
            if slot == glen - 1:
                dma_out(K, usv[:, :, grp0 * 128:(grp0 + glen) * 128], ut.h[:, :, 0:glen * 128], reads=[ut.b])
        tails.append(tail)

    prep(0)
    for c in range(NT):
        if c + 1 < NT:
            prep(c + 1)
        body(c)
    while tails:
        tails.pop(0)()


def phase_e(K, l, last):
    P, I, S, sb = K.P, K.I, K.S, K.sb
    K.off = K.base
    wbr = [sb(f"wbr{i}", [128, 4, D], BF16) for i in range(3)]
    wo = sb("wo", [128, 8, D], BF16)
    wstage = [sb(f"wstage{i}", [128, 4, D], F32) for i in range(1)]
    uts = [sb(f"ut{i}", [128, 3, 4, 512], BF16) for i in range(2)]
    gts = [sb(f"gt{i}", [128, 3, 4, 512], BF16) for i in range(3)]
    yTs = [sb(f"yT{i}", [128, 8, 512], BF16) for i in range(2)]
    M2 = [[sb(f"M{j}_{i}", [128, 512], F32) for i in range(3)] for j in range(2)]
    xts = [sb(f"ext{i}", [128, D], F32) for i in range(3)]
    xns = [sb(f"exn{i}", [128, D], F32) for i in range(3)]
    tmp = [sb(f"etmp{i}", [128, 512], F32) for i in range(2)]
    srcs = [(I.w_of[l], wbr[0].h), (I.w_oa[l], wbr[1].h), (I.w_os[l], wbr[2].h),
            (I.w_out[l, 0:512, :], wo.h[:, 0:4, :]), (I.w_out[l, 512:1024, :], wo.h[:, 4:8, :])]
    wbufs = [wbr[0].b, wbr[1].b, wbr[2].b, wo.b, wo.b]
    for i, (src, dst) in enumerate(srcs):
        ws = wstage[0]
        dma_in(K, ws.h, src.rearrange("(kc p) n -> p kc n", p=128), [ws.b])
        P.op("vector", lambda e, ws=ws, dst=dst: e.tensor_copy(out=dst[:, 0:2, :], in_=ws.h[:, 0:2, :]), reads=[ws.b], writes=[wbufs[i]])
        P.op("gpsimd", lambda e, ws=ws, dst=dst: e.tensor_copy(out=dst[:, 2:4, :], in_=ws.h[:, 2:4, :]), reads=[ws.b], writes=[wbufs[i]])
    if last:
        nfB = sb("nfB", [128, D], F32)
        s2 = sb("es2", [128, 2], F32)
        junk = sb("ejunk", [128, D], BF16)
        dma_in(K, nfB.h, I.norm_f.partition_broadcast(128), [nfB.b])
    uviews = [t_.rearrange("(g p) t -> p g t", p=128) for t_ in (S.ufT, S.uaT, S.usT)]
    gv = S.gT.rearrange("(c p) t -> p c t", p=128)
    xsrc = I.xin if l == 0 else S.xs
    state = {"xi": 0, "gi": 0}
    stiles = [tl for tl in tok_tiles() if not (tl[2] and last)]

    def merge(si):
        tok0, n, ctx = stiles[si]
        ut = uts[si % 2]
        yT = yTs[si % 2]
        for br in range(3):
            dma_in(K, ut.h[:, br, :, 0:n], uviews[br][:, :, tok0:tok0 + n], [ut.b])
        for oc in range(8):
            if oc % 4 == 0:
                gth = gts[state["gi"] % 3]
                state["gi"] += 1
                for br in range(3):
                    dma_in(K, gth.h[:, br, :, 0:n], gv[:, br * 8 + oc:br * 8 + oc + 4, tok0:tok0 + n], [gth.b])
            M = M2[oc % 2]
            banks = [K.ps[br + 3 * (oc % 2)] for br in range(3)]
            for br in range(3):
                mm(K, banks[br].h[:, 0:n], [(wbr[br].h[:, kc, oc * 128:(oc + 1) * 128], ut.h[:, br, kc, 0:n]) for kc in range(4)],
                   reads=[wbr[br].b, ut.b], writes=[banks[br].b])
            for br in range(3):
                P.op("vector", lambda e, br=br, oc=oc, n=n, bank=banks[br], M=M, gth=gth: e.tensor_tensor(
                    out=M[br].h[:, 0:n], in0=bank.h[:, 0:n], in1=gth.h[:, br, oc % 4, 0:n], op=ALU.mult),
                    reads=[banks[br].b, gth.b], writes=[M[br].b])
            P.op("gpsimd", lambda e, n=n, M=M: e.tensor_tensor(out=M[0].h[:, 0:n], in0=M[0].h[:, 0:n], in1=M[1].h[:, 0:n], op=ALU.add),
                 reads=[M[0].b, M[1].b], writes=[M[0].b])
            P.op("gpsimd", lambda e, n=n, oc=oc, yT=yT, M=M: e.tensor_tensor(out=yT.h[:, oc, 0:n], in0=M[0].h[:, 0:n], in1=M[2].h[:, 0:n], op=ALU.add),
                 reads=[M[0].b, M[2].b], writes=[yT.b])

    def outproj(si):
        tok0, n, ctx = stiles[si]
        yT = yTs[si % 2]
        gA = K.mod["g_c" if ctx else "g_l"]
        for tt in range(n // 128):
            r0 = tok0 + tt * 128
            xt, xn = xts[state["xi"] % 3], xns[state["xi"] % 3]
            state["xi"] += 1
            dma_in(K, xt.h, xsrc[r0:r0 + 128, :], [xt.b])
            for half in range(2):
                bank = K.ps[6 + half]
                tp = tmp[half]
                hs = slice(half * 512, (half + 1) * 512)
                mm(K, bank.h[:, :], [(yT.h[:, kc, tt * 128:(tt + 1) * 128], wo.h[:, kc, hs]) for kc in range(8)],
                   reads=[yT.b, wo.b], writes=[bank.b])
                P.op("vector", lambda e, bank=bank, tp=tp, hs=hs, gA=gA: e.tensor_tensor(out=tp.h, in0=bank.h[:, :], in1=gA.h[:, hs], op=ALU.mult),
                     reads=[bank.b, gA.b], writes=[tp.b])
                P.op("vector", lambda e, tp=tp, xt=xt, xn=xn, hs=hs: e.tensor_tensor(out=xn.h[:, hs], in0=tp.h, in1=xt.h[:, hs], op=ALU.add),
                     reads=[tp.b, xt.b], writes=[xn.b])
            if not last:
                dma_out(K, S.xs[r0:r0 + 128, :], xn.h, reads=[xn.b])
            else:
                P.op("scalar", lambda e, xn=xn: e.activation(out=junk.h, in_=xn.h, func=AF.Square, accum_out=s2.h[:, 0:1]),
                     reads=[xn.b], writes=[junk.b, s2.b])
                P.op("scalar", lambda e: e.activation(out=s2.h[:, 1:2], in_=s2.h[:, 0:1], func=AF.Ln, scale=1.0 / D, bias=EPS),
                     reads=[s2.b], writes=[s2.b])
                P.op("scalar", lambda e: e.activation(out=s2.h[:, 1:2], in_=s2.h[:, 1:2], func=AF.Exp, scale=-0.5),
                     reads=[s2.b], writes=[s2.b])
                P.op("vector", lambda e, xn=xn, xt=xt: e.scalar_tensor_tensor(out=xt.h, in0=xn.h, scalar=s2.h[:, 1:2], in1=nfB.h,
                                                                               op0=ALU.mult, op1=ALU.mult),
                     reads=[xn.b, s2.b, nfB.b], writes=[xt.b])
                dma_out(K, K.out[r0 - NCTX:r0 - NCTX + 128, :], xt.h, reads=[xt.b])

    merge(0)
    for si in range(len(stiles)):
        if si + 1 < len(stiles):
            merge(si + 1)
        outproj(si)


_CONST = {}


def _constants():
    if _CONST:
        return _CONST
    bf = ml_dtypes.bfloat16
    c = _CONST
    c["ident"] = np.eye(128, dtype=np.float32).astype(bf)
    rows = np.repeat(np.arange(SEQ // 64, dtype=np.float32), 64)
    cols = np.tile(np.arange(64, dtype=np.float32), SEQ // 64)
    freqs = (np.float32(10000.0) ** (-np.arange(0, 32, 2, dtype=np.float32) / np.float32(32))).astype(np.float32)
    ang_r = rows[:, None] * freqs
    ang_c = cols[:, None] * freqs
    ang = np.concatenate([ang_r, ang_r, ang_c, ang_c], axis=-1).astype(np.float32)
    cosT = np.cos(ang).astype(np.float32).T
    sinT = np.sin(ang).astype(np.float32).T
    c["cosT"] = np.ascontiguousarray(np.concatenate([cosT, cosT], axis=0))
    c["sinT"] = np.ascontiguousarray(np.concatenate([sinT, sinT], axis=0))
    j = np.arange(128, dtype=np.float64)
    angB = 2 * np.pi * np.outer(j, j) / 128.0
    c["dftB"] = (np.concatenate([np.cos(angB), -np.sin(angB)], axis=1) / np.sqrt(128.0)).astype(np.float32).astype(bf)
    for key, n in (("4", SEQ), ("2", NCTX)):
        idx = np.arange(n, dtype=np.int64)
        ph = (np.outer(idx, idx) % n).astype(np.float64) * (2 * np.pi / n)
        sc = 1.0 / np.sqrt(float(n))
        c["C" + key] = (np.cos(ph) * sc).astype(np.float32).astype(bf)
        c["S" + key] = (np.sin(ph) * sc).astype(np.float32).astype(bf)
    s_ = np.arange(128)[:, None]
    l_ = np.arange(128)[None, :]
    c["tri"] = (s_ <= l_).astype(np.float32)
    c["triT"] = (s_ >= l_).astype(np.float32)
    c["mask_f"] = (l_ >= s_).astype(np.float32)
    c["mask_b"] = (l_ <= s_).astype(np.float32)
    return c


def make_in_maps(inputs):
    f = lambda a: np.ascontiguousarray(np.asarray(a, dtype=np.float32))
    x, c, ctx = f(inputs["x"]), f(inputs["c"]), f(inputs["ctx"])
    shared = {
        "cctx_t": np.ascontiguousarray(f(inputs["c_ctx"]).reshape(8, 128).T),
        "w_mod": f(inputs["w_mod"]), "b_mod": f(inputs["b_mod"]), "norm_w": f(inputs["norm_w"]),
        "w_in": f(inputs["w_in"]),
        "convw_t": np.ascontiguousarray(f(inputs["conv_w"]).reshape(DEPTH, 5, 8, 128).transpose(0, 3, 2, 1)),
        "convb_t": np.ascontiguousarray(f(inputs["conv_b"]).reshape(DEPTH, 8, 128).transpose(0, 2, 1)),
        "a_log": f(inputs["a_log"]).reshape(DEPTH, 16), "dt_bias": f(inputs["dt_bias"]).reshape(DEPTH, 16),
        "d_skip": f(inputs["d_skip"]), "ssd_norm_w": f(inputs["ssd_norm_w"]),
        "lam": f(inputs["lam"]).reshape(DEPTH, 256), "subln_w": f(inputs["subln_w"]),
        "w_of": f(inputs["w_of"]), "w_oa": f(inputs["w_oa"]), "w_os": f(inputs["w_os"]), "w_out": f(inputs["w_out"]),
        "norm_f": f(inputs["norm_f"]),
    }
    shared.update(_constants())
    maps = []
    for b in range(x.shape[0]):
        m = dict(shared)
        m["xin"] = np.ascontiguousarray(np.concatenate([ctx[b], x[b]], axis=0))
        m["c_t"] = np.ascontiguousarray(c[b].reshape(8, 128).T)
        maps.append(m)
    return maps


_NC_CACHE = {}


def kernel(**inputs):
    if "nc" not in _NC_CACHE:
        _NC_CACHE["nc"] = build_program()
    nc = _NC_CACHE["nc"]
    maps = make_in_maps(inputs)
    res = run_bass_kernel_spmd(nc, maps, core_ids=list(range(len(maps))))
    return np.stack([np.asarray(r["out"], dtype=np.float32) for r in res.results], axis=0)
```
